# Optimizing a Trainium2 kernel written in Bass

```python
import math
import jax
import jax.numpy as jnp
from jax import lax
import numpy as np

D_MODEL = 1024
BATCH = 4
SEQ = 4096
DEPTH = 4

CHUNK = 64
EPS = 1e-6
D_FF = 2816

SSD_HEADS = 8
SSD_HEAD_DIM = 64
SSD_INNER = SSD_HEADS * SSD_HEAD_DIM
SSD_GROUPS = 2
SSD_STATE = 128
SSD_CONV = 4
SSD_XBC = SSD_INNER + 2 * SSD_GROUPS * SSD_STATE
SSD_HPG = SSD_HEADS // SSD_GROUPS

S5_WIDTH = 512
S5_GROUP = 16
S5_GROUPS = S5_WIDTH // S5_GROUP
S5_STATE = 64

ATT_HEADS = 8
ATT_HEAD_DIM = 64
ATT_WIDTH = ATT_HEADS * ATT_HEAD_DIM
LEFT_CHUNKS = 8
BAND_CHUNKS = LEFT_CHUNKS + 1
BAND = BAND_CHUNKS * CHUNK
MAX_REL = 128

POOL_WINDOWS = (2, 4, 8, 16)
POOL_GROUPS = len(POOL_WINDOWS)
POOL_WIDTH = 512
POOL_GROUP = POOL_WIDTH // POOL_GROUPS
POOL_MAX = max(POOL_WINDOWS)

EVEN_IN = SSD_INNER + SSD_XBC + SSD_HEADS + S5_WIDTH
EVEN_OUT = SSD_INNER + S5_WIDTH
ODD_IN = 3 * ATT_WIDTH + POOL_WIDTH
ODD_OUT = ATT_WIDTH + POOL_WIDTH
N_EVEN = (DEPTH + 1) // 2
N_ODD = DEPTH // 2

kernel_name = 'hybrid_chunk_causal_encoder'


def _rmsnorm(x, gain):
    xf = x.astype(jnp.float32)
    y = xf * lax.rsqrt(jnp.mean(xf * xf, axis=-1, keepdims=True) + EPS)
    return (y * gain.astype(jnp.float32)).astype(x.dtype)


def _swiglu(h, w_gate, w_up, w_down):
    return (jax.nn.silu(h @ w_gate) * (h @ w_up)) @ w_down


def _causal_depthwise_conv(x, w, b):
    c = x.shape[-1]
    y = lax.conv_general_dilated(
        x.astype(jnp.float32), w.astype(jnp.float32)[:, None, :],
        window_strides=(1,), padding=[(SSD_CONV - 1, 0)],
        dimension_numbers=('NWC', 'WIO', 'NWC'), feature_group_count=c)
    return y + b.astype(jnp.float32)


def _ssd_mixer(zxbcdt, conv_w, conv_b, dt_bias, a_log, d_skip, norm_gain):
    f32 = jnp.float32
    bsz, s, _ = zxbcdt.shape
    nc = s // CHUNK
    z, xbc, dt = jnp.split(zxbcdt, [SSD_INNER, SSD_INNER + SSD_XBC], axis=-1)
    xbc = jax.nn.silu(_causal_depthwise_conv(xbc, conv_w, conv_b))
    xs, bm, cm = jnp.split(xbc, [SSD_INNER, SSD_INNER + SSD_GROUPS * SSD_STATE], axis=-1)
    xs = xs.reshape(bsz, nc, CHUNK, SSD_GROUPS, SSD_HPG, SSD_HEAD_DIM)
    bm = bm.reshape(bsz, nc, CHUNK, SSD_GROUPS, SSD_STATE)
    cm = cm.reshape(bsz, nc, CHUNK, SSD_GROUPS, SSD_STATE)
    dt = jax.nn.softplus(dt.astype(f32) + dt_bias.astype(f32))
    dt = dt.reshape(bsz, nc, CHUNK, SSD_GROUPS, SSD_HPG)
    a = -jnp.exp(a_log.astype(f32)).reshape(SSD_GROUPS, SSD_HPG)
    a_cs = jnp.cumsum(dt * a, axis=2)
    xdt = xs * dt[..., None]
    causal = jnp.tril(jnp.ones((CHUNK, CHUNK), dtype=bool))[:, :, None, None]
    seg = a_cs[:, :, :, None] - a_cs[:, :, None, :]
    decay = jnp.exp(jnp.where(causal, seg, -jnp.inf))
    cb = jnp.einsum('bclgn,bcsgn->bclsg', cm, bm)
    y_diag = jnp.einsum('bclsg,bclsgj,bcsgjp->bclgjp', cb, decay, xdt)
    to_end = jnp.exp(a_cs[:, :, -1:] - a_cs)
    states = jnp.einsum('bclgn,bclgj,bclgjp->bcgjpn', bm, to_end, xdt)
    chunk_decay = jnp.exp(a_cs[:, :, -1])

    def step(h, inp):
        st, dec = inp
        return dec[..., None, None] * h + st, h

    h0 = jnp.zeros((bsz, SSD_GROUPS, SSD_HPG, SSD_HEAD_DIM, SSD_STATE), f32)
    _, h_prev = lax.scan(step, h0, (jnp.moveaxis(states, 1, 0), jnp.moveaxis(chunk_decay, 1, 0)))
    h_prev = jnp.moveaxis(h_prev, 0, 1)
    y_off = jnp.einsum('bclgn,bcgjpn,bclgj->bclgjp', cm, h_prev, jnp.exp(a_cs))
    y = y_diag + y_off + d_skip.astype(f32).reshape(SSD_GROUPS, SSD_HPG, 1) * xs
    y = y.reshape(bsz, s, SSD_INNER) * jax.nn.silu(z.astype(f32))
    yg = y.reshape(bsz, s, SSD_GROUPS, SSD_INNER // SSD_GROUPS)
    yg = yg * lax.rsqrt(jnp.mean(yg * yg, axis=-1, keepdims=True) + EPS)
    return (yg.reshape(bsz, s, SSD_INNER) * norm_gain.astype(f32)).astype(zxbcdt.dtype)


def _complex_affine_combine(e1, e2):
    a1r, a1i, x1r, x1i = e1
    a2r, a2i, x2r, x2i = e2
    return (a2r * a1r - a2i * a1i,
            a2r * a1i + a2i * a1r,
            a2r * x1r - a2i * x1i + x2r,
            a2r * x1i + a2i * x1r + x2i)


def _s5_mixer(u, a_re, a_im, log_dt, b_re, b_im, c_re, c_im, d_skip, w_glu, b_glu):
    f32 = jnp.float32
    bsz, s, _ = u.shape
    uf = u.astype(f32).reshape(bsz, s, S5_GROUPS, S5_GROUP)
    ar = a_re.astype(f32)
    ai = a_im.astype(f32)
    dt = jnp.exp(log_dt.astype(f32))[:, None]
    mag = jnp.exp(dt * ar)
    abar_re = mag * jnp.cos(dt * ai)
    abar_im = mag * jnp.sin(dt * ai)
    den = ar * ar + ai * ai
    f_re = ((abar_re - 1.0) * ar + abar_im * ai) / den
    f_im = (abar_im * ar - (abar_re - 1.0) * ai) / den
    br = b_re.astype(f32)
    bi = b_im.astype(f32)
    bb_re = f_re[..., None] * br - f_im[..., None] * bi
    bb_im = f_re[..., None] * bi + f_im[..., None] * br
    drive_re = jnp.einsum('bsgi,gpi->sbgp', uf, bb_re)
    drive_im = jnp.einsum('bsgi,gpi->sbgp', uf, bb_im)
    shape_a = (s, 1, S5_GROUPS, S5_STATE)
    dec_re = jnp.broadcast_to(abar_re, shape_a)
    dec_im = jnp.broadcast_to(abar_im, shape_a)
    _, _, h_re, h_im = lax.associative_scan(
        _complex_affine_combine, (dec_re, dec_im, drive_re, drive_im), axis=0)
    y = (jnp.einsum('gip,sbgp->bsgi', c_re.astype(f32), h_re)
         - jnp.einsum('gip,sbgp->bsgi', c_im.astype(f32), h_im)
         + d_skip.astype(f32) * uf)
    y = y.reshape(bsz, s, S5_WIDTH)
    gate = jax.nn.gelu(y) @ w_glu.astype(f32) + b_glu.astype(f32)
    return (y * jax.nn.sigmoid(gate)).astype(u.dtype)


def _chunk_band_attention(q, k, v, q_gain, k_gain, rel_bias):
    f32 = jnp.float32
    bsz, s, _ = q.shape
    nc = s // CHUNK
    q = _rmsnorm(q.reshape(bsz, nc, CHUNK, ATT_HEADS, ATT_HEAD_DIM), q_gain)
    k = _rmsnorm(k.reshape(bsz, s, ATT_HEADS, ATT_HEAD_DIM), k_gain)
    v = v.reshape(bsz, s, ATT_HEADS, ATT_HEAD_DIM)
    front = ((0, 0), (LEFT_CHUNKS * CHUNK, 0), (0, 0), (0, 0))
    kp = jnp.pad(k, front).reshape(bsz, nc + LEFT_CHUNKS, CHUNK, ATT_HEADS, ATT_HEAD_DIM)
    vp = jnp.pad(v, front).reshape(bsz, nc + LEFT_CHUNKS, CHUNK, ATT_HEADS, ATT_HEAD_DIM)
    band_idx = jnp.arange(nc)[:, None] + jnp.arange(BAND_CHUNKS)[None, :]
    kb = kp[:, band_idx].reshape(bsz, nc, BAND, ATT_HEADS, ATT_HEAD_DIM)
    vb = vp[:, band_idx].reshape(bsz, nc, BAND, ATT_HEADS, ATT_HEAD_DIM)
    scores = jnp.einsum('bclhd,bckhd->bhclk', q.astype(f32), kb.astype(f32)) * (ATT_HEAD_DIM ** -0.5)
    q_off = LEFT_CHUNKS * CHUNK + jnp.arange(CHUNK)
    dist = q_off[:, None] - jnp.arange(BAND)[None, :]
    bias = rel_bias.astype(f32)[:, jnp.clip(dist, -MAX_REL, MAX_REL) + MAX_REL]
    valid = jnp.repeat(band_idx >= LEFT_CHUNKS, CHUNK, axis=1)
    scores = jnp.where(valid[None, None, :, None, :], scores + bias[None, :, None], -jnp.inf)
    probs = jax.nn.softmax(scores, axis=-1)
    out = jnp.einsum('bhclk,bckhd->bclhd', probs, vb.astype(f32))
    return out.reshape(bsz, s, ATT_WIDTH).astype(q.dtype)


def _multiscale_pool(u, w_pool, scale):
    f32 = jnp.float32
    bsz, s, _ = u.shape
    uf = u.astype(f32).reshape(bsz, s, POOL_GROUPS, POOL_GROUP)
    csum = jnp.pad(jnp.cumsum(uf, axis=1), ((0, 0), (POOL_MAX, 0), (0, 0), (0, 0)))
    pos = jnp.arange(s)
    means = []
    for g, w in enumerate(POOL_WINDOWS):
        win = csum[:, POOL_MAX:, g] - csum[:, POOL_MAX - w:POOL_MAX - w + s, g]
        count = jnp.minimum(pos + 1, w).astype(f32)[None, :, None]
        means.append(win / count)
    pooled = jnp.stack(means, axis=2) - uf
    y = jnp.einsum('bsgi,gio->bsgo', pooled, w_pool.astype(f32)).reshape(bsz, s, POOL_WIDTH)
    return (y * scale.astype(f32)).astype(u.dtype)


def setup_inputs(seed: int = 0) -> dict:
    key = jax.random.key(seed)
    keys = iter(jax.random.split(key, 48))
    f32 = jnp.float32

    def normal(shape, scale):
        return scale * jax.random.normal(next(keys), shape, f32)

    def gain(shape):
        return 1.0 + 0.05 * jax.random.normal(next(keys), shape, f32)

    def log_uniform(shape, lo, hi):
        return jax.random.uniform(next(keys), shape, f32, math.log(lo), math.log(hi))

    ssd_dt = jnp.exp(log_uniform((N_EVEN, SSD_HEADS), 1e-3, 1e-1))
    ssd_dt_bias = ssd_dt + jnp.log(-jnp.expm1(-ssd_dt))
    ssd_a_log = jnp.log(jax.random.uniform(next(keys), (N_EVEN, SSD_HEADS), f32, 1.0, 16.0))
    s5_shape = (N_EVEN, S5_GROUPS, S5_STATE)
    s5_a_re = -0.5 + normal(s5_shape, 0.01)
    s5_a_im = math.pi * jnp.arange(S5_STATE, dtype=f32) + normal(s5_shape, 0.01)
    s5_log_dt = log_uniform((N_EVEN, S5_GROUPS), 1e-3, 1e-1)
    return {
        'x': normal((BATCH, SEQ, D_MODEL), 1.0),
        'ffn1_norm': gain((DEPTH, D_MODEL)),
        'ffn1_w_gate': normal((DEPTH, D_MODEL, D_FF), D_MODEL ** -0.5),
        'ffn1_w_up': normal((DEPTH, D_MODEL, D_FF), D_MODEL ** -0.5),
        'ffn1_w_down': normal((DEPTH, D_FF, D_MODEL), D_FF ** -0.5),
        'mix_norm': gain((DEPTH, D_MODEL)),
        'even_w_in': normal((N_EVEN, D_MODEL, EVEN_IN), D_MODEL ** -0.5),
        'even_w_out': normal((N_EVEN, EVEN_OUT, D_MODEL), EVEN_OUT ** -0.5),
        'ssd_conv_w': normal((N_EVEN, SSD_CONV, SSD_XBC), SSD_CONV ** -0.5),
        'ssd_conv_b': normal((N_EVEN, SSD_XBC), 0.02),
        'ssd_dt_bias': ssd_dt_bias,
        'ssd_a_log': ssd_a_log,
        'ssd_d': gain((N_EVEN, SSD_HEADS)),
        'ssd_norm': gain((N_EVEN, SSD_INNER)),
        's5_a_re': s5_a_re,
        's5_a_im': s5_a_im,
        's5_log_dt': s5_log_dt,
        's5_b_re': normal((N_EVEN, S5_GROUPS, S5_STATE, S5_GROUP), (2 * S5_GROUP) ** -0.5),
        's5_b_im': normal((N_EVEN, S5_GROUPS, S5_STATE, S5_GROUP), (2 * S5_GROUP) ** -0.5),
        's5_c_re': normal((N_EVEN, S5_GROUPS, S5_GROUP, S5_STATE), S5_STATE ** -0.5),
        's5_c_im': normal((N_EVEN, S5_GROUPS, S5_GROUP, S5_STATE), S5_STATE ** -0.5),
        's5_d': normal((N_EVEN, S5_GROUPS, S5_GROUP), 1.0),
        's5_w_glu': normal((N_EVEN, S5_WIDTH, S5_WIDTH), S5_WIDTH ** -0.5),
        's5_b_glu': normal((N_EVEN, S5_WIDTH), 0.02),
        'odd_w_in': normal((N_ODD, D_MODEL, ODD_IN), D_MODEL ** -0.5),
        'odd_w_out': normal((N_ODD, ODD_OUT, D_MODEL), ODD_OUT ** -0.5),
        'attn_q_norm': gain((N_ODD, ATT_HEAD_DIM)),
        'attn_k_norm': gain((N_ODD, ATT_HEAD_DIM)),
        'attn_rel_bias': normal((N_ODD, ATT_HEADS, 2 * MAX_REL + 1), 0.1),
        'pool_w': normal((N_ODD, POOL_GROUPS, POOL_GROUP, POOL_GROUP), POOL_GROUP ** -0.5),
        'pool_scale': gain((N_ODD, POOL_WIDTH)),
        'ffn2_norm': gain((DEPTH, D_MODEL)),
        'ffn2_w_gate': normal((DEPTH, D_MODEL, D_FF), D_MODEL ** -0.5),
        'ffn2_w_up': normal((DEPTH, D_MODEL, D_FF), D_MODEL ** -0.5),
        'ffn2_w_down': normal((DEPTH, D_FF, D_MODEL), D_FF ** -0.5),
    }


def reference(x, ffn1_norm, ffn1_w_gate, ffn1_w_up, ffn1_w_down, mix_norm,
              even_w_in, even_w_out, ssd_conv_w, ssd_conv_b, ssd_dt_bias, ssd_a_log, ssd_d, ssd_norm,
              s5_a_re, s5_a_im, s5_log_dt, s5_b_re, s5_b_im, s5_c_re, s5_c_im, s5_d, s5_w_glu, s5_b_glu,
              odd_w_in, odd_w_out, attn_q_norm, attn_k_norm, attn_rel_bias, pool_w, pool_scale,
              ffn2_norm, ffn2_w_gate, ffn2_w_up, ffn2_w_down):
    for layer in range(DEPTH):
        i = layer // 2
        h = _rmsnorm(x, ffn1_norm[layer])
        x = x + 0.5 * _swiglu(h, ffn1_w_gate[layer], ffn1_w_up[layer], ffn1_w_down[layer])
        h = _rmsnorm(x, mix_norm[layer])
        if layer % 2 == 0:
            proj = h @ even_w_in[i]
            zxbcdt, u = jnp.split(proj, [SSD_INNER + SSD_XBC + SSD_HEADS], axis=-1)
            y_a = _ssd_mixer(zxbcdt, ssd_conv_w[i], ssd_conv_b[i], ssd_dt_bias[i],
                             ssd_a_log[i], ssd_d[i], ssd_norm[i])
            y_b = _s5_mixer(u, s5_a_re[i], s5_a_im[i], s5_log_dt[i], s5_b_re[i], s5_b_im[i],
                            s5_c_re[i], s5_c_im[i], s5_d[i], s5_w_glu[i], s5_b_glu[i])
            mixed = jnp.concatenate([y_a, y_b], axis=-1).astype(x.dtype) @ even_w_out[i]
        else:
            proj = h @ odd_w_in[i]
            q, k, v, u = jnp.split(proj, [ATT_WIDTH, 2 * ATT_WIDTH, 3 * ATT_WIDTH], axis=-1)
            y_c = _chunk_band_attention(q, k, v, attn_q_norm[i], attn_k_norm[i], attn_rel_bias[i])
            y_d = _multiscale_pool(u, pool_w[i], pool_scale[i])
            mixed = jnp.concatenate([y_c, y_d], axis=-1).astype(x.dtype) @ odd_w_out[i]
        x = x + mixed.astype(x.dtype)
        h = _rmsnorm(x, ffn2_norm[layer])
        x = x + 0.5 * _swiglu(h, ffn2_w_gate[layer], ffn2_w_up[layer], ffn2_w_down[layer])
    return x
```

```python
from concourse.bass_utils import run_bass_kernel_spmd
import concourse.bass as bass
import concourse.mybir as mybir

ENGS = ("pe", "act", "dve", "pool", "sp")
SEM_CAP = 24000
N_DMA_SLOTS = 24


class Reg:
    __slots__ = ("name", "w", "rs", "excl")

    def __init__(self, name, excl=False):
        self.name = name
        self.excl = excl
        self.w = None
        self.rs = []


class Op:
    __slots__ = ("eng", "pos", "fn", "waits", "inc", "dma_slot", "dma_val", "dma_prev")

    def __init__(self, eng, pos, fn):
        self.eng = eng
        self.pos = pos
        self.fn = fn
        self.waits = []
        self.inc = False
        self.dma_slot = None
        self.dma_val = 0


class Prog:
    def __init__(self, nc, es):
        self.nc = nc
        self.ops = {e: [] for e in ENGS}
        self.seen = {e: {e2: -1 for e2 in ENGS} for e in ENGS}
        self.seen_dma = {e: {} for e in ENGS}
        self.handles = {"pe": nc.tensor, "act": nc.scalar, "dve": nc.vector, "pool": nc.gpsimd, "sp": nc.sync}
        self.csems = {e: [] for e in ENGS}
        self.es = es
        self.dma_sems = [es.enter_context(nc.semaphore("dq%d" % i)) for i in range(N_DMA_SLOTS)]
        self.dma_uses = [0] * N_DMA_SLOTS
        self.dma_last = [None] * N_DMA_SLOTS
        self.dma_rr = 0
        self.base_counts = {e: 0 for e in ENGS}
        self.n_emitted = 0
        self.cum_hist = {}
        self.flushed = {e: 0 for e in ENGS}
        self.last_real = {e: -1 for e in ENGS}

    def _need(self, op, tok, skip_same=False):
        if tok is None:
            return
        e = op.eng
        if tok[0] == "e":
            _, e2, pos = tok
            if skip_same and e2 == e:
                return
            if e2 == e and e == "pe":
                if self.seen[e][e2] < pos:
                    self.seen[e][e2] = pos
                return
            if e2 == e and pos == op.pos:
                return
            if self.seen[e][e2] >= pos:
                return
            self.seen[e][e2] = pos
            self.ops[e2][pos - self.flushed[e2]].inc = True
            op.waits.append(tok)
        else:
            _, slot, val = tok
            if self.seen_dma[e].get(slot, 0) >= val:
                return
            self.seen_dma[e][slot] = val
            op.waits.append(tok)

    def I(self, eng, name, reads=(), writes=(), **kw):
        return self.op(eng, lambda h: getattr(h, name)(**kw), reads, writes)

    def D(self, eng, reads=(), writes=(), **kw):
        return self.dma(eng, lambda h: h.dma_start(**kw), reads, writes)

    def op(self, eng, fn, reads=(), writes=()):
        pos = self.flushed[eng] + len(self.ops[eng])
        o = Op(eng, pos, fn)
        locks = [x for x in list(reads) + list(writes) if x.excl]
        reads = [x for x in reads if not x.excl]
        writes = [x for x in writes if not x.excl]
        for r in reads:
            self._need(o, r.w)
        for w in writes:
            self._need(o, w.w)
            for t in w.rs:
                self._need(o, t)
        for l in locks:
            self._need(o, l.w, skip_same=True)
        tok = ("e", eng, pos)
        for r in reads:
            r.rs.append(tok)
        for w in writes:
            w.w = tok
            w.rs = []
        for l in locks:
            l.w = tok
        self.ops[eng].append(o)
        self.last_real[eng] = pos
        return o

    def dma(self, eng, fn, reads=(), writes=()):
        pos = self.flushed[eng] + len(self.ops[eng])
        o = Op(eng, pos, fn)
        for r in reads:
            self._need(o, r.w)
        for w in writes:
            self._need(o, w.w)
            for t in w.rs:
                self._need(o, t)
        slot = self.dma_rr
        self.dma_rr = (self.dma_rr + 1) % N_DMA_SLOTS
        if self.dma_uses[slot] > 0:
            self._need(o, ("d", slot, 16 * self.dma_uses[slot]))
        self.dma_uses[slot] += 1
        o.dma_slot = slot
        o.dma_val = 16 * self.dma_uses[slot]
        tok = ("d", slot, o.dma_val)
        for r in reads:
            r.rs.append(tok)
        for w in writes:
            w.w = tok
            w.rs = []
        self.ops[eng].append(o)
        return o

    def barrier_tokens(self):
        toks = []
        for e in ENGS:
            if self.last_real[e] >= 0:
                toks.append(("e", e, self.last_real[e]))
        for s in range(N_DMA_SLOTS):
            if self.dma_uses[s] > 0:
                toks.append(("d", s, 16 * self.dma_uses[s]))
        return toks

    def all_barrier(self):
        toks = self.barrier_tokens()
        for e in ENGS:
            pos = self.flushed[e] + len(self.ops[e])
            o = Op(e, pos, None)
            for t in toks:
                self._need(o, t)
            self.ops[e].append(o)

    def _csem(self, e, epoch):
        while len(self.csems[e]) <= epoch:
            self.csems[e].append(self.es.enter_context(self.nc.semaphore("c_%s_%d" % (e, len(self.csems[e])))))
        return self.csems[e][epoch]

    def flush(self, block):
        cum = {}
        for e in ENGS:
            c = self.base_counts[e]
            for o in self.ops[e]:
                if o.inc:
                    c += 1
                cum[(e, o.pos)] = c
        self.cum_hist.update(cum)

        def emit(e, h):
            for o in self.ops[e]:
                for t in o.waits:
                    if t[0] == "e":
                        c = self.cum_hist[(t[1], t[2])]
                        ep, v = divmod(c - 1, SEM_CAP)
                        h.wait_ge(self._csem(t[1], ep), v + 1)
                    else:
                        h.wait_ge(self.dma_sems[t[1]], t[2])
                if o.fn is None:
                    continue
                ins = o.fn(h)
                if o.dma_slot is not None:
                    ins.then_inc(self.dma_sems[o.dma_slot], 16)
                elif o.inc:
                    c = self.cum_hist[(e, o.pos)]
                    ep, v = divmod(c - 1, SEM_CAP)
                    ins.then_inc(self._csem(e, ep), 1)
                self.n_emitted += 1

        for e in ENGS:
            c = self.base_counts[e]
            for o in self.ops[e]:
                if o.inc:
                    c += 1
            if c > 0:
                self._csem(e, (c - 1) // SEM_CAP)

        block.tensor(lambda h: emit("pe", h))
        block.scalar(lambda h: emit("act", h))
        block.vector(lambda h: emit("dve", h))
        block.gpsimd(lambda h: emit("pool", h))
        block.sync(lambda h: emit("sp", h))
        for e in ENGS:
            if self.ops[e]:
                self.base_counts[e] = cum[(e, self.ops[e][-1].pos)]
            self.flushed[e] += len(self.ops[e])
            self.ops[e] = []


import numpy as np
from contextlib import ExitStack
import concourse.bass as bass
import concourse.mybir as mybir

F32 = mybir.dt.float32
BF16 = mybir.dt.bfloat16
AF = mybir.ActivationFunctionType
ALU = mybir.AluOpType
AX = mybir.AxisListType

D = 1024
KC = 8
FF = 2816
FC = 22
NT = 512
EPS = 1e-6


class Ctx:
    def __init__(self, nc, P):
        self.nc = nc
        self.P = P
        self.es = None
        self.n = 0

    def sb(self, shape, dt, name=None):
        self.n += 1
        return self.es.enter_context(self.nc.sbuf_tensor("%s_%d" % (name or "t", self.n), shape, dt))

    def ps(self, shape, dt, name=None):
        self.n += 1
        return self.es.enter_context(self.nc.psum_tensor("%s_%d" % (name or "p", self.n), shape, dt))


def aslist(x):
    return list(x) if isinstance(x, (list, tuple)) else [x]


def regs(prefix, n, excl=False):
    return [Reg("%s%d" % (prefix, i), excl) for i in range(n)]


def emit_consts(C):
    P, nc = C.P, C.nc
    C.ones_f = C.sb([128, 128], F32, "ones")
    C.R_ones = Reg("ones")
    P.I("dve", "memset", writes=[C.R_ones], ap=C.ones_f[:], constant=1.0)


def emit_norm(C, X, RX, gain, Rgain, hb, Rh, sq, Rsq, ssq_ps, Rssq, rt, Rrt, rstd, Rrstd):
    P = C.P
    for kc in range(KC):
        s = kc % 2
        P.I("act", "activation", reads=[RX[kc]], writes=[Rsq[s]], out=sq[s][:], in_=X[:, kc, :], func=AF.Square)
        P.I("pe", "matmul", reads=[Rsq[s], C.R_ones], writes=[Rssq], out=ssq_ps[:], lhsT=C.ones_f[:], rhs=sq[s][:],
            start=(kc == 0), stop=(kc == KC - 1))
    P.I("act", "activation", reads=[Rssq], writes=[Rrt], out=rt[:], in_=ssq_ps[:], func=AF.Sqrt, scale=1.0 / D, bias=EPS)
    P.I("dve", "reciprocal", reads=[Rrt], writes=[Rrstd], out=rstd[:], in_=rt[:])
    for kc in range(KC):
        P.I("dve", "scalar_tensor_tensor", reads=[RX[kc], Rgain, Rrstd], writes=[Rh[kc]], out=hb[:, kc, :],
            in0=X[:, kc, :], scalar=gain[:, kc:kc + 1], in1=rstd[:], op0=ALU.mult, op1=ALU.mult)


def emit_ffn(C, xsrc, xdst, Rxsrc, Rxdst, wg_d, wu_d, wd_d, gain_d, ntiles):
    P, nc = C.P, C.nc
    Wg = C.sb([128, KC, FF], BF16, "Wg")
    Wu = C.sb([128, KC, FF], BF16, "Wu")
    Wd = C.sb([128, FC, D], BF16, "Wd")
    gain = C.sb([128, KC], F32, "gain")
    xt = [C.sb([128, KC, NT], F32, "xt") for _ in range(2)]
    sq = [C.sb([128, NT], F32, "sq") for _ in range(2)]
    hb = C.sb([128, KC, NT], BF16, "h")
    act = C.sb([128, FC, NT], BF16, "act")
    sg = [C.sb([128, NT], BF16, "sg") for _ in range(2)]
    rt = C.sb([128, NT], F32, "rt")
    rstd = C.sb([128, NT], F32, "rstd")
    ssq_ps = C.ps([128, NT], F32, "ssq")
    g_ps = [C.ps([128, NT], F32, "g") for _ in range(2)]
    u_ps = [C.ps([128, NT], F32, "u") for _ in range(2)]
    o_ps = [C.ps([128, NT], F32, "o") for _ in range(2)]

    RWg, RWu, RWd = regs("Wg", KC), regs("Wu", KC), regs("Wd", FC)
    Rgain = Reg("gain")
    Rxt = [regs("xt%d_" % b, KC) for b in range(2)]
    Rsq = regs("sq", 2)
    Rh = regs("h", KC)
    Ract = regs("act", FC)
    Rsg = regs("sg", 2)
    Rrt, Rrstd, Rssq = Reg("rt"), Reg("rstd"), Reg("ssq", True)
    Rg, Ru, Ro = regs("g", 2, True), regs("u", 2, True), regs("o", 2, True)

    wg_v = wg_d.rearrange("(kc p) f -> p kc f", p=128)
    wu_v = wu_d.rearrange("(kc p) f -> p kc f", p=128)
    wd_v = wd_d.rearrange("(fc p) d -> p fc d", p=128)
    P.D("sp", writes=[Rgain], out=gain[:], in_=gain_d)
    for kc in range(KC):
        P.D("pool", writes=[RWg[kc]], out=Wg[:, kc, :], in_=wg_v[:, kc, :])
        P.D("pool", writes=[RWu[kc]], out=Wu[:, kc, :], in_=wu_v[:, kc, :])
    for fc in range(FC):
        P.D("pool", writes=[RWd[fc]], out=Wd[:, fc, :], in_=wd_v[:, fc, :])

    xs_v = xsrc.rearrange("(kc p) t -> p kc t", p=128)
    xd_v = xdst.rearrange("(kc p) t -> p kc t", p=128)

    def load(t):
        b = t % 2
        P.D("sp", reads=aslist(Rxsrc[t]), writes=Rxt[b], out=xt[b][:], in_=xs_v[:, :, t * NT:(t + 1) * NT])

    load(0)
    for t in range(ntiles):
        b = t % 2
        if t + 1 < ntiles:
            load(t + 1)
        X = xt[b]
        emit_norm(C, X, Rxt[b], gain, Rgain, hb, Rh, sq, Rsq, ssq_ps, Rssq, rt, Rrt, rstd, Rrstd)
        for f in range(FC):
            s = f % 2
            for kc in range(KC):
                P.I("pe", "matmul", reads=[RWg[kc], Rh[kc]], writes=[Rg[s]], out=g_ps[s][:], lhsT=Wg[:, kc, f * 128:(f + 1) * 128], rhs=hb[:, kc, :],
                    start=(kc == 0), stop=(kc == KC - 1))
            for kc in range(KC):
                P.I("pe", "matmul", reads=[RWu[kc], Rh[kc]], writes=[Ru[s]], out=u_ps[s][:], lhsT=Wu[:, kc, f * 128:(f + 1) * 128], rhs=hb[:, kc, :],
                    start=(kc == 0), stop=(kc == KC - 1))
            P.I("act", "activation", reads=[Rg[s]], writes=[Rsg[s]], out=sg[s][:], in_=g_ps[s][:], func=AF.Silu)
            P.I("dve", "tensor_tensor", reads=[Ru[s], Rsg[s]], writes=[Ract[f]], out=act[:, f, :], in0=u_ps[s][:], in1=sg[s][:], op=ALU.mult)
        for dc in range(KC):
            s = dc % 2
            for f in range(FC):
                P.I("pe", "matmul", reads=[RWd[f], Ract[f]], writes=[Ro[s]], out=o_ps[s][:], lhsT=Wd[:, f, dc * 128:(dc + 1) * 128], rhs=act[:, f, :],
                    start=(f == 0), stop=(f == FC - 1))
            P.I("dve", "scalar_tensor_tensor", reads=[Ro[s], Rxt[b][dc]], writes=[Rxt[b][dc]], out=X[:, dc, :], in0=o_ps[s][:], scalar=0.5, in1=X[:, dc, :], op0=ALU.mult, op1=ALU.add)
        P.D("sp", reads=Rxt[b], writes=aslist(Rxdst[t]), out=xd_v[:, :, t * NT:(t + 1) * NT], in_=X[:])


def emit_odd(C, xsrc, xdst, Rxsrc, Rxdst, win_d, wout_d, poolw_d, bt_d, qg_d, kg_d, pscale_d, gain_d, ident_d, ntiles):
    P, nc = C.P, C.nc
    T = ntiles * NT
    Win = C.sb([128, KC, 2048], BF16, "Win")
    Wout = C.sb([128, KC, D], BF16, "Wout")
    Wp = C.sb([128, 4, 128], BF16, "Wp")
    bt = C.sb([128, 8, 640], F32, "bt")
    qg = C.sb([128, 1], F32, "qg")
    kg = C.sb([128, 1], F32, "kg")
    pscale = C.sb([128, 4], F32, "pscale")
    gain = C.sb([128, KC], F32, "gain")
    ident_f = C.sb([128, 128], F32, "identf")
    ident = C.sb([128, 128], BF16, "ident")
    ones_bf = C.sb([128, 64], BF16, "onesbf")
    bones = C.sb([128, 128], F32, "bones")
    KnT = C.sb([128, 4, T], BF16, "KnT")
    V = C.sb([128, ntiles * 4, 512], BF16, "V")
    X = C.sb([128, KC, NT], F32, "X")
    hb = C.sb([128, KC, NT], BF16, "h")
    qnT = C.sb([128, 4, NT], BF16, "qnT")
    yT = C.sb([128, 8, NT], BF16, "yT")
    U = C.sb([128, 4, 16 + NT], F32, "U")
    sA = C.sb([128, 16 + NT], F32, "sA")
    sB = C.sb([128, 16 + NT], F32, "sB")
    t16 = C.sb([128, 16], F32, "t16")
    pooled = C.sb([128, 4, NT], BF16, "pooled")
    rc = C.sb([128, 4, 16], F32, "rc")
    sq = [C.sb([128, NT], F32, "sq") for _ in range(2)]
    rt = C.sb([128, NT], F32, "rt")
    rstd = C.sb([128, NT], F32, "rstd")
    sbs = C.sb([128, 640], F32, "sbs")
    nmx = C.sb([128, 1], F32, "nmx")
    Pb = C.sb([128, 640], BF16, "Pb")
    PT = [C.sb([128, 5, 128], BF16, "PT") for _ in range(2)]
    rinv = C.sb([128, 128], F32, "rinv")
    pj = [C.ps([128, NT], F32, "pj") for _ in range(2)]
    ss = C.ps([128, NT], F32, "ss")
    S = C.ps([128, 1024], F32, "S")
    PTp = C.ps([128, 8, 128], BF16, "PTp")
    o_ps = C.ps([128, 512], F32, "o")
    r_ps = C.ps([128, 512], F32, "r")

    RWin, RWout = regs("Win", KC), regs("Wout", KC)
    RWp, Rbt, Rqg, Rkg, Rpsc, Rgain = Reg("Wp"), Reg("bt"), Reg("qg"), Reg("kg"), Reg("psc"), Reg("gain")
    Rident, Ridf, Rones, Rbones, Rrc = Reg("ident"), Reg("identf"), Reg("onesbf"), Reg("bones"), Reg("rc")
    RKn = [regs("Kn%d_" % t, 4) for t in range(ntiles)]
    RV = regs("V", ntiles * 4)
    RX, Rh = regs("X", KC), regs("h", KC)
    Rqn, Ry = regs("qn", 4), regs("y", 8)
    RU, RsA, RsB, Rt16, Rpooled = regs("U", 4), Reg("sA"), Reg("sB"), Reg("t16"), regs("pooled", 4)
    Rsq = regs("sq", 2)
    Rrt, Rrstd, Rss = Reg("rt"), Reg("rstd"), Reg("ss", True)
    Rsb, Rmx, RPb, RPT, Rrinv = Reg("sb"), Reg("mx"), Reg("Pb"), regs("PT", 2), Reg("rinv")
    Rpj, RS, RPTp, Ro, Rr = regs("pj", 2, True), Reg("S", True), Reg("PTp", True), Reg("o", True), Reg("r", True)

    P.D("sp", writes=[Rgain], out=gain[:], in_=gain_d)
    P.D("sp", writes=[Rqg], out=qg[:], in_=qg_d)
    P.D("sp", writes=[Rkg], out=kg[:], in_=kg_d)
    P.D("sp", writes=[Rpsc], out=pscale[:], in_=pscale_d)
    P.D("sp", writes=[Ridf], out=ident_f[:], in_=ident_d)
    P.D("sp", writes=[Rbt], out=bt[:], in_=bt_d)
    win_v = win_d.rearrange("(kc p) f -> p kc f", p=128)
    wout_v = wout_d.rearrange("(kc p) f -> p kc f", p=128)
    for kc in range(KC):
        P.D("pool", writes=[RWin[kc]], out=Win[:, kc, :], in_=win_v[:, kc, :])
    for kc in range(KC):
        P.D("pool", writes=[RWout[kc]], out=Wout[:, kc, :], in_=wout_v[:, kc, :])
    P.D("pool", writes=[RWp], out=Wp[:], in_=poolw_d.rearrange("g i o -> i g o"))
    P.I("dve", "tensor_copy", reads=[Ridf], writes=[Rident], out=ident[:], in_=ident_f[:])
    P.I("dve", "memset", writes=[Rones], ap=ones_bf[:], constant=1.0)
    P.I("dve", "memset", writes=[Rbones], ap=bones[:], constant=0.0)
    P.I("dve", "memset", reads=[], writes=[Rbones], ap=bones[0:64, 0:64], constant=1.0)
    P.I("dve", "memset", reads=[], writes=[Rbones], ap=bones[64:128, 64:128], constant=1.0)
    P.I("dve", "memset", writes=[Rbt], ap=bt[0:64, :, 576:640], constant=-1e30)
    P.I("dve", "memset", writes=[Rbt], ap=bt[64:128, :, 0:64], constant=-1e30)
    for g in range(4):
        P.I("pool", "memset", writes=[RU[g]], ap=U[:, g, 0:16], constant=0.0)
    WIN = (2, 4, 8, 16)
    for g, w in enumerate(WIN):
        for pos in range(w - 1):
            P.I("pool", "memset", writes=[Rrc], ap=rc[:, g, pos:pos + 1], constant=1.0 / (pos + 1))
        P.I("pool", "memset", writes=[Rrc], ap=rc[:, g, w - 1:16], constant=1.0 / w)

    xs_v = xsrc.rearrange("(kc p) t -> p kc t", p=128)
    xd_v = xdst.rearrange("(kc p) t -> p kc t", p=128)
    pjn = [0]

    def nextpj():
        pjn[0] += 1
        return pjn[0] % 2

    for t in range(ntiles):
        tsl = slice(t * NT, (t + 1) * NT)
        P.D("sp", reads=aslist(Rxsrc[t]), writes=RX, out=X[:], in_=xs_v[:, :, tsl])
        emit_norm(C, X, RX, gain, Rgain, hb, Rh, sq, Rsq, ss, Rss, rt, Rrt, rstd, Rrstd)
        for col0, gv, Rgv, is_q in ((0, qg, Rqg, True), (512, kg, Rkg, False)):
            for c in range(4):
                s = nextpj()
                for kc in range(KC):
                    P.I("pe", "matmul", reads=[RWin[kc], Rh[kc]], writes=[Rpj[s]], out=pj[s][:],
                        lhsT=Win[:, kc, col0 + c * 128:col0 + (c + 1) * 128], rhs=hb[:, kc, :],
                        start=(kc == 0), stop=(kc == KC - 1))
                P.I("act", "activation", reads=[Rpj[s]], writes=[Rsq[0]], out=sq[0][:], in_=pj[s][:], func=AF.Square)
                P.I("pe", "matmul", reads=[Rsq[0], Rbones], writes=[Rss], out=ss[:], lhsT=bones[:], rhs=sq[0][:],
                    start=True, stop=True)
                P.I("act", "activation", reads=[Rss], writes=[Rrt], out=rt[:], in_=ss[:], func=AF.Sqrt,
                    scale=1.0 / 64, bias=EPS)
                P.I("dve", "reciprocal", reads=[Rrt], writes=[Rrstd], out=rstd[:], in_=rt[:])
                if is_q:
                    dst, Rdst = qnT[:, c, :], Rqn[c]
                else:
                    dst, Rdst = KnT[:, c, tsl], RKn[t][c]
                P.I("dve", "scalar_tensor_tensor", reads=[Rpj[s], Rgv, Rrstd], writes=[Rdst], out=dst,
                    in0=pj[s][:], scalar=gv[:, 0:1], in1=rstd[:], op0=ALU.mult, op1=ALU.mult)
        for tb in range(4):
            s = nextpj()
            for kc in range(KC):
                P.I("pe", "matmul", reads=[RWin[kc], Rh[kc]], writes=[Rpj[s]], out=pj[s][:],
                    lhsT=hb[:, kc, tb * 128:(tb + 1) * 128], rhs=Win[:, kc, 1024:1536],
                    start=(kc == 0), stop=(kc == KC - 1))
            P.I("act", "activation", reads=[Rpj[s]], writes=[RV[4 * t + tb]], out=V[:, 4 * t + tb, :], in_=pj[s][:],
                func=AF.Copy)
        for g, w in enumerate(WIN):
            s = nextpj()
            for kc in range(KC):
                P.I("pe", "matmul", reads=[RWin[kc], Rh[kc]], writes=[Rpj[s]], out=pj[s][:],
                    lhsT=Win[:, kc, 1536 + g * 128:1536 + (g + 1) * 128], rhs=hb[:, kc, :],
                    start=(kc == 0), stop=(kc == KC - 1))
            P.I("act", "activation", reads=[Rpj[s]], writes=[RU[g]], out=U[:, g, 16:16 + NT], in_=pj[s][:], func=AF.Copy)
            L = 16 + NT
            P.I("pool", "tensor_tensor", reads=[RU[g]], writes=[RsA], out=sA[:, 1:L], in0=U[:, g, 1:L], in1=U[:, g, 0:L - 1], op=ALU.add)
            fin, Rfin = sA, RsA
            if w >= 4:
                P.I("pool", "tensor_tensor", reads=[RsA], writes=[RsB], out=sB[:, 3:L], in0=sA[:, 3:L], in1=sA[:, 1:L - 2], op=ALU.add)
                fin, Rfin = sB, RsB
            if w >= 8:
                P.I("pool", "tensor_tensor", reads=[RsB], writes=[RsA], out=sA[:, 7:L], in0=sB[:, 7:L], in1=sB[:, 3:L - 4], op=ALU.add)
                fin, Rfin = sA, RsA
            if w >= 16:
                P.I("pool", "tensor_tensor", reads=[RsA], writes=[RsB], out=sB[:, 15:L], in0=sA[:, 15:L], in1=sA[:, 7:L - 8], op=ALU.add)
                fin, Rfin = sB, RsB
            P.I("dve", "scalar_tensor_tensor", reads=[Rfin, RU[g]], writes=[Rpooled[g]], out=pooled[:, g, :],
                in0=fin[:, 16:L], scalar=1.0 / w, in1=U[:, g, 16:L], op0=ALU.mult, op1=ALU.subtract)
            if t == 0:
                P.I("dve", "tensor_tensor", reads=[Rfin, Rrc], writes=[Rt16], out=t16[:], in0=fin[:, 16:32], in1=rc[:, g, :], op=ALU.mult)
                P.I("dve", "tensor_tensor", reads=[Rt16, RU[g]], writes=[Rpooled[g]], out=pooled[:, g, 0:16], in0=t16[:], in1=U[:, g, 16:32], op=ALU.subtract)
            P.I("pool", "tensor_copy", reads=[RU[g]], writes=[RU[g]], out=U[:, g, 0:16], in_=U[:, g, NT:NT + 16])
            s = nextpj()
            P.I("pe", "matmul", reads=[RWp, Rpooled[g]], writes=[Rpj[s]], out=pj[s][:], lhsT=Wp[:, g, :], rhs=pooled[:, g, :],
                start=True, stop=True)
            P.I("dve", "tensor_scalar", reads=[Rpj[s], Rpsc], writes=[Ry[4 + g]], out=yT[:, 4 + g, :], in0=pj[s][:],
                scalar1=pscale[:, g:g + 1], scalar2=None, op0=ALU.mult)
        for mb in range(4):
            m = 4 * t + mb
            nkb = min(m, 4) + 1
            j0 = m - (nkb - 1)
            koff = (5 - nkb) * 128
            nk = nkb * 128
            lsl = slice(mb * 128, (mb + 1) * 128)
            kt_regs = lambda c: [RKn[tt][c] for tt in range(j0 // 4, t + 1)]
            for c in range(4):
                for e in range(2):
                    hh = 2 * c + e
                    hs = e * 64
                    for a, b_ in ([(0, min(nk, 512))] + ([(512, nk)] if nk > 512 else [])):
                        P.I("pe", "matmul", reads=[Rqn[c]] + kt_regs(c), writes=[RS], out=S[:, a:b_],
                            lhsT=qnT[hs:hs + 64, c, lsl], rhs=KnT[hs:hs + 64, c, j0 * 128 + a:j0 * 128 + b_],
                            start=True, stop=True)
                    P.I("dve", "scalar_tensor_tensor", reads=[RS, Rbt], writes=[Rsb], out=sbs[:, 0:nk], in0=S[:, 0:nk],
                        scalar=0.125, in1=bt[:, hh, koff:koff + nk], op0=ALU.mult, op1=ALU.add)
                    P.I("dve", "tensor_reduce", reads=[Rsb], writes=[Rmx], out=nmx[:], in_=sbs[:, 0:nk], axis=AX.X,
                        op=ALU.max, negate=True)
                    P.I("act", "activation", reads=[Rsb, Rmx], writes=[RPb], out=Pb[:, 0:nk], in_=sbs[:, 0:nk],
                        func=AF.Exp, bias=nmx[:], scale=1.0)
                    for kb in range(nkb):
                        P.I("pe", "transpose", reads=[RPb, Rident], writes=[RPTp], out=PTp[:, kb, :],
                            in_=Pb[:, kb * 128:(kb + 1) * 128], identity=ident[:])
                    P.I("act", "activation", reads=[RPTp], writes=[RPT[e]], out=PT[e][:, 0:nkb, :], in_=PTp[:, 0:nkb, :],
                        func=AF.Copy)
                    for kb in range(nkb):
                        P.I("pe", "matmul", reads=[RV[j0 + kb], RPT[e]], writes=[Ro], out=o_ps[hs:hs + 64, 0:128],
                            lhsT=V[:, j0 + kb, hh * 64:(hh + 1) * 64], rhs=PT[e][:, kb, :],
                            start=(kb == 0), stop=(kb == nkb - 1))
                    for kb in range(nkb):
                        P.I("pe", "matmul", reads=[Rones, RPT[e]], writes=[Rr], out=r_ps[hs:hs + 64, 0:128],
                            lhsT=ones_bf[:, 0:64], rhs=PT[e][:, kb, :], start=(kb == 0), stop=(kb == nkb - 1))
                P.I("dve", "reciprocal", reads=[Rr], writes=[Rrinv], out=rinv[:], in_=r_ps[:, 0:128])
                P.I("dve", "tensor_tensor", reads=[Ro, Rrinv], writes=[Ry[c]], out=yT[:, c, lsl], in0=o_ps[:, 0:128],
                    in1=rinv[:], op=ALU.mult)
        for dc in range(KC):
            s = nextpj()
            for c in range(8):
                P.I("pe", "matmul", reads=[RWout[c], Ry[c]], writes=[Rpj[s]], out=pj[s][:],
                    lhsT=Wout[:, c, dc * 128:(dc + 1) * 128], rhs=yT[:, c, :], start=(c == 0), stop=(c == 7))
            P.I("dve", "tensor_tensor", reads=[Rpj[s], RX[dc]], writes=[RX[dc]], out=X[:, dc, :], in0=pj[s][:],
                in1=X[:, dc, :], op=ALU.add)
        P.D("sp", reads=RX, writes=aslist(Rxdst[t]), out=xd_v[:, :, tsl], in_=X[:])


NTE = 256
QC = 128


def bview(ap, axis, shape):
    return ap.unsqueeze(axis).broadcast_to(shape)


def emit_even(C, xsrc, xdst, Rxsrc, Rxdst, dd, ntiles):
    P, nc = C.P, C.nc
    sb, ps = C.sb, C.ps
    Win = sb([128, KC, 2056], BF16, "Win")
    Wout = sb([128, KC, D], BF16, "Wout")
    Wglu = sb([128, 4, 512], BF16, "Wglu")
    diagw = sb([128, 8, 4, 128], BF16, "diagw")
    diagD = sb([128, 4, 128], BF16, "diagD")
    diagD5 = sb([128, 4, 128], BF16, "diagD5")
    gain = sb([128, KC], F32, "gain")
    convw = sb([128, 8, 4], F32, "convw")
    convb = sb([128, 8], F32, "convb")
    dtb = sb([8, 1], F32, "dtb")
    alog = sb([8, 1], F32, "alog")
    aneg = sb([8, 1], F32, "aneg")
    dskip = sb([128, 4], F32, "dskip")
    ngain = sb([128, 4], F32, "ngain")
    d5 = sb([128, 4], F32, "d5")
    bglu = sb([128, 4], F32, "bglu")
    ident_f = sb([128, 128], F32, "identf")
    ident = sb([128, 128], BF16, "ident")
    selH = sb([8, 8, 128], F32, "selH")
    tri = sb([128, 128], F32, "tri")
    maskneg = sb([128, 128], F32, "maskneg")
    X = sb([128, KC, NTE], F32, "X")
    hb = sb([128, KC, NTE], BF16, "h")
    zs = sb([128, 4, NTE], BF16, "zs")
    xbc = sb([128, 8, 3 + NTE], BF16, "xbc")
    xsT = sb([128, 4, NTE], BF16, "xsT")
    BT = sb([128, 2, NTE], BF16, "BT")
    CT = sb([128, 2, NTE], BF16, "CT")
    uT = sb([128, 4, NTE], BF16, "uT")
    yT = sb([128, 8, NTE], BF16, "yT")
    dte1 = sb([8, NTE], F32, "dte1")
    dtsp = sb([8, NTE], F32, "dtsp")
    dA = sb([8, NTE], F32, "dA")
    cs = sb([8, NTE], F32, "cs")
    sq = [sb([128, NTE], F32, "sq") for _ in range(2)]
    rt = sb([128, NTE], F32, "rt")
    rstd = sb([128, NTE], F32, "rstd")
    csdt = sb([128, 16], F32, "csdt")
    tmpL = sb([128, 8, 128], F32, "tmpL")
    ecs = sb([128, 8, 128], BF16, "ecs")
    eend = sb([128, 8], F32, "eend")
    d1 = sb([128, 8], F32, "d1")
    dte = sb([128, 8], F32, "dte")
    Mt = sb([128, 8, 128], BF16, "Mt")
    Cs = sb([128, 8, 128], BF16, "Cs")
    xdt = sb([128, 512], BF16, "xdt")
    xw = sb([128, 512], BF16, "xw")
    Btok = sb([128, 2, 128], BF16, "Btok")
    yg = sb([128, 4, 128], F32, "yg")
    sqy = [sb([128, 128], F32, "sqy") for _ in range(2)]
    rtg = sb([128, 128], F32, "rtg")
    rstdg = sb([128, 128], F32, "rstdg")
    ST = sb([128, 512], F32, "ST")
    STb = sb([128, 512], BF16, "STb")
    LD = sb([128, 16, 2, 128], BF16, "LD")
    LO = sb([128, 16, 2, 128], BF16, "LO")
    Ec = sb([128, 16, 128], F32, "Ec")
    Es = sb([128, 16, 128], F32, "Es")
    rtab = sb([128, 16, 128], F32, "rtab")
    big = sb([128, 2048], F32, "big")
    Zall = big[:].rearrange("p (a g i) -> p a g i", a=16, g=8)
    maskZ = sb([128, 16, 8], F32, "maskZ")
    nat = {k: sb([128, 16, 16], F32, k) for k in ("bre", "bim", "cre", "cim", "NBre", "NBim", "n1", "n2")}
    sm = {k: sb([128, 16], F32, k) for k in ("are", "aim", "ldt", "dt", "dar", "mag", "th", "c", "s", "ta", "tb", "c2", "s2",
                                              "abr", "abi", "abr1", "den", "rden", "fre", "fim", "u1", "u2",
                                              "ire", "iim", "v1", "v2")}
    cm = [sb([128, 16], F32, "cm%d" % k) for k in range(7)]
    smm = [sb([128, 16], F32, "sm%d" % k) for k in range(7)]
    Hp = sb([128, 16, 2], F32, "Hp")
    t1 = sb([128, 4, 128], F32, "t1")
    t2 = sb([128, 4, 128], F32, "t2")
    t3 = sb([128, 4, 128], F32, "t3")
    t4 = sb([128, 4, 128], F32, "t4")
    xm = big[:, 0:1024].rearrange("p (a q l) -> p a q l", a=4, q=2)
    ht = big[:, 1024:2048].rearrange("p (a q l) -> p a q l", a=4, q=2)
    H = sb([128, 4, 2, 128], F32, "H")
    Hb = sb([128, 4, 2, 128], BF16, "Hb")
    y5s = sb([128, 4, 128], F32, "y5s")
    ga = sb([128, 128], F32, "ga")
    gb = sb([128, 128], F32, "gb")
    gth = sb([128, 128], F32, "gth")
    ge = sb([128, 4, 128], BF16, "ge")
    sig = sb([128, 128], F32, "sig")
    PJ = ps([128, 2, 512], F32, "PJ")
    M1 = ps([128, 1024], BF16, "M1")
    M2 = ps([128, 512], F32, "M2")
    BC = ps([128, 8, 128], F32, "BC")
    GY = ps([128, 512], F32, "GY")
    SN = ps([128, 512], F32, "SN")
    PJf = PJ[:].rearrange("p s n -> p (s n)")
    drv = PJf.rearrange("p (r q n) -> p r q n", r=4, q=2)
    xs_tok = M1[:, 0:512].rearrange("p (c n) -> p c n", c=4)
    Btok_ps = M1[:, 512:768].rearrange("p (g n) -> p g n", g=2)
    csdt_ps = M2[:, 0:16]
    ss_ps = M2[:, 128:384].rearrange("p (g n) -> p g n", g=2)
    gate_ps = M2[:, 384:512]
    G_ps = GY[:, 0:256].rearrange("p (g n) -> p g n", g=2)
    y_ps = GY[:, 256:384]
    y5_ps = GY[:, 384:512]

    R = {}
    alias = {"csdt_ps": "M2", "ss": "M2", "gate": "M2", "xs_tok": "M1", "Btok_ps": "M1", "G": "GY", "y": "GY", "y5": "GY"}
    psum_names = ("M1", "M2", "GY", "BC", "SN")

    def r(name):
        name = alias.get(name, name)
        if name not in R:
            R[name] = Reg(name, name in psum_names)
        return R[name]

    RWin, RWout = regs("Win", KC), regs("Wout", KC)
    RX, Rh = regs("X", KC), regs("h", KC)
    Ry = regs("y", 8)
    Rpj = regs("pj", 2, True)
    R["pj0"], R["pj1"] = Rpj

    sp_loads = [("gain", gain), ("convw", convw), ("convb", convb), ("dtb", dtb), ("alog", alog), ("dskip", dskip),
                ("ngain", ngain), ("d5", d5), ("bglu", bglu), ("ident", ident_f), ("are", sm["are"]), ("aim", sm["aim"]),
                ("ldt", sm["ldt"]), ("bre", nat["bre"]), ("bim", nat["bim"]), ("cre", nat["cre"]), ("cim", nat["cim"])]
    for k, tgt in sp_loads:
        P.D("sp", writes=[r(k)], out=tgt[:], in_=dd[k])
    win_v = dd["win"].rearrange("(kc p) f -> p kc f", p=128)
    wout_v = dd["wout"].rearrange("(kc p) f -> p kc f", p=128)
    for kc in range(KC):
        P.D("pool", writes=[RWin[kc]], out=Win[:, kc, :], in_=win_v[:, kc, :])
    for kc in range(KC):
        P.D("pool", writes=[RWout[kc]], out=Wout[:, kc, :], in_=wout_v[:, kc, :])
    P.D("pool", writes=[r("Wglu")], out=Wglu[:], in_=dd["wglu"].rearrange("(c p) o -> p c o", p=128))

    I = P.I
    I("dve", "tensor_copy", reads=[r("ident")], writes=[r("identb")], out=ident[:], in_=ident_f[:])
    I("act", "activation", reads=[r("alog")], writes=[r("aneg")], out=aneg[:], in_=alog[:], func=AF.Exp)
    I("dve", "tensor_scalar", reads=[r("aneg")], writes=[r("aneg")], out=aneg[:], in0=aneg[:], scalar1=-1.0, scalar2=None,
      op0=ALU.mult)
    for c8 in range(8):
        for k in range(4):
            I("dve", "tensor_scalar", reads=[r("ident"), r("convw")], writes=[r("diagw")], out=diagw[:, c8, k, :],
              in0=ident_f[:], scalar1=convw[:, c8, k:k + 1], scalar2=None, op0=ALU.mult)
    for c in range(4):
        I("dve", "tensor_scalar", reads=[r("ident"), r("dskip")], writes=[r("diagD")], out=diagD[:, c, :], in0=ident_f[:],
          scalar1=dskip[:, c:c + 1], scalar2=None, op0=ALU.mult)
        I("dve", "tensor_scalar", reads=[r("ident"), r("d5")], writes=[r("diagD5")], out=diagD5[:, c, :], in0=ident_f[:],
          scalar1=d5[:, c:c + 1], scalar2=None, op0=ALU.mult)
    I("dve", "tensor_copy", reads=[r("ident")], writes=[r("selH")], out=selH[:], in_=bview(ident_f[0:8, 0:8], 2, [8, 8, 128]))
    I("dve", "tensor_tensor_scan", reads=[r("ident"), C.R_ones], writes=[r("tri")], out=tri[:], data0=C.ones_f[:],
      data1=ident_f[:], initial=0.0, op0=ALU.mult, op1=ALU.add)
    I("dve", "tensor_scalar", reads=[r("tri")], writes=[r("maskneg")], out=maskneg[:], in0=tri[:], scalar1=1.0, scalar2=1e30,
      op0=ALU.subtract, op1=ALU.mult)
    I("dve", "memset", writes=[r("ST")], ap=ST[:], constant=0.0)
    I("dve", "memset", writes=[r("STb")], ap=STb[:], constant=0.0)
    I("dve", "memset", writes=[r("xbc%d" % c8) for c8 in range(8)], ap=xbc[:, :, 0:3], constant=0.0)
    I("dve", "memset", writes=[r("ire"), r("iim")], ap=sm["ire"][:], constant=0.0)
    I("dve", "memset", writes=[r("iim")], ap=sm["iim"][:], constant=0.0)
    s = sm

    def tt(out, a, b, op, eng="dve", rd=(), wr=()):
        I(eng, "tensor_tensor", reads=[r(x) for x in rd], writes=[r(x) for x in wr], out=out, in0=a, in1=b, op=op)

    I("act", "activation", reads=[r("ldt")], writes=[r("dt")], out=s["dt"][:], in_=s["ldt"][:], func=AF.Exp)
    tt(s["dar"][:], s["dt"][:], s["are"][:], ALU.mult, rd=["dt", "are"], wr=["dar"])
    I("act", "activation", reads=[r("dar")], writes=[r("mag")], out=s["mag"][:], in_=s["dar"][:], func=AF.Exp)
    tt(s["th"][:], s["dt"][:], s["aim"][:], ALU.mult, rd=["dt", "aim"], wr=["th"])
    I("act", "activation", reads=[r("th")], writes=[r("s")], out=s["s"][:], in_=s["th"][:], func=AF.Sin, scale=1.0 / 32)
    I("act", "activation", reads=[r("th")], writes=[r("c")], out=s["c"][:], in_=s["th"][:], func=AF.Sin, scale=1.0 / 32,
      bias=float(np.pi / 2))

    def double_cs():
        tt(s["ta"][:], s["c"][:], s["c"][:], ALU.mult, rd=["c"], wr=["ta"])
        tt(s["tb"][:], s["s"][:], s["s"][:], ALU.mult, rd=["s"], wr=["tb"])
        I("dve", "scalar_tensor_tensor", reads=[r("c"), r("s")], writes=[r("s2")], out=s["s2"][:], in0=s["c"][:], scalar=2.0,
          in1=s["s"][:], op0=ALU.mult, op1=ALU.mult)
        tt(s["c"][:], s["ta"][:], s["tb"][:], ALU.subtract, rd=["ta", "tb"], wr=["c"])
        I("dve", "tensor_copy", reads=[r("s2")], writes=[r("s")], out=s["s"][:], in_=s["s2"][:])

    for _ in range(5):
        double_cs()
    for k in range(7):
        I("dve", "tensor_copy", reads=[r("c")], writes=[r("cm%d" % k)], out=cm[k][:], in_=s["c"][:])
        I("dve", "tensor_copy", reads=[r("s")], writes=[r("sm%d" % k)], out=smm[k][:], in_=s["s"][:])
        if k < 6:
            double_cs()
    c1, s1 = cm[0], smm[0]
    tt(s["abr"][:], s["mag"][:], c1[:], ALU.mult, rd=["mag", "cm0"], wr=["abr"])
    tt(s["abi"][:], s["mag"][:], s1[:], ALU.mult, rd=["mag", "sm0"], wr=["abi"])
    I("dve", "tensor_scalar", reads=[r("abr")], writes=[r("abr1")], out=s["abr1"][:], in0=s["abr"][:], scalar1=-1.0,
      scalar2=None, op0=ALU.add)
    tt(s["ta"][:], s["are"][:], s["are"][:], ALU.mult, rd=["are"], wr=["ta"])
    tt(s["tb"][:], s["aim"][:], s["aim"][:], ALU.mult, rd=["aim"], wr=["tb"])
    tt(s["den"][:], s["ta"][:], s["tb"][:], ALU.add, rd=["ta", "tb"], wr=["den"])
    I("dve", "reciprocal", reads=[r("den")], writes=[r("rden")], out=s["rden"][:], in_=s["den"][:])
    tt(s["u1"][:], s["abr1"][:], s["are"][:], ALU.mult, rd=["abr1", "are"], wr=["u1"])
    tt(s["u2"][:], s["abi"][:], s["aim"][:], ALU.mult, rd=["abi", "aim"], wr=["u2"])
    tt(s["u1"][:], s["u1"][:], s["u2"][:], ALU.add, rd=["u1", "u2"], wr=["u1"])
    tt(s["fre"][:], s["u1"][:], s["rden"][:], ALU.mult, rd=["u1", "rden"], wr=["fre"])
    tt(s["u1"][:], s["abi"][:], s["are"][:], ALU.mult, rd=["abi", "are"], wr=["u1"])
    tt(s["u2"][:], s["abr1"][:], s["aim"][:], ALU.mult, rd=["abr1", "aim"], wr=["u2"])
    tt(s["u1"][:], s["u1"][:], s["u2"][:], ALU.subtract, rd=["u1", "u2"], wr=["u1"])
    tt(s["fim"][:], s["u1"][:], s["rden"][:], ALU.mult, rd=["u1", "rden"], wr=["fim"])
    fre_b = bview(s["fre"][:], 2, [128, 16, 16])
    fim_b = bview(s["fim"][:], 2, [128, 16, 16])
    tt(nat["n1"][:], nat["bre"][:], fre_b, ALU.mult, rd=["bre", "fre"], wr=["n1"])
    tt(nat["n2"][:], nat["bim"][:], fim_b, ALU.mult, rd=["bim", "fim"], wr=["n2"])
    tt(nat["NBre"][:], nat["n1"][:], nat["n2"][:], ALU.subtract, rd=["n1", "n2"], wr=["NBre"])
    tt(nat["n1"][:], nat["bim"][:], fre_b, ALU.mult, rd=["bim", "fre"], wr=["n1"])
    tt(nat["n2"][:], nat["bre"][:], fim_b, ALU.mult, rd=["bre", "fim"], wr=["n2"])
    tt(nat["NBim"][:], nat["n1"][:], nat["n2"][:], ALU.add, rd=["n1", "n2"], wr=["NBim"])
    I("dve", "memset", writes=[r("maskZ")], ap=maskZ[:], constant=0.0)
    for pr in range(16):
        rr = pr % 4
        I("dve", "memset", writes=[r("maskZ")], ap=maskZ[0:64, pr, 2 * rr:2 * rr + 1], constant=1.0)
        I("dve", "memset", writes=[r("maskZ")], ap=maskZ[64:128, pr, 2 * rr + 1:2 * rr + 2], constant=1.0)
    mz_b = bview(maskZ[:], 3, [128, 16, 8, 16])
    LOv = LO[:].rearrange("p a q (g i) -> p a q g i", g=8)
    LDv = LD[:]
    for q, key in enumerate(("NBre", "NBim")):
        tt(Zall, bview(nat[key][:], 2, [128, 16, 8, 16]), mz_b, ALU.mult, rd=[key, "maskZ"], wr=["Zall"])
        for j in range(4):
            for k4 in range(4):
                I("pe", "transpose", reads=[r("Zall"), r("ident")], writes=[r("BC")], out=BC[:, k4, :],
                  in_=Zall[:, 4 * j + k4, :, :].rearrange("p g i -> p (g i)"), identity=ident_f[:])
            I("act", "activation", reads=[r("BC")], writes=[r("LD")], out=LDv[:, 4 * j:4 * j + 4, q, :], in_=BC[:, 0:4, :],
              func=AF.Copy)
    tt(LOv[:, :, 0, :, :], bview(nat["cre"][:], 2, [128, 16, 8, 16]), mz_b, ALU.mult, rd=["cre", "maskZ"], wr=["LO"])
    I("dve", "tensor_scalar", reads=[r("cim")], writes=[r("n1")], out=nat["n1"][:], in0=nat["cim"][:], scalar1=-1.0, scalar2=None,
      op0=ALU.mult)
    tt(LOv[:, :, 1, :, :], bview(nat["n1"][:], 2, [128, 16, 8, 16]), mz_b, ALU.mult, rd=["n1", "maskZ"], wr=["LO"])
    I("dve", "memset", writes=[r("Ec")], ap=Ec[:, :, 0:1], constant=1.0)
    I("dve", "memset", writes=[r("Es")], ap=Es[:, :, 0:1], constant=0.0)
    tA = tmpL[:].rearrange("p h l -> p (h l)")
    tB = H[:].rearrange("p a q l -> p (a q l)")
    for k in range(7):
        m = 1 << k
        cb = bview(cm[k][:], 2, [128, 16, m])
        sbv = bview(smm[k][:], 2, [128, 16, m])
        vA = tA[:, 0:16 * m].rearrange("p (a m) -> p a m", a=16)
        vB = tB[:, 0:16 * m].rearrange("p (a m) -> p a m", a=16)
        tt(vA, Ec[:, :, 0:m], cb, ALU.mult, rd=["Ec", "cm%d" % k], wr=["tmpL"])
        tt(vB, Es[:, :, 0:m], sbv, ALU.mult, rd=["Es", "sm%d" % k], wr=["H"])
        tt(Ec[:, :, m:2 * m], vA, vB, ALU.subtract, rd=["tmpL", "H"], wr=["Ec"])
        tt(vA, Es[:, :, 0:m], cb, ALU.mult, rd=["Es", "cm%d" % k], wr=["tmpL"])
        tt(vB, Ec[:, :, 0:m], sbv, ALU.mult, rd=["Ec", "sm%d" % k], wr=["H"])
        tt(Es[:, :, m:2 * m], vA, vB, ALU.add, rd=["tmpL", "H"], wr=["Es"])
    I("dve", "tensor_copy", reads=[r("mag")], writes=[r("rtab")], out=rtab[:], in_=bview(s["mag"][:], 2, [128, 16, 128]))

    xs_v = xsrc.rearrange("(kc p) t -> p kc t", p=128)
    xd_v = xdst.rearrange("(kc p) t -> p kc t", p=128)
    pjn = [0]

    def nextpj():
        pjn[0] += 1
        return pjn[0] % 2

    def proj(col0, ncols, rhs_slice=None):
        sl = nextpj()
        for kc in range(KC):
            I("pe", "matmul", reads=[RWin[kc], Rh[kc]], writes=[Rpj[sl]], out=PJ[0:ncols, sl, 0:NTE],
              lhsT=Win[:, kc, col0:col0 + ncols], rhs=hb[:, kc, :], start=(kc == 0), stop=(kc == KC - 1))
        return sl

    for t in range(ntiles):
        tsl = slice(t * NTE, (t + 1) * NTE)
        P.D("sp", reads=aslist(Rxsrc[t]), writes=RX, out=X[:], in_=xs_v[:, :, tsl])
        sl0 = nextpj()
        for kc in range(KC):
            s2 = kc % 2
            I("act", "activation", reads=[RX[kc]], writes=[r("sq%d" % s2)], out=sq[s2][:], in_=X[:, kc, :], func=AF.Square)
            I("pe", "matmul", reads=[r("sq%d" % s2), C.R_ones], writes=[Rpj[sl0]], out=PJ[:, sl0, 0:NTE], lhsT=C.ones_f[:],
              rhs=sq[s2][:], start=(kc == 0), stop=(kc == KC - 1))
        I("act", "activation", reads=[Rpj[sl0]], writes=[r("rt")], out=rt[:], in_=PJ[:, sl0, 0:NTE], func=AF.Sqrt,
          scale=1.0 / D, bias=EPS)
        I("dve", "reciprocal", reads=[r("rt")], writes=[r("rstd")], out=rstd[:], in_=rt[:])
        for kc in range(KC):
            I("dve", "scalar_tensor_tensor", reads=[RX[kc], r("gain"), r("rstd")], writes=[Rh[kc]], out=hb[:, kc, :],
              in0=X[:, kc, :], scalar=gain[:, kc:kc + 1], in1=rstd[:], op0=ALU.mult, op1=ALU.mult)
        for c in range(4):
            sl = proj(c * 128, 128)
            I("act", "activation", reads=[Rpj[sl]], writes=[r("zs%d" % c)], out=zs[:, c, :], in_=PJ[:, sl, 0:NTE], func=AF.Silu)
        for c8 in range(8):
            sl = proj(512 + c8 * 128, 128)
            I("act", "activation", reads=[Rpj[sl]], writes=[r("xbc%d" % c8)], out=xbc[:, c8, 3:3 + NTE], in_=PJ[:, sl, 0:NTE],
              func=AF.Copy)
            sl = nextpj()
            for k in range(4):
                I("pe", "matmul", reads=[r("diagw"), r("xbc%d" % c8)], writes=[Rpj[sl]], out=PJ[:, sl, 0:NTE],
                  lhsT=diagw[:, c8, k, :], rhs=xbc[:, c8, k:k + NTE], start=(k == 0), stop=(k == 3))
            if c8 < 4:
                dst, rn = xsT[:, c8, :], "xsT%d" % c8
            elif c8 < 6:
                dst, rn = BT[:, c8 - 4, :], "BT%d" % (c8 - 4)
            else:
                dst, rn = CT[:, c8 - 6, :], "CT%d" % (c8 - 6)
            I("act", "activation", reads=[Rpj[sl], r("convb")], writes=[r(rn)], out=dst, in_=PJ[:, sl, 0:NTE], func=AF.Silu,
              bias=convb[:, c8:c8 + 1])
            I("pool", "tensor_copy", reads=[r("xbc%d" % c8)], writes=[r("xbc%d" % c8)], out=xbc[:, c8, 0:3],
              in_=xbc[:, c8, NTE:NTE + 3])
        sl = proj(1536, 8)
        I("act", "activation", reads=[Rpj[sl], r("dtb")], writes=[r("dte1")], out=dte1[:], in_=PJ[0:8, sl, 0:NTE], func=AF.Exp,
          bias=dtb[:, 0:1])
        I("act", "activation", reads=[r("dte1")], writes=[r("dtsp")], out=dtsp[:], in_=dte1[:], func=AF.Ln, bias=1.0)
        I("dve", "tensor_scalar", reads=[r("dtsp"), r("aneg")], writes=[r("dA")], out=dA[:], in0=dtsp[:], scalar1=aneg[:, 0:1],
          scalar2=None, op0=ALU.mult)
        for c in range(4):
            sl = proj(1544 + c * 128, 128)
            I("act", "activation", reads=[Rpj[sl]], writes=[r("uT%d" % c)], out=uT[:, c, :], in_=PJ[:, sl, 0:NTE], func=AF.Copy)

        for qc in range(NTE // QC):
            lr = slice(qc * QC, (qc + 1) * QC)
            I("dve", "tensor_tensor_scan", reads=[r("dA"), C.R_ones], writes=[r("cs")], out=cs[:, lr], data0=C.ones_f[0:8, :],
              data1=dA[:, lr], initial=0.0, op0=ALU.mult, op1=ALU.add)
            I("pe", "transpose", reads=[r("cs"), r("ident")], writes=[r("csdt_ps")], out=csdt_ps[:, 0:8], in_=cs[:, lr],
              identity=ident_f[0:8, 0:8])
            I("pe", "transpose", reads=[r("dtsp"), r("ident")], writes=[r("csdt_ps")], out=csdt_ps[:, 8:16], in_=dtsp[:, lr],
              identity=ident_f[0:8, 0:8])
            I("act", "activation", reads=[r("csdt_ps")], writes=[r("csdt")], out=csdt[:], in_=csdt_ps, func=AF.Copy)
            for hh in range(8):
                I("pe", "matmul", reads=[r("selH"), r("cs")], writes=[r("BC")], out=BC[:, hh, :], lhsT=selH[:, hh, :],
                  rhs=cs[:, lr], start=True, stop=True)
            for hh in range(8):
                I("dve", "scalar_tensor_tensor", reads=[r("BC"), r("csdt"), r("maskneg")], writes=[r("tmpL")],
                  out=tmpL[:, hh, :], in0=BC[:, hh, :], scalar=csdt[:, hh:hh + 1], in1=maskneg[:], op0=ALU.subtract,
                  op1=ALU.add)
            I("act", "activation", reads=[r("tmpL")], writes=[r("tmpL")], out=tmpL[:], in_=tmpL[:], func=AF.Exp)
            I("act", "activation", reads=[r("BC")], writes=[r("ecs")], out=ecs[:], in_=BC[:], func=AF.Exp)
            I("act", "activation", reads=[r("BC")], writes=[r("eend")], out=eend[:], in_=BC[:, :, QC - 1], func=AF.Exp)
            I("dve", "tensor_tensor", reads=[r("BC"), r("csdt")], writes=[r("d1")], out=d1[:], in0=BC[:, :, QC - 1],
              in1=csdt[:, 0:8], op=ALU.subtract)
            I("act", "activation", reads=[r("d1")], writes=[r("dte")], out=dte[:], in_=d1[:], func=AF.Exp)
            for g in range(2):
                I("pe", "matmul", reads=[r("BT%d" % g), r("CT%d" % g)], writes=[r("G")], out=G_ps[:, g, :], lhsT=BT[:, g, lr],
                  rhs=CT[:, g, lr], start=True, stop=True)
            I("dve", "tensor_tensor", reads=[r("tmpL"), r("G")], writes=[r("Mt")],
              out=Mt[:].rearrange("p (g j) l -> p g j l", g=2), in0=tmpL[:].rearrange("p (g j) l -> p g j l", g=2),
              in1=bview(G_ps, 2, [128, 2, 4, 128]), op=ALU.mult)
            I("pool", "tensor_tensor", reads=[r("CT0"), r("CT1"), r("ecs")], writes=[r("Cs")],
              out=Cs[:].rearrange("p (g j) l -> p g j l", g=2), in0=ecs[:].rearrange("p (g j) l -> p g j l", g=2),
              in1=bview(CT[:, :, lr], 2, [128, 2, 4, 128]), op=ALU.mult)
            for c in range(4):
                I("pe", "transpose", reads=[r("xsT%d" % c), r("identb")], writes=[r("xs_tok")], out=xs_tok[:, c, :],
                  in_=xsT[:, c, lr], identity=ident[:])
            I("dve", "tensor_tensor", reads=[r("xs_tok"), r("csdt")], writes=[r("xdt")],
              out=xdt[:].rearrange("p (h q) -> p h q", h=8), in0=M1[:, 0:512].rearrange("p (h q) -> p h q", h=8),
              in1=bview(csdt[:, 8:16], 2, [128, 8, 64]), op=ALU.mult)
            I("pool", "tensor_tensor", reads=[r("xdt"), r("dte")], writes=[r("xw")],
              out=xw[:].rearrange("p (h q) -> p h q", h=8), in0=xdt[:].rearrange("p (h q) -> p h q", h=8),
              in1=bview(dte[:], 2, [128, 8, 64]), op=ALU.mult)
            for g in range(2):
                I("pe", "transpose", reads=[r("BT%d" % g), r("identb")], writes=[r("Btok_ps")], out=Btok_ps[:, g, :],
                  in_=BT[:, g, lr], identity=ident[:])
            I("act", "activation", reads=[r("Btok_ps")], writes=[r("Btok")], out=Btok[:], in_=Btok_ps, func=AF.Copy)
            for c in range(4):
                g = c // 2
                for e in range(2):
                    hh = 2 * c + e
                    ysl = y_ps[e * 64:(e + 1) * 64, :]
                    I("pe", "matmul", reads=[r("xdt"), r("Mt")], writes=[r("y")], out=ysl, lhsT=xdt[:, hh * 64:(hh + 1) * 64],
                      rhs=Mt[:, hh, :], start=True, stop=False)
                    I("pe", "matmul", reads=[r("STb"), r("Cs")], writes=[r("y")], out=ysl, lhsT=STb[:, hh * 64:(hh + 1) * 64],
                      rhs=Cs[:, hh, :], start=False, stop=False)
                    I("pe", "matmul", reads=[r("diagD"), r("xsT%d" % c)], writes=[r("y")], out=ysl,
                      lhsT=diagD[:, c, e * 64:(e + 1) * 64], rhs=xsT[:, c, lr], start=False, stop=True)
                I("dve", "tensor_tensor", reads=[r("y"), r("zs%d" % c)], writes=[r("yg%d" % c)], out=yg[:, c, :], in0=y_ps,
                  in1=zs[:, c, lr], op=ALU.mult)
                I("act", "activation", reads=[r("yg%d" % c)], writes=[r("sqy%d" % (c % 2))], out=sqy[c % 2][:], in_=yg[:, c, :],
                  func=AF.Square)
                I("pe", "matmul", reads=[r("sqy%d" % (c % 2)), C.R_ones], writes=[r("ss")], out=ss_ps[:, g, :], lhsT=C.ones_f[:],
                  rhs=sqy[c % 2][:], start=(c % 2 == 0), stop=(c % 2 == 1))
                if c % 2 == 1:
                    I("act", "activation", reads=[r("ss")], writes=[r("rtg")], out=rtg[:], in_=ss_ps[:, g, :], func=AF.Sqrt,
                      scale=1.0 / 256, bias=EPS)
                    I("dve", "reciprocal", reads=[r("rtg")], writes=[r("rstdg")], out=rstdg[:], in_=rtg[:])
                    for c2 in (c - 1, c):
                        I("dve", "scalar_tensor_tensor", reads=[r("yg%d" % c2), r("ngain"), r("rstdg")], writes=[Ry[c2]],
                          out=yT[:, c2, lr], in0=yg[:, c2, :], scalar=ngain[:, c2:c2 + 1], in1=rstdg[:], op0=ALU.mult,
                          op1=ALU.mult)
            for g in range(2):
                I("pe", "matmul", reads=[r("Btok"), r("xw")], writes=[r("SN")], out=SN[:, g * 256:(g + 1) * 256],
                  lhsT=Btok[:, g, :], rhs=xw[:, g * 256:(g + 1) * 256], start=True, stop=True)
            I("pool", "tensor_tensor", reads=[r("ST"), r("eend")], writes=[r("ST")], out=ST[:].rearrange("p (h q) -> p h q", h=8),
              in0=ST[:].rearrange("p (h q) -> p h q", h=8), in1=bview(eend[:], 2, [128, 8, 64]), op=ALU.mult)
            I("dve", "tensor_tensor", reads=[r("SN"), r("ST")], writes=[r("ST")], out=ST[:], in0=SN[:], in1=ST[:], op=ALU.add)
            I("act", "activation", reads=[r("ST")], writes=[r("STb")], out=STb[:], in_=ST[:], func=AF.Copy)

            for T4 in range(4):
                p4 = slice(4 * T4, 4 * T4 + 4)
                for rr in range(4):
                    for q in range(2):
                        I("pe", "matmul", reads=[r("LD"), r("uT%d" % T4)], writes=[Rpj[0], Rpj[1]], out=drv[:, rr, q, :],
                          lhsT=LD[:, 4 * T4 + rr, q, :], rhs=uT[:, T4, lr], start=True, stop=True)
                dre, dim_ = drv[:, :, 0, :], drv[:, :, 1, :]
                Ec4, Es4 = Ec[:, p4, :], Es[:, p4, :]
                tt(t1[:], dre, Ec4, ALU.mult, rd=["pj0", "pj1", "Ec"], wr=["t1"])
                tt(t2[:], dim_, Es4, ALU.mult, rd=["pj0", "pj1", "Es"], wr=["t2"])
                tt(xm[:, :, 0, :], t1[:], t2[:], ALU.add, eng="pool", rd=["t1", "t2"], wr=["xmre", "Zall"])
                tt(t3[:], dim_, Ec4, ALU.mult, rd=["pj0", "pj1", "Ec"], wr=["t3"])
                tt(t4[:], dre, Es4, ALU.mult, rd=["pj0", "pj1", "Es"], wr=["t4"])
                tt(xm[:, :, 1, :], t3[:], t4[:], ALU.subtract, eng="pool", rd=["t3", "t4"], wr=["xmim", "Zall"])
                for rr in range(4):
                    pr = 4 * T4 + rr
                    I("dve", "tensor_tensor_scan", reads=[r("xmre"), r("rtab"), r("ire")], writes=[r("ht"), r("Zall")], out=ht[:, rr, 0, :],
                      data0=rtab[:, pr, :], data1=xm[:, rr, 0, :], initial=s["ire"][:, pr:pr + 1], op0=ALU.mult, op1=ALU.add)
                    I("dve", "tensor_tensor_scan", reads=[r("xmim"), r("rtab"), r("iim")], writes=[r("ht"), r("Zall")], out=ht[:, rr, 1, :],
                      data0=rtab[:, pr, :], data1=xm[:, rr, 1, :], initial=s["iim"][:, pr:pr + 1], op0=ALU.mult, op1=ALU.add)
                tt(t1[:], ht[:, :, 0, :], Ec4, ALU.mult, rd=["ht", "Ec"], wr=["t1"])
                tt(t2[:], ht[:, :, 1, :], Es4, ALU.mult, rd=["ht", "Es"], wr=["t2"])
                tt(H[:, :, 0, :], t1[:], t2[:], ALU.subtract, eng="pool", rd=["t1", "t2"], wr=["H"])
                tt(t3[:], ht[:, :, 1, :], Ec4, ALU.mult, rd=["ht", "Ec"], wr=["t3"])
                tt(t4[:], ht[:, :, 0, :], Es4, ALU.mult, rd=["ht", "Es"], wr=["t4"])
                tt(H[:, :, 1, :], t3[:], t4[:], ALU.add, eng="pool", rd=["t3", "t4"], wr=["H"])
                I("act", "activation", reads=[r("H")], writes=[r("Hb")], out=Hb[:], in_=H[:], func=AF.Copy)
                I("pool", "tensor_copy", reads=[r("H")], writes=[r("Hp")], out=Hp[:, p4, :], in_=H[:, :, :, QC - 1])
                n = 0
                for rr in range(4):
                    for q in range(2):
                        I("pe", "matmul", reads=[r("LO"), r("Hb")], writes=[r("y5")], out=y5_ps, lhsT=LO[:, 4 * T4 + rr, q, :],
                          rhs=Hb[:, rr, q, :], start=(n == 0), stop=False)
                        n += 1
                I("pe", "matmul", reads=[r("diagD5"), r("uT%d" % T4)], writes=[r("y5")], out=y5_ps, lhsT=diagD5[:, T4, :],
                  rhs=uT[:, T4, lr], start=False, stop=True)
                I("act", "activation", reads=[r("y5")], writes=[r("y5s%d" % T4)], out=y5s[:, T4, :], in_=y5_ps, func=AF.Copy)
                yv = y5s[:, T4, :]
                tt(ga[:], yv, yv, ALU.mult, rd=["y5s%d" % T4], wr=["ga"])
                I("dve", "tensor_scalar", reads=[r("ga")], writes=[r("ga")], out=ga[:], in0=ga[:], scalar1=0.044715, scalar2=1.0,
                  op0=ALU.mult, op1=ALU.add)
                tt(gb[:], ga[:], yv, ALU.mult, rd=["ga", "y5s%d" % T4], wr=["gb"])
                I("act", "activation", reads=[r("gb")], writes=[r("gth")], out=gth[:], in_=gb[:], func=AF.Tanh,
                  scale=0.7978845608028654)
                I("dve", "scalar_tensor_tensor", reads=[r("gth"), r("y5s%d" % T4)], writes=[r("ge%d" % T4)], out=ge[:, T4, :],
                  in0=gth[:], scalar=1.0, in1=yv, op0=ALU.add, op1=ALU.mult)
            tt(s["v1"][:], Hp[:, :, 0], c1[:], ALU.mult, eng="pool", rd=["Hp", "cm0"], wr=["v1"])
            tt(s["v2"][:], Hp[:, :, 1], s1[:], ALU.mult, eng="pool", rd=["Hp", "sm0"], wr=["v2"])
            tt(s["ire"][:], s["v1"][:], s["v2"][:], ALU.subtract, eng="pool", rd=["v1", "v2"], wr=["ire"])
            tt(s["v1"][:], Hp[:, :, 1], c1[:], ALU.mult, eng="pool", rd=["Hp", "cm0"], wr=["v1"])
            tt(s["v2"][:], Hp[:, :, 0], s1[:], ALU.mult, eng="pool", rd=["Hp", "sm0"], wr=["v2"])
            tt(s["iim"][:], s["v1"][:], s["v2"][:], ALU.add, eng="pool", rd=["v1", "v2"], wr=["iim"])
            for o in range(4):
                for T4 in range(4):
                    I("pe", "matmul", reads=[r("Wglu"), r("ge%d" % T4)], writes=[r("gate")], out=gate_ps,
                      lhsT=Wglu[:, T4, o * 128:(o + 1) * 128], rhs=ge[:, T4, :], start=(T4 == 0), stop=(T4 == 3))
                I("act", "activation", reads=[r("gate"), r("bglu")], writes=[r("sig")], out=sig[:], in_=gate_ps, func=AF.Sigmoid,
                  scale=0.5, bias=bglu[:, o:o + 1])
                I("dve", "tensor_tensor", reads=[r("sig"), r("y5s%d" % o)], writes=[Ry[4 + o]], out=yT[:, 4 + o, lr],
                  in0=y5s[:, o, :], in1=sig[:], op=ALU.mult)
        for dc in range(KC):
            sl = nextpj()
            for c in range(8):
                I("pe", "matmul", reads=[RWout[c], Ry[c]], writes=[Rpj[sl]], out=PJ[:, sl, 0:NTE],
                  lhsT=Wout[:, c, dc * 128:(dc + 1) * 128], rhs=yT[:, c, :], start=(c == 0), stop=(c == 7))
            I("dve", "tensor_tensor", reads=[Rpj[sl], RX[dc]], writes=[RX[dc]], out=X[:, dc, :], in0=PJ[:, sl, 0:NTE],
              in1=X[:, dc, :], op=ALU.add)
        P.D("sp", reads=RX, writes=aslist(Rxdst[t]), out=xd_v[:, :, tsl], in_=X[:])


DEPTH = 4
SEQ = 4096
NREG = SEQ // 256


def _bias_tiles(rel_bias):
    l = np.arange(128)[:, None]
    k = np.arange(640)[None, :]
    idx = np.clip(512 + l - k, -128, 128) + 128
    return np.ascontiguousarray(np.transpose(rel_bias[:, idx], (1, 0, 2)))


def _col(v):
    n = v.shape[0] // 128
    return np.ascontiguousarray(v.reshape(n, 128).T)


def _even_inputs(inp, i, layer):
    nat = lambda a: np.ascontiguousarray(a.reshape(16, 2, 64).transpose(1, 2, 0).reshape(128, 16))
    return {
        "win": inp['even_w_in'][i], "wout": inp['even_w_out'][i], "wglu": inp['s5_w_glu'][i],
        "gain": _col(inp['mix_norm'][layer]),
        "convw": np.ascontiguousarray(inp['ssd_conv_w'][i].reshape(4, 8, 128).transpose(2, 1, 0)),
        "convb": _col(inp['ssd_conv_b'][i]),
        "dtb": inp['ssd_dt_bias'][i].reshape(8, 1).copy(), "alog": inp['ssd_a_log'][i].reshape(8, 1).copy(),
        "dskip": _col(np.repeat(inp['ssd_d'][i], 64)),
        "ngain": _col(inp['ssd_norm'][i]),
        "d5": _col(inp['s5_d'][i].reshape(-1)),
        "bglu": _col(inp['s5_b_glu'][i]),
        "are": nat(inp['s5_a_re'][i]), "aim": nat(inp['s5_a_im'][i]),
        "ldt": np.ascontiguousarray(
            np.repeat(inp['s5_log_dt'][i].reshape(16, 2, 1), 64, axis=2).transpose(1, 2, 0).reshape(128, 16)),
        "bre": np.ascontiguousarray(inp['s5_b_re'][i].reshape(16, 2, 64, 16).transpose(1, 2, 0, 3).reshape(128, 16, 16)),
        "bim": np.ascontiguousarray(inp['s5_b_im'][i].reshape(16, 2, 64, 16).transpose(1, 2, 0, 3).reshape(128, 16, 16)),
        "cre": np.ascontiguousarray(inp['s5_c_re'][i].reshape(16, 2, 16, 64).transpose(1, 3, 0, 2).reshape(128, 16, 16)),
        "cim": np.ascontiguousarray(inp['s5_c_im'][i].reshape(16, 2, 16, 64).transpose(1, 3, 0, 2).reshape(128, 16, 16)),
    }


def _odd_inputs(inp, i, layer):
    return {
        "win": inp['odd_w_in'][i], "wout": inp['odd_w_out'][i], "pw": inp['pool_w'][i],
        "bt": _bias_tiles(inp['attn_rel_bias'][i]),
        "qg": np.tile(inp['attn_q_norm'][i], 2).reshape(128, 1).copy(),
        "kg": np.tile(inp['attn_k_norm'][i], 2).reshape(128, 1).copy(),
        "psc": _col(inp['pool_scale'][i]),
        "gain": _col(inp['mix_norm'][layer]),
    }


def host_layout(inp):
    m = {"ident": np.eye(128, dtype=np.float32)}
    for l in range(DEPTH):
        for w in (1, 2):
            m["f%d_%d_wg" % (w, l)] = inp['ffn%d_w_gate' % w][l]
            m["f%d_%d_wu" % (w, l)] = inp['ffn%d_w_up' % w][l]
            m["f%d_%d_wd" % (w, l)] = inp['ffn%d_w_down' % w][l]
            m["f%d_%d_gn" % (w, l)] = _col(inp['ffn%d_norm' % w][l])
        d = _even_inputs(inp, l // 2, l) if l % 2 == 0 else _odd_inputs(inp, l // 2, l)
        for k, v in d.items():
            m["m%d_%s" % (l, k)] = v
    return {k: np.ascontiguousarray(v, dtype=np.float32) for k, v in m.items()}


def build_program(shapes, phases=None, seq=SEQ):
    nc = bass.Bass("TRN2", target_bir_lowering=False)
    di = lambda n, s: nc.dram_tensor(n, list(s), F32, kind="ExternalInput").ap()
    xin = di("xT", [D, seq])
    dd = {k: di(k, s) for k, s in shapes.items()}
    yout = nc.dram_tensor("yT", [D, seq], F32, kind="ExternalOutput").ap()
    xs = nc.dram_tensor("xscratch", [D, seq], F32, kind="Internal").ap()
    if phases is None:
        phases = [(l, k) for l in range(DEPTH) for k in ("f1", "mix", "f2")]
    nreg = seq // 256
    pair = lambda R_: [[R_[2 * t], R_[2 * t + 1]] for t in range(len(R_) // 2)]
    with ExitStack() as es:
        P = Prog(nc, es)
        C = Ctx(nc, P)
        C.es = es
        emit_consts(C)
        Rin, Rsc, Rout = regs("xin", nreg), regs("xsc", nreg), regs("xout", nreg)
        for pi, (l, kind) in enumerate(phases):
            src, Rs = (xin, Rin) if pi == 0 else (xs, Rsc)
            dst, Rd = (yout, Rout) if pi == len(phases) - 1 else (xs, Rsc)
            with ExitStack() as pes:
                C.es = pes
                if kind in ("f1", "f2"):
                    w = 1 if kind == "f1" else 2
                    pre = "f%d_%d_" % (w, l)
                    emit_ffn(C, src, dst, pair(Rs), pair(Rd), dd[pre + "wg"], dd[pre + "wu"], dd[pre + "wd"], dd[pre + "gn"], seq // NT)
                elif l % 2 == 0:
                    sub = {k[len("m%d_" % l):]: v for k, v in dd.items() if k.startswith("m%d_" % l)}
                    sub["ident"] = dd["ident"]
                    emit_even(C, src, dst, Rs, Rd, sub, seq // NTE)
                else:
                    pre = "m%d_" % l
                    emit_odd(C, src, dst, pair(Rs), pair(Rd), dd[pre + "win"], dd[pre + "wout"], dd[pre + "pw"], dd[pre + "bt"],
                             dd[pre + "qg"], dd[pre + "kg"], dd[pre + "psc"], dd[pre + "gain"], dd["ident"], seq // NT)
                P.all_barrier()
                with nc.Block() as block:
                    P.flush(block)
    return nc, P


_CACHE = {}


def kernel(**inputs):
    inp = {k: np.asarray(v) for k, v in inputs.items()}
    x = inp["x"]
    B = x.shape[0]
    wts = host_layout(inp)
    key = "full"
    if key not in _CACHE:
        _CACHE[key] = build_program({k: v.shape for k, v in wts.items()})
    nc, _ = _CACHE[key]
    n_cores = 8
    in_maps = []
    for c in range(n_cores):
        m = dict(wts)
        m["xT"] = np.ascontiguousarray(x[c % B].T)
        in_maps.append(m)
    res = run_bass_kernel_spmd(nc, in_maps, core_ids=list(range(n_cores)))
    out = np.stack([np.ascontiguousarray(res.results[b]["yT"].T) for b in range(B)], axis=0)
    return out.astype(np.float32)
```

```python
from concourse.bass_utils import run_bass_kernel_spmd
import concourse.bass as bass
import concourse.mybir as mybir

ENGS = ("pe", "act", "dve", "pool", "sp")
SEM_CAP = 24000
N_DMA_SLOTS = 24


class Reg:
    __slots__ = ("name", "w", "rs", "excl")

    def __init__(self, name, excl=False):
        self.name = name
        self.excl = excl
        self.w = None
        self.rs = []


class Op:
    __slots__ = ("eng", "pos", "fn", "waits", "inc", "dma_slot", "dma_val", "dma_prev")

    def __init__(self, eng, pos, fn):
        self.eng = eng
        self.pos = pos
        self.fn = fn
        self.waits = []
        self.inc = False
        self.dma_slot = None
        self.dma_val = 0


class Prog:
    def __init__(self, nc, es):
        self.nc = nc
        self.ops = {e: [] for e in ENGS}
        self.seen = {e: {e2: -1 for e2 in ENGS} for e in ENGS}
        self.seen_dma = {e: {} for e in ENGS}
        self.handles = {"pe": nc.tensor, "act": nc.scalar, "dve": nc.vector, "pool": nc.gpsimd, "sp": nc.sync}
        self.csems = {e: [] for e in ENGS}
        self.es = es
        self.dma_sems = [es.enter_context(nc.semaphore("dq%d" % i)) for i in range(N_DMA_SLOTS)]
        self.dma_uses = [0] * N_DMA_SLOTS
        self.dma_last = [None] * N_DMA_SLOTS
        self.dma_rr = 0
        self.base_counts = {e: 0 for e in ENGS}
        self.n_emitted = 0
        self.cum_hist = {}
        self.flushed = {e: 0 for e in ENGS}
        self.last_real = {e: -1 for e in ENGS}

    def _need(self, op, tok, skip_same=False):
        if tok is None:
            return
        e = op.eng
        if tok[0] == "e":
            _, e2, pos = tok
            if skip_same and e2 == e:
                return
            if e2 == e and e == "pe":
                if self.seen[e][e2] < pos:
                    self.seen[e][e2] = pos
                return
            if e2 == e and pos == op.pos:
                return
            if self.seen[e][e2] >= pos:
                return
            self.seen[e][e2] = pos
            self.ops[e2][pos - self.flushed[e2]].inc = True
            op.waits.append(tok)
        else:
            _, slot, val = tok
            if self.seen_dma[e].get(slot, 0) >= val:
                return
            self.seen_dma[e][slot] = val
            op.waits.append(tok)

    def I(self, eng, name, reads=(), writes=(), **kw):
        return self.op(eng, lambda h: getattr(h, name)(**kw), reads, writes)

    def D(self, eng, reads=(), writes=(), **kw):
        return self.dma(eng, lambda h: h.dma_start(**kw), reads, writes)

    def op(self, eng, fn, reads=(), writes=()):
        pos = self.flushed[eng] + len(self.ops[eng])
        o = Op(eng, pos, fn)
        locks = [x for x in list(reads) + list(writes) if x.excl]
        reads = [x for x in reads if not x.excl]
        writes = [x for x in writes if not x.excl]
        for r in reads:
            self._need(o, r.w)
        for w in writes:
            self._need(o, w.w)
            for t in w.rs:
                self._need(o, t)
        for l in locks:
            self._need(o, l.w, skip_same=True)
        tok = ("e", eng, pos)
        for r in reads:
            r.rs.append(tok)
        for w in writes:
            w.w = tok
            w.rs = []
        for l in locks:
            l.w = tok
        self.ops[eng].append(o)
        self.last_real[eng] = pos
        return o

    def dma(self, eng, fn, reads=(), writes=()):
        pos = self.flushed[eng] + len(self.ops[eng])
        o = Op(eng, pos, fn)
        for r in reads:
            self._need(o, r.w)
        for w in writes:
            self._need(o, w.w)
            for t in w.rs:
                self._need(o, t)
        slot = self.dma_rr
        self.dma_rr = (self.dma_rr + 1) % N_DMA_SLOTS
        if self.dma_uses[slot] > 0:
            self._need(o, ("d", slot, 16 * self.dma_uses[slot]))
        self.dma_uses[slot] += 1
        o.dma_slot = slot
        o.dma_val = 16 * self.dma_uses[slot]
        tok = ("d", slot, o.dma_val)
        for r in reads:
            r.rs.append(tok)
        for w in writes:
            w.w = tok
            w.rs = []
        self.ops[eng].append(o)
        return o

    def barrier_tokens(self):
        toks = []
        for e in ENGS:
            if self.last_real[e] >= 0:
                toks.append(("e", e, self.last_real[e]))
        for s in range(N_DMA_SLOTS):
            if self.dma_uses[s] > 0:
                toks.append(("d", s, 16 * self.dma_uses[s]))
        return toks

    def all_barrier(self):
        toks = self.barrier_tokens()
        for e in ENGS:
            pos = self.flushed[e] + len(self.ops[e])
            o = Op(e, pos, None)
            for t in toks:
                self._need(o, t)
            self.ops[e].append(o)

    def _csem(self, e, epoch):
        while len(self.csems[e]) <= epoch:
            self.csems[e].append(self.es.enter_context(self.nc.semaphore("c_%s_%d" % (e, len(self.csems[e])))))
        return self.csems[e][epoch]

    def flush(self, block):
        cum = {}
        for e in ENGS:
            c = self.base_counts[e]
            for o in self.ops[e]:
                if o.inc:
                    c += 1
                cum[(e, o.pos)] = c
        self.cum_hist.update(cum)

        def emit(e, h):
            for o in self.ops[e]:
                for t in o.waits:
                    if t[0] == "e":
                        c = self.cum_hist[(t[1], t[2])]
                        ep, v = divmod(c - 1, SEM_CAP)
                        h.wait_ge(self._csem(t[1], ep), v + 1)
                    else:
                        h.wait_ge(self.dma_sems[t[1]], t[2])
                if o.fn is None:
                    continue
                ins = o.fn(h)
                if o.dma_slot is not None:
                    ins.then_inc(self.dma_sems[o.dma_slot], 16)
                elif o.inc:
                    c = self.cum_hist[(e, o.pos)]
                    ep, v = divmod(c - 1, SEM_CAP)
                    ins.then_inc(self._csem(e, ep), 1)
                self.n_emitted += 1

        for e in ENGS:
            c = self.base_counts[e]
            for o in self.ops[e]:
                if o.inc:
                    c += 1
            if c > 0:
                self._csem(e, (c - 1) // SEM_CAP)

        block.tensor(lambda h: emit("pe", h))
        block.scalar(lambda h: emit("act", h))
        block.vector(lambda h: emit("dve", h))
        block.gpsimd(lambda h: emit("pool", h))
        block.sync(lambda h: emit("sp", h))
        for e in ENGS:
            if self.ops[e]:
                self.base_counts[e] = cum[(e, self.ops[e][-1].pos)]
            self.flushed[e] += len(self.ops[e])
            self.ops[e] = []


import numpy as np
from contextlib import ExitStack
import concourse.bass as bass
import concourse.mybir as mybir

F32 = mybir.dt.float32
BF16 = mybir.dt.bfloat16
AF = mybir.ActivationFunctionType
ALU = mybir.AluOpType
AX = mybir.AxisListType

D = 1024
KC = 8
FF = 2816
FC = 22
NT = 512
EPS = 1e-6


class Ctx:
    def __init__(self, nc, P):
        self.nc = nc
        self.P = P
        self.es = None
        self.n = 0

    def sb(self, shape, dt, name=None):
        self.n += 1
        return self.es.enter_context(self.nc.sbuf_tensor("%s_%d" % (name or "t", self.n), shape, dt))

    def ps(self, shape, dt, name=None):
        self.n += 1
        return self.es.enter_context(self.nc.psum_tensor("%s_%d" % (name or "p", self.n), shape, dt))


def aslist(x):
    return list(x) if isinstance(x, (list, tuple)) else [x]


def regs(prefix, n, excl=False):
    return [Reg("%s%d" % (prefix, i), excl) for i in range(n)]


def emit_consts(C):
    P, nc = C.P, C.nc
    C.ones_f = C.sb([128, 128], F32, "ones")
    C.R_ones = Reg("ones")
    P.I("dve", "memset", writes=[C.R_ones], ap=C.ones_f[:], constant=1.0)


def emit_norm(C, X, RX, gain, Rgain, hb, Rh, sq, Rsq, ssq_ps, Rssq, rt, Rrt, rstd, Rrstd):
    P = C.P
    for kc in range(KC):
        s = kc % 2
        P.I("act", "activation", reads=[RX[kc]], writes=[Rsq[s]], out=sq[s][:], in_=X[:, kc, :], func=AF.Square)
        P.I("pe", "matmul", reads=[Rsq[s], C.R_ones], writes=[Rssq], out=ssq_ps[:], lhsT=C.ones_f[:], rhs=sq[s][:],
            start=(kc == 0), stop=(kc == KC - 1))
    P.I("act", "activation", reads=[Rssq], writes=[Rrt], out=rt[:], in_=ssq_ps[:], func=AF.Ln, scale=1.0 / D, bias=EPS)
    P.I("act", "activation", reads=[Rrt], writes=[Rrstd], out=rstd[:], in_=rt[:], func=AF.Exp, scale=-0.5)
    for kc in range(KC):
        P.I("dve", "scalar_tensor_tensor", reads=[RX[kc], Rgain, Rrstd], writes=[Rh[kc]], out=hb[:, kc, :],
            in0=X[:, kc, :], scalar=gain[:, kc:kc + 1], in1=rstd[:], op0=ALU.mult, op1=ALU.mult)


def emit_ffn(C, xsrc, xdst, Rxsrc, Rxdst, wg_d, wu_d, wd_d, gain_d, ntiles):
    P, nc = C.P, C.nc
    Wg = C.sb([128, KC, FF], BF16, "Wg")
    Wu = C.sb([128, KC, FF], BF16, "Wu")
    Wd = C.sb([128, FC, D], BF16, "Wd")
    gain = C.sb([128, KC], F32, "gain")
    xt = [C.sb([128, KC, NT], F32, "xt") for _ in range(2)]
    sq = [C.sb([128, NT], F32, "sq") for _ in range(2)]
    hb = C.sb([128, KC, NT], BF16, "h")
    act = C.sb([128, FC, NT], BF16, "act")
    sg = [C.sb([128, NT], BF16, "sg") for _ in range(2)]
    rt = C.sb([128, NT], F32, "rt")
    rstd = C.sb([128, NT], F32, "rstd")
    ssq_ps = C.ps([128, NT], F32, "ssq")
    g_ps = [C.ps([128, NT], F32, "g") for _ in range(2)]
    u_ps = [C.ps([128, NT], F32, "u") for _ in range(2)]
    o_ps = [C.ps([128, NT], F32, "o") for _ in range(2)]

    RWg, RWu, RWd = regs("Wg", KC), regs("Wu", KC), regs("Wd", FC)
    Rgain = Reg("gain")
    Rxt = [regs("xt%d_" % b, KC) for b in range(2)]
    Rsq = regs("sq", 2)
    Rh = regs("h", KC)
    Ract = regs("act", FC)
    Rsg = regs("sg", 2)
    Rrt, Rrstd, Rssq = Reg("rt"), Reg("rstd"), Reg("ssq", True)
    Rg, Ru, Ro = regs("g", 2, True), regs("u", 2, True), regs("o", 2, True)

    wg_v = wg_d.rearrange("(kc p) f -> p kc f", p=128)
    wu_v = wu_d.rearrange("(kc p) f -> p kc f", p=128)
    wd_v = wd_d.rearrange("(fc p) d -> p fc d", p=128)
    P.D("sp", writes=[Rgain], out=gain[:], in_=gain_d)
    for kc in range(KC):
        P.D("pool", writes=[RWg[kc]], out=Wg[:, kc, :], in_=wg_v[:, kc, :])
        P.D("pool", writes=[RWu[kc]], out=Wu[:, kc, :], in_=wu_v[:, kc, :])
    for fc in range(FC):
        P.D("pool", writes=[RWd[fc]], out=Wd[:, fc, :], in_=wd_v[:, fc, :])

    xs_v = xsrc.rearrange("(kc p) t -> p kc t", p=128)
    xd_v = xdst.rearrange("(kc p) t -> p kc t", p=128)

    def load(t):
        b = t % 2
        P.D("sp", reads=aslist(Rxsrc[t]), writes=Rxt[b], out=xt[b][:], in_=xs_v[:, :, t * NT:(t + 1) * NT])

    load(0)
    for t in range(ntiles):
        b = t % 2
        if t + 1 < ntiles:
            load(t + 1)
        X = xt[b]
        emit_norm(C, X, Rxt[b], gain, Rgain, hb, Rh, sq, Rsq, ssq_ps, Rssq, rt, Rrt, rstd, Rrstd)
        for f in range(FC):
            s = f % 2
            for kc in range(KC):
                P.I("pe", "matmul", reads=[RWg[kc], Rh[kc]], writes=[Rg[s]], out=g_ps[s][:], lhsT=Wg[:, kc, f * 128:(f + 1) * 128], rhs=hb[:, kc, :],
                    start=(kc == 0), stop=(kc == KC - 1))
            for kc in range(KC):
                P.I("pe", "matmul", reads=[RWu[kc], Rh[kc]], writes=[Ru[s]], out=u_ps[s][:], lhsT=Wu[:, kc, f * 128:(f + 1) * 128], rhs=hb[:, kc, :],
                    start=(kc == 0), stop=(kc == KC - 1))
            P.I("act", "activation", reads=[Rg[s]], writes=[Rsg[s]], out=sg[s][:], in_=g_ps[s][:], func=AF.Silu)
            P.I("dve", "tensor_tensor", reads=[Ru[s], Rsg[s]], writes=[Ract[f]], out=act[:, f, :], in0=u_ps[s][:], in1=sg[s][:], op=ALU.mult)
        for dc in range(KC):
            s = dc % 2
            for f in range(FC):
                P.I("pe", "matmul", reads=[RWd[f], Ract[f]], writes=[Ro[s]], out=o_ps[s][:], lhsT=Wd[:, f, dc * 128:(dc + 1) * 128], rhs=act[:, f, :],
                    start=(f == 0), stop=(f == FC - 1))
            P.I("dve", "scalar_tensor_tensor", reads=[Ro[s], Rxt[b][dc]], writes=[Rxt[b][dc]], out=X[:, dc, :], in0=o_ps[s][:], scalar=0.5, in1=X[:, dc, :], op0=ALU.mult, op1=ALU.add)
        P.D("sp", reads=Rxt[b], writes=aslist(Rxdst[t]), out=xd_v[:, :, t * NT:(t + 1) * NT], in_=X[:])


def emit_odd(C, xsrc, xdst, Rxsrc, Rxdst, win_d, wout_d, poolw_d, bt_d, qg_d, kg_d, pscale_d, gain_d, ident_d, ntiles):
    P, nc = C.P, C.nc
    T = ntiles * NT
    Win = C.sb([128, KC, 2048], BF16, "Win")
    Wout = C.sb([128, KC, D], BF16, "Wout")
    Wp = C.sb([128, 4, 128], BF16, "Wp")
    bt = C.sb([128, 8, 640], F32, "bt")
    qg = C.sb([128, 1], F32, "qg")
    kg = C.sb([128, 1], F32, "kg")
    pscale = C.sb([128, 4], F32, "pscale")
    gain = C.sb([128, KC], F32, "gain")
    ident_f = C.sb([128, 128], F32, "identf")
    ident = C.sb([128, 128], BF16, "ident")
    ones_bf = C.sb([128, 64], BF16, "onesbf")
    bones = C.sb([128, 128], F32, "bones")
    KnT = C.sb([128, 4, T], BF16, "KnT")
    V = C.sb([128, ntiles * 4, 512], BF16, "V")
    X = C.sb([128, KC, NT], F32, "X")
    hb = C.sb([128, KC, NT], BF16, "h")
    qnT = C.sb([128, 4, NT], BF16, "qnT")
    yT = C.sb([128, 8, NT], BF16, "yT")
    U = C.sb([128, 4, 16 + NT], F32, "U")
    sA = C.sb([128, 16 + NT], F32, "sA")
    sB = C.sb([128, 16 + NT], F32, "sB")
    t16 = C.sb([128, 16], F32, "t16")
    pooled = C.sb([128, 4, NT], BF16, "pooled")
    rc = C.sb([128, 4, 16], F32, "rc")
    sq = [C.sb([128, NT], F32, "sq") for _ in range(2)]
    rt = C.sb([128, NT], F32, "rt")
    rstd = C.sb([128, NT], F32, "rstd")
    sbs = [C.sb([128, 640], F32, "sbs") for _ in range(2)]
    nmx = [C.sb([128, 1], F32, "nmx") for _ in range(2)]
    Pb = [C.sb([128, 640], BF16, "Pb") for _ in range(2)]
    PT = [C.sb([128, 5, 128], BF16, "PT") for _ in range(2)]
    rinv = C.sb([128, 128], F32, "rinv")
    pj = [C.ps([128, NT], F32, "pj") for _ in range(2)]
    S = [C.ps([128, 1024], F32, "S") for _ in range(2)]
    PTp = C.ps([128, 8, 128], BF16, "PTp")
    OR = C.ps([128, 512], F32, "OR")

    RWin, RWout = regs("Win", KC), regs("Wout", KC)
    RWp, Rbt, Rqg, Rkg, Rpsc, Rgain = Reg("Wp"), Reg("bt"), Reg("qg"), Reg("kg"), Reg("psc"), Reg("gain")
    Rident, Ridf, Rones, Rbones, Rrc = Reg("ident"), Reg("identf"), Reg("onesbf"), Reg("bones"), Reg("rc")
    RKn = [regs("Kn%d_" % t, 4) for t in range(ntiles)]
    RV = regs("V", ntiles * 4)
    RX, Rh = regs("X", KC), regs("h", KC)
    Rqn, Ry = regs("qn", 4), regs("y", 8)
    RU, RsA, RsB, Rt16, Rpooled = regs("U", 4), Reg("sA"), Reg("sB"), Reg("t16"), regs("pooled", 4)
    Rsq = regs("sq", 2)
    Rrt, Rrstd = Reg("rt"), Reg("rstd")
    Rsb, Rmx, RPb, RPT, Rrinv = regs("sb", 2), regs("mx", 2), regs("Pb", 2), regs("PT", 2), Reg("rinv")
    Rpj, RS, RPTp, ROR = regs("pj", 2, True), regs("S", 2, True), Reg("PTp", True), Reg("OR", True)

    P.D("sp", writes=[Rgain], out=gain[:], in_=gain_d)
    P.D("sp", writes=[Rqg], out=qg[:], in_=qg_d)
    P.D("sp", writes=[Rkg], out=kg[:], in_=kg_d)
    P.D("sp", writes=[Rpsc], out=pscale[:], in_=pscale_d)
    P.D("sp", writes=[Ridf], out=ident_f[:], in_=ident_d)
    P.D("sp", writes=[Rbt], out=bt[:], in_=bt_d)
    win_v = win_d.rearrange("(kc p) f -> p kc f", p=128)
    wout_v = wout_d.rearrange("(kc p) f -> p kc f", p=128)
    for kc in range(KC):
        P.D("pool", writes=[RWin[kc]], out=Win[:, kc, :], in_=win_v[:, kc, :])
    for kc in range(KC):
        P.D("pool", writes=[RWout[kc]], out=Wout[:, kc, :], in_=wout_v[:, kc, :])
    P.D("pool", writes=[RWp], out=Wp[:], in_=poolw_d.rearrange("g i o -> i g o"))
    P.I("dve", "tensor_copy", reads=[Ridf], writes=[Rident], out=ident[:], in_=ident_f[:])
    P.I("dve", "memset", writes=[Rones], ap=ones_bf[:], constant=1.0)
    P.I("dve", "memset", writes=[Rbones], ap=bones[:], constant=0.0)
    P.I("dve", "memset", reads=[], writes=[Rbones], ap=bones[0:64, 0:64], constant=1.0)
    P.I("dve", "memset", reads=[], writes=[Rbones], ap=bones[64:128, 64:128], constant=1.0)
    P.I("dve", "memset", writes=[Rbt], ap=bt[0:64, :, 576:640], constant=-1e30)
    P.I("dve", "memset", writes=[Rbt], ap=bt[64:128, :, 0:64], constant=-1e30)
    for g in range(4):
        P.I("pool", "memset", writes=[RU[g]], ap=U[:, g, 0:16], constant=0.0)
    WIN = (2, 4, 8, 16)
    for g, w in enumerate(WIN):
        for pos in range(w - 1):
            P.I("pool", "memset", writes=[Rrc], ap=rc[:, g, pos:pos + 1], constant=1.0 / (pos + 1))
        P.I("pool", "memset", writes=[Rrc], ap=rc[:, g, w - 1:16], constant=1.0 / w)

    xs_v = xsrc.rearrange("(kc p) t -> p kc t", p=128)
    xd_v = xdst.rearrange("(kc p) t -> p kc t", p=128)
    pjn = [0]

    def nextpj():
        pjn[0] += 1
        return pjn[0] % 2

    for t in range(ntiles):
        tsl = slice(t * NT, (t + 1) * NT)
        P.D("sp", reads=aslist(Rxsrc[t]), writes=RX, out=X[:], in_=xs_v[:, :, tsl])
        s0 = nextpj()
        emit_norm(C, X, RX, gain, Rgain, hb, Rh, sq, Rsq, pj[s0], Rpj[s0], rt, Rrt, rstd, Rrstd)
        for col0, gv, Rgv, is_q in ((0, qg, Rqg, True), (512, kg, Rkg, False)):
            for c in range(4):
                s = nextpj()
                for kc in range(KC):
                    P.I("pe", "matmul", reads=[RWin[kc], Rh[kc]], writes=[Rpj[s]], out=pj[s][:],
                        lhsT=Win[:, kc, col0 + c * 128:col0 + (c + 1) * 128], rhs=hb[:, kc, :],
                        start=(kc == 0), stop=(kc == KC - 1))
                s3 = 1 - s
                P.I("act", "activation", reads=[Rpj[s]], writes=[Rsq[0]], out=sq[0][:], in_=pj[s][:], func=AF.Square)
                P.I("pe", "matmul", reads=[Rsq[0], Rbones], writes=[Rpj[s3]], out=pj[s3][:], lhsT=bones[:], rhs=sq[0][:],
                    start=True, stop=True)
                P.I("act", "activation", reads=[Rpj[s3]], writes=[Rrt], out=rt[:], in_=pj[s3][:], func=AF.Ln,
                    scale=1.0 / 64, bias=EPS)
                P.I("act", "activation", reads=[Rrt], writes=[Rrstd], out=rstd[:], in_=rt[:], func=AF.Exp, scale=-0.5)
                if is_q:
                    dst, Rdst = qnT[:, c, :], Rqn[c]
                else:
                    dst, Rdst = KnT[:, c, tsl], RKn[t][c]
                P.I("dve", "scalar_tensor_tensor", reads=[Rpj[s], Rgv, Rrstd], writes=[Rdst], out=dst,
                    in0=pj[s][:], scalar=gv[:, 0:1], in1=rstd[:], op0=ALU.mult, op1=ALU.mult)
        for tb in range(4):
            s = nextpj()
            for kc in range(KC):
                P.I("pe", "matmul", reads=[RWin[kc], Rh[kc]], writes=[Rpj[s]], out=pj[s][:],
                    lhsT=hb[:, kc, tb * 128:(tb + 1) * 128], rhs=Win[:, kc, 1024:1536],
                    start=(kc == 0), stop=(kc == KC - 1))
            P.I("act", "activation", reads=[Rpj[s]], writes=[RV[4 * t + tb]], out=V[:, 4 * t + tb, :], in_=pj[s][:],
                func=AF.Copy)
        for g, w in enumerate(WIN):
            s = nextpj()
            for kc in range(KC):
                P.I("pe", "matmul", reads=[RWin[kc], Rh[kc]], writes=[Rpj[s]], out=pj[s][:],
                    lhsT=Win[:, kc, 1536 + g * 128:1536 + (g + 1) * 128], rhs=hb[:, kc, :],
                    start=(kc == 0), stop=(kc == KC - 1))
            P.I("act", "activation", reads=[Rpj[s]], writes=[RU[g]], out=U[:, g, 16:16 + NT], in_=pj[s][:], func=AF.Copy)
            L = 16 + NT
            P.I("pool", "tensor_tensor", reads=[RU[g]], writes=[RsA], out=sA[:, 1:L], in0=U[:, g, 1:L], in1=U[:, g, 0:L - 1], op=ALU.add)
            fin, Rfin = sA, RsA
            if w >= 4:
                P.I("pool", "tensor_tensor", reads=[RsA], writes=[RsB], out=sB[:, 3:L], in0=sA[:, 3:L], in1=sA[:, 1:L - 2], op=ALU.add)
                fin, Rfin = sB, RsB
            if w >= 8:
                P.I("pool", "tensor_tensor", reads=[RsB], writes=[RsA], out=sA[:, 7:L], in0=sB[:, 7:L], in1=sB[:, 3:L - 4], op=ALU.add)
                fin, Rfin = sA, RsA
            if w >= 16:
                P.I("pool", "tensor_tensor", reads=[RsA], writes=[RsB], out=sB[:, 15:L], in0=sA[:, 15:L], in1=sA[:, 7:L - 8], op=ALU.add)
                fin, Rfin = sB, RsB
            P.I("dve", "scalar_tensor_tensor", reads=[Rfin, RU[g]], writes=[Rpooled[g]], out=pooled[:, g, :],
                in0=fin[:, 16:L], scalar=1.0 / w, in1=U[:, g, 16:L], op0=ALU.mult, op1=ALU.subtract)
            if t == 0:
                P.I("dve", "tensor_tensor", reads=[Rfin, Rrc], writes=[Rt16], out=t16[:], in0=fin[:, 16:32], in1=rc[:, g, :], op=ALU.mult)
                P.I("dve", "tensor_tensor", reads=[Rt16, RU[g]], writes=[Rpooled[g]], out=pooled[:, g, 0:16], in0=t16[:], in1=U[:, g, 16:32], op=ALU.subtract)
            P.I("pool", "tensor_copy", reads=[RU[g]], writes=[RU[g]], out=U[:, g, 0:16], in_=U[:, g, NT:NT + 16])
            s = nextpj()
            P.I("pe", "matmul", reads=[RWp, Rpooled[g]], writes=[Rpj[s]], out=pj[s][:], lhsT=Wp[:, g, :], rhs=pooled[:, g, :],
                start=True, stop=True)
            P.I("dve", "tensor_scalar", reads=[Rpj[s], Rpsc], writes=[Ry[4 + g]], out=yT[:, 4 + g, :], in0=pj[s][:],
                scalar1=pscale[:, g:g + 1], scalar2=None, op0=ALU.mult)
        items = [(mb, c, e) for mb in range(4) for c in range(4) for e in range(2)]

        def geom(mb):
            m = 4 * t + mb
            nkb = min(m, 4) + 1
            j0 = m - (nkb - 1)
            return m, nkb, j0, (5 - nkb) * 128, nkb * 128

        def stageA(i):
            mb, c, e = items[i]
            m, nkb, j0, koff, nk = geom(mb)
            z = i % 2
            hh = 2 * c + e
            hs = e * 64
            lsl = slice(mb * 128, (mb + 1) * 128)
            kregs = [RKn[tt][c] for tt in range(j0 // 4, t + 1)]
            for a_, b_ in ([(0, min(nk, 512))] + ([(512, nk)] if nk > 512 else [])):
                P.I("pe", "matmul", reads=[Rqn[c]] + kregs, writes=[RS[z]], out=S[z][:, a_:b_],
                    lhsT=qnT[hs:hs + 64, c, lsl], rhs=KnT[hs:hs + 64, c, j0 * 128 + a_:j0 * 128 + b_],
                    start=True, stop=True)
            P.I("dve", "scalar_tensor_tensor", reads=[RS[z], Rbt], writes=[Rsb[z]], out=sbs[z][:, 0:nk], in0=S[z][:, 0:nk],
                scalar=0.125, in1=bt[:, hh, koff:koff + nk], op0=ALU.mult, op1=ALU.add)
            P.I("dve", "tensor_reduce", reads=[Rsb[z]], writes=[Rmx[z]], out=nmx[z][:], in_=sbs[z][:, 0:nk], axis=AX.X,
                op=ALU.max, negate=True)
            P.I("act", "activation", reads=[Rsb[z], Rmx[z]], writes=[RPb[z]], out=Pb[z][:, 0:nk], in_=sbs[z][:, 0:nk],
                func=AF.Exp, bias=nmx[z][:], scale=1.0)

        def stageB1(i):
            mb, c, e = items[i]
            m, nkb, j0, koff, nk = geom(mb)
            z = i % 2
            for kb in range(nkb):
                P.I("pe", "transpose", reads=[RPb[z], Rident], writes=[RPTp], out=PTp[:, kb, :],
                    in_=Pb[z][:, kb * 128:(kb + 1) * 128], identity=ident[:])
            P.I("act", "activation", reads=[RPTp], writes=[RPT[e]], out=PT[e][:, 0:nkb, :], in_=PTp[:, 0:nkb, :],
                func=AF.Copy)

        def stageB2(i):
            mb, c, e = items[i]
            m, nkb, j0, koff, nk = geom(mb)
            hh = 2 * c + e
            hs = e * 64
            lsl = slice(mb * 128, (mb + 1) * 128)
            for kb in range(nkb):
                P.I("pe", "matmul", reads=[RV[j0 + kb], RPT[e]], writes=[ROR], out=OR[hs:hs + 64, 0:128],
                    lhsT=V[:, j0 + kb, hh * 64:(hh + 1) * 64], rhs=PT[e][:, kb, :],
                    start=(kb == 0), stop=(kb == nkb - 1))
            for kb in range(nkb):
                P.I("pe", "matmul", reads=[Rones, RPT[e]], writes=[ROR], out=OR[hs:hs + 64, 128:256],
                    lhsT=ones_bf[:, 0:64], rhs=PT[e][:, kb, :], start=(kb == 0), stop=(kb == nkb - 1))
            if e == 1:
                P.I("dve", "reciprocal", reads=[ROR], writes=[Rrinv], out=rinv[:], in_=OR[:, 128:256])
                P.I("dve", "tensor_tensor", reads=[ROR, Rrinv], writes=[Ry[c]], out=yT[:, c, lsl], in0=OR[:, 0:128],
                    in1=rinv[:], op=ALU.mult)

        n_it = len(items)
        stageA(0)
        stageA(1)
        stageB1(0)
        for i in range(n_it):
            if i + 2 < n_it:
                stageA(i + 2)
            if i + 1 < n_it:
                stageB1(i + 1)
            stageB2(i)
        for dc in range(KC):
            s = nextpj()
            for c in range(8):
                P.I("pe", "matmul", reads=[RWout[c], Ry[c]], writes=[Rpj[s]], out=pj[s][:],
                    lhsT=Wout[:, c, dc * 128:(dc + 1) * 128], rhs=yT[:, c, :], start=(c == 0), stop=(c == 7))
            P.I("dve", "tensor_tensor", reads=[Rpj[s], RX[dc]], writes=[RX[dc]], out=X[:, dc, :], in0=pj[s][:],
                in1=X[:, dc, :], op=ALU.add)
        P.D("sp", reads=RX, writes=aslist(Rxdst[t]), out=xd_v[:, :, tsl], in_=X[:])


NTE = 256
QC = 128


def bview(ap, axis, shape):
    return ap.unsqueeze(axis).broadcast_to(shape)


def emit_even(C, xsrc, xdst, Rxsrc, Rxdst, dd, ntiles):
    P, nc = C.P, C.nc
    sb, ps = C.sb, C.ps
    Win = sb([128, KC, 2056], BF16, "Win")
    Wout = sb([128, KC, D], BF16, "Wout")
    Wglu = sb([128, 4, 512], BF16, "Wglu")
    diagw = sb([128, 8, 4, 128], BF16, "diagw")
    diagD = sb([128, 4, 128], BF16, "diagD")
    diagD5 = sb([128, 4, 128], BF16, "diagD5")
    gain = sb([128, KC], F32, "gain")
    convw = sb([128, 8, 4], F32, "convw")
    convb = sb([128, 8], F32, "convb")
    dtb = sb([8, 1], F32, "dtb")
    alog = sb([8, 1], F32, "alog")
    aneg = sb([8, 1], F32, "aneg")
    dskip = sb([128, 4], F32, "dskip")
    ngain = sb([128, 4], F32, "ngain")
    d5 = sb([128, 4], F32, "d5")
    bglu = sb([128, 4], F32, "bglu")
    ident_f = sb([128, 128], F32, "identf")
    ident = sb([128, 128], BF16, "ident")
    selH = sb([8, 8, 128], F32, "selH")
    tri = sb([128, 128], F32, "tri")
    maskneg = sb([128, 128], F32, "maskneg")
    X = sb([128, KC, NTE], F32, "X")
    hb = sb([128, KC, NTE], BF16, "h")
    zs = sb([128, 4, NTE], BF16, "zs")
    xbc = sb([128, 8, 3 + NTE], BF16, "xbc")
    xsT = sb([128, 4, NTE], BF16, "xsT")
    BT = sb([128, 2, NTE], BF16, "BT")
    CT = sb([128, 2, NTE], BF16, "CT")
    uT = sb([128, 4, NTE], BF16, "uT")
    yT = sb([128, 8, NTE], BF16, "yT")
    dte1 = sb([8, NTE], F32, "dte1")
    dtsp = sb([8, NTE], F32, "dtsp")
    dA = sb([8, NTE], F32, "dA")
    cs = sb([8, NTE], F32, "cs")
    sq = [sb([128, NTE], F32, "sq") for _ in range(2)]
    rt = sb([128, NTE], F32, "rt")
    rstd = sb([128, NTE], F32, "rstd")
    csdt = sb([128, 16], F32, "csdt")
    tmpL = sb([128, 8, 128], F32, "tmpL")
    ecs = sb([128, 8, 128], BF16, "ecs")
    eend = sb([128, 8], F32, "eend")
    d1 = sb([128, 8], F32, "d1")
    dte = sb([128, 8], F32, "dte")
    Mt = sb([128, 8, 128], BF16, "Mt")
    Cs = sb([128, 8, 128], BF16, "Cs")
    xdt = sb([128, 512], BF16, "xdt")
    xw = sb([128, 512], BF16, "xw")
    Btok = sb([128, 2, 128], BF16, "Btok")
    yg = sb([128, 4, 128], F32, "yg")
    sqy = [sb([128, 128], F32, "sqy") for _ in range(2)]
    rtg = sb([128, 128], F32, "rtg")
    rstdg = sb([128, 128], F32, "rstdg")
    ST = sb([128, 512], F32, "ST")
    STb = sb([128, 512], BF16, "STb")
    LD = sb([128, 16, 2, 128], BF16, "LD")
    LO = sb([128, 16, 2, 128], BF16, "LO")
    Ec = sb([128, 16, 128], F32, "Ec")
    Es = sb([128, 16, 128], F32, "Es")
    big = sb([128, 2048], F32, "big")
    Zall = big[:].rearrange("p (a g i) -> p a g i", a=16, g=8)
    maskZ = sb([128, 16, 8], F32, "maskZ")
    nat = {k: sb([128, 16, 16], F32, k) for k in ("bre", "bim", "cre", "cim", "NBre", "NBim", "n1", "n2")}
    sm = {k: sb([128, 16], F32, k) for k in ("are", "aim", "ldt", "dt", "dar", "mag", "th", "c", "s", "ta", "tb", "c2", "s2",
                                              "abr", "abi", "abr1", "den", "rden", "fre", "fim", "u1", "u2",
                                              "ire", "iim", "v1", "v2")}
    cm = [sb([128, 16], F32, "cm%d" % k) for k in range(7)]
    smm = [sb([128, 16], F32, "sm%d" % k) for k in range(7)]
    Hp = sb([128, 16, 2], F32, "Hp")
    t1 = sb([128, 4, 128], F32, "t1")
    t2 = sb([128, 4, 128], F32, "t2")
    t3 = sb([128, 4, 128], F32, "t3")
    t4 = sb([128, 4, 128], F32, "t4")
    t5 = sb([128, 4, 128], F32, "t5")
    t6 = sb([128, 4, 128], F32, "t6")
    t7 = sb([128, 4, 128], F32, "t7")
    t8 = sb([128, 4, 128], F32, "t8")
    xm2 = sb([128, 4, 2, 128], F32, "xm2")
    xm = big[:, 0:1024].rearrange("p (a q l) -> p a q l", a=4, q=2)
    ht = big[:, 1024:2048].rearrange("p (a q l) -> p a q l", a=4, q=2)
    xmb = [xm, xm2[:]]
    H = sb([128, 4, 2, 128], F32, "H")
    Hb = sb([128, 4, 2, 128], BF16, "Hb")
    y5s = sb([128, 4, 128], F32, "y5s")
    ga = sb([128, 128], F32, "ga")
    gb = sb([128, 128], F32, "gb")
    gth = sb([128, 128], F32, "gth")
    ge = sb([128, 4, 128], BF16, "ge")
    sig = sb([128, 128], F32, "sig")
    PJ = ps([128, 2, 512], F32, "PJ")
    M1 = ps([128, 1024], BF16, "M1")
    M2 = ps([128, 512], F32, "M2")
    BC = ps([128, 8, 128], F32, "BC")
    GY = ps([128, 512], F32, "GY")
    SN = ps([128, 512], F32, "SN")
    PJf = PJ[:].rearrange("p s n -> p (s n)")
    drv = PJf.rearrange("p (r q n) -> p r q n", r=4, q=2)
    xs_tok = M1[:, 0:512].rearrange("p (c n) -> p c n", c=4)
    Btok_ps = M1[:, 512:768].rearrange("p (g n) -> p g n", g=2)
    csdt_ps = M2[:, 0:16]
    ss_ps = M2[:, 128:384].rearrange("p (g n) -> p g n", g=2)
    gate_ps = M2[:, 384:512]
    G_ps = GY[:, 0:256].rearrange("p (g n) -> p g n", g=2)
    y_ps = GY[:, 256:384]
    y5_ps = GY[:, 384:512]

    R = {}
    alias = {"csdt_ps": "M2", "ss": "M2", "gate": "M2", "xs_tok": "M1", "Btok_ps": "M1", "G": "GY", "y": "GY", "y5": "GY"}
    psum_names = ("M1", "M2", "GY", "BC", "SN")

    def r(name):
        name = alias.get(name, name)
        if name not in R:
            R[name] = Reg(name, name in psum_names)
        return R[name]

    RWin, RWout = regs("Win", KC), regs("Wout", KC)
    RX, Rh = regs("X", KC), regs("h", KC)
    Ry = regs("y", 8)
    Rpj = regs("pj", 2, True)
    R["pj0"], R["pj1"] = Rpj

    sp_loads = [("gain", gain), ("convw", convw), ("convb", convb), ("dtb", dtb), ("alog", alog), ("dskip", dskip),
                ("ngain", ngain), ("d5", d5), ("bglu", bglu), ("ident", ident_f), ("are", sm["are"]), ("aim", sm["aim"]),
                ("ldt", sm["ldt"]), ("bre", nat["bre"]), ("bim", nat["bim"]), ("cre", nat["cre"]), ("cim", nat["cim"])]
    for k, tgt in sp_loads:
        P.D("sp", writes=[r(k)], out=tgt[:], in_=dd[k])
    win_v = dd["win"].rearrange("(kc p) f -> p kc f", p=128)
    wout_v = dd["wout"].rearrange("(kc p) f -> p kc f", p=128)
    for kc in range(KC):
        P.D("pool", writes=[RWin[kc]], out=Win[:, kc, :], in_=win_v[:, kc, :])
    for kc in range(KC):
        P.D("pool", writes=[RWout[kc]], out=Wout[:, kc, :], in_=wout_v[:, kc, :])
    P.D("pool", writes=[r("Wglu")], out=Wglu[:], in_=dd["wglu"].rearrange("(c p) o -> p c o", p=128))

    I = P.I
    I("dve", "tensor_copy", reads=[r("ident")], writes=[r("identb")], out=ident[:], in_=ident_f[:])
    I("act", "activation", reads=[r("alog")], writes=[r("aneg")], out=aneg[:], in_=alog[:], func=AF.Exp)
    I("dve", "tensor_scalar", reads=[r("aneg")], writes=[r("aneg")], out=aneg[:], in0=aneg[:], scalar1=-1.0, scalar2=None,
      op0=ALU.mult)
    for c8 in range(8):
        for k in range(4):
            I("dve", "tensor_scalar", reads=[r("ident"), r("convw")], writes=[r("diagw")], out=diagw[:, c8, k, :],
              in0=ident_f[:], scalar1=convw[:, c8, k:k + 1], scalar2=None, op0=ALU.mult)
    for c in range(4):
        I("dve", "tensor_scalar", reads=[r("ident"), r("dskip")], writes=[r("diagD")], out=diagD[:, c, :], in0=ident_f[:],
          scalar1=dskip[:, c:c + 1], scalar2=None, op0=ALU.mult)
        I("dve", "tensor_scalar", reads=[r("ident"), r("d5")], writes=[r("diagD5")], out=diagD5[:, c, :], in0=ident_f[:],
          scalar1=d5[:, c:c + 1], scalar2=None, op0=ALU.mult)
    I("dve", "tensor_copy", reads=[r("ident")], writes=[r("selH")], out=selH[:], in_=bview(ident_f[0:8, 0:8], 2, [8, 8, 128]))
    I("dve", "tensor_tensor_scan", reads=[r("ident"), C.R_ones], writes=[r("tri")], out=tri[:], data0=C.ones_f[:],
      data1=ident_f[:], initial=0.0, op0=ALU.mult, op1=ALU.add)
    I("dve", "tensor_scalar", reads=[r("tri")], writes=[r("maskneg")], out=maskneg[:], in0=tri[:], scalar1=1.0, scalar2=1e30,
      op0=ALU.subtract, op1=ALU.mult)
    I("dve", "memset", writes=[r("ST")], ap=ST[:], constant=0.0)
    I("dve", "memset", writes=[r("STb")], ap=STb[:], constant=0.0)
    I("dve", "memset", writes=[r("xbc%d" % c8) for c8 in range(8)], ap=xbc[:, :, 0:3], constant=0.0)
    I("dve", "memset", writes=[r("ire"), r("iim")], ap=sm["ire"][:], constant=0.0)
    I("dve", "memset", writes=[r("iim")], ap=sm["iim"][:], constant=0.0)
    s = sm

    def tt(out, a, b, op, eng="dve", rd=(), wr=()):
        I(eng, "tensor_tensor", reads=[r(x) for x in rd], writes=[r(x) for x in wr], out=out, in0=a, in1=b, op=op)

    I("act", "activation", reads=[r("ldt")], writes=[r("dt")], out=s["dt"][:], in_=s["ldt"][:], func=AF.Exp)
    tt(s["dar"][:], s["dt"][:], s["are"][:], ALU.mult, rd=["dt", "are"], wr=["dar"])
    I("act", "activation", reads=[r("dar")], writes=[r("mag")], out=s["mag"][:], in_=s["dar"][:], func=AF.Exp)
    tt(s["th"][:], s["dt"][:], s["aim"][:], ALU.mult, rd=["dt", "aim"], wr=["th"])
    I("act", "activation", reads=[r("th")], writes=[r("s")], out=s["s"][:], in_=s["th"][:], func=AF.Sin, scale=1.0 / 32)
    I("act", "activation", reads=[r("th")], writes=[r("c")], out=s["c"][:], in_=s["th"][:], func=AF.Sin, scale=1.0 / 32,
      bias=float(np.pi / 2))

    def double_cs():
        tt(s["ta"][:], s["c"][:], s["c"][:], ALU.mult, rd=["c"], wr=["ta"])
        tt(s["tb"][:], s["s"][:], s["s"][:], ALU.mult, rd=["s"], wr=["tb"])
        I("dve", "scalar_tensor_tensor", reads=[r("c"), r("s")], writes=[r("s2")], out=s["s2"][:], in0=s["c"][:], scalar=2.0,
          in1=s["s"][:], op0=ALU.mult, op1=ALU.mult)
        tt(s["c"][:], s["ta"][:], s["tb"][:], ALU.subtract, rd=["ta", "tb"], wr=["c"])
        I("dve", "tensor_copy", reads=[r("s2")], writes=[r("s")], out=s["s"][:], in_=s["s2"][:])

    for _ in range(5):
        double_cs()
    for k in range(7):
        I("dve", "tensor_copy", reads=[r("c")], writes=[r("cm%d" % k)], out=cm[k][:], in_=s["c"][:])
        I("dve", "tensor_copy", reads=[r("s")], writes=[r("sm%d" % k)], out=smm[k][:], in_=s["s"][:])
        if k < 6:
            double_cs()
    c1, s1 = cm[0], smm[0]
    tt(s["abr"][:], s["mag"][:], c1[:], ALU.mult, rd=["mag", "cm0"], wr=["abr"])
    tt(s["abi"][:], s["mag"][:], s1[:], ALU.mult, rd=["mag", "sm0"], wr=["abi"])
    I("dve", "tensor_scalar", reads=[r("abr")], writes=[r("abr1")], out=s["abr1"][:], in0=s["abr"][:], scalar1=-1.0,
      scalar2=None, op0=ALU.add)
    tt(s["ta"][:], s["are"][:], s["are"][:], ALU.mult, rd=["are"], wr=["ta"])
    tt(s["tb"][:], s["aim"][:], s["aim"][:], ALU.mult, rd=["aim"], wr=["tb"])
    tt(s["den"][:], s["ta"][:], s["tb"][:], ALU.add, rd=["ta", "tb"], wr=["den"])
    I("dve", "reciprocal", reads=[r("den")], writes=[r("rden")], out=s["rden"][:], in_=s["den"][:])
    tt(s["u1"][:], s["abr1"][:], s["are"][:], ALU.mult, rd=["abr1", "are"], wr=["u1"])
    tt(s["u2"][:], s["abi"][:], s["aim"][:], ALU.mult, rd=["abi", "aim"], wr=["u2"])
    tt(s["u1"][:], s["u1"][:], s["u2"][:], ALU.add, rd=["u1", "u2"], wr=["u1"])
    tt(s["fre"][:], s["u1"][:], s["rden"][:], ALU.mult, rd=["u1", "rden"], wr=["fre"])
    tt(s["u1"][:], s["abi"][:], s["are"][:], ALU.mult, rd=["abi", "are"], wr=["u1"])
    tt(s["u2"][:], s["abr1"][:], s["aim"][:], ALU.mult, rd=["abr1", "aim"], wr=["u2"])
    tt(s["u1"][:], s["u1"][:], s["u2"][:], ALU.subtract, rd=["u1", "u2"], wr=["u1"])
    tt(s["fim"][:], s["u1"][:], s["rden"][:], ALU.mult, rd=["u1", "rden"], wr=["fim"])
    fre_b = bview(s["fre"][:], 2, [128, 16, 16])
    fim_b = bview(s["fim"][:], 2, [128, 16, 16])
    tt(nat["n1"][:], nat["bre"][:], fre_b, ALU.mult, rd=["bre", "fre"], wr=["n1"])
    tt(nat["n2"][:], nat["bim"][:], fim_b, ALU.mult, rd=["bim", "fim"], wr=["n2"])
    tt(nat["NBre"][:], nat["n1"][:], nat["n2"][:], ALU.subtract, rd=["n1", "n2"], wr=["NBre"])
    tt(nat["n1"][:], nat["bim"][:], fre_b, ALU.mult, rd=["bim", "fre"], wr=["n1"])
    tt(nat["n2"][:], nat["bre"][:], fim_b, ALU.mult, rd=["bre", "fim"], wr=["n2"])
    tt(nat["NBim"][:], nat["n1"][:], nat["n2"][:], ALU.add, rd=["n1", "n2"], wr=["NBim"])
    I("dve", "memset", writes=[r("maskZ")], ap=maskZ[:], constant=0.0)
    for pr in range(16):
        rr = pr % 4
        I("dve", "memset", writes=[r("maskZ")], ap=maskZ[0:64, pr, 2 * rr:2 * rr + 1], constant=1.0)
        I("dve", "memset", writes=[r("maskZ")], ap=maskZ[64:128, pr, 2 * rr + 1:2 * rr + 2], constant=1.0)
    mz_b = bview(maskZ[:], 3, [128, 16, 8, 16])
    LOv = LO[:].rearrange("p a q (g i) -> p a q g i", g=8)
    LDv = LD[:]
    for q, key in enumerate(("NBre", "NBim")):
        tt(Zall, bview(nat[key][:], 2, [128, 16, 8, 16]), mz_b, ALU.mult, rd=[key, "maskZ"], wr=["Zall"])
        for j in range(4):
            for k4 in range(4):
                I("pe", "transpose", reads=[r("Zall"), r("ident")], writes=[r("BC")], out=BC[:, k4, :],
                  in_=Zall[:, 4 * j + k4, :, :].rearrange("p g i -> p (g i)"), identity=ident_f[:])
            I("act", "activation", reads=[r("BC")], writes=[r("LD")], out=LDv[:, 4 * j:4 * j + 4, q, :], in_=BC[:, 0:4, :],
              func=AF.Copy)
    tt(LOv[:, :, 0, :, :], bview(nat["cre"][:], 2, [128, 16, 8, 16]), mz_b, ALU.mult, rd=["cre", "maskZ"], wr=["LO"])
    I("dve", "tensor_scalar", reads=[r("cim")], writes=[r("n1")], out=nat["n1"][:], in0=nat["cim"][:], scalar1=-1.0, scalar2=None,
      op0=ALU.mult)
    tt(LOv[:, :, 1, :, :], bview(nat["n1"][:], 2, [128, 16, 8, 16]), mz_b, ALU.mult, rd=["n1", "maskZ"], wr=["LO"])
    I("dve", "memset", writes=[r("Ec")], ap=Ec[:, :, 0:1], constant=1.0)
    I("dve", "memset", writes=[r("Es")], ap=Es[:, :, 0:1], constant=0.0)
    tA = tmpL[:].rearrange("p h l -> p (h l)")
    tB = H[:].rearrange("p a q l -> p (a q l)")
    for k in range(7):
        m = 1 << k
        cb = bview(cm[k][:], 2, [128, 16, m])
        sbv = bview(smm[k][:], 2, [128, 16, m])
        vA = tA[:, 0:16 * m].rearrange("p (a m) -> p a m", a=16)
        vB = tB[:, 0:16 * m].rearrange("p (a m) -> p a m", a=16)
        tt(vA, Ec[:, :, 0:m], cb, ALU.mult, rd=["Ec", "cm%d" % k], wr=["tmpL"])
        tt(vB, Es[:, :, 0:m], sbv, ALU.mult, rd=["Es", "sm%d" % k], wr=["H"])
        tt(Ec[:, :, m:2 * m], vA, vB, ALU.subtract, rd=["tmpL", "H"], wr=["Ec"])
        tt(vA, Es[:, :, 0:m], cb, ALU.mult, rd=["Es", "cm%d" % k], wr=["tmpL"])
        tt(vB, Ec[:, :, 0:m], sbv, ALU.mult, rd=["Ec", "sm%d" % k], wr=["H"])
        tt(Es[:, :, m:2 * m], vA, vB, ALU.add, rd=["tmpL", "H"], wr=["Es"])

    xs_v = xsrc.rearrange("(kc p) t -> p kc t", p=128)
    xd_v = xdst.rearrange("(kc p) t -> p kc t", p=128)
    pjn = [0]

    def nextpj():
        pjn[0] += 1
        return pjn[0] % 2

    def proj(col0, ncols, rhs_slice=None):
        sl = nextpj()
        for kc in range(KC):
            I("pe", "matmul", reads=[RWin[kc], Rh[kc]], writes=[Rpj[sl]], out=PJ[0:ncols, sl, 0:NTE],
              lhsT=Win[:, kc, col0:col0 + ncols], rhs=hb[:, kc, :], start=(kc == 0), stop=(kc == KC - 1))
        return sl

    for t in range(ntiles):
        tsl = slice(t * NTE, (t + 1) * NTE)
        P.D("sp", reads=aslist(Rxsrc[t]), writes=RX, out=X[:], in_=xs_v[:, :, tsl])
        sl0 = nextpj()
        for kc in range(KC):
            s2 = kc % 2
            I("act", "activation", reads=[RX[kc]], writes=[r("sq%d" % s2)], out=sq[s2][:], in_=X[:, kc, :], func=AF.Square)
            I("pe", "matmul", reads=[r("sq%d" % s2), C.R_ones], writes=[Rpj[sl0]], out=PJ[:, sl0, 0:NTE], lhsT=C.ones_f[:],
              rhs=sq[s2][:], start=(kc == 0), stop=(kc == KC - 1))
        I("act", "activation", reads=[Rpj[sl0]], writes=[r("rt")], out=rt[:], in_=PJ[:, sl0, 0:NTE], func=AF.Ln,
          scale=1.0 / D, bias=EPS)
        I("act", "activation", reads=[r("rt")], writes=[r("rstd")], out=rstd[:], in_=rt[:], func=AF.Exp, scale=-0.5)
        for kc in range(KC):
            I("dve", "scalar_tensor_tensor", reads=[RX[kc], r("gain"), r("rstd")], writes=[Rh[kc]], out=hb[:, kc, :],
              in0=X[:, kc, :], scalar=gain[:, kc:kc + 1], in1=rstd[:], op0=ALU.mult, op1=ALU.mult)
        for c in range(4):
            sl = proj(c * 128, 128)
            I("act", "activation", reads=[Rpj[sl]], writes=[r("zs%d" % c)], out=zs[:, c, :], in_=PJ[:, sl, 0:NTE], func=AF.Silu)
        for c8 in range(8):
            sl = proj(512 + c8 * 128, 128)
            I("act", "activation", reads=[Rpj[sl]], writes=[r("xbc%d" % c8)], out=xbc[:, c8, 3:3 + NTE], in_=PJ[:, sl, 0:NTE],
              func=AF.Copy)
            sl = nextpj()
            for k in range(4):
                I("pe", "matmul", reads=[r("diagw"), r("xbc%d" % c8)], writes=[Rpj[sl]], out=PJ[:, sl, 0:NTE],
                  lhsT=diagw[:, c8, k, :], rhs=xbc[:, c8, k:k + NTE], start=(k == 0), stop=(k == 3))
            if c8 < 4:
                dst, rn = xsT[:, c8, :], "xsT%d" % c8
            elif c8 < 6:
                dst, rn = BT[:, c8 - 4, :], "BT%d" % (c8 - 4)
            else:
                dst, rn = CT[:, c8 - 6, :], "CT%d" % (c8 - 6)
            I("act", "activation", reads=[Rpj[sl], r("convb")], writes=[r(rn)], out=dst, in_=PJ[:, sl, 0:NTE], func=AF.Silu,
              bias=convb[:, c8:c8 + 1])
            I("pool", "tensor_copy", reads=[r("xbc%d" % c8)], writes=[r("xbc%d" % c8)], out=xbc[:, c8, 0:3],
              in_=xbc[:, c8, NTE:NTE + 3])
        sl = proj(1536, 8)
        I("act", "activation", reads=[Rpj[sl], r("dtb")], writes=[r("dte1")], out=dte1[:], in_=PJ[0:8, sl, 0:NTE], func=AF.Exp,
          bias=dtb[:, 0:1])
        I("act", "activation", reads=[r("dte1")], writes=[r("dtsp")], out=dtsp[:], in_=dte1[:], func=AF.Ln, bias=1.0)
        I("dve", "tensor_scalar", reads=[r("dtsp"), r("aneg")], writes=[r("dA")], out=dA[:], in0=dtsp[:], scalar1=aneg[:, 0:1],
          scalar2=None, op0=ALU.mult)
        for c in range(4):
            sl = proj(1544 + c * 128, 128)
            I("act", "activation", reads=[Rpj[sl]], writes=[r("uT%d" % c)], out=uT[:, c, :], in_=PJ[:, sl, 0:NTE], func=AF.Copy)

        for qc in range(NTE // QC):
            lr = slice(qc * QC, (qc + 1) * QC)

            def seg0():
                I("dve", "tensor_tensor_scan", reads=[r("dA"), C.R_ones], writes=[r("cs")], out=cs[:, lr],
                  data0=C.ones_f[0:8, :], data1=dA[:, lr], initial=0.0, op0=ALU.mult, op1=ALU.add)
                I("pe", "transpose", reads=[r("cs"), r("ident")], writes=[r("csdt_ps")], out=csdt_ps[:, 0:8], in_=cs[:, lr],
                  identity=ident_f[0:8, 0:8])
                I("pe", "transpose", reads=[r("dtsp"), r("ident")], writes=[r("csdt_ps")], out=csdt_ps[:, 8:16], in_=dtsp[:, lr],
                  identity=ident_f[0:8, 0:8])
                I("act", "activation", reads=[r("csdt_ps")], writes=[r("csdt")], out=csdt[:], in_=csdt_ps, func=AF.Copy)
                for hh in range(8):
                    I("pe", "matmul", reads=[r("selH"), r("cs")], writes=[r("BC")], out=BC[:, hh, :], lhsT=selH[:, hh, :],
                      rhs=cs[:, lr], start=True, stop=True)
                for hh in range(8):
                    I("dve", "scalar_tensor_tensor", reads=[r("BC"), r("csdt"), r("maskneg")], writes=[r("tmpL")],
                      out=tmpL[:, hh, :], in0=BC[:, hh, :], scalar=csdt[:, hh:hh + 1], in1=maskneg[:], op0=ALU.subtract,
                      op1=ALU.add)
                I("dve", "tensor_tensor", reads=[r("BC"), r("csdt")], writes=[r("d1")], out=d1[:], in0=BC[:, :, QC - 1],
                  in1=csdt[:, 0:8], op=ALU.subtract)
                I("act", "activation", reads=[r("BC")], writes=[r("ecs")], out=ecs[:], in_=BC[:], func=AF.Exp)
                I("act", "activation", reads=[r("BC")], writes=[r("eend")], out=eend[:], in_=BC[:, :, QC - 1], func=AF.Exp)
                I("act", "activation", reads=[r("tmpL")], writes=[r("tmpL")], out=tmpL[:], in_=tmpL[:], func=AF.Exp)
                I("act", "activation", reads=[r("d1")], writes=[r("dte")], out=dte[:], in_=d1[:], func=AF.Exp)
                for g in range(2):
                    I("pe", "matmul", reads=[r("BT%d" % g), r("CT%d" % g)], writes=[r("G")], out=G_ps[:, g, :],
                      lhsT=BT[:, g, lr], rhs=CT[:, g, lr], start=True, stop=True)

            def seg1():
                I("dve", "tensor_tensor", reads=[r("tmpL"), r("G")], writes=[r("Mt")],
                  out=Mt[:].rearrange("p (g j) l -> p g j l", g=2), in0=tmpL[:].rearrange("p (g j) l -> p g j l", g=2),
                  in1=bview(G_ps, 2, [128, 2, 4, 128]), op=ALU.mult)
                I("pool", "tensor_tensor", reads=[r("CT0"), r("CT1"), r("ecs")], writes=[r("Cs")],
                  out=Cs[:].rearrange("p (g j) l -> p g j l", g=2), in0=ecs[:].rearrange("p (g j) l -> p g j l", g=2),
                  in1=bview(CT[:, :, lr], 2, [128, 2, 4, 128]), op=ALU.mult)
                for c in range(4):
                    I("pe", "transpose", reads=[r("xsT%d" % c), r("identb")], writes=[r("xs_tok")], out=xs_tok[:, c, :],
                      in_=xsT[:, c, lr], identity=ident[:])
                for g in range(2):
                    I("pe", "transpose", reads=[r("BT%d" % g), r("identb")], writes=[r("Btok_ps")], out=Btok_ps[:, g, :],
                      in_=BT[:, g, lr], identity=ident[:])
                I("dve", "tensor_tensor", reads=[r("xs_tok"), r("csdt")], writes=[r("xdt")],
                  out=xdt[:].rearrange("p (h q) -> p h q", h=8), in0=M1[:, 0:512].rearrange("p (h q) -> p h q", h=8),
                  in1=bview(csdt[:, 8:16], 2, [128, 8, 64]), op=ALU.mult)
                I("act", "activation", reads=[r("Btok_ps")], writes=[r("Btok")], out=Btok[:], in_=Btok_ps, func=AF.Copy)
                I("pool", "tensor_tensor", reads=[r("xdt"), r("dte")], writes=[r("xw")],
                  out=xw[:].rearrange("p (h q) -> p h q", h=8), in0=xdt[:].rearrange("p (h q) -> p h q", h=8),
                  in1=bview(dte[:], 2, [128, 8, 64]), op=ALU.mult)

            def seg_y(g):
                for c in (2 * g, 2 * g + 1):
                    for e in range(2):
                        hh = 2 * c + e
                        ysl = y_ps[e * 64:(e + 1) * 64, :]
                        I("pe", "matmul", reads=[r("xdt"), r("Mt")], writes=[r("y")], out=ysl,
                          lhsT=xdt[:, hh * 64:(hh + 1) * 64], rhs=Mt[:, hh, :], start=True, stop=False)
                        I("pe", "matmul", reads=[r("STb"), r("Cs")], writes=[r("y")], out=ysl,
                          lhsT=STb[:, hh * 64:(hh + 1) * 64], rhs=Cs[:, hh, :], start=False, stop=False)
                        I("pe", "matmul", reads=[r("diagD"), r("xsT%d" % c)], writes=[r("y")], out=ysl,
                          lhsT=diagD[:, c, e * 64:(e + 1) * 64], rhs=xsT[:, c, lr], start=False, stop=True)
                    I("dve", "tensor_tensor", reads=[r("y"), r("zs%d" % c)], writes=[r("yg%d" % c)], out=yg[:, c, :], in0=y_ps,
                      in1=zs[:, c, lr], op=ALU.mult)
                    I("act", "activation", reads=[r("yg%d" % c)], writes=[r("sqy%d" % (c % 2))], out=sqy[c % 2][:],
                      in_=yg[:, c, :], func=AF.Square)
                    I("pe", "matmul", reads=[r("sqy%d" % (c % 2)), C.R_ones], writes=[r("ss")], out=ss_ps[:, g, :],
                      lhsT=C.ones_f[:], rhs=sqy[c % 2][:], start=(c % 2 == 0), stop=(c % 2 == 1))
                I("act", "activation", reads=[r("ss")], writes=[r("rtg")], out=rtg[:], in_=ss_ps[:, g, :], func=AF.Ln,
                  scale=1.0 / 256, bias=EPS)
                I("act", "activation", reads=[r("rtg")], writes=[r("rstdg")], out=rstdg[:], in_=rtg[:], func=AF.Exp, scale=-0.5)
                for c2 in (2 * g, 2 * g + 1):
                    I("dve", "scalar_tensor_tensor", reads=[r("yg%d" % c2), r("ngain"), r("rstdg")], writes=[Ry[c2]],
                      out=yT[:, c2, lr], in0=yg[:, c2, :], scalar=ngain[:, c2:c2 + 1], in1=rstdg[:], op0=ALU.mult,
                      op1=ALU.mult)

            def seg_state():
                for g in range(2):
                    I("pe", "matmul", reads=[r("Btok"), r("xw")], writes=[r("SN")], out=SN[:, g * 256:(g + 1) * 256],
                      lhsT=Btok[:, g, :], rhs=xw[:, g * 256:(g + 1) * 256], start=True, stop=True)
                I("pool", "tensor_tensor", reads=[r("ST"), r("eend")], writes=[r("ST")],
                  out=ST[:].rearrange("p (h q) -> p h q", h=8), in0=ST[:].rearrange("p (h q) -> p h q", h=8),
                  in1=bview(eend[:], 2, [128, 8, 64]), op=ALU.mult)
                I("dve", "tensor_tensor", reads=[r("SN"), r("ST")], writes=[r("ST")], out=ST[:], in0=SN[:], in1=ST[:], op=ALU.add)
                I("act", "activation", reads=[r("ST")], writes=[r("STb")], out=STb[:], in_=ST[:], func=AF.Copy)

            def s5A(T4):
                p4 = slice(4 * T4, 4 * T4 + 4)
                xz = xmb[T4 % 2]
                for rr in range(4):
                    for q in range(2):
                        I("pe", "matmul", reads=[r("LD"), r("uT%d" % T4)], writes=[Rpj[0], Rpj[1]], out=drv[:, rr, q, :],
                          lhsT=LD[:, 4 * T4 + rr, q, :], rhs=uT[:, T4, lr], start=True, stop=True)
                dre, dim_ = drv[:, :, 0, :], drv[:, :, 1, :]
                Ec4, Es4 = Ec[:, p4, :], Es[:, p4, :]
                tt(t1[:], dre, Ec4, ALU.mult, rd=["pj0", "pj1", "Ec"], wr=["t1"])
                tt(t2[:], dim_, Es4, ALU.mult, rd=["pj0", "pj1", "Es"], wr=["t2"])
                tt(t3[:], dim_, Ec4, ALU.mult, rd=["pj0", "pj1", "Ec"], wr=["t3"])
                tt(t4[:], dre, Es4, ALU.mult, rd=["pj0", "pj1", "Es"], wr=["t4"])
                tt(xz[:, :, 0, :], t1[:], t2[:], ALU.add, eng="pool", rd=["t1", "t2"], wr=["xm%d" % (T4 % 2), "Zall"])
                tt(xz[:, :, 1, :], t3[:], t4[:], ALU.subtract, eng="pool", rd=["t3", "t4"], wr=["xm%d" % (T4 % 2), "Zall"])

            def s5B(T4):
                p4 = slice(4 * T4, 4 * T4 + 4)
                xz = xmb[T4 % 2]
                Ec4, Es4 = Ec[:, p4, :], Es[:, p4, :]
                for rr in range(4):
                    pr = 4 * T4 + rr
                    mg = s["mag"][:, pr:pr + 1].broadcast_to([128, QC])
                    I("dve", "tensor_tensor_scan", reads=[r("xm%d" % (T4 % 2)), r("mag"), r("ire")], writes=[r("ht"), r("Zall")],
                      out=ht[:, rr, 0, :], data0=mg, data1=xz[:, rr, 0, :], initial=s["ire"][:, pr:pr + 1], op0=ALU.mult,
                      op1=ALU.add)
                    I("dve", "tensor_tensor_scan", reads=[r("xm%d" % (T4 % 2)), r("mag"), r("iim")], writes=[r("ht"), r("Zall")],
                      out=ht[:, rr, 1, :], data0=mg, data1=xz[:, rr, 1, :], initial=s["iim"][:, pr:pr + 1], op0=ALU.mult,
                      op1=ALU.add)
                tt(t5[:], ht[:, :, 0, :], Ec4, ALU.mult, rd=["ht", "Ec"], wr=["t5"])
                tt(t6[:], ht[:, :, 1, :], Es4, ALU.mult, rd=["ht", "Es"], wr=["t6"])
                tt(H[:, :, 0, :], t5[:], t6[:], ALU.subtract, eng="pool", rd=["t5", "t6"], wr=["H"])
                tt(t7[:], ht[:, :, 1, :], Ec4, ALU.mult, eng="pool", rd=["ht", "Ec"], wr=["t7"])
                tt(t8[:], ht[:, :, 0, :], Es4, ALU.mult, eng="pool", rd=["ht", "Es"], wr=["t8"])
                tt(H[:, :, 1, :], t7[:], t8[:], ALU.add, eng="pool", rd=["t7", "t8"], wr=["H"])
                I("act", "activation", reads=[r("H")], writes=[r("Hb")], out=Hb[:], in_=H[:], func=AF.Copy)
                I("pool", "tensor_copy", reads=[r("H")], writes=[r("Hp")], out=Hp[:, p4, :], in_=H[:, :, :, QC - 1])
                n = 0
                for rr in range(4):
                    for q in range(2):
                        I("pe", "matmul", reads=[r("LO"), r("Hb")], writes=[r("y5")], out=y5_ps, lhsT=LO[:, 4 * T4 + rr, q, :],
                          rhs=Hb[:, rr, q, :], start=(n == 0), stop=False)
                        n += 1
                I("pe", "matmul", reads=[r("diagD5"), r("uT%d" % T4)], writes=[r("y5")], out=y5_ps, lhsT=diagD5[:, T4, :],
                  rhs=uT[:, T4, lr], start=False, stop=True)
                I("act", "activation", reads=[r("y5")], writes=[r("y5s%d" % T4)], out=y5s[:, T4, :], in_=y5_ps, func=AF.Copy)

            def s5C(T4):
                yv = y5s[:, T4, :]
                tt(ga[:], yv, yv, ALU.mult, rd=["y5s%d" % T4], wr=["ga"])
                I("dve", "tensor_scalar", reads=[r("ga")], writes=[r("ga")], out=ga[:], in0=ga[:], scalar1=0.044715, scalar2=1.0,
                  op0=ALU.mult, op1=ALU.add)
                tt(gb[:], ga[:], yv, ALU.mult, rd=["ga", "y5s%d" % T4], wr=["gb"])
                I("act", "activation", reads=[r("gb")], writes=[r("gth")], out=gth[:], in_=gb[:], func=AF.Tanh,
                  scale=0.7978845608028654)
                I("dve", "scalar_tensor_tensor", reads=[r("gth"), r("y5s%d" % T4)], writes=[r("ge%d" % T4)], out=ge[:, T4, :],
                  in0=gth[:], scalar=1.0, in1=yv, op0=ALU.add, op1=ALU.mult)

            s5A(0)
            seg0()
            s5A(1)
            s5B(0)
            seg1()
            s5A(2)
            s5B(1)
            s5C(0)
            seg_y(0)
            s5A(3)
            s5B(2)
            s5C(1)
            seg_y(1)
            s5B(3)
            s5C(2)
            seg_state()
            s5C(3)
            tt(s["v1"][:], Hp[:, :, 0], c1[:], ALU.mult, eng="pool", rd=["Hp", "cm0"], wr=["v1"])
            tt(s["v2"][:], Hp[:, :, 1], s1[:], ALU.mult, eng="pool", rd=["Hp", "sm0"], wr=["v2"])
            tt(s["ire"][:], s["v1"][:], s["v2"][:], ALU.subtract, eng="pool", rd=["v1", "v2"], wr=["ire"])
            tt(s["v1"][:], Hp[:, :, 1], c1[:], ALU.mult, eng="pool", rd=["Hp", "cm0"], wr=["v1"])
            tt(s["v2"][:], Hp[:, :, 0], s1[:], ALU.mult, eng="pool", rd=["Hp", "sm0"], wr=["v2"])
            tt(s["iim"][:], s["v1"][:], s["v2"][:], ALU.add, eng="pool", rd=["v1", "v2"], wr=["iim"])
            for o in range(4):
                for T4 in range(4):
                    I("pe", "matmul", reads=[r("Wglu"), r("ge%d" % T4)], writes=[r("gate")], out=gate_ps,
                      lhsT=Wglu[:, T4, o * 128:(o + 1) * 128], rhs=ge[:, T4, :], start=(T4 == 0), stop=(T4 == 3))
                I("act", "activation", reads=[r("gate"), r("bglu")], writes=[r("sig")], out=sig[:], in_=gate_ps, func=AF.Sigmoid,
                  scale=0.5, bias=bglu[:, o:o + 1])
                I("dve", "tensor_tensor", reads=[r("sig"), r("y5s%d" % o)], writes=[Ry[4 + o]], out=yT[:, 4 + o, lr],
                  in0=y5s[:, o, :], in1=sig[:], op=ALU.mult)
        for dc in range(KC):
            sl = nextpj()
            for c in range(8):
                I("pe", "matmul", reads=[RWout[c], Ry[c]], writes=[Rpj[sl]], out=PJ[:, sl, 0:NTE],
                  lhsT=Wout[:, c, dc * 128:(dc + 1) * 128], rhs=yT[:, c, :], start=(c == 0), stop=(c == 7))
            I("dve", "tensor_tensor", reads=[Rpj[sl], RX[dc]], writes=[RX[dc]], out=X[:, dc, :], in0=PJ[:, sl, 0:NTE],
              in1=X[:, dc, :], op=ALU.add)
        P.D("sp", reads=RX, writes=aslist(Rxdst[t]), out=xd_v[:, :, tsl], in_=X[:])


DEPTH = 4
SEQ = 4096
NREG = SEQ // 256


def _bias_tiles(rel_bias):
    l = np.arange(128)[:, None]
    k = np.arange(640)[None, :]
    idx = np.clip(512 + l - k, -128, 128) + 128
    return np.ascontiguousarray(np.transpose(rel_bias[:, idx], (1, 0, 2)))


def _col(v):
    n = v.shape[0] // 128
    return np.ascontiguousarray(v.reshape(n, 128).T)


def _even_inputs(inp, i, layer):
    nat = lambda a: np.ascontiguousarray(a.reshape(16, 2, 64).transpose(1, 2, 0).reshape(128, 16))
    return {
        "win": inp['even_w_in'][i], "wout": inp['even_w_out'][i], "wglu": inp['s5_w_glu'][i],
        "gain": _col(inp['mix_norm'][layer]),
        "convw": np.ascontiguousarray(inp['ssd_conv_w'][i].reshape(4, 8, 128).transpose(2, 1, 0)),
        "convb": _col(inp['ssd_conv_b'][i]),
        "dtb": inp['ssd_dt_bias'][i].reshape(8, 1).copy(), "alog": inp['ssd_a_log'][i].reshape(8, 1).copy(),
        "dskip": _col(np.repeat(inp['ssd_d'][i], 64)),
        "ngain": _col(inp['ssd_norm'][i]),
        "d5": _col(inp['s5_d'][i].reshape(-1)),
        "bglu": _col(inp['s5_b_glu'][i]),
        "are": nat(inp['s5_a_re'][i]), "aim": nat(inp['s5_a_im'][i]),
        "ldt": np.ascontiguousarray(
            np.repeat(inp['s5_log_dt'][i].reshape(16, 2, 1), 64, axis=2).transpose(1, 2, 0).reshape(128, 16)),
        "bre": np.ascontiguousarray(inp['s5_b_re'][i].reshape(16, 2, 64, 16).transpose(1, 2, 0, 3).reshape(128, 16, 16)),
        "bim": np.ascontiguousarray(inp['s5_b_im'][i].reshape(16, 2, 64, 16).transpose(1, 2, 0, 3).reshape(128, 16, 16)),
        "cre": np.ascontiguousarray(inp['s5_c_re'][i].reshape(16, 2, 16, 64).transpose(1, 3, 0, 2).reshape(128, 16, 16)),
        "cim": np.ascontiguousarray(inp['s5_c_im'][i].reshape(16, 2, 16, 64).transpose(1, 3, 0, 2).reshape(128, 16, 16)),
    }


def _odd_inputs(inp, i, layer):
    return {
        "win": inp['odd_w_in'][i], "wout": inp['odd_w_out'][i], "pw": inp['pool_w'][i],
        "bt": _bias_tiles(inp['attn_rel_bias'][i]),
        "qg": np.tile(inp['attn_q_norm'][i], 2).reshape(128, 1).copy(),
        "kg": np.tile(inp['attn_k_norm'][i], 2).reshape(128, 1).copy(),
        "psc": _col(inp['pool_scale'][i]),
        "gain": _col(inp['mix_norm'][layer]),
    }


def host_layout(inp):
    m = {"ident": np.eye(128, dtype=np.float32)}
    f1 = (inp['ffn1_w_gate'], inp['ffn1_w_up'], inp['ffn1_w_down'], inp['ffn1_norm'])
    f2 = (inp['ffn2_w_gate'], inp['ffn2_w_up'], inp['ffn2_w_down'], inp['ffn2_norm'])
    for l in range(DEPTH):
        for w, (wg, wu, wd, gn) in ((1, f1), (2, f2)):
            m["f%d_%d_wg" % (w, l)] = wg[l]
            m["f%d_%d_wu" % (w, l)] = wu[l]
            m["f%d_%d_wd" % (w, l)] = wd[l]
            m["f%d_%d_gn" % (w, l)] = _col(gn[l])
        d = _even_inputs(inp, l // 2, l) if l % 2 == 0 else _odd_inputs(inp, l // 2, l)
        for k, v in d.items():
            m["m%d_%s" % (l, k)] = v
    return {k: np.ascontiguousarray(v, dtype=np.float32) for k, v in m.items()}


def build_program(shapes, phases=None, seq=SEQ):
    nc = bass.Bass("TRN2", target_bir_lowering=False)
    di = lambda n, s: nc.dram_tensor(n, list(s), F32, kind="ExternalInput").ap()
    xin = di("xT", [D, seq])
    dd = {k: di(k, s) for k, s in shapes.items()}
    yout = nc.dram_tensor("yT", [D, seq], F32, kind="ExternalOutput").ap()
    xs = nc.dram_tensor("xscratch", [D, seq], F32, kind="Internal").ap()
    if phases is None:
        phases = [(l, k) for l in range(DEPTH) for k in ("f1", "mix", "f2")]
    nreg = seq // 256
    pair = lambda R_: [[R_[2 * t], R_[2 * t + 1]] for t in range(len(R_) // 2)]
    with ExitStack() as es:
        P = Prog(nc, es)
        C = Ctx(nc, P)
        C.es = es
        emit_consts(C)
        Rin, Rsc, Rout = regs("xin", nreg), regs("xsc", nreg), regs("xout", nreg)
        for pi, (l, kind) in enumerate(phases):
            src, Rs = (xin, Rin) if pi == 0 else (xs, Rsc)
            dst, Rd = (yout, Rout) if pi == len(phases) - 1 else (xs, Rsc)
            with ExitStack() as pes:
                C.es = pes
                if kind in ("f1", "f2"):
                    w = 1 if kind == "f1" else 2
                    pre = "f%d_%d_" % (w, l)
                    emit_ffn(C, src, dst, pair(Rs), pair(Rd), dd[pre + "wg"], dd[pre + "wu"], dd[pre + "wd"], dd[pre + "gn"], seq // NT)
                elif l % 2 == 0:
                    sub = {k[len("m%d_" % l):]: v for k, v in dd.items() if k.startswith("m%d_" % l)}
                    sub["ident"] = dd["ident"]
                    emit_even(C, src, dst, Rs, Rd, sub, seq // NTE)
                else:
                    pre = "m%d_" % l
                    emit_odd(C, src, dst, pair(Rs), pair(Rd), dd[pre + "win"], dd[pre + "wout"], dd[pre + "pw"], dd[pre + "bt"],
                             dd[pre + "qg"], dd[pre + "kg"], dd[pre + "psc"], dd[pre + "gain"], dd["ident"], seq // NT)
                P.all_barrier()
                with nc.Block() as block:
                    P.flush(block)
    return nc, P


_CACHE = {}


def kernel(**inputs):
    inp = {k: np.asarray(v) for k, v in inputs.items()}
    x = inp["x"]
    B = x.shape[0]
    wts = host_layout(inp)
    key = "full"
    if key not in _CACHE:
        _CACHE[key] = build_program({k: v.shape for k, v in wts.items()})
    nc, _ = _CACHE[key]
    n_cores = 8
    work = [0, 1, 4, 5][:B]
    zeros = {k: np.zeros_like(v) for k, v in wts.items()}
    zeros["xT"] = np.zeros((x.shape[2], x.shape[1]), np.float32)
    in_maps = []
    for c in range(n_cores):
        if c in work:
            m = dict(wts)
            m["xT"] = np.ascontiguousarray(x[work.index(c)].T)
        else:
            m = zeros
        in_maps.append(m)
    res = run_bass_kernel_spmd(nc, in_maps, core_ids=list(range(n_cores)))
    out = np.stack([np.ascontiguousarray(res.results[c]["yT"].T) for c in work], axis=0)
    return out.astype(np.float32)
```

```python
from concourse.bass_utils import run_bass_kernel_spmd
import concourse.bass as bass
import concourse.mybir as mybir

ENGS = ("pe", "act", "dve", "pool", "sp")
SEM_CAP = 24000
N_DMA_SLOTS = 24
SCHED_WINDOW = 40


class Reg:
    __slots__ = ("name", "w", "rs", "excl")

    def __init__(self, name, excl=False):
        self.name = name
        self.excl = excl
        self.w = None
        self.rs = []


class Op:
    __slots__ = ("eng", "seq", "fn", "deps", "is_dma", "dur", "lat", "grp_open", "grp_cont",
                 "pos", "inc", "waits", "dma_slot", "dma_val", "flushed", "t_end", "done")

    def __init__(self, eng, seq, fn):
        self.eng = eng
        self.seq = seq
        self.fn = fn
        self.deps = []
        self.is_dma = False
        self.dur = 0.1
        self.lat = 0.0
        self.grp_open = False
        self.grp_cont = False
        self.pos = -1
        self.inc = False
        self.waits = []
        self.dma_slot = None
        self.dma_val = 0
        self.flushed = False
        self.t_end = 0.0
        self.done = False


def _free(ap):
    try:
        return int(ap.free_size())
    except Exception:
        return 128


class Prog:
    def __init__(self, nc, es):
        self.nc = nc
        self.es = es
        self.rec = {e: [] for e in ENGS}
        self.seq = 0
        self.csems = {e: [] for e in ENGS}
        self.dma_sems = [es.enter_context(nc.semaphore("dq%d" % i)) for i in range(N_DMA_SLOTS)]
        self.dma_uses = [0] * N_DMA_SLOTS
        self.dma_rr = 0
        self.counts = {e: 0 for e in ENGS}
        self.n_emitted = 0
        self.reorder = True
        self.pe_open = False

    def _new(self, eng, fn):
        o = Op(eng, self.seq, fn)
        self.seq += 1
        return o

    def _dep(self, o, p):
        if p is None or p is o or p.flushed:
            return
        o.deps.append(p)

    def _track(self, o, reads, writes):
        locks = [x for x in list(reads) + list(writes) if x.excl]
        reads = [x for x in reads if not x.excl]
        writes = [x for x in writes if not x.excl]
        for r in reads:
            self._dep(o, r.w)
        for w in writes:
            self._dep(o, w.w)
            for t in w.rs:
                self._dep(o, t)
        for l in locks:
            self._dep(o, l.w)
        for r in reads:
            r.rs.append(o)
        for w in writes:
            w.w = o
            w.rs = []
        for l in locks:
            l.w = o

    def I(self, eng, name, reads=(), writes=(), **kw):
        o = self._new(eng, lambda h: getattr(h, name)(**kw))
        n = _free(kw.get("out", kw.get("ap")))
        if eng == "pe":
            if name == "matmul":
                n = _free(kw["rhs"])
                o.dur = 0.03 + 0.00046 * n
                if kw.get("lhsT") is not None and kw["lhsT"].dtype == mybir.dt.float32:
                    o.dur *= 4
                st, sp_ = kw.get("start"), kw.get("stop")
                o.grp_cont = (st is False)
                o.grp_open = (sp_ is False)
            else:
                o.dur = 0.12
        elif eng == "dve":
            o.dur = 0.08 + 0.00105 * n * (2 if name == "tensor_tensor_scan" else 1)
        elif eng == "act":
            o.dur = 0.22 + 0.00085 * n
        else:
            o.dur = 0.2 + 0.0023 * n
        self._track(o, reads, writes)
        self.rec[eng].append(o)
        return o

    def D(self, eng, reads=(), writes=(), **kw):
        o = self._new(eng, lambda h: h.dma_start(**kw))
        o.is_dma = True
        o.dur = 0.06
        o.lat = 2.0 + _free(kw["out"]) * 128 * 4 / 150e3
        self._track(o, reads, writes)
        self.rec[eng].append(o)
        return o

    def all_barrier(self):
        allops = [o for e in ENGS for o in self.rec[e] if o.fn is not None]
        for e in ENGS:
            o = self._new(e, None)
            o.dur = 0.0
            o.deps = list(allops)
            self.rec[e].append(o)

    def _schedule(self):
        rec = self.rec
        if not self.reorder:
            for e in ENGS:
                for o in rec[e]:
                    o.t_end = float(o.seq)
                    o.lat = 0.0
            return {e: list(rec[e]) for e in ENGS}
        order = {e: [] for e in ENGS}
        nxt = {e: 0 for e in ENGS}
        taken = {e: [False] * len(rec[e]) for e in ENGS}
        tfree = {e: 0.0 for e in ENGS}
        remaining = sum(len(rec[e]) for e in ENGS)
        for e in ENGS:
            for o in rec[e]:
                o.done = False
        XLAT, SLAT = 1.3, 0.6
        pe_open = 0
        pe_last = -1

        def ready_time(o, e):
            t = 0.0
            for p in o.deps:
                if not p.done:
                    return None
                c = p.t_end + (SLAT if (p.eng == e and not p.is_dma) else XLAT)
                if c > t:
                    t = c
            return t

        while remaining:
            best = None
            for e in ENGS:
                ol = rec[e]
                tk = taken[e]
                i = nxt[e]
                while i < len(ol) and tk[i]:
                    i += 1
                nxt[e] = i
                if i >= len(ol):
                    continue
                if e == "pe" and pe_open > 0:
                    q = pe_last + 1
                    while q < len(ol) and tk[q]:
                        q += 1
                    cand = [q] if q < len(ol) else []
                elif e == "sp":
                    cand = [i]
                else:
                    cand = []
                    j = i
                    while j < len(ol) and len(cand) < SCHED_WINDOW:
                        if not tk[j]:
                            o = ol[j]
                            if o.is_dma and j != i:
                                break
                            if not (e == "pe" and o.grp_cont and j != i):
                                cand.append(j)
                            if o.is_dma:
                                break
                        j += 1
                for j in cand:
                    o = ol[j]
                    rt = ready_time(o, e)
                    if rt is None:
                        continue
                    if e == "pe" and j != i and o.grp_open and not o.grp_cont and pe_open == 0:
                        depth, k, ok = 0, j, True
                        members = set()
                        while k < len(ol):
                            m = ol[k]
                            if not tk[k]:
                                members.add(id(m))
                                if m.grp_open and not m.grp_cont:
                                    depth += 1
                                elif m.grp_cont and not m.grp_open:
                                    depth -= 1
                                for p in m.deps:
                                    if not p.done and id(p) not in members:
                                        ok = False
                                        break
                                if not ok or depth == 0:
                                    break
                            k += 1
                        if not ok:
                            continue
                    st = rt if rt > tfree[e] else tfree[e]
                    key = (st, o.seq)
                    if best is None or key < best[0]:
                        best = (key, e, j, o, st)
                    if st <= tfree[e]:
                        break
            if best is None:
                for e in ENGS:
                    i = nxt[e]
                    if i < len(rec[e]):
                        o = rec[e][i]
                        print("STUCK", e, i, "of", len(rec[e]), "seq", o.seq, "dma", o.is_dma, "cont", o.grp_cont, "open", o.grp_open,
                              "pe_open", pe_open, "pe_last", pe_last,
                              "unmet", [(p.eng, p.seq, p.is_dma, rec[p.eng].index(p) if p in rec[p.eng] else -1) for p in o.deps if not p.done][:6])
            assert best is not None, "scheduler deadlock (dependency cycle?)"
            _, e, j, o, st = best
            o.t_end = st + o.dur + o.lat
            o.done = True
            order[e].append(o)
            tfree[e] = st + o.dur
            taken[e][j] = True
            remaining -= 1
            if e == "pe":
                pe_last = j
                if o.grp_open and not o.grp_cont:
                    pe_open += 1
                elif o.grp_cont and not o.grp_open:
                    pe_open -= 1
        return order

    def _csem(self, e, epoch):
        while len(self.csems[e]) <= epoch:
            self.csems[e].append(self.es.enter_context(self.nc.semaphore("c_%s_%d" % (e, len(self.csems[e])))))
        return self.csems[e][epoch]

    def flush(self, block):
        order = self._schedule()
        for e in ENGS:
            for i, o in enumerate(order[e]):
                o.pos = i
        dmas = sorted([o for e in ENGS for o in order[e] if o.is_dma], key=lambda o: (o.t_end - o.lat, o.seq))
        slot_prev = {}
        for o in dmas:
            slot = self.dma_rr
            self.dma_rr = (self.dma_rr + 1) % N_DMA_SLOTS
            if slot in slot_prev:
                o.deps.append(slot_prev[slot])
            slot_prev[slot] = o
            self.dma_uses[slot] += 1
            o.dma_slot = slot
            o.dma_val = 16 * self.dma_uses[slot]
        seen = {e: {e2: -1 for e2 in ENGS} for e in ENGS}
        seen_dma = {e: {} for e in ENGS}
        for e in ENGS:
            for o in order[e]:
                o.waits = []
                for p in o.deps:
                    if p.flushed:
                        continue
                    if p.is_dma:
                        if seen_dma[e].get(p.dma_slot, 0) >= p.dma_val:
                            continue
                        seen_dma[e][p.dma_slot] = p.dma_val
                        o.waits.append(p)
                    else:
                        if p.eng == e:
                            assert p.pos < o.pos, "same-engine dependency order violated"
                            if e == "pe":
                                continue
                        if seen[e][p.eng] >= p.pos:
                            continue
                        seen[e][p.eng] = p.pos
                        p.inc = True
                        o.waits.append(p)
        cnt = {}
        for e in ENGS:
            c = self.counts[e]
            for o in order[e]:
                if o.inc and not o.is_dma and o.fn is not None:
                    c += 1
                    cnt[id(o)] = c
            if c > 0:
                self._csem(e, (c - 1) // SEM_CAP)
            self.counts[e] = c

        def emit(e, h):
            for o in order[e]:
                for p in o.waits:
                    if p.is_dma:
                        h.wait_ge(self.dma_sems[p.dma_slot], p.dma_val)
                    else:
                        ep, v = divmod(cnt[id(p)] - 1, SEM_CAP)
                        h.wait_ge(self._csem(p.eng, ep), v + 1)
                if o.fn is None:
                    continue
                ins = o.fn(h)
                if o.is_dma:
                    ins.then_inc(self.dma_sems[o.dma_slot], 16)
                elif o.inc:
                    ep, v = divmod(cnt[id(o)] - 1, SEM_CAP)
                    ins.then_inc(self._csem(e, ep), 1)
                self.n_emitted += 1

        block.tensor(lambda h: emit("pe", h))
        block.scalar(lambda h: emit("act", h))
        block.vector(lambda h: emit("dve", h))
        block.gpsimd(lambda h: emit("pool", h))
        block.sync(lambda h: emit("sp", h))
        for e in ENGS:
            for o in order[e]:
                o.flushed = True
                o.deps = []
                o.fn = None
            self.rec[e] = []


import numpy as np
from contextlib import ExitStack
import concourse.bass as bass
import concourse.mybir as mybir

F32 = mybir.dt.float32
BF16 = mybir.dt.bfloat16
AF = mybir.ActivationFunctionType
ALU = mybir.AluOpType
AX = mybir.AxisListType

D = 1024
KC = 8
FF = 2816
FC = 22
NT = 512
EPS = 1e-6


class Ctx:
    def __init__(self, nc, P):
        self.nc = nc
        self.P = P
        self.es = None
        self.n = 0

    def sb(self, shape, dt, name=None):
        self.n += 1
        return self.es.enter_context(self.nc.sbuf_tensor("%s_%d" % (name or "t", self.n), shape, dt))

    def ps(self, shape, dt, name=None):
        self.n += 1
        return self.es.enter_context(self.nc.psum_tensor("%s_%d" % (name or "p", self.n), shape, dt))


def aslist(x):
    return list(x) if isinstance(x, (list, tuple)) else [x]


def regs(prefix, n, excl=False):
    return [Reg("%s%d" % (prefix, i), excl) for i in range(n)]


def emit_consts(C):
    P, nc = C.P, C.nc
    C.ones_f = C.sb([128, 128], F32, "ones")
    C.R_ones = Reg("ones")
    P.I("dve", "memset", writes=[C.R_ones], ap=C.ones_f[:], constant=1.0)


def emit_norm(C, X, RX, gain, Rgain, hb, Rh, sq, Rsq, ssq_ps, Rssq, rt, Rrt, rstd, Rrstd):
    P = C.P
    for kc in range(KC):
        s = kc % 2
        P.I("act", "activation", reads=[RX[kc]], writes=[Rsq[s]], out=sq[s][:], in_=X[:, kc, :], func=AF.Square)
        P.I("pe", "matmul", reads=[Rsq[s], C.R_ones], writes=[Rssq], out=ssq_ps[:], lhsT=C.ones_f[:], rhs=sq[s][:],
            start=(kc == 0), stop=(kc == KC - 1))
    P.I("act", "activation", reads=[Rssq], writes=[Rrt], out=rt[:], in_=ssq_ps[:], func=AF.Ln, scale=1.0 / D, bias=EPS)
    P.I("act", "activation", reads=[Rrt], writes=[Rrstd], out=rstd[:], in_=rt[:], func=AF.Exp, scale=-0.5)
    for kc in range(KC):
        P.I("dve", "scalar_tensor_tensor", reads=[RX[kc], Rgain, Rrstd], writes=[Rh[kc]], out=hb[:, kc, :],
            in0=X[:, kc, :], scalar=gain[:, kc:kc + 1], in1=rstd[:], op0=ALU.mult, op1=ALU.mult)


def emit_ffn(C, xsrc, xdst, Rxsrc, Rxdst, wg_d, wu_d, wd_d, gain_d, ntiles):
    P, nc = C.P, C.nc
    Wg = C.sb([128, KC, FF], BF16, "Wg")
    Wu = C.sb([128, KC, FF], BF16, "Wu")
    Wd = C.sb([128, FC, D], BF16, "Wd")
    gain = C.sb([128, KC], F32, "gain")
    xt = [C.sb([128, KC, NT], F32, "xt") for _ in range(2)]
    sq = [C.sb([128, NT], F32, "sq") for _ in range(2)]
    hb = C.sb([128, KC, NT], BF16, "h")
    act = C.sb([128, FC, NT], BF16, "act")
    sg = [C.sb([128, NT], BF16, "sg") for _ in range(2)]
    rt = C.sb([128, NT], F32, "rt")
    rstd = C.sb([128, NT], F32, "rstd")
    ssq_ps = C.ps([128, NT], F32, "ssq")
    g_ps = [C.ps([128, NT], F32, "g") for _ in range(2)]
    u_ps = [C.ps([128, NT], F32, "u") for _ in range(2)]
    o_ps = [C.ps([128, NT], F32, "o") for _ in range(2)]

    RWg, RWu, RWd = regs("Wg", KC), regs("Wu", KC), regs("Wd", FC)
    Rgain = Reg("gain")
    Rxt = [regs("xt%d_" % b, KC) for b in range(2)]
    Rsq = regs("sq", 2)
    Rh = regs("h", KC)
    Ract = regs("act", FC)
    Rsg = regs("sg", 2)
    Rrt, Rrstd, Rssq = Reg("rt"), Reg("rstd"), Reg("ssq", True)
    Rg, Ru, Ro = regs("g", 2, True), regs("u", 2, True), regs("o", 2, True)

    xs_v = xsrc.rearrange("(kc p) t -> p kc t", p=128)
    xd_v = xdst.rearrange("(kc p) t -> p kc t", p=128)

    def load(t, q="sp"):
        b = t % 2
        P.D(q, reads=aslist(Rxsrc[t]), writes=Rxt[b], out=xt[b][:], in_=xs_v[:, :, t * NT:(t + 1) * NT])

    P.D("pool", writes=[Rgain], out=gain[:], in_=gain_d)
    load(0, "pool")
    if ntiles > 1:
        load(1, "pool")
    wg_v = wg_d.rearrange("(kc p) f -> p kc f", p=128)
    wu_v = wu_d.rearrange("(kc p) f -> p kc f", p=128)
    wd_v = wd_d.rearrange("(fc p) d -> p fc d", p=128)
    FG = [(0, 6), (6, 12), (12, 17), (17, 22)]
    RWg, RWu = regs("Wg", len(FG)), regs("Wu", len(FG))
    fgrp = {}
    for gi, (f0, f1) in enumerate(FG):
        for f in range(f0, f1):
            fgrp[f] = gi
        P.D("pool", writes=[RWg[gi]], out=Wg[:, :, f0 * 128:f1 * 128], in_=wg_v[:, :, f0 * 128:f1 * 128])
        P.D("pool", writes=[RWu[gi]], out=Wu[:, :, f0 * 128:f1 * 128], in_=wu_v[:, :, f0 * 128:f1 * 128])
    for fc in range(FC):
        P.D("pool", writes=[RWd[fc]], out=Wd[:, fc, :], in_=wd_v[:, fc, :])

    for t in range(ntiles):
        b = t % 2
        if t + 1 < ntiles and t >= 1:
            load(t + 1)
        X = xt[b]
        emit_norm(C, X, Rxt[b], gain, Rgain, hb, Rh, sq, Rsq, ssq_ps, Rssq, rt, Rrt, rstd, Rrstd)
        for f in range(FC):
            s = f % 2
            for kc in range(KC):
                P.I("pe", "matmul", reads=[RWg[fgrp[f]], Rh[kc]], writes=[Rg[s]], out=g_ps[s][:], lhsT=Wg[:, kc, f * 128:(f + 1) * 128], rhs=hb[:, kc, :],
                    start=(kc == 0), stop=(kc == KC - 1))
            for kc in range(KC):
                P.I("pe", "matmul", reads=[RWu[fgrp[f]], Rh[kc]], writes=[Ru[s]], out=u_ps[s][:], lhsT=Wu[:, kc, f * 128:(f + 1) * 128], rhs=hb[:, kc, :],
                    start=(kc == 0), stop=(kc == KC - 1))
            P.I("act", "activation", reads=[Rg[s]], writes=[Rsg[s]], out=sg[s][:], in_=g_ps[s][:], func=AF.Silu)
            P.I("dve", "tensor_tensor", reads=[Ru[s], Rsg[s]], writes=[Ract[f]], out=act[:, f, :], in0=u_ps[s][:], in1=sg[s][:], op=ALU.mult)
        for dc in range(KC):
            s = dc % 2
            for f in range(FC):
                P.I("pe", "matmul", reads=[RWd[f], Ract[f]], writes=[Ro[s]], out=o_ps[s][:], lhsT=Wd[:, f, dc * 128:(dc + 1) * 128], rhs=act[:, f, :],
                    start=(f == 0), stop=(f == FC - 1))
            P.I("dve", "scalar_tensor_tensor", reads=[Ro[s], Rxt[b][dc]], writes=[Rxt[b][dc]], out=X[:, dc, :], in0=o_ps[s][:], scalar=0.5, in1=X[:, dc, :], op0=ALU.mult, op1=ALU.add)
        P.D("sp", reads=Rxt[b], writes=aslist(Rxdst[t]), out=xd_v[:, :, t * NT:(t + 1) * NT], in_=X[:])


def emit_odd(C, xsrc, xdst, Rxsrc, Rxdst, win_d, wout_d, poolw_d, bt_d, qg_d, kg_d, pscale_d, gain_d, ident_d, ntiles):
    P, nc = C.P, C.nc
    T = ntiles * NT
    Win = C.sb([128, KC, 2048], BF16, "Win")
    Wout = C.sb([128, KC, D], BF16, "Wout")
    Wp = C.sb([128, 4, 128], BF16, "Wp")
    bt = C.sb([128, 8, 640], F32, "bt")
    qg = C.sb([128, 1], F32, "qg")
    kg = C.sb([128, 1], F32, "kg")
    pscale = C.sb([128, 4], F32, "pscale")
    gain = C.sb([128, KC], F32, "gain")
    ident_f = C.sb([128, 128], F32, "identf")
    ident = C.sb([128, 128], BF16, "ident")
    ones_bf = C.sb([128, 64], BF16, "onesbf")
    bones = C.sb([128, 128], F32, "bones")
    KnT = C.sb([128, 4, T], BF16, "KnT")
    V = C.sb([128, ntiles * 4, 512], BF16, "V")
    X = C.sb([128, KC, NT], F32, "X")
    hb = C.sb([128, KC, NT], BF16, "h")
    qnT = C.sb([128, 4, NT], BF16, "qnT")
    yT = C.sb([128, 8, NT], BF16, "yT")
    U = C.sb([128, 4, 16 + NT], F32, "U")
    sA = C.sb([128, 16 + NT], F32, "sA")
    sB = C.sb([128, 16 + NT], F32, "sB")
    t16 = C.sb([128, 16], F32, "t16")
    pooled = C.sb([128, 4, NT], BF16, "pooled")
    rc = C.sb([128, 4, 16], F32, "rc")
    sq = [C.sb([128, NT], F32, "sq") for _ in range(2)]
    rt = C.sb([128, NT], F32, "rt")
    rstd = C.sb([128, NT], F32, "rstd")
    sbs = [C.sb([128, 640], F32, "sbs") for _ in range(2)]
    nmx = [C.sb([128, 1], F32, "nmx") for _ in range(2)]
    Pb = [C.sb([128, 640], BF16, "Pb") for _ in range(2)]
    PT = [C.sb([128, 5, 128], BF16, "PT") for _ in range(2)]
    rinv = C.sb([128, 128], F32, "rinv")
    pj = [C.ps([128, NT], F32, "pj") for _ in range(2)]
    S = [C.ps([128, 1024], F32, "S") for _ in range(2)]
    PTp = C.ps([128, 8, 128], BF16, "PTp")
    OR = C.ps([128, 512], F32, "OR")

    RWin, RWout = regs("Win", KC), regs("Wout", KC)
    RWp, Rbt, Rqg, Rkg, Rpsc, Rgain = Reg("Wp"), Reg("bt"), Reg("qg"), Reg("kg"), Reg("psc"), Reg("gain")
    Rident, Ridf, Rones, Rbones, Rrc = Reg("ident"), Reg("identf"), Reg("onesbf"), Reg("bones"), Reg("rc")
    RKn = [regs("Kn%d_" % t, 4) for t in range(ntiles)]
    RV = regs("V", ntiles * 4)
    RX, Rh = regs("X", KC), regs("h", KC)
    Rqn, Ry = regs("qn", 4), regs("y", 8)
    RU, RsA, RsB, Rt16, Rpooled = regs("U", 4), Reg("sA"), Reg("sB"), Reg("t16"), regs("pooled", 4)
    Rsq = regs("sq", 2)
    Rrt, Rrstd = Reg("rt"), Reg("rstd")
    Rsb, Rmx, RPb, RPT, Rrinv = regs("sb", 2), regs("mx", 2), regs("Pb", 2), regs("PT", 2), Reg("rinv")
    Rpj, RS, RPTp, ROR = regs("pj", 2, True), regs("S", 2, True), Reg("PTp", True), Reg("OR", True)

    P.D("pool", writes=[Rgain], out=gain[:], in_=gain_d)
    P.D("pool", writes=[Rqg], out=qg[:], in_=qg_d)
    P.D("pool", writes=[Rkg], out=kg[:], in_=kg_d)
    P.D("pool", writes=[Rpsc], out=pscale[:], in_=pscale_d)
    P.D("pool", writes=[Ridf], out=ident_f[:], in_=ident_d)
    P.D("pool", writes=[Rbt], out=bt[:], in_=bt_d)
    xs_v = xsrc.rearrange("(kc p) t -> p kc t", p=128)
    xd_v = xdst.rearrange("(kc p) t -> p kc t", p=128)
    P.D("pool", reads=aslist(Rxsrc[0]), writes=RX, out=X[:], in_=xs_v[:, :, 0:NT])
    win_v = win_d.rearrange("(kc p) f -> p kc f", p=128)
    wout_v = wout_d.rearrange("(kc p) f -> p kc f", p=128)
    for kc in range(KC):
        P.D("pool", writes=[RWin[kc]], out=Win[:, kc, :], in_=win_v[:, kc, :])
    for kc in range(KC):
        P.D("pool", writes=[RWout[kc]], out=Wout[:, kc, :], in_=wout_v[:, kc, :])
    P.D("pool", writes=[RWp], out=Wp[:], in_=poolw_d.rearrange("g i o -> i g o"))
    P.I("dve", "tensor_copy", reads=[Ridf], writes=[Rident], out=ident[:], in_=ident_f[:])
    P.I("dve", "memset", writes=[Rones], ap=ones_bf[:], constant=1.0)
    P.I("dve", "memset", writes=[Rbones], ap=bones[:], constant=0.0)
    P.I("dve", "memset", reads=[], writes=[Rbones], ap=bones[0:64, 0:64], constant=1.0)
    P.I("dve", "memset", reads=[], writes=[Rbones], ap=bones[64:128, 64:128], constant=1.0)
    P.I("dve", "memset", writes=[Rbt], ap=bt[0:64, :, 576:640], constant=-1e30)
    P.I("dve", "memset", writes=[Rbt], ap=bt[64:128, :, 0:64], constant=-1e30)
    for g in range(4):
        P.I("pool", "memset", writes=[RU[g]], ap=U[:, g, 0:16], constant=0.0)
    WIN = (2, 4, 8, 16)
    for g, w in enumerate(WIN):
        for pos in range(w - 1):
            P.I("pool", "memset", writes=[Rrc], ap=rc[:, g, pos:pos + 1], constant=1.0 / (pos + 1))
        P.I("pool", "memset", writes=[Rrc], ap=rc[:, g, w - 1:16], constant=1.0 / w)

    xs_v = xsrc.rearrange("(kc p) t -> p kc t", p=128)
    xd_v = xdst.rearrange("(kc p) t -> p kc t", p=128)
    pjn = [0]

    def nextpj():
        pjn[0] += 1
        return pjn[0] % 2

    for t in range(ntiles):
        tsl = slice(t * NT, (t + 1) * NT)
        if t > 0:
            P.D("sp", reads=aslist(Rxsrc[t]), writes=RX, out=X[:], in_=xs_v[:, :, tsl])
        s0 = nextpj()
        emit_norm(C, X, RX, gain, Rgain, hb, Rh, sq, Rsq, pj[s0], Rpj[s0], rt, Rrt, rstd, Rrstd)
        for col0, gv, Rgv, is_q in ((0, qg, Rqg, True), (512, kg, Rkg, False)):
            for c in range(4):
                s = nextpj()
                for kc in range(KC):
                    P.I("pe", "matmul", reads=[RWin[kc], Rh[kc]], writes=[Rpj[s]], out=pj[s][:],
                        lhsT=Win[:, kc, col0 + c * 128:col0 + (c + 1) * 128], rhs=hb[:, kc, :],
                        start=(kc == 0), stop=(kc == KC - 1))
                s3 = 1 - s
                P.I("act", "activation", reads=[Rpj[s]], writes=[Rsq[0]], out=sq[0][:], in_=pj[s][:], func=AF.Square)
                P.I("pe", "matmul", reads=[Rsq[0], Rbones], writes=[Rpj[s3]], out=pj[s3][:], lhsT=bones[:], rhs=sq[0][:],
                    start=True, stop=True)
                P.I("act", "activation", reads=[Rpj[s3]], writes=[Rrt], out=rt[:], in_=pj[s3][:], func=AF.Ln,
                    scale=1.0 / 64, bias=EPS)
                P.I("act", "activation", reads=[Rrt], writes=[Rrstd], out=rstd[:], in_=rt[:], func=AF.Exp, scale=-0.5)
                if is_q:
                    dst, Rdst = qnT[:, c, :], Rqn[c]
                else:
                    dst, Rdst = KnT[:, c, tsl], RKn[t][c]
                P.I("dve", "scalar_tensor_tensor", reads=[Rpj[s], Rgv, Rrstd], writes=[Rdst], out=dst,
                    in0=pj[s][:], scalar=gv[:, 0:1], in1=rstd[:], op0=ALU.mult, op1=ALU.mult)
        for tb in range(4):
            s = nextpj()
            for kc in range(KC):
                P.I("pe", "matmul", reads=[RWin[kc], Rh[kc]], writes=[Rpj[s]], out=pj[s][:],
                    lhsT=hb[:, kc, tb * 128:(tb + 1) * 128], rhs=Win[:, kc, 1024:1536],
                    start=(kc == 0), stop=(kc == KC - 1))
            P.I("act", "activation", reads=[Rpj[s]], writes=[RV[4 * t + tb]], out=V[:, 4 * t + tb, :], in_=pj[s][:],
                func=AF.Copy)
        for g, w in enumerate(WIN):
            s = nextpj()
            for kc in range(KC):
                P.I("pe", "matmul", reads=[RWin[kc], Rh[kc]], writes=[Rpj[s]], out=pj[s][:],
                    lhsT=Win[:, kc, 1536 + g * 128:1536 + (g + 1) * 128], rhs=hb[:, kc, :],
                    start=(kc == 0), stop=(kc == KC - 1))
            P.I("act", "activation", reads=[Rpj[s]], writes=[RU[g]], out=U[:, g, 16:16 + NT], in_=pj[s][:], func=AF.Copy)
            L = 16 + NT
            P.I("pool", "tensor_tensor", reads=[RU[g]], writes=[RsA], out=sA[:, 1:L], in0=U[:, g, 1:L], in1=U[:, g, 0:L - 1], op=ALU.add)
            fin, Rfin = sA, RsA
            if w >= 4:
                P.I("pool", "tensor_tensor", reads=[RsA], writes=[RsB], out=sB[:, 3:L], in0=sA[:, 3:L], in1=sA[:, 1:L - 2], op=ALU.add)
                fin, Rfin = sB, RsB
            if w >= 8:
                P.I("pool", "tensor_tensor", reads=[RsB], writes=[RsA], out=sA[:, 7:L], in0=sB[:, 7:L], in1=sB[:, 3:L - 4], op=ALU.add)
                fin, Rfin = sA, RsA
            if w >= 16:
                P.I("pool", "tensor_tensor", reads=[RsA], writes=[RsB], out=sB[:, 15:L], in0=sA[:, 15:L], in1=sA[:, 7:L - 8], op=ALU.add)
                fin, Rfin = sB, RsB
            P.I("dve", "scalar_tensor_tensor", reads=[Rfin, RU[g]], writes=[Rpooled[g]], out=pooled[:, g, :],
                in0=fin[:, 16:L], scalar=1.0 / w, in1=U[:, g, 16:L], op0=ALU.mult, op1=ALU.subtract)
            if t == 0:
                P.I("dve", "tensor_tensor", reads=[Rfin, Rrc], writes=[Rt16], out=t16[:], in0=fin[:, 16:32], in1=rc[:, g, :], op=ALU.mult)
                P.I("dve", "tensor_tensor", reads=[Rt16, RU[g]], writes=[Rpooled[g]], out=pooled[:, g, 0:16], in0=t16[:], in1=U[:, g, 16:32], op=ALU.subtract)
            P.I("pool", "tensor_copy", reads=[RU[g]], writes=[RU[g]], out=U[:, g, 0:16], in_=U[:, g, NT:NT + 16])
            s = nextpj()
            P.I("pe", "matmul", reads=[RWp, Rpooled[g]], writes=[Rpj[s]], out=pj[s][:], lhsT=Wp[:, g, :], rhs=pooled[:, g, :],
                start=True, stop=True)
            P.I("dve", "tensor_scalar", reads=[Rpj[s], Rpsc], writes=[Ry[4 + g]], out=yT[:, 4 + g, :], in0=pj[s][:],
                scalar1=pscale[:, g:g + 1], scalar2=None, op0=ALU.mult)
        items = [(mb, c, e) for mb in range(4) for c in range(4) for e in range(2)]

        def geom(mb):
            m = 4 * t + mb
            nkb = min(m, 4) + 1
            j0 = m - (nkb - 1)
            return m, nkb, j0, (5 - nkb) * 128, nkb * 128

        def stageA(i):
            mb, c, e = items[i]
            m, nkb, j0, koff, nk = geom(mb)
            z = i % 2
            hh = 2 * c + e
            hs = e * 64
            lsl = slice(mb * 128, (mb + 1) * 128)
            kregs = [RKn[tt][c] for tt in range(j0 // 4, t + 1)]
            for a_, b_ in ([(0, min(nk, 512))] + ([(512, nk)] if nk > 512 else [])):
                P.I("pe", "matmul", reads=[Rqn[c]] + kregs, writes=[RS[z]], out=S[z][:, a_:b_],
                    lhsT=qnT[hs:hs + 64, c, lsl], rhs=KnT[hs:hs + 64, c, j0 * 128 + a_:j0 * 128 + b_],
                    start=True, stop=True)
            P.I("dve", "scalar_tensor_tensor", reads=[RS[z], Rbt], writes=[Rsb[z]], out=sbs[z][:, 0:nk], in0=S[z][:, 0:nk],
                scalar=0.125, in1=bt[:, hh, koff:koff + nk], op0=ALU.mult, op1=ALU.add)
            P.I("dve", "tensor_reduce", reads=[Rsb[z]], writes=[Rmx[z]], out=nmx[z][:], in_=sbs[z][:, 0:nk], axis=AX.X,
                op=ALU.max, negate=True)
            P.I("act", "activation", reads=[Rsb[z], Rmx[z]], writes=[RPb[z]], out=Pb[z][:, 0:nk], in_=sbs[z][:, 0:nk],
                func=AF.Exp, bias=nmx[z][:], scale=1.0)

        def stageB1(i):
            mb, c, e = items[i]
            m, nkb, j0, koff, nk = geom(mb)
            z = i % 2
            for kb in range(nkb):
                P.I("pe", "transpose", reads=[RPb[z], Rident], writes=[RPTp], out=PTp[:, kb, :],
                    in_=Pb[z][:, kb * 128:(kb + 1) * 128], identity=ident[:])
            P.I("act", "activation", reads=[RPTp], writes=[RPT[e]], out=PT[e][:, 0:nkb, :], in_=PTp[:, 0:nkb, :],
                func=AF.Copy)

        def stageB2(i):
            mb, c, e = items[i]
            m, nkb, j0, koff, nk = geom(mb)
            hh = 2 * c + e
            hs = e * 64
            lsl = slice(mb * 128, (mb + 1) * 128)
            for kb in range(nkb):
                P.I("pe", "matmul", reads=[RV[j0 + kb], RPT[e]], writes=[ROR], out=OR[hs:hs + 64, 0:128],
                    lhsT=V[:, j0 + kb, hh * 64:(hh + 1) * 64], rhs=PT[e][:, kb, :],
                    start=(kb == 0), stop=(kb == nkb - 1))
            for kb in range(nkb):
                P.I("pe", "matmul", reads=[Rones, RPT[e]], writes=[ROR], out=OR[hs:hs + 64, 128:256],
                    lhsT=ones_bf[:, 0:64], rhs=PT[e][:, kb, :], start=(kb == 0), stop=(kb == nkb - 1))
            if e == 1:
                P.I("dve", "reciprocal", reads=[ROR], writes=[Rrinv], out=rinv[:], in_=OR[:, 128:256])
                P.I("dve", "tensor_tensor", reads=[ROR, Rrinv], writes=[Ry[c]], out=yT[:, c, lsl], in0=OR[:, 0:128],
                    in1=rinv[:], op=ALU.mult)

        n_it = len(items)
        stageA(0)
        stageA(1)
        stageB1(0)
        for i in range(n_it):
            if i + 2 < n_it:
                stageA(i + 2)
            if i + 1 < n_it:
                stageB1(i + 1)
            stageB2(i)
        for dc in range(KC):
            s = nextpj()
            for c in range(8):
                P.I("pe", "matmul", reads=[RWout[c], Ry[c]], writes=[Rpj[s]], out=pj[s][:],
                    lhsT=Wout[:, c, dc * 128:(dc + 1) * 128], rhs=yT[:, c, :], start=(c == 0), stop=(c == 7))
            P.I("dve", "tensor_tensor", reads=[Rpj[s], RX[dc]], writes=[RX[dc]], out=X[:, dc, :], in0=pj[s][:],
                in1=X[:, dc, :], op=ALU.add)
        P.D("sp", reads=RX, writes=aslist(Rxdst[t]), out=xd_v[:, :, tsl], in_=X[:])


NTE = 256
QC = 128


def bview(ap, axis, shape):
    return ap.unsqueeze(axis).broadcast_to(shape)


def emit_even(C, xsrc, xdst, Rxsrc, Rxdst, dd, ntiles):
    P, nc = C.P, C.nc
    sb, ps = C.sb, C.ps
    Win = sb([128, KC, 2056], BF16, "Win")
    Wout = sb([128, KC, D], BF16, "Wout")
    Wglu = sb([128, 4, 512], BF16, "Wglu")
    diagw = sb([128, 8, 4, 128], BF16, "diagw")
    diagD = sb([128, 4, 128], BF16, "diagD")
    diagD5 = sb([128, 4, 128], BF16, "diagD5")
    gain = sb([128, KC], F32, "gain")
    convw = sb([128, 8, 4], F32, "convw")
    convb = sb([128, 8], F32, "convb")
    dtb = sb([8, 1], F32, "dtb")
    alog = sb([8, 1], F32, "alog")
    aneg = sb([8, 1], F32, "aneg")
    dskip = sb([128, 4], F32, "dskip")
    ngain = sb([128, 4], F32, "ngain")
    d5 = sb([128, 4], F32, "d5")
    bglu = sb([128, 4], F32, "bglu")
    ident_f = sb([128, 128], F32, "identf")
    ident = sb([128, 128], BF16, "ident")
    selH = sb([8, 8, 128], F32, "selH")
    tri = sb([128, 128], F32, "tri")
    maskneg = sb([128, 128], F32, "maskneg")
    X = sb([128, KC, NTE], F32, "X")
    hb = sb([128, KC, NTE], BF16, "h")
    zs = sb([128, 4, NTE], BF16, "zs")
    xbc = sb([128, 8, 3 + NTE], BF16, "xbc")
    xsT = sb([128, 4, NTE], BF16, "xsT")
    BT = sb([128, 2, NTE], BF16, "BT")
    CT = sb([128, 2, NTE], BF16, "CT")
    uT = sb([128, 4, NTE], BF16, "uT")
    yT = sb([128, 8, NTE], BF16, "yT")
    dte1 = sb([8, NTE], F32, "dte1")
    dtsp = sb([8, NTE], F32, "dtsp")
    dA = sb([8, NTE], F32, "dA")
    cs = sb([8, NTE], F32, "cs")
    sq = [sb([128, NTE], F32, "sq") for _ in range(2)]
    rt = sb([128, NTE], F32, "rt")
    rstd = sb([128, NTE], F32, "rstd")
    csdt = sb([128, 16], F32, "csdt")
    tmpL = sb([128, 8, 128], F32, "tmpL")
    ecs = sb([128, 8, 128], BF16, "ecs")
    eend = sb([128, 8], F32, "eend")
    d1 = sb([128, 8], F32, "d1")
    dte = sb([128, 8], F32, "dte")
    Mt = sb([128, 8, 128], BF16, "Mt")
    Cs = sb([128, 8, 128], BF16, "Cs")
    xdt = sb([128, 512], BF16, "xdt")
    xw = sb([128, 512], BF16, "xw")
    Btok = sb([128, 2, 128], BF16, "Btok")
    yg = sb([128, 4, 128], F32, "yg")
    sqy = [sb([128, 128], F32, "sqy") for _ in range(2)]
    rtg = sb([128, 128], F32, "rtg")
    rstdg = sb([128, 128], F32, "rstdg")
    ST = sb([128, 512], F32, "ST")
    STb = sb([128, 512], BF16, "STb")
    LD = sb([128, 16, 2, 128], BF16, "LD")
    LO = sb([128, 16, 2, 128], BF16, "LO")
    Ec = sb([128, 16, 128], F32, "Ec")
    Es = sb([128, 16, 128], F32, "Es")
    big = sb([128, 2048], F32, "big")
    Zall = big[:].rearrange("p (a g i) -> p a g i", a=16, g=8)
    maskZ = sb([128, 16, 8], F32, "maskZ")
    sm = {k: sb([128, 16], F32, k) for k in ("are", "aim", "ldt", "dt", "dar", "mag", "th", "c", "s", "ta", "tb", "c2", "s2",
                                              "abr", "abi", "abr1", "den", "rden", "fre", "fim", "u1", "u2",
                                              "ire", "iim", "v1", "v2")}
    cm = [sb([128, 16], F32, "cm%d" % k) for k in range(7)]
    smm = [sb([128, 16], F32, "sm%d" % k) for k in range(7)]
    Hp = sb([128, 16, 2], F32, "Hp")
    t1 = sb([128, 4, 128], F32, "t1")
    t2 = sb([128, 4, 128], F32, "t2")
    t3 = sb([128, 4, 128], F32, "t3")
    t4 = sb([128, 4, 128], F32, "t4")
    t5 = sb([128, 4, 128], F32, "t5")
    t6 = sb([128, 4, 128], F32, "t6")
    t7 = sb([128, 4, 128], F32, "t7")
    t8 = sb([128, 4, 128], F32, "t8")
    xm2 = sb([128, 4, 2, 128], F32, "xm2")
    nat = {}
    for nk_, tb_ in zip(("bre", "bim", "cre", "cim", "NBre", "NBim", "n1", "n2"), (t1, t2, t3, t4, t5, t6, t7, t8)):
        nat[nk_] = tb_[:].rearrange("p a l -> p (a l)")[:, 0:256].rearrange("p (a j) -> p a j", a=16)
    xm = big[:, 0:1024].rearrange("p (a q l) -> p a q l", a=4, q=2)
    ht = big[:, 1024:2048].rearrange("p (a q l) -> p a q l", a=4, q=2)
    xmb = [xm, xm2[:]]
    H = sb([128, 4, 2, 128], F32, "H")
    Hbb = [sb([128, 4, 2, 128], BF16, "Hb") for _ in range(2)]
    y5sb = [sb([128, 4, 128], F32, "y5s") for _ in range(2)]
    ga = sb([128, 128], F32, "ga")
    gb = sb([128, 128], F32, "gb")
    gthb = [sb([128, 128], F32, "gth") for _ in range(2)]
    geb = [sb([128, 4, 128], BF16, "ge") for _ in range(2)]
    sig = sb([128, 128], F32, "sig")
    PJ = ps([128, 2, 512], F32, "PJ")
    M1 = ps([128, 1024], BF16, "M1")
    M2 = ps([128, 512], F32, "M2")
    BC = ps([128, 8, 128], F32, "BC")
    GY = ps([128, 512], F32, "GY")
    SN = ps([128, 512], F32, "SN")
    PJf = PJ[:].rearrange("p s n -> p (s n)")
    drv = PJf.rearrange("p (r q n) -> p r q n", r=4, q=2)
    xs_tok = M1[:, 0:512].rearrange("p (c n) -> p c n", c=4)
    Btok_ps = M1[:, 512:768].rearrange("p (g n) -> p g n", g=2)
    csdt_ps = M2[:, 0:16]
    ss_ps = M2[:, 128:384].rearrange("p (g n) -> p g n", g=2)
    gate_ps = M2[:, 384:512]
    G_ps = GY[:, 0:256].rearrange("p (g n) -> p g n", g=2)
    y_ps = GY[:, 256:384]
    y5_ps = GY[:, 384:512]

    R = {}
    alias = {"csdt_ps": "M2", "ss": "M2", "gate": "M2", "xs_tok": "M1", "Btok_ps": "M1", "G": "GY", "y": "GY", "y5": "GY"}
    psum_names = ("M1", "M2", "GY", "BC", "SN")

    def r(name):
        name = alias.get(name, name)
        if name not in R:
            R[name] = Reg(name, name in psum_names)
        return R[name]

    RWin, RWout = regs("Win", KC), regs("Wout", KC)
    RX, Rh = regs("X", KC), regs("h", KC)
    Ry = regs("y", 8)
    Rpj = regs("pj", 2, True)
    R["pj0"], R["pj1"] = Rpj
    for nk_, tn_ in zip(("bre", "bim", "cre", "cim", "NBre", "NBim", "n1", "n2"), range(1, 9)):
        R[nk_] = r("t%d" % tn_)

    sp_loads = [("gain", gain), ("convw", convw), ("convb", convb), ("dtb", dtb), ("alog", alog), ("dskip", dskip),
                ("ngain", ngain), ("d5", d5), ("bglu", bglu), ("ident", ident_f), ("are", sm["are"]), ("aim", sm["aim"]),
                ("ldt", sm["ldt"]), ("bre", nat["bre"]), ("bim", nat["bim"]), ("cre", nat["cre"]), ("cim", nat["cim"])]
    for k, tgt in sp_loads:
        P.D("pool", writes=[r(k)], out=tgt[:], in_=dd[k])
    xs_v = xsrc.rearrange("(kc p) t -> p kc t", p=128)
    xd_v = xdst.rearrange("(kc p) t -> p kc t", p=128)
    P.D("pool", reads=aslist(Rxsrc[0]), writes=RX, out=X[:], in_=xs_v[:, :, 0:NTE])
    win_v = dd["win"].rearrange("(kc p) f -> p kc f", p=128)
    wout_v = dd["wout"].rearrange("(kc p) f -> p kc f", p=128)
    for kc in range(KC):
        P.D("pool", writes=[RWin[kc]], out=Win[:, kc, :], in_=win_v[:, kc, :])
    for kc in range(KC):
        P.D("pool", writes=[RWout[kc]], out=Wout[:, kc, :], in_=wout_v[:, kc, :])
    P.D("pool", writes=[r("Wglu")], out=Wglu[:], in_=dd["wglu"].rearrange("(c p) o -> p c o", p=128))

    I = P.I
    I("dve", "tensor_copy", reads=[r("ident")], writes=[r("identb")], out=ident[:], in_=ident_f[:])
    I("act", "activation", reads=[r("alog")], writes=[r("aneg")], out=aneg[:], in_=alog[:], func=AF.Exp)
    I("dve", "tensor_scalar", reads=[r("aneg")], writes=[r("aneg")], out=aneg[:], in0=aneg[:], scalar1=-1.0, scalar2=None,
      op0=ALU.mult)
    for c8 in range(8):
        for k in range(4):
            I("dve", "tensor_scalar", reads=[r("ident"), r("convw")], writes=[r("diagw")], out=diagw[:, c8, k, :],
              in0=ident_f[:], scalar1=convw[:, c8, k:k + 1], scalar2=None, op0=ALU.mult)
    for c in range(4):
        I("dve", "tensor_scalar", reads=[r("ident"), r("dskip")], writes=[r("diagD")], out=diagD[:, c, :], in0=ident_f[:],
          scalar1=dskip[:, c:c + 1], scalar2=None, op0=ALU.mult)
        I("dve", "tensor_scalar", reads=[r("ident"), r("d5")], writes=[r("diagD5")], out=diagD5[:, c, :], in0=ident_f[:],
          scalar1=d5[:, c:c + 1], scalar2=None, op0=ALU.mult)
    I("dve", "tensor_copy", reads=[r("ident")], writes=[r("selH")], out=selH[:], in_=bview(ident_f[0:8, 0:8], 2, [8, 8, 128]))
    I("dve", "tensor_tensor_scan", reads=[r("ident"), C.R_ones], writes=[r("tri")], out=tri[:], data0=C.ones_f[:],
      data1=ident_f[:], initial=0.0, op0=ALU.mult, op1=ALU.add)
    I("dve", "tensor_scalar", reads=[r("tri")], writes=[r("maskneg")], out=maskneg[:], in0=tri[:], scalar1=1.0, scalar2=1e30,
      op0=ALU.subtract, op1=ALU.mult)
    I("dve", "memset", writes=[r("ST")], ap=ST[:], constant=0.0)
    I("dve", "memset", writes=[r("STb")], ap=STb[:], constant=0.0)
    I("dve", "memset", writes=[r("xbc%d" % c8) for c8 in range(8)], ap=xbc[:, :, 0:3], constant=0.0)
    I("dve", "memset", writes=[r("ire%d" % j) for j in range(4)] + [r("iim%d" % j) for j in range(4)], ap=Hp[:], constant=0.0)
    s = sm

    def tt(out, a, b, op, eng="dve", rd=(), wr=()):
        I(eng, "tensor_tensor", reads=[r(x) for x in rd], writes=[r(x) for x in wr], out=out, in0=a, in1=b, op=op)

    I("act", "activation", reads=[r("ldt")], writes=[r("dt")], out=s["dt"][:], in_=s["ldt"][:], func=AF.Exp)
    tt(s["dar"][:], s["dt"][:], s["are"][:], ALU.mult, rd=["dt", "are"], wr=["dar"])
    I("act", "activation", reads=[r("dar")], writes=[r("mag")], out=s["mag"][:], in_=s["dar"][:], func=AF.Exp)
    tt(s["th"][:], s["dt"][:], s["aim"][:], ALU.mult, rd=["dt", "aim"], wr=["th"])
    I("act", "activation", reads=[r("th")], writes=[r("s")], out=s["s"][:], in_=s["th"][:], func=AF.Sin, scale=1.0 / 32)
    I("act", "activation", reads=[r("th")], writes=[r("c")], out=s["c"][:], in_=s["th"][:], func=AF.Sin, scale=1.0 / 32,
      bias=float(np.pi / 2))

    def double_cs():
        tt(s["ta"][:], s["c"][:], s["c"][:], ALU.mult, rd=["c"], wr=["ta"])
        tt(s["tb"][:], s["s"][:], s["s"][:], ALU.mult, rd=["s"], wr=["tb"])
        I("dve", "scalar_tensor_tensor", reads=[r("c"), r("s")], writes=[r("s2")], out=s["s2"][:], in0=s["c"][:], scalar=2.0,
          in1=s["s"][:], op0=ALU.mult, op1=ALU.mult)
        tt(s["c"][:], s["ta"][:], s["tb"][:], ALU.subtract, rd=["ta", "tb"], wr=["c"])
        I("dve", "tensor_copy", reads=[r("s2")], writes=[r("s")], out=s["s"][:], in_=s["s2"][:])

    for _ in range(5):
        double_cs()
    for k in range(7):
        I("dve", "tensor_copy", reads=[r("c")], writes=[r("cm%d" % k)], out=cm[k][:], in_=s["c"][:])
        I("dve", "tensor_copy", reads=[r("s")], writes=[r("sm%d" % k)], out=smm[k][:], in_=s["s"][:])
        if k < 6:
            double_cs()
    c1, s1 = cm[0], smm[0]
    tt(s["abr"][:], s["mag"][:], c1[:], ALU.mult, rd=["mag", "cm0"], wr=["abr"])
    tt(s["abi"][:], s["mag"][:], s1[:], ALU.mult, rd=["mag", "sm0"], wr=["abi"])
    I("dve", "tensor_scalar", reads=[r("abr")], writes=[r("abr1")], out=s["abr1"][:], in0=s["abr"][:], scalar1=-1.0,
      scalar2=None, op0=ALU.add)
    tt(s["ta"][:], s["are"][:], s["are"][:], ALU.mult, rd=["are"], wr=["ta"])
    tt(s["tb"][:], s["aim"][:], s["aim"][:], ALU.mult, rd=["aim"], wr=["tb"])
    tt(s["den"][:], s["ta"][:], s["tb"][:], ALU.add, rd=["ta", "tb"], wr=["den"])
    I("dve", "reciprocal", reads=[r("den")], writes=[r("rden")], out=s["rden"][:], in_=s["den"][:])
    tt(s["u1"][:], s["abr1"][:], s["are"][:], ALU.mult, rd=["abr1", "are"], wr=["u1"])
    tt(s["u2"][:], s["abi"][:], s["aim"][:], ALU.mult, rd=["abi", "aim"], wr=["u2"])
    tt(s["u1"][:], s["u1"][:], s["u2"][:], ALU.add, rd=["u1", "u2"], wr=["u1"])
    tt(s["fre"][:], s["u1"][:], s["rden"][:], ALU.mult, rd=["u1", "rden"], wr=["fre"])
    tt(s["u1"][:], s["abi"][:], s["are"][:], ALU.mult, rd=["abi", "are"], wr=["u1"])
    tt(s["u2"][:], s["abr1"][:], s["aim"][:], ALU.mult, rd=["abr1", "aim"], wr=["u2"])
    tt(s["u1"][:], s["u1"][:], s["u2"][:], ALU.subtract, rd=["u1", "u2"], wr=["u1"])
    tt(s["fim"][:], s["u1"][:], s["rden"][:], ALU.mult, rd=["u1", "rden"], wr=["fim"])
    fre_b = bview(s["fre"][:], 2, [128, 16, 16])
    fim_b = bview(s["fim"][:], 2, [128, 16, 16])
    tt(nat["n1"][:], nat["bre"][:], fre_b, ALU.mult, rd=["bre", "fre"], wr=["n1"])
    tt(nat["n2"][:], nat["bim"][:], fim_b, ALU.mult, rd=["bim", "fim"], wr=["n2"])
    tt(nat["NBre"][:], nat["n1"][:], nat["n2"][:], ALU.subtract, rd=["n1", "n2"], wr=["NBre"])
    tt(nat["n1"][:], nat["bim"][:], fre_b, ALU.mult, rd=["bim", "fre"], wr=["n1"])
    tt(nat["n2"][:], nat["bre"][:], fim_b, ALU.mult, rd=["bre", "fim"], wr=["n2"])
    tt(nat["NBim"][:], nat["n1"][:], nat["n2"][:], ALU.add, rd=["n1", "n2"], wr=["NBim"])
    I("dve", "memset", writes=[r("maskZ")], ap=maskZ[:], constant=0.0)
    for pr in range(16):
        rr = pr % 4
        I("dve", "memset", writes=[r("maskZ")], ap=maskZ[0:64, pr, 2 * rr:2 * rr + 1], constant=1.0)
        I("dve", "memset", writes=[r("maskZ")], ap=maskZ[64:128, pr, 2 * rr + 1:2 * rr + 2], constant=1.0)
    mz_b = bview(maskZ[:], 3, [128, 16, 8, 16])
    LOv = LO[:].rearrange("p a q (g i) -> p a q g i", g=8)
    LDv = LD[:]
    for q, key in enumerate(("NBre", "NBim")):
        tt(Zall, bview(nat[key][:], 2, [128, 16, 8, 16]), mz_b, ALU.mult, rd=[key, "maskZ"], wr=["Zall"])
        for j in range(4):
            for k4 in range(4):
                I("pe", "transpose", reads=[r("Zall"), r("ident")], writes=[r("BC")], out=BC[:, k4, :],
                  in_=Zall[:, 4 * j + k4, :, :].rearrange("p g i -> p (g i)"), identity=ident_f[:])
            I("act", "activation", reads=[r("BC")], writes=[r("LD")], out=LDv[:, 4 * j:4 * j + 4, q, :], in_=BC[:, 0:4, :],
              func=AF.Copy)
    tt(LOv[:, :, 0, :, :], bview(nat["cre"][:], 2, [128, 16, 8, 16]), mz_b, ALU.mult, rd=["cre", "maskZ"], wr=["LO"])
    I("dve", "tensor_scalar", reads=[r("cim")], writes=[r("n1")], out=nat["n1"][:], in0=nat["cim"][:], scalar1=-1.0, scalar2=None,
      op0=ALU.mult)
    tt(LOv[:, :, 1, :, :], bview(nat["n1"][:], 2, [128, 16, 8, 16]), mz_b, ALU.mult, rd=["n1", "maskZ"], wr=["LO"])
    I("dve", "tensor_copy", reads=[r("cm0")], writes=[r("Ec")], out=Ec[:, :, 0:1], in_=bview(cm[0][:], 2, [128, 16, 1]))
    I("dve", "tensor_copy", reads=[r("sm0")], writes=[r("Es")], out=Es[:, :, 0:1], in_=bview(smm[0][:], 2, [128, 16, 1]))
    TLN = ["tmpL%d" % j for j in range(8)]
    tA = tmpL[:].rearrange("p h l -> p (h l)")
    tB = H[:].rearrange("p a q l -> p (a q l)")
    for k in range(7):
        m = 1 << k
        cb = bview(cm[k][:], 2, [128, 16, m])
        sbv = bview(smm[k][:], 2, [128, 16, m])
        vA = tA[:, 0:16 * m].rearrange("p (a m) -> p a m", a=16)
        vB = tB[:, 0:16 * m].rearrange("p (a m) -> p a m", a=16)
        tt(vA, Ec[:, :, 0:m], cb, ALU.mult, rd=["Ec", "cm%d" % k], wr=TLN)
        tt(vB, Es[:, :, 0:m], sbv, ALU.mult, rd=["Es", "sm%d" % k], wr=["H"])
        tt(Ec[:, :, m:2 * m], vA, vB, ALU.subtract, rd=TLN + ["H"], wr=["Ec"])
        tt(vA, Es[:, :, 0:m], cb, ALU.mult, rd=["Es", "cm%d" % k], wr=TLN)
        tt(vB, Ec[:, :, 0:m], sbv, ALU.mult, rd=["Ec", "sm%d" % k], wr=["H"])
        tt(Es[:, :, m:2 * m], vA, vB, ALU.add, rd=TLN + ["H"], wr=["Es"])

    xs_v = xsrc.rearrange("(kc p) t -> p kc t", p=128)
    xd_v = xdst.rearrange("(kc p) t -> p kc t", p=128)
    pjn = [0]

    def nextpj():
        pjn[0] += 1
        return pjn[0] % 2

    def proj(col0, ncols, rhs_slice=None):
        sl = nextpj()
        for kc in range(KC):
            I("pe", "matmul", reads=[RWin[kc], Rh[kc]], writes=[Rpj[sl]], out=PJ[0:ncols, sl, 0:NTE],
              lhsT=Win[:, kc, col0:col0 + ncols], rhs=hb[:, kc, :], start=(kc == 0), stop=(kc == KC - 1))
        return sl

    for t in range(ntiles):
        tsl = slice(t * NTE, (t + 1) * NTE)
        if t > 0:
            P.D("sp", reads=aslist(Rxsrc[t]), writes=RX, out=X[:], in_=xs_v[:, :, tsl])
        sl0 = nextpj()
        for kc in range(KC):
            s2 = kc % 2
            I("act", "activation", reads=[RX[kc]], writes=[r("sq%d" % s2)], out=sq[s2][:], in_=X[:, kc, :], func=AF.Square)
            I("pe", "matmul", reads=[r("sq%d" % s2), C.R_ones], writes=[Rpj[sl0]], out=PJ[:, sl0, 0:NTE], lhsT=C.ones_f[:],
              rhs=sq[s2][:], start=(kc == 0), stop=(kc == KC - 1))
        I("act", "activation", reads=[Rpj[sl0]], writes=[r("rt")], out=rt[:], in_=PJ[:, sl0, 0:NTE], func=AF.Ln,
          scale=1.0 / D, bias=EPS)
        I("act", "activation", reads=[r("rt")], writes=[r("rstd")], out=rstd[:], in_=rt[:], func=AF.Exp, scale=-0.5)
        for kc in range(KC):
            I("dve", "scalar_tensor_tensor", reads=[RX[kc], r("gain"), r("rstd")], writes=[Rh[kc]], out=hb[:, kc, :],
              in0=X[:, kc, :], scalar=gain[:, kc:kc + 1], in1=rstd[:], op0=ALU.mult, op1=ALU.mult)
        for c in range(4):
            sl = proj(c * 128, 128)
            I("act", "activation", reads=[Rpj[sl]], writes=[r("zs%d" % c)], out=zs[:, c, :], in_=PJ[:, sl, 0:NTE], func=AF.Silu)
        for c8 in range(8):
            sl = proj(512 + c8 * 128, 128)
            I("act", "activation", reads=[Rpj[sl]], writes=[r("xbc%d" % c8)], out=xbc[:, c8, 3:3 + NTE], in_=PJ[:, sl, 0:NTE],
              func=AF.Copy)
            sl = nextpj()
            for k in range(4):
                I("pe", "matmul", reads=[r("diagw"), r("xbc%d" % c8)], writes=[Rpj[sl]], out=PJ[:, sl, 0:NTE],
                  lhsT=diagw[:, c8, k, :], rhs=xbc[:, c8, k:k + NTE], start=(k == 0), stop=(k == 3))
            if c8 < 4:
                dst, rn = xsT[:, c8, :], "xsT%d" % c8
            elif c8 < 6:
                dst, rn = BT[:, c8 - 4, :], "BT%d" % (c8 - 4)
            else:
                dst, rn = CT[:, c8 - 6, :], "CT%d" % (c8 - 6)
            I("act", "activation", reads=[Rpj[sl], r("convb")], writes=[r(rn)], out=dst, in_=PJ[:, sl, 0:NTE], func=AF.Silu,
              bias=convb[:, c8:c8 + 1])
            I("pool", "tensor_copy", reads=[r("xbc%d" % c8)], writes=[r("xbc%d" % c8)], out=xbc[:, c8, 0:3],
              in_=xbc[:, c8, NTE:NTE + 3])
        sl = proj(1536, 8)
        I("act", "activation", reads=[Rpj[sl], r("dtb")], writes=[r("dte1")], out=dte1[:], in_=PJ[0:8, sl, 0:NTE], func=AF.Exp,
          bias=dtb[:, 0:1])
        I("act", "activation", reads=[r("dte1")], writes=[r("dtsp")], out=dtsp[:], in_=dte1[:], func=AF.Ln, bias=1.0)
        I("dve", "tensor_scalar", reads=[r("dtsp"), r("aneg")], writes=[r("dA")], out=dA[:], in0=dtsp[:], scalar1=aneg[:, 0:1],
          scalar2=None, op0=ALU.mult)
        for c in range(4):
            sl = proj(1544 + c * 128, 128)
            I("act", "activation", reads=[Rpj[sl]], writes=[r("uT%d" % c)], out=uT[:, c, :], in_=PJ[:, sl, 0:NTE], func=AF.Copy)

        tail_prev = None
        for qc in range(NTE // QC):
            lr = slice(qc * QC, (qc + 1) * QC)

            def seg0():
                I("dve", "tensor_tensor_scan", reads=[r("dA"), C.R_ones], writes=[r("cs")], out=cs[:, lr],
                  data0=C.ones_f[0:8, :], data1=dA[:, lr], initial=0.0, op0=ALU.mult, op1=ALU.add)
                I("pe", "transpose", reads=[r("cs"), r("ident")], writes=[r("csdt_ps")], out=csdt_ps[:, 0:8], in_=cs[:, lr],
                  identity=ident_f[0:8, 0:8])
                I("pe", "transpose", reads=[r("dtsp"), r("ident")], writes=[r("csdt_ps")], out=csdt_ps[:, 8:16], in_=dtsp[:, lr],
                  identity=ident_f[0:8, 0:8])
                I("act", "activation", reads=[r("csdt_ps")], writes=[r("csdt")], out=csdt[:], in_=csdt_ps, func=AF.Copy)
                for hh in range(8):
                    I("pe", "matmul", reads=[r("selH"), r("cs")], writes=[r("BC")], out=BC[:, hh, :], lhsT=selH[:, hh, :],
                      rhs=cs[:, lr], start=True, stop=True)
                for hh in range(8):
                    I("dve", "scalar_tensor_tensor", reads=[r("BC"), r("csdt"), r("maskneg")], writes=[r("tmpL%d" % hh)],
                      out=tmpL[:, hh, :], in0=BC[:, hh, :], scalar=csdt[:, hh:hh + 1], in1=maskneg[:], op0=ALU.subtract,
                      op1=ALU.add)
                I("dve", "tensor_tensor", reads=[r("BC"), r("csdt")], writes=[r("d1")], out=d1[:], in0=BC[:, :, QC - 1],
                  in1=csdt[:, 0:8], op=ALU.subtract)
                I("act", "activation", reads=[r("BC")], writes=[r("ecs")], out=ecs[:], in_=BC[:], func=AF.Exp)
                I("act", "activation", reads=[r("BC")], writes=[r("eend")], out=eend[:], in_=BC[:, :, QC - 1], func=AF.Exp)
                I("act", "activation", reads=[], writes=[r("tmpL%d" % j) for j in range(8)], out=tmpL[:], in_=tmpL[:], func=AF.Exp)
                I("act", "activation", reads=[r("d1")], writes=[r("dte")], out=dte[:], in_=d1[:], func=AF.Exp)
                for g in range(2):
                    I("pe", "matmul", reads=[r("BT%d" % g), r("CT%d" % g)], writes=[r("G")], out=G_ps[:, g, :],
                      lhsT=BT[:, g, lr], rhs=CT[:, g, lr], start=True, stop=True)

            def seg1():
                I("dve", "tensor_tensor", reads=[r("tmpL%d" % j) for j in range(8)] + [r("G")], writes=[r("Mt")],
                  out=Mt[:].rearrange("p (g j) l -> p g j l", g=2), in0=tmpL[:].rearrange("p (g j) l -> p g j l", g=2),
                  in1=bview(G_ps, 2, [128, 2, 4, 128]), op=ALU.mult)
                I("pool", "tensor_tensor", reads=[r("CT0"), r("CT1"), r("ecs")], writes=[r("Cs")],
                  out=Cs[:].rearrange("p (g j) l -> p g j l", g=2), in0=ecs[:].rearrange("p (g j) l -> p g j l", g=2),
                  in1=bview(CT[:, :, lr], 2, [128, 2, 4, 128]), op=ALU.mult)
                for c in range(4):
                    I("pe", "transpose", reads=[r("xsT%d" % c), r("identb")], writes=[r("xs_tok")], out=xs_tok[:, c, :],
                      in_=xsT[:, c, lr], identity=ident[:])
                for g in range(2):
                    I("pe", "transpose", reads=[r("BT%d" % g), r("identb")], writes=[r("Btok_ps")], out=Btok_ps[:, g, :],
                      in_=BT[:, g, lr], identity=ident[:])
                I("dve", "tensor_tensor", reads=[r("xs_tok"), r("csdt")], writes=[r("xdt")],
                  out=xdt[:].rearrange("p (h q) -> p h q", h=8), in0=M1[:, 0:512].rearrange("p (h q) -> p h q", h=8),
                  in1=bview(csdt[:, 8:16], 2, [128, 8, 64]), op=ALU.mult)
                I("act", "activation", reads=[r("Btok_ps")], writes=[r("Btok")], out=Btok[:], in_=Btok_ps, func=AF.Copy)
                I("pool", "tensor_tensor", reads=[r("xdt"), r("dte")], writes=[r("xw")],
                  out=xw[:].rearrange("p (h q) -> p h q", h=8), in0=xdt[:].rearrange("p (h q) -> p h q", h=8),
                  in1=bview(dte[:], 2, [128, 8, 64]), op=ALU.mult)

            def seg_y(g):
                for c in (2 * g, 2 * g + 1):
                    for e in range(2):
                        hh = 2 * c + e
                        ysl = y_ps[e * 64:(e + 1) * 64, :]
                        I("pe", "matmul", reads=[r("xdt"), r("Mt")], writes=[r("y")], out=ysl,
                          lhsT=xdt[:, hh * 64:(hh + 1) * 64], rhs=Mt[:, hh, :], start=True, stop=False)
                        I("pe", "matmul", reads=[r("STb"), r("Cs")], writes=[r("y")], out=ysl,
                          lhsT=STb[:, hh * 64:(hh + 1) * 64], rhs=Cs[:, hh, :], start=False, stop=False)
                        I("pe", "matmul", reads=[r("diagD"), r("xsT%d" % c)], writes=[r("y")], out=ysl,
                          lhsT=diagD[:, c, e * 64:(e + 1) * 64], rhs=xsT[:, c, lr], start=False, stop=True)
                    I("dve", "tensor_tensor", reads=[r("y"), r("zs%d" % c)], writes=[r("yg%d" % c)], out=yg[:, c, :], in0=y_ps,
                      in1=zs[:, c, lr], op=ALU.mult)
                    I("act", "activation", reads=[r("yg%d" % c)], writes=[r("sqy%d" % (c % 2))], out=sqy[c % 2][:],
                      in_=yg[:, c, :], func=AF.Square)
                    I("pe", "matmul", reads=[r("sqy%d" % (c % 2)), C.R_ones], writes=[r("ss")], out=ss_ps[:, g, :],
                      lhsT=C.ones_f[:], rhs=sqy[c % 2][:], start=(c % 2 == 0), stop=(c % 2 == 1))
                I("act", "activation", reads=[r("ss")], writes=[r("rtg")], out=rtg[:], in_=ss_ps[:, g, :], func=AF.Ln,
                  scale=1.0 / 256, bias=EPS)
                I("act", "activation", reads=[r("rtg")], writes=[r("rstdg")], out=rstdg[:], in_=rtg[:], func=AF.Exp, scale=-0.5)
                for c2 in (2 * g, 2 * g + 1):
                    I("dve", "scalar_tensor_tensor", reads=[r("yg%d" % c2), r("ngain"), r("rstdg")], writes=[Ry[c2]],
                      out=yT[:, c2, lr], in0=yg[:, c2, :], scalar=ngain[:, c2:c2 + 1], in1=rstdg[:], op0=ALU.mult,
                      op1=ALU.mult)

            def seg_state():
                for g in range(2):
                    I("pe", "matmul", reads=[r("Btok"), r("xw")], writes=[r("SN")], out=SN[:, g * 256:(g + 1) * 256],
                      lhsT=Btok[:, g, :], rhs=xw[:, g * 256:(g + 1) * 256], start=True, stop=True)
                I("pool", "tensor_tensor", reads=[r("ST"), r("eend")], writes=[r("ST")],
                  out=ST[:].rearrange("p (h q) -> p h q", h=8), in0=ST[:].rearrange("p (h q) -> p h q", h=8),
                  in1=bview(eend[:], 2, [128, 8, 64]), op=ALU.mult)
                I("dve", "tensor_tensor", reads=[r("SN"), r("ST")], writes=[r("ST")], out=ST[:], in0=SN[:], in1=ST[:], op=ALU.add)
                I("act", "activation", reads=[r("ST")], writes=[r("STb")], out=STb[:], in_=ST[:], func=AF.Copy)

            zc = qc % 2

            def s5A(T4, lr=lr):
                p4 = slice(4 * T4, 4 * T4 + 4)
                xz = xmb[T4 % 2]
                for rr in range(4):
                    for q in range(2):
                        I("pe", "matmul", reads=[r("LD"), r("uT%d" % T4)], writes=[Rpj[0], Rpj[1]], out=drv[:, rr, q, :],
                          lhsT=LD[:, 4 * T4 + rr, q, :], rhs=uT[:, T4, lr], start=True, stop=True)
                dre, dim_ = drv[:, :, 0, :], drv[:, :, 1, :]
                Ec4, Es4 = Ec[:, p4, :], Es[:, p4, :]
                tt(t1[:], dre, Ec4, ALU.mult, rd=["pj0", "pj1", "Ec"], wr=["t1"])
                tt(t2[:], dim_, Es4, ALU.mult, rd=["pj0", "pj1", "Es"], wr=["t2"])
                tt(t3[:], dim_, Ec4, ALU.mult, rd=["pj0", "pj1", "Ec"], wr=["t3"])
                tt(t4[:], dre, Es4, ALU.mult, rd=["pj0", "pj1", "Es"], wr=["t4"])
                tt(xz[:, :, 0, :], t1[:], t2[:], ALU.add, eng="pool", rd=["t1", "t2"], wr=["xm%d" % (T4 % 2), "Zall"])
                tt(xz[:, :, 1, :], t3[:], t4[:], ALU.subtract, eng="pool", rd=["t3", "t4"], wr=["xm%d" % (T4 % 2), "Zall"])

            def s5B(T4):
                p4 = slice(4 * T4, 4 * T4 + 4)
                xz = xmb[T4 % 2]
                hz = T4 % 2
                Ec4, Es4 = Ec[:, p4, :], Es[:, p4, :]
                for rr in range(4):
                    pr = 4 * T4 + rr
                    mg = s["mag"][:, pr:pr + 1].broadcast_to([128, QC])
                    I("dve", "tensor_tensor_scan", reads=[r("xm%d" % (T4 % 2)), r("mag"), r("ire%d" % T4)],
                      writes=[r("ht%d_0" % rr), r("Zall")], out=ht[:, rr, 0, :], data0=mg, data1=xz[:, rr, 0, :],
                      initial=Hp[:, pr, 0:1], op0=ALU.mult, op1=ALU.add)
                    I("dve", "tensor_tensor_scan", reads=[r("xm%d" % (T4 % 2)), r("mag"), r("iim%d" % T4)],
                      writes=[r("ht%d_1" % rr), r("Zall")], out=ht[:, rr, 1, :], data0=mg, data1=xz[:, rr, 1, :],
                      initial=Hp[:, pr, 1:2], op0=ALU.mult, op1=ALU.add)
                tt(t5[:], ht[:, :, 0, :], Ec4, ALU.mult, rd=["ht%d_0" % j for j in range(4)] + ["Ec"], wr=["t5"])
                tt(t6[:], ht[:, :, 1, :], Es4, ALU.mult, rd=["ht%d_1" % j for j in range(4)] + ["Es"], wr=["t6"])
                tt(H[:, :, 0, :], t5[:], t6[:], ALU.subtract, eng="pool", rd=["t5", "t6"], wr=["H"])
                tt(t7[:], ht[:, :, 1, :], Ec4, ALU.mult, eng="pool", rd=["ht%d_1" % j for j in range(4)] + ["Ec"], wr=["t7"])
                tt(t8[:], ht[:, :, 0, :], Es4, ALU.mult, eng="pool", rd=["ht%d_0" % j for j in range(4)] + ["Es"], wr=["t8"])
                tt(H[:, :, 1, :], t7[:], t8[:], ALU.add, eng="pool", rd=["t7", "t8"], wr=["H"])
                I("act", "activation", reads=[r("H")], writes=[r("Hb%d" % hz)], out=Hbb[hz][:], in_=H[:], func=AF.Copy)
                I("act", "activation", reads=[r("H")], writes=[r("ire%d" % T4), r("iim%d" % T4)], out=Hp[:, p4, :],
                  in_=H[:, :, :, QC - 1], func=AF.Copy)

            def s5C1(T4, lr=lr, zc=zc):
                hz = T4 % 2
                n = 0
                for rr in range(4):
                    for q in range(2):
                        I("pe", "matmul", reads=[r("LO"), r("Hb%d" % hz)], writes=[r("y5")], out=y5_ps,
                          lhsT=LO[:, 4 * T4 + rr, q, :], rhs=Hbb[hz][:, rr, q, :], start=(n == 0), stop=False)
                        n += 1
                I("pe", "matmul", reads=[r("diagD5"), r("uT%d" % T4)], writes=[r("y5")], out=y5_ps, lhsT=diagD5[:, T4, :],
                  rhs=uT[:, T4, lr], start=False, stop=True)
                I("act", "activation", reads=[r("y5")], writes=[r("y5s%d_%d" % (zc, T4))], out=y5sb[zc][:, T4, :], in_=y5_ps,
                  func=AF.Copy)

            def s5C2(T4, zc=zc):
                yv = y5sb[zc][:, T4, :]
                gz = T4 % 2
                I("act", "activation", reads=[r("y5s%d_%d" % (zc, T4))], writes=[r("ga")], out=ga[:], in_=yv, func=AF.Square,
                  scale=0.21145921496651535)
                I("dve", "scalar_tensor_tensor", reads=[r("ga"), r("y5s%d_%d" % (zc, T4))], writes=[r("gb")], out=gb[:], in0=ga[:],
                  scalar=1.0, in1=yv, op0=ALU.add, op1=ALU.mult)
                I("act", "activation", reads=[r("gb")], writes=[r("gth%d" % gz)], out=gthb[gz][:], in_=gb[:], func=AF.Tanh,
                  scale=0.7978845608028654)

            def s5C3(T4, zc=zc):
                yv = y5sb[zc][:, T4, :]
                gz = T4 % 2
                I("dve", "scalar_tensor_tensor", reads=[r("gth%d" % gz), r("y5s%d_%d" % (zc, T4))], writes=[r("ge%d_%d" % (zc, T4))],
                  out=geb[zc][:, T4, :], in0=gthb[gz][:], scalar=1.0, in1=yv, op0=ALU.add, op1=ALU.mult)

            def epilogue(lr=lr, zc=zc):
                for o in range(4):
                    for T4 in range(4):
                        I("pe", "matmul", reads=[r("Wglu"), r("ge%d_%d" % (zc, T4))], writes=[r("gate")], out=gate_ps,
                          lhsT=Wglu[:, T4, o * 128:(o + 1) * 128], rhs=geb[zc][:, T4, :], start=(T4 == 0), stop=(T4 == 3))
                    I("act", "activation", reads=[r("gate"), r("bglu")], writes=[r("sig")], out=sig[:], in_=gate_ps,
                      func=AF.Sigmoid, scale=0.5, bias=bglu[:, o:o + 1])
                    I("dve", "tensor_tensor", reads=[r("sig"), r("y5s%d_%d" % (zc, o))], writes=[Ry[4 + o]], out=yT[:, 4 + o, lr],
                      in0=y5sb[zc][:, o, :], in1=sig[:], op=ALU.mult)

            s5A(0)
            seg0()
            s5A(1)
            if tail_prev is not None:
                tail_prev()
            s5B(0)
            seg1()
            s5A(2)
            s5B(1)
            s5C1(0)
            seg_y(0)
            s5A(3)
            s5B(2)
            s5C1(1)
            s5C2(0)
            seg_y(1)
            s5B(3)
            s5C1(2)
            s5C2(1)
            s5C3(0)
            seg_state()
            s5C1(3)
            s5C2(2)
            s5C3(1)

            def tail(c2=s5C2, c3=s5C3, ep=epilogue):
                c2(3)
                c3(2)
                c3(3)
                ep()
            tail_prev = tail
        tail_prev()
        tail_prev = None
        for dc in range(KC):
            sl = nextpj()
            for c in range(8):
                I("pe", "matmul", reads=[RWout[c], Ry[c]], writes=[Rpj[sl]], out=PJ[:, sl, 0:NTE],
                  lhsT=Wout[:, c, dc * 128:(dc + 1) * 128], rhs=yT[:, c, :], start=(c == 0), stop=(c == 7))
            I("dve", "tensor_tensor", reads=[Rpj[sl], RX[dc]], writes=[RX[dc]], out=X[:, dc, :], in0=PJ[:, sl, 0:NTE],
              in1=X[:, dc, :], op=ALU.add)
        P.D("sp", reads=RX, writes=aslist(Rxdst[t]), out=xd_v[:, :, tsl], in_=X[:])


DEPTH = 4
SEQ = 4096
NREG = SEQ // 256


def _bias_tiles(rel_bias):
    l = np.arange(128)[:, None]
    k = np.arange(640)[None, :]
    idx = np.clip(512 + l - k, -128, 128) + 128
    return np.ascontiguousarray(np.transpose(rel_bias[:, idx], (1, 0, 2)))


def _col(v):
    n = v.shape[0] // 128
    return np.ascontiguousarray(v.reshape(n, 128).T)


def _even_inputs(inp, i, layer):
    nat = lambda a: np.ascontiguousarray(a.reshape(16, 2, 64).transpose(1, 2, 0).reshape(128, 16))
    return {
        "win": inp['even_w_in'][i], "wout": inp['even_w_out'][i], "wglu": inp['s5_w_glu'][i],
        "gain": _col(inp['mix_norm'][layer]),
        "convw": np.ascontiguousarray(inp['ssd_conv_w'][i].reshape(4, 8, 128).transpose(2, 1, 0)),
        "convb": _col(inp['ssd_conv_b'][i]),
        "dtb": inp['ssd_dt_bias'][i].reshape(8, 1).copy(), "alog": inp['ssd_a_log'][i].reshape(8, 1).copy(),
        "dskip": _col(np.repeat(inp['ssd_d'][i], 64)),
        "ngain": _col(inp['ssd_norm'][i]),
        "d5": _col(inp['s5_d'][i].reshape(-1)),
        "bglu": _col(inp['s5_b_glu'][i]),
        "are": nat(inp['s5_a_re'][i]), "aim": nat(inp['s5_a_im'][i]),
        "ldt": np.ascontiguousarray(
            np.repeat(inp['s5_log_dt'][i].reshape(16, 2, 1), 64, axis=2).transpose(1, 2, 0).reshape(128, 16)),
        "bre": np.ascontiguousarray(inp['s5_b_re'][i].reshape(16, 2, 64, 16).transpose(1, 2, 0, 3).reshape(128, 16, 16)),
        "bim": np.ascontiguousarray(inp['s5_b_im'][i].reshape(16, 2, 64, 16).transpose(1, 2, 0, 3).reshape(128, 16, 16)),
        "cre": np.ascontiguousarray(inp['s5_c_re'][i].reshape(16, 2, 16, 64).transpose(1, 3, 0, 2).reshape(128, 16, 16)),
        "cim": np.ascontiguousarray(inp['s5_c_im'][i].reshape(16, 2, 16, 64).transpose(1, 3, 0, 2).reshape(128, 16, 16)),
    }


def _odd_inputs(inp, i, layer):
    return {
        "win": inp['odd_w_in'][i], "wout": inp['odd_w_out'][i], "pw": inp['pool_w'][i],
        "bt": _bias_tiles(inp['attn_rel_bias'][i]),
        "qg": np.tile(inp['attn_q_norm'][i], 2).reshape(128, 1).copy(),
        "kg": np.tile(inp['attn_k_norm'][i], 2).reshape(128, 1).copy(),
        "psc": _col(inp['pool_scale'][i]),
        "gain": _col(inp['mix_norm'][layer]),
    }


def host_layout(inp):
    m = {"ident": np.eye(128, dtype=np.float32)}
    f1 = (inp['ffn1_w_gate'], inp['ffn1_w_up'], inp['ffn1_w_down'], inp['ffn1_norm'])
    f2 = (inp['ffn2_w_gate'], inp['ffn2_w_up'], inp['ffn2_w_down'], inp['ffn2_norm'])
    for l in range(DEPTH):
        for w, (wg, wu, wd, gn) in ((1, f1), (2, f2)):
            m["f%d_%d_wg" % (w, l)] = wg[l]
            m["f%d_%d_wu" % (w, l)] = wu[l]
            m["f%d_%d_wd" % (w, l)] = wd[l]
            m["f%d_%d_gn" % (w, l)] = _col(gn[l])
        d = _even_inputs(inp, l // 2, l) if l % 2 == 0 else _odd_inputs(inp, l // 2, l)
        for k, v in d.items():
            m["m%d_%s" % (l, k)] = v
    return {k: np.ascontiguousarray(v, dtype=np.float32) for k, v in m.items()}


def build_program(shapes, phases=None, seq=SEQ):
    nc = bass.Bass("TRN2", target_bir_lowering=False)
    di = lambda n, s: nc.dram_tensor(n, list(s), F32, kind="ExternalInput").ap()
    xin = di("xT", [D, seq])
    dd = {k: di(k, s) for k, s in shapes.items()}
    yout = nc.dram_tensor("yT", [D, seq], F32, kind="ExternalOutput").ap()
    xs = nc.dram_tensor("xscratch", [D, seq], F32, kind="Internal").ap()
    if phases is None:
        phases = [(l, k) for l in range(DEPTH) for k in ("f1", "mix", "f2")]
    nreg = seq // 256
    pair = lambda R_: [[R_[2 * t], R_[2 * t + 1]] for t in range(len(R_) // 2)]
    with ExitStack() as es:
        P = Prog(nc, es)
        C = Ctx(nc, P)
        C.es = es
        emit_consts(C)
        Rin, Rsc, Rout = regs("xin", nreg), regs("xsc", nreg), regs("xout", nreg)
        with nc.Block() as block:
            for pi, (l, kind) in enumerate(phases):
                src, Rs = (xin, Rin) if pi == 0 else (xs, Rsc)
                dst, Rd = (yout, Rout) if pi == len(phases) - 1 else (xs, Rsc)
                with ExitStack() as pes:
                    C.es = pes
                    if kind in ("f1", "f2"):
                        w = 1 if kind == "f1" else 2
                        pre = "f%d_%d_" % (w, l)
                        emit_ffn(C, src, dst, pair(Rs), pair(Rd), dd[pre + "wg"], dd[pre + "wu"], dd[pre + "wd"], dd[pre + "gn"], seq // NT)
                    elif l % 2 == 0:
                        sub = {k[len("m%d_" % l):]: v for k, v in dd.items() if k.startswith("m%d_" % l)}
                        sub["ident"] = dd["ident"]
                        emit_even(C, src, dst, Rs, Rd, sub, seq // NTE)
                    else:
                        pre = "m%d_" % l
                        emit_odd(C, src, dst, pair(Rs), pair(Rd), dd[pre + "win"], dd[pre + "wout"], dd[pre + "pw"], dd[pre + "bt"],
                                 dd[pre + "qg"], dd[pre + "kg"], dd[pre + "psc"], dd[pre + "gain"], dd["ident"], seq // NT)
                    P.all_barrier()
                    P.reorder = (kind == "mix" and l % 2 == 0)
                    P.flush(block)
    return nc, P


_CACHE = {}


def kernel(**inputs):
    inp = {k: np.asarray(v) for k, v in inputs.items()}
    x = inp["x"]
    B = x.shape[0]
    wts = host_layout(inp)
    key = "full"
    if key not in _CACHE:
        _CACHE[key] = build_program({k: v.shape for k, v in wts.items()})
    nc, _ = _CACHE[key]
    n_cores = 8
    work = [0, 1, 4, 5][:B]
    zeros = {k: np.zeros_like(v) for k, v in wts.items()}
    zeros["xT"] = np.zeros((x.shape[2], x.shape[1]), np.float32)
    in_maps = []
    for c in range(n_cores):
        if c in work:
            m = dict(wts)
            m["xT"] = np.ascontiguousarray(x[work.index(c)].T)
        else:
            m = zeros
        in_maps.append(m)
    res = run_bass_kernel_spmd(nc, in_maps, core_ids=list(range(n_cores)))
    out = np.stack([np.ascontiguousarray(res.results[c]["yT"].T) for c in work], axis=0)
    return out.astype(np.float32)
```

```python
from concourse.bass_utils import run_bass_kernel_spmd
import concourse.bass as bass
import concourse.mybir as mybir

ENGS = ("pe", "act", "dve", "pool", "sp")
SEM_CAP = 24000
N_DMA_SLOTS = 24
import os as _os0
SCHED_WINDOW = int(_os0.environ.get('K_WIN', '40'))


class Reg:
    __slots__ = ("name", "w", "rs", "excl")

    def __init__(self, name, excl=False):
        self.name = name
        self.excl = excl
        self.w = None
        self.rs = []


class Op:
    __slots__ = ("eng", "seq", "fn", "deps", "is_dma", "dur", "lat", "grp_open", "grp_cont",
                 "pos", "inc", "waits", "dma_slot", "dma_val", "flushed", "t_end", "done")

    def __init__(self, eng, seq, fn):
        self.eng = eng
        self.seq = seq
        self.fn = fn
        self.deps = []
        self.is_dma = False
        self.dur = 0.1
        self.lat = 0.0
        self.grp_open = False
        self.grp_cont = False
        self.pos = -1
        self.inc = False
        self.waits = []
        self.dma_slot = None
        self.dma_val = 0
        self.flushed = False
        self.t_end = 0.0
        self.done = False


def _free(ap):
    try:
        return int(ap.free_size())
    except Exception:
        return 128


class Prog:
    def __init__(self, nc, es):
        self.nc = nc
        self.es = es
        self.rec = {e: [] for e in ENGS}
        self.seq = 0
        self.csems = {e: [] for e in ENGS}
        self.dma_sems = [es.enter_context(nc.semaphore("dq%d" % i)) for i in range(N_DMA_SLOTS)]
        self.dma_uses = [0] * N_DMA_SLOTS
        self.dma_rr = 0
        self.counts = {e: 0 for e in ENGS}
        self.n_emitted = 0
        self.reorder = True
        self.pe_open = False

    def _new(self, eng, fn):
        o = Op(eng, self.seq, fn)
        self.seq += 1
        return o

    def _dep(self, o, p):
        if p is None or p is o or p.flushed:
            return
        o.deps.append(p)

    def _track(self, o, reads, writes):
        locks = [x for x in list(reads) + list(writes) if x.excl]
        reads = [x for x in reads if not x.excl]
        writes = [x for x in writes if not x.excl]
        for r in reads:
            self._dep(o, r.w)
        for w in writes:
            self._dep(o, w.w)
            for t in w.rs:
                self._dep(o, t)
        for l in locks:
            self._dep(o, l.w)
        for r in reads:
            r.rs.append(o)
        for w in writes:
            w.w = o
            w.rs = []
        for l in locks:
            l.w = o

    def I(self, eng, name, reads=(), writes=(), **kw):
        o = self._new(eng, lambda h: getattr(h, name)(**kw))
        n = _free(kw.get("out", kw.get("ap")))
        if eng == "pe":
            if name == "matmul":
                n = _free(kw["rhs"])
                o.dur = 0.03 + 0.00046 * n
                if kw.get("lhsT") is not None and kw["lhsT"].dtype == mybir.dt.float32:
                    o.dur *= 4
                st, sp_ = kw.get("start"), kw.get("stop")
                o.grp_cont = (st is False)
                o.grp_open = (sp_ is False)
            else:
                o.dur = 0.12
        elif eng == "dve":
            o.dur = 0.08 + 0.00105 * n * (2 if name == "tensor_tensor_scan" else 1)
        elif eng == "act":
            o.dur = 0.22 + 0.00085 * n
        else:
            o.dur = 0.2 + 0.0023 * n
        self._track(o, reads, writes)
        self.rec[eng].append(o)
        return o

    def D(self, eng, reads=(), writes=(), **kw):
        o = self._new(eng, lambda h: h.dma_start(**kw))
        o.is_dma = True
        o.dur = 0.06
        o.lat = 2.0 + _free(kw["out"]) * 128 * 4 / 150e3
        self._track(o, reads, writes)
        self.rec[eng].append(o)
        return o

    def all_barrier(self):
        allops = [o for e in ENGS for o in self.rec[e] if o.fn is not None]
        for e in ENGS:
            o = self._new(e, None)
            o.dur = 0.0
            o.deps = list(allops)
            self.rec[e].append(o)

    def _schedule(self):
        rec = self.rec
        if not self.reorder:
            for e in ENGS:
                for o in rec[e]:
                    o.t_end = float(o.seq)
                    o.lat = 0.0
            return {e: list(rec[e]) for e in ENGS}
        order = {e: [] for e in ENGS}
        nxt = {e: 0 for e in ENGS}
        taken = {e: [False] * len(rec[e]) for e in ENGS}
        tfree = {e: 0.0 for e in ENGS}
        remaining = sum(len(rec[e]) for e in ENGS)
        for e in ENGS:
            for o in rec[e]:
                o.done = False
        import os as _os
        XLAT = float(_os.environ.get('K_XLAT', '1.3'))
        SLAT = float(_os.environ.get('K_SLAT', '0.6'))
        pe_open = 0
        pe_last = -1

        def ready_time(o, e):
            t = 0.0
            for p in o.deps:
                if not p.done:
                    return None
                c = p.t_end + (SLAT if (p.eng == e and not p.is_dma) else XLAT)
                if c > t:
                    t = c
            return t

        while remaining:
            best = None
            for e in ENGS:
                ol = rec[e]
                tk = taken[e]
                i = nxt[e]
                while i < len(ol) and tk[i]:
                    i += 1
                nxt[e] = i
                if i >= len(ol):
                    continue
                if e == "pe" and pe_open > 0:
                    q = pe_last + 1
                    while q < len(ol) and tk[q]:
                        q += 1
                    cand = [q] if q < len(ol) else []
                elif e == "sp":
                    cand = [i]
                else:
                    cand = []
                    j = i
                    while j < len(ol) and len(cand) < SCHED_WINDOW:
                        if not tk[j]:
                            o = ol[j]
                            if o.is_dma and j != i:
                                break
                            if not (e == "pe" and o.grp_cont and j != i):
                                cand.append(j)
                            if o.is_dma:
                                break
                        j += 1
                for j in cand:
                    o = ol[j]
                    rt = ready_time(o, e)
                    if rt is None:
                        continue
                    if e == "pe" and j != i and o.grp_open and not o.grp_cont and pe_open == 0:
                        depth, k, ok = 0, j, True
                        members = set()
                        while k < len(ol):
                            m = ol[k]
                            if not tk[k]:
                                members.add(id(m))
                                if m.grp_open and not m.grp_cont:
                                    depth += 1
                                elif m.grp_cont and not m.grp_open:
                                    depth -= 1
                                for p in m.deps:
                                    if not p.done and id(p) not in members:
                                        ok = False
                                        break
                                if not ok or depth == 0:
                                    break
                            k += 1
                        if not ok:
                            continue
                    st = rt if rt > tfree[e] else tfree[e]
                    key = (st, o.seq)
                    if best is None or key < best[0]:
                        best = (key, e, j, o, st)
                    if st <= tfree[e]:
                        break
            if best is None:
                for e in ENGS:
                    i = nxt[e]
                    if i < len(rec[e]):
                        o = rec[e][i]
                        print("STUCK", e, i, "of", len(rec[e]), "seq", o.seq, "dma", o.is_dma, "cont", o.grp_cont, "open", o.grp_open,
                              "pe_open", pe_open, "pe_last", pe_last,
                              "unmet", [(p.eng, p.seq, p.is_dma, rec[p.eng].index(p) if p in rec[p.eng] else -1) for p in o.deps if not p.done][:6])
            assert best is not None, "scheduler deadlock (dependency cycle?)"
            _, e, j, o, st = best
            o.t_end = st + o.dur + o.lat
            o.done = True
            order[e].append(o)
            tfree[e] = st + o.dur
            taken[e][j] = True
            remaining -= 1
            if e == "pe":
                pe_last = j
                if o.grp_open and not o.grp_cont:
                    pe_open += 1
                elif o.grp_cont and not o.grp_open:
                    pe_open -= 1
        self.sim_makespan = max(tfree.values())
        self.sim_busy = {e: sum(o.dur for o in rec[e]) for e in ENGS}
        return order

    def _csem(self, e, epoch):
        while len(self.csems[e]) <= epoch:
            self.csems[e].append(self.es.enter_context(self.nc.semaphore("c_%s_%d" % (e, len(self.csems[e])))))
        return self.csems[e][epoch]

    def flush(self, block):
        order = self._schedule()
        for e in ENGS:
            for i, o in enumerate(order[e]):
                o.pos = i
        dmas = sorted([o for e in ENGS for o in order[e] if o.is_dma], key=lambda o: (o.t_end - o.lat, o.seq))
        slot_prev = {}
        for o in dmas:
            slot = self.dma_rr
            self.dma_rr = (self.dma_rr + 1) % N_DMA_SLOTS
            if slot in slot_prev:
                o.deps.append(slot_prev[slot])
            slot_prev[slot] = o
            self.dma_uses[slot] += 1
            o.dma_slot = slot
            o.dma_val = 16 * self.dma_uses[slot]
        seen = {e: {e2: -1 for e2 in ENGS} for e in ENGS}
        seen_dma = {e: {} for e in ENGS}
        for e in ENGS:
            for o in order[e]:
                o.waits = []
                for p in o.deps:
                    if p.flushed:
                        continue
                    if p.is_dma:
                        if seen_dma[e].get(p.dma_slot, 0) >= p.dma_val:
                            continue
                        seen_dma[e][p.dma_slot] = p.dma_val
                        o.waits.append(p)
                    else:
                        if p.eng == e:
                            assert p.pos < o.pos, "same-engine dependency order violated"
                            if e == "pe":
                                continue
                        if seen[e][p.eng] >= p.pos:
                            continue
                        seen[e][p.eng] = p.pos
                        p.inc = True
                        o.waits.append(p)
        cnt = {}
        for e in ENGS:
            c = self.counts[e]
            for o in order[e]:
                if o.inc and not o.is_dma and o.fn is not None:
                    c += 1
                    cnt[id(o)] = c
            if c > 0:
                self._csem(e, (c - 1) // SEM_CAP)
            self.counts[e] = c

        def emit(e, h):
            for o in order[e]:
                for p in o.waits:
                    if p.is_dma:
                        h.wait_ge(self.dma_sems[p.dma_slot], p.dma_val)
                    else:
                        ep, v = divmod(cnt[id(p)] - 1, SEM_CAP)
                        h.wait_ge(self._csem(p.eng, ep), v + 1)
                if o.fn is None:
                    continue
                ins = o.fn(h)
                if o.is_dma:
                    ins.then_inc(self.dma_sems[o.dma_slot], 16)
                elif o.inc:
                    ep, v = divmod(cnt[id(o)] - 1, SEM_CAP)
                    ins.then_inc(self._csem(e, ep), 1)
                self.n_emitted += 1

        block.tensor(lambda h: emit("pe", h))
        block.scalar(lambda h: emit("act", h))
        block.vector(lambda h: emit("dve", h))
        block.gpsimd(lambda h: emit("pool", h))
        block.sync(lambda h: emit("sp", h))
        for e in ENGS:
            for o in order[e]:
                o.flushed = True
                o.deps = []
                o.fn = None
            self.rec[e] = []


import numpy as np
from contextlib import ExitStack
import concourse.bass as bass
import concourse.mybir as mybir

F32 = mybir.dt.float32
BF16 = mybir.dt.bfloat16
AF = mybir.ActivationFunctionType
ALU = mybir.AluOpType
AX = mybir.AxisListType

D = 1024
KC = 8
FF = 2816
FC = 22
NT = 512
EPS = 1e-6


class Ctx:
    def __init__(self, nc, P):
        self.nc = nc
        self.P = P
        self.es = None
        self.n = 0

    def sb(self, shape, dt, name=None):
        self.n += 1
        return self.es.enter_context(self.nc.sbuf_tensor("%s_%d" % (name or "t", self.n), shape, dt))

    def ps(self, shape, dt, name=None):
        self.n += 1
        return self.es.enter_context(self.nc.psum_tensor("%s_%d" % (name or "p", self.n), shape, dt))


def aslist(x):
    return list(x) if isinstance(x, (list, tuple)) else [x]


def regs(prefix, n, excl=False):
    return [Reg("%s%d" % (prefix, i), excl) for i in range(n)]


def emit_consts(C):
    P, nc = C.P, C.nc
    C.ones_f = C.sb([128, 128], F32, "ones")
    C.R_ones = Reg("ones")
    P.I("dve", "memset", writes=[C.R_ones], ap=C.ones_f[:], constant=1.0)


def emit_norm(C, X, RX, gain, Rgain, hb, Rh, sq, Rsq, ssq_ps, Rssq, rt, Rrt, rstd, Rrstd):
    P = C.P
    for kc in range(KC):
        s = kc % 2
        P.I("act", "activation", reads=[RX[kc]], writes=[Rsq[s]], out=sq[s][:], in_=X[:, kc, :], func=AF.Square)
        P.I("pe", "matmul", reads=[Rsq[s], C.R_ones], writes=[Rssq], out=ssq_ps[:], lhsT=C.ones_f[:], rhs=sq[s][:],
            start=(kc == 0), stop=(kc == KC - 1))
    P.I("act", "activation", reads=[Rssq], writes=[Rrt], out=rt[:], in_=ssq_ps[:], func=AF.Ln, scale=1.0 / D, bias=EPS)
    P.I("act", "activation", reads=[Rrt], writes=[Rrstd], out=rstd[:], in_=rt[:], func=AF.Exp, scale=-0.5)
    for kc in range(KC):
        P.I("dve", "scalar_tensor_tensor", reads=[RX[kc], Rgain, Rrstd], writes=[Rh[kc]], out=hb[:, kc, :],
            in0=X[:, kc, :], scalar=gain[:, kc:kc + 1], in1=rstd[:], op0=ALU.mult, op1=ALU.mult)


def emit_ffn(C, xsrc, xdst, Rxsrc, Rxdst, wg_d, wu_d, wd_d, gain_d, ntiles):
    P, nc = C.P, C.nc
    Wg = C.sb([128, KC, FF], BF16, "Wg")
    Wu = C.sb([128, KC, FF], BF16, "Wu")
    Wd = C.sb([128, FC, D], BF16, "Wd")
    gain = C.sb([128, KC], F32, "gain")
    xt = [C.sb([128, KC, NT], F32, "xt") for _ in range(2)]
    sq = [C.sb([128, NT], F32, "sq") for _ in range(2)]
    hb = C.sb([128, KC, NT], BF16, "h")
    act = C.sb([128, FC, NT], BF16, "act")
    sg = [C.sb([128, NT], BF16, "sg") for _ in range(2)]
    rt = C.sb([128, NT], F32, "rt")
    rstd = C.sb([128, NT], F32, "rstd")
    ssq_ps = C.ps([128, NT], F32, "ssq")
    g_ps = [C.ps([128, NT], F32, "g") for _ in range(2)]
    u_ps = [C.ps([128, NT], F32, "u") for _ in range(2)]
    o_ps = [C.ps([128, NT], F32, "o") for _ in range(2)]

    RWg, RWu, RWd = regs("Wg", KC), regs("Wu", KC), regs("Wd", FC)
    Rgain = Reg("gain")
    Rxt = [regs("xt%d_" % b, KC) for b in range(2)]
    Rsq = regs("sq", 2)
    Rh = regs("h", KC)
    Ract = regs("act", FC)
    Rsg = regs("sg", 2)
    Rrt, Rrstd, Rssq = Reg("rt"), Reg("rstd"), Reg("ssq", True)
    Rg, Ru, Ro = regs("g", 2, True), regs("u", 2, True), regs("o", 2, True)

    xs_v = xsrc.rearrange("(kc p) t -> p kc t", p=128)
    xd_v = xdst.rearrange("(kc p) t -> p kc t", p=128)

    def load(t, q="sp"):
        b = t % 2
        P.D(q, reads=aslist(Rxsrc[t]), writes=Rxt[b], out=xt[b][:], in_=xs_v[:, :, t * NT:(t + 1) * NT])

    P.D("pool", writes=[Rgain], out=gain[:], in_=gain_d)
    load(0, "pool")
    if ntiles > 1:
        load(1, "pool")
    wg_v = wg_d.rearrange("(kc p) f -> p kc f", p=128)
    wu_v = wu_d.rearrange("(kc p) f -> p kc f", p=128)
    wd_v = wd_d.rearrange("(fc p) d -> p fc d", p=128)
    FG = [(0, 6), (6, 12), (12, 17), (17, 22)]
    RWg, RWu = regs("Wg", len(FG)), regs("Wu", len(FG))
    fgrp = {}
    for gi, (f0, f1) in enumerate(FG):
        for f in range(f0, f1):
            fgrp[f] = gi
        P.D("pool", writes=[RWg[gi]], out=Wg[:, :, f0 * 128:f1 * 128], in_=wg_v[:, :, f0 * 128:f1 * 128])
        P.D("pool", writes=[RWu[gi]], out=Wu[:, :, f0 * 128:f1 * 128], in_=wu_v[:, :, f0 * 128:f1 * 128])
    for fc in range(FC):
        P.D("pool", writes=[RWd[fc]], out=Wd[:, fc, :], in_=wd_v[:, fc, :])

    for t in range(ntiles):
        b = t % 2
        if t + 1 < ntiles and t >= 1:
            load(t + 1)
        X = xt[b]
        emit_norm(C, X, Rxt[b], gain, Rgain, hb, Rh, sq, Rsq, ssq_ps, Rssq, rt, Rrt, rstd, Rrstd)
        for f in range(FC):
            s = f % 2
            for kc in range(KC):
                P.I("pe", "matmul", reads=[RWg[fgrp[f]], Rh[kc]], writes=[Rg[s]], out=g_ps[s][:], lhsT=Wg[:, kc, f * 128:(f + 1) * 128], rhs=hb[:, kc, :],
                    start=(kc == 0), stop=(kc == KC - 1))
            for kc in range(KC):
                P.I("pe", "matmul", reads=[RWu[fgrp[f]], Rh[kc]], writes=[Ru[s]], out=u_ps[s][:], lhsT=Wu[:, kc, f * 128:(f + 1) * 128], rhs=hb[:, kc, :],
                    start=(kc == 0), stop=(kc == KC - 1))
            P.I("act", "activation", reads=[Rg[s]], writes=[Rsg[s]], out=sg[s][:], in_=g_ps[s][:], func=AF.Silu)
            P.I("dve", "tensor_tensor", reads=[Ru[s], Rsg[s]], writes=[Ract[f]], out=act[:, f, :], in0=u_ps[s][:], in1=sg[s][:], op=ALU.mult)
        for dc in range(KC):
            s = dc % 2
            for f in range(FC):
                P.I("pe", "matmul", reads=[RWd[f], Ract[f]], writes=[Ro[s]], out=o_ps[s][:], lhsT=Wd[:, f, dc * 128:(dc + 1) * 128], rhs=act[:, f, :],
                    start=(f == 0), stop=(f == FC - 1))
            P.I("dve", "scalar_tensor_tensor", reads=[Ro[s], Rxt[b][dc]], writes=[Rxt[b][dc]], out=X[:, dc, :], in0=o_ps[s][:], scalar=0.5, in1=X[:, dc, :], op0=ALU.mult, op1=ALU.add)
        P.D("sp", reads=Rxt[b], writes=aslist(Rxdst[t]), out=xd_v[:, :, t * NT:(t + 1) * NT], in_=X[:])


def emit_odd(C, xsrc, xdst, Rxsrc, Rxdst, win_d, wout_d, poolw_d, bt_d, qg_d, kg_d, pscale_d, gain_d, ident_d, ntiles):
    P, nc = C.P, C.nc
    T = ntiles * NT
    Win = C.sb([128, KC, 2048], BF16, "Win")
    Wout = C.sb([128, KC, D], BF16, "Wout")
    Wp = C.sb([128, 4, 128], BF16, "Wp")
    bt = C.sb([128, 8, 640], F32, "bt")
    qg = C.sb([128, 1], F32, "qg")
    kg = C.sb([128, 1], F32, "kg")
    pscale = C.sb([128, 4], F32, "pscale")
    gain = C.sb([128, KC], F32, "gain")
    ident_f = C.sb([128, 128], F32, "identf")
    ident = C.sb([128, 128], BF16, "ident")
    ones_bf = C.sb([128, 64], BF16, "onesbf")
    bones = C.sb([128, 128], F32, "bones")
    KnT = C.sb([128, 4, T], BF16, "KnT")
    V = C.sb([128, ntiles * 4, 512], BF16, "V")
    X = C.sb([128, KC, NT], F32, "X")
    hb = C.sb([128, KC, NT], BF16, "h")
    qnT = C.sb([128, 4, NT], BF16, "qnT")
    yT = C.sb([128, 8, NT], BF16, "yT")
    U = C.sb([128, 4, 16 + NT], F32, "U")
    sA = C.sb([128, 16 + NT], F32, "sA")
    sB = C.sb([128, 16 + NT], F32, "sB")
    t16 = C.sb([128, 16], F32, "t16")
    pooled = C.sb([128, 4, NT], BF16, "pooled")
    rc = C.sb([128, 4, 16], F32, "rc")
    sq = [C.sb([128, NT], F32, "sq") for _ in range(2)]
    rt = C.sb([128, NT], F32, "rt")
    rstd = C.sb([128, NT], F32, "rstd")
    sbs = [C.sb([128, 640], F32, "sbs") for _ in range(2)]
    nmx = [C.sb([128, 1], F32, "nmx") for _ in range(2)]
    Pb = [C.sb([128, 640], BF16, "Pb") for _ in range(2)]
    PT = [C.sb([128, 5, 128], BF16, "PT") for _ in range(2)]
    rinv = C.sb([128, 128], F32, "rinv")
    pj = [C.ps([128, NT], F32, "pj") for _ in range(2)]
    S = [C.ps([128, 1024], F32, "S") for _ in range(2)]
    PTp = C.ps([128, 8, 128], BF16, "PTp")
    OR = C.ps([128, 512], F32, "OR")

    RWin, RWout = regs("Win", KC), regs("Wout", KC)
    RWp, Rbt, Rqg, Rkg, Rpsc, Rgain = Reg("Wp"), Reg("bt"), Reg("qg"), Reg("kg"), Reg("psc"), Reg("gain")
    Rident, Ridf, Rones, Rbones, Rrc = Reg("ident"), Reg("identf"), Reg("onesbf"), Reg("bones"), Reg("rc")
    RKn = [regs("Kn%d_" % t, 4) for t in range(ntiles)]
    RV = regs("V", ntiles * 4)
    RX, Rh = regs("X", KC), regs("h", KC)
    Rqn, Ry = regs("qn", 4), regs("y", 8)
    RU, RsA, RsB, Rt16, Rpooled = regs("U", 4), Reg("sA"), Reg("sB"), Reg("t16"), regs("pooled", 4)
    Rsq = regs("sq", 2)
    Rrt, Rrstd = Reg("rt"), Reg("rstd")
    Rsb, Rmx, RPb, RPT, Rrinv = regs("sb", 2), regs("mx", 2), regs("Pb", 2), regs("PT", 2), Reg("rinv")
    Rpj, RS, RPTp, ROR = regs("pj", 2, True), regs("S", 2, True), Reg("PTp", True), Reg("OR", True)

    P.D("pool", writes=[Rgain], out=gain[:], in_=gain_d)
    P.D("pool", writes=[Rqg], out=qg[:], in_=qg_d)
    P.D("pool", writes=[Rkg], out=kg[:], in_=kg_d)
    P.D("pool", writes=[Rpsc], out=pscale[:], in_=pscale_d)
    P.D("pool", writes=[Ridf], out=ident_f[:], in_=ident_d)
    P.D("pool", writes=[Rbt], out=bt[:], in_=bt_d)
    xs_v = xsrc.rearrange("(kc p) t -> p kc t", p=128)
    xd_v = xdst.rearrange("(kc p) t -> p kc t", p=128)
    P.D("pool", reads=aslist(Rxsrc[0]), writes=RX, out=X[:], in_=xs_v[:, :, 0:NT])
    win_v = win_d.rearrange("(kc p) f -> p kc f", p=128)
    wout_v = wout_d.rearrange("(kc p) f -> p kc f", p=128)
    for kc in range(KC):
        P.D("pool", writes=[RWin[kc]], out=Win[:, kc, :], in_=win_v[:, kc, :])
    for kc in range(KC):
        P.D("pool", writes=[RWout[kc]], out=Wout[:, kc, :], in_=wout_v[:, kc, :])
    P.D("pool", writes=[RWp], out=Wp[:], in_=poolw_d.rearrange("g i o -> i g o"))
    P.I("dve", "tensor_copy", reads=[Ridf], writes=[Rident], out=ident[:], in_=ident_f[:])
    P.I("dve", "memset", writes=[Rones], ap=ones_bf[:], constant=1.0)
    P.I("dve", "memset", writes=[Rbones], ap=bones[:], constant=0.0)
    P.I("dve", "memset", reads=[], writes=[Rbones], ap=bones[0:64, 0:64], constant=1.0)
    P.I("dve", "memset", reads=[], writes=[Rbones], ap=bones[64:128, 64:128], constant=1.0)
    P.I("dve", "memset", writes=[Rbt], ap=bt[0:64, :, 576:640], constant=-1e30)
    P.I("dve", "memset", writes=[Rbt], ap=bt[64:128, :, 0:64], constant=-1e30)
    for g in range(4):
        P.I("pool", "memset", writes=[RU[g]], ap=U[:, g, 0:16], constant=0.0)
    WIN = (2, 4, 8, 16)
    for g, w in enumerate(WIN):
        for pos in range(w - 1):
            P.I("pool", "memset", writes=[Rrc], ap=rc[:, g, pos:pos + 1], constant=1.0 / (pos + 1))
        P.I("pool", "memset", writes=[Rrc], ap=rc[:, g, w - 1:16], constant=1.0 / w)

    xs_v = xsrc.rearrange("(kc p) t -> p kc t", p=128)
    xd_v = xdst.rearrange("(kc p) t -> p kc t", p=128)
    pjn = [0]

    def nextpj():
        pjn[0] += 1
        return pjn[0] % len(pj)

    for t in range(ntiles):
        tsl = slice(t * NT, (t + 1) * NT)
        if t > 0:
            P.D("sp", reads=aslist(Rxsrc[t]), writes=RX, out=X[:], in_=xs_v[:, :, tsl])
        s0 = nextpj()
        emit_norm(C, X, RX, gain, Rgain, hb, Rh, sq, Rsq, pj[s0], Rpj[s0], rt, Rrt, rstd, Rrstd)
        for col0, gv, Rgv, is_q in ((0, qg, Rqg, True), (512, kg, Rkg, False)):
            for c in range(4):
                s = nextpj()
                for kc in range(KC):
                    P.I("pe", "matmul", reads=[RWin[kc], Rh[kc]], writes=[Rpj[s]], out=pj[s][:],
                        lhsT=Win[:, kc, col0 + c * 128:col0 + (c + 1) * 128], rhs=hb[:, kc, :],
                        start=(kc == 0), stop=(kc == KC - 1))
                s3 = 1 - s
                P.I("act", "activation", reads=[Rpj[s]], writes=[Rsq[0]], out=sq[0][:], in_=pj[s][:], func=AF.Square)
                P.I("pe", "matmul", reads=[Rsq[0], Rbones], writes=[Rpj[s3]], out=pj[s3][:], lhsT=bones[:], rhs=sq[0][:],
                    start=True, stop=True)
                P.I("act", "activation", reads=[Rpj[s3]], writes=[Rrt], out=rt[:], in_=pj[s3][:], func=AF.Ln,
                    scale=1.0 / 64, bias=EPS)
                P.I("act", "activation", reads=[Rrt], writes=[Rrstd], out=rstd[:], in_=rt[:], func=AF.Exp, scale=-0.5)
                if is_q:
                    dst, Rdst = qnT[:, c, :], Rqn[c]
                else:
                    dst, Rdst = KnT[:, c, tsl], RKn[t][c]
                P.I("dve", "scalar_tensor_tensor", reads=[Rpj[s], Rgv, Rrstd], writes=[Rdst], out=dst,
                    in0=pj[s][:], scalar=gv[:, 0:1], in1=rstd[:], op0=ALU.mult, op1=ALU.mult)
        for tb in range(4):
            s = nextpj()
            for kc in range(KC):
                P.I("pe", "matmul", reads=[RWin[kc], Rh[kc]], writes=[Rpj[s]], out=pj[s][:],
                    lhsT=hb[:, kc, tb * 128:(tb + 1) * 128], rhs=Win[:, kc, 1024:1536],
                    start=(kc == 0), stop=(kc == KC - 1))
            P.I("act", "activation", reads=[Rpj[s]], writes=[RV[4 * t + tb]], out=V[:, 4 * t + tb, :], in_=pj[s][:],
                func=AF.Copy)
        for g, w in enumerate(WIN):
            s = nextpj()
            for kc in range(KC):
                P.I("pe", "matmul", reads=[RWin[kc], Rh[kc]], writes=[Rpj[s]], out=pj[s][:],
                    lhsT=Win[:, kc, 1536 + g * 128:1536 + (g + 1) * 128], rhs=hb[:, kc, :],
                    start=(kc == 0), stop=(kc == KC - 1))
            P.I("act", "activation", reads=[Rpj[s]], writes=[RU[g]], out=U[:, g, 16:16 + NT], in_=pj[s][:], func=AF.Copy)
            L = 16 + NT
            P.I("pool", "tensor_tensor", reads=[RU[g]], writes=[RsA], out=sA[:, 1:L], in0=U[:, g, 1:L], in1=U[:, g, 0:L - 1], op=ALU.add)
            fin, Rfin = sA, RsA
            if w >= 4:
                P.I("pool", "tensor_tensor", reads=[RsA], writes=[RsB], out=sB[:, 3:L], in0=sA[:, 3:L], in1=sA[:, 1:L - 2], op=ALU.add)
                fin, Rfin = sB, RsB
            if w >= 8:
                P.I("pool", "tensor_tensor", reads=[RsB], writes=[RsA], out=sA[:, 7:L], in0=sB[:, 7:L], in1=sB[:, 3:L - 4], op=ALU.add)
                fin, Rfin = sA, RsA
            if w >= 16:
                P.I("pool", "tensor_tensor", reads=[RsA], writes=[RsB], out=sB[:, 15:L], in0=sA[:, 15:L], in1=sA[:, 7:L - 8], op=ALU.add)
                fin, Rfin = sB, RsB
            P.I("dve", "scalar_tensor_tensor", reads=[Rfin, RU[g]], writes=[Rpooled[g]], out=pooled[:, g, :],
                in0=fin[:, 16:L], scalar=1.0 / w, in1=U[:, g, 16:L], op0=ALU.mult, op1=ALU.subtract)
            if t == 0:
                P.I("dve", "tensor_tensor", reads=[Rfin, Rrc], writes=[Rt16], out=t16[:], in0=fin[:, 16:32], in1=rc[:, g, :], op=ALU.mult)
                P.I("dve", "tensor_tensor", reads=[Rt16, RU[g]], writes=[Rpooled[g]], out=pooled[:, g, 0:16], in0=t16[:], in1=U[:, g, 16:32], op=ALU.subtract)
            P.I("pool", "tensor_copy", reads=[RU[g]], writes=[RU[g]], out=U[:, g, 0:16], in_=U[:, g, NT:NT + 16])
            s = nextpj()
            P.I("pe", "matmul", reads=[RWp, Rpooled[g]], writes=[Rpj[s]], out=pj[s][:], lhsT=Wp[:, g, :], rhs=pooled[:, g, :],
                start=True, stop=True)
            P.I("dve", "tensor_scalar", reads=[Rpj[s], Rpsc], writes=[Ry[4 + g]], out=yT[:, 4 + g, :], in0=pj[s][:],
                scalar1=pscale[:, g:g + 1], scalar2=None, op0=ALU.mult)
        items = [(mb, c, e) for mb in range(4) for c in range(4) for e in range(2)]

        def geom(mb):
            m = 4 * t + mb
            nkb = min(m, 4) + 1
            j0 = m - (nkb - 1)
            return m, nkb, j0, (5 - nkb) * 128, nkb * 128

        def stageA(i):
            mb, c, e = items[i]
            m, nkb, j0, koff, nk = geom(mb)
            z = i % 2
            hh = 2 * c + e
            hs = e * 64
            lsl = slice(mb * 128, (mb + 1) * 128)
            kregs = [RKn[tt][c] for tt in range(j0 // 4, t + 1)]
            for a_, b_ in ([(0, min(nk, 512))] + ([(512, nk)] if nk > 512 else [])):
                P.I("pe", "matmul", reads=[Rqn[c]] + kregs, writes=[RS[z]], out=S[z][:, a_:b_],
                    lhsT=qnT[hs:hs + 64, c, lsl], rhs=KnT[hs:hs + 64, c, j0 * 128 + a_:j0 * 128 + b_],
                    start=True, stop=True)
            P.I("dve", "scalar_tensor_tensor", reads=[RS[z], Rbt], writes=[Rsb[z]], out=sbs[z][:, 0:nk], in0=S[z][:, 0:nk],
                scalar=0.125, in1=bt[:, hh, koff:koff + nk], op0=ALU.mult, op1=ALU.add)
            P.I("dve", "tensor_reduce", reads=[Rsb[z]], writes=[Rmx[z]], out=nmx[z][:], in_=sbs[z][:, 0:nk], axis=AX.X,
                op=ALU.max, negate=True)
            P.I("act", "activation", reads=[Rsb[z], Rmx[z]], writes=[RPb[z]], out=Pb[z][:, 0:nk], in_=sbs[z][:, 0:nk],
                func=AF.Exp, bias=nmx[z][:], scale=1.0)

        def stageB1(i):
            mb, c, e = items[i]
            m, nkb, j0, koff, nk = geom(mb)
            z = i % 2
            for kb in range(nkb):
                P.I("pe", "transpose", reads=[RPb[z], Rident], writes=[RPTp], out=PTp[:, kb, :],
                    in_=Pb[z][:, kb * 128:(kb + 1) * 128], identity=ident[:])
            P.I("act", "activation", reads=[RPTp], writes=[RPT[e]], out=PT[e][:, 0:nkb, :], in_=PTp[:, 0:nkb, :],
                func=AF.Copy)

        def stageB2(i):
            mb, c, e = items[i]
            m, nkb, j0, koff, nk = geom(mb)
            hh = 2 * c + e
            hs = e * 64
            lsl = slice(mb * 128, (mb + 1) * 128)
            for kb in range(nkb):
                P.I("pe", "matmul", reads=[RV[j0 + kb], RPT[e]], writes=[ROR], out=OR[hs:hs + 64, 0:128],
                    lhsT=V[:, j0 + kb, hh * 64:(hh + 1) * 64], rhs=PT[e][:, kb, :],
                    start=(kb == 0), stop=(kb == nkb - 1))
            for kb in range(nkb):
                P.I("pe", "matmul", reads=[Rones, RPT[e]], writes=[ROR], out=OR[hs:hs + 64, 128:256],
                    lhsT=ones_bf[:, 0:64], rhs=PT[e][:, kb, :], start=(kb == 0), stop=(kb == nkb - 1))
            if e == 1:
                P.I("dve", "reciprocal", reads=[ROR], writes=[Rrinv], out=rinv[:], in_=OR[:, 128:256])
                P.I("dve", "tensor_tensor", reads=[ROR, Rrinv], writes=[Ry[c]], out=yT[:, c, lsl], in0=OR[:, 0:128],
                    in1=rinv[:], op=ALU.mult)

        n_it = len(items)
        stageA(0)
        stageA(1)
        stageB1(0)
        for i in range(n_it):
            if i + 2 < n_it:
                stageA(i + 2)
            if i + 1 < n_it:
                stageB1(i + 1)
            stageB2(i)
        for dc in range(KC):
            s = nextpj()
            for c in range(8):
                P.I("pe", "matmul", reads=[RWout[c], Ry[c]], writes=[Rpj[s]], out=pj[s][:],
                    lhsT=Wout[:, c, dc * 128:(dc + 1) * 128], rhs=yT[:, c, :], start=(c == 0), stop=(c == 7))
            P.I("dve", "tensor_tensor", reads=[Rpj[s], RX[dc]], writes=[RX[dc]], out=X[:, dc, :], in0=pj[s][:],
                in1=X[:, dc, :], op=ALU.add)
        P.D("sp", reads=RX, writes=aslist(Rxdst[t]), out=xd_v[:, :, tsl], in_=X[:])


NTE = 256
QC = 128


def bview(ap, axis, shape):
    return ap.unsqueeze(axis).broadcast_to(shape)


def emit_even(C, xsrc, xdst, Rxsrc, Rxdst, dd, ntiles):
    P, nc = C.P, C.nc
    sb, ps = C.sb, C.ps
    Win = sb([128, KC, 2056], BF16, "Win")
    Wout = sb([128, KC, D], BF16, "Wout")
    Wglu = sb([128, 4, 512], BF16, "Wglu")
    diagw = sb([128, 8, 4, 128], BF16, "diagw")
    diagD = sb([128, 4, 128], BF16, "diagD")
    diagD5 = sb([128, 4, 128], BF16, "diagD5")
    gain = sb([128, KC], F32, "gain")
    convw = sb([128, 8, 4], F32, "convw")
    convb = sb([128, 8], F32, "convb")
    dtb = sb([8, 1], F32, "dtb")
    alog = sb([8, 1], F32, "alog")
    aneg = sb([8, 1], F32, "aneg")
    dskip = sb([128, 4], F32, "dskip")
    ngain = sb([128, 4], F32, "ngain")
    d5 = sb([128, 4], F32, "d5")
    bglu = sb([128, 4], F32, "bglu")
    ident_f = sb([128, 128], F32, "identf")
    ident = sb([128, 128], BF16, "ident")
    selH = sb([8, 8, 128], F32, "selH")
    tri = sb([128, 128], F32, "tri")
    maskneg = sb([128, 128], F32, "maskneg")
    X = sb([128, KC, NTE], F32, "X")
    hb = sb([128, KC, NTE], BF16, "h")
    zs = sb([128, 4, NTE], BF16, "zs")
    xbc = sb([128, 8, 3 + NTE], BF16, "xbc")
    xsT = sb([128, 4, NTE], BF16, "xsT")
    BT = sb([128, 2, NTE], BF16, "BT")
    CT = sb([128, 2, NTE], BF16, "CT")
    uT = sb([128, 4, NTE], BF16, "uT")
    yT = sb([128, 8, NTE], BF16, "yT")
    dte1 = sb([8, NTE], F32, "dte1")
    dtsp = sb([8, NTE], F32, "dtsp")
    dA = sb([8, NTE], F32, "dA")
    cs = sb([8, NTE], F32, "cs")
    sq = [sb([128, NTE], F32, "sq") for _ in range(2)]
    rt = sb([128, NTE], F32, "rt")
    rstd = sb([128, NTE], F32, "rstd")
    csdt = sb([128, 16], F32, "csdt")
    tmpL = sb([128, 8, 128], F32, "tmpL")
    ecs = sb([128, 8, 128], BF16, "ecs")
    eend = sb([128, 8], F32, "eend")
    d1 = sb([128, 8], F32, "d1")
    dte = sb([128, 8], F32, "dte")
    Mt = sb([128, 8, 128], BF16, "Mt")
    Cs = sb([128, 8, 128], BF16, "Cs")
    xdt = sb([128, 512], BF16, "xdt")
    xw = sb([128, 512], BF16, "xw")
    Btok = sb([128, 2, 128], BF16, "Btok")
    yg = sb([128, 4, 128], F32, "yg")
    sqy = [sb([128, 128], F32, "sqy") for _ in range(2)]
    rtg = sb([128, 128], F32, "rtg")
    rstdg = sb([128, 128], F32, "rstdg")
    ST = sb([128, 512], F32, "ST")
    STb = sb([128, 512], BF16, "STb")
    LD = sb([128, 16, 2, 128], BF16, "LD")
    LO = sb([128, 16, 2, 128], BF16, "LO")
    Ec = sb([128, 16, 128], F32, "Ec")
    Es = sb([128, 16, 128], F32, "Es")
    big = sb([128, 2048], F32, "big")
    Zall = big[:].rearrange("p (a g i) -> p a g i", a=16, g=8)
    maskZ = sb([128, 16, 8], F32, "maskZ")
    sm = {k: sb([128, 16], F32, k) for k in ("are", "aim", "ldt", "dt", "dar", "mag", "th", "c", "s", "ta", "tb", "c2", "s2",
                                              "abr", "abi", "abr1", "den", "rden", "fre", "fim", "u1", "u2",
                                              "ire", "iim", "v1", "v2")}
    cm = [sb([128, 16], F32, "cm%d" % k) for k in range(7)]
    smm = [sb([128, 16], F32, "sm%d" % k) for k in range(7)]
    Hp = sb([128, 16, 2], F32, "Hp")
    t1 = sb([128, 4, 128], F32, "t1")
    t2 = sb([128, 4, 128], F32, "t2")
    t3 = sb([128, 4, 128], F32, "t3")
    t4 = sb([128, 4, 128], F32, "t4")
    t5 = sb([128, 4, 128], F32, "t5")
    t6 = sb([128, 4, 128], F32, "t6")
    t7 = sb([128, 4, 128], F32, "t7")
    t8 = sb([128, 4, 128], F32, "t8")
    xm2 = sb([128, 4, 2, 128], F32, "xm2")
    nat = {}
    for nk_, tb_ in zip(("bre", "bim", "cre", "cim", "NBre", "NBim", "n1", "n2"), (t1, t2, t3, t4, t5, t6, t7, t8)):
        nat[nk_] = tb_[:].rearrange("p a l -> p (a l)")[:, 0:256].rearrange("p (a j) -> p a j", a=16)
    xm = big[:, 0:1024].rearrange("p (a q l) -> p a q l", a=4, q=2)
    ht = big[:, 1024:2048].rearrange("p (a q l) -> p a q l", a=4, q=2)
    xmb = [xm, xm2[:]]
    H = sb([128, 4, 2, 128], F32, "H")
    Hbb = [sb([128, 4, 2, 128], BF16, "Hb") for _ in range(2)]
    y5sb = [sb([128, 4, 128], F32, "y5s") for _ in range(2)]
    ga = sb([128, 128], F32, "ga")
    gb = sb([128, 128], F32, "gb")
    gthb = [sb([128, 128], F32, "gth") for _ in range(2)]
    geb = [sb([128, 4, 128], BF16, "ge") for _ in range(2)]
    sig = sb([128, 128], F32, "sig")
    PJ = ps([128, 2, 512], F32, "PJ")
    M1 = ps([128, 1024], BF16, "M1")
    M2 = ps([128, 512], F32, "M2")
    BC = ps([128, 8, 128], F32, "BC")
    GY = ps([128, 512], F32, "GY")
    SN = ps([128, 512], F32, "SN")
    PJf = PJ[:].rearrange("p s n -> p (s n)")
    drv = PJf.rearrange("p (r q n) -> p r q n", r=4, q=2)
    xs_tok = M1[:, 0:512].rearrange("p (c n) -> p c n", c=4)
    Btok_ps = M1[:, 512:768].rearrange("p (g n) -> p g n", g=2)
    csdt_ps = M2[:, 0:16]
    ss_ps = M2[:, 128:384].rearrange("p (g n) -> p g n", g=2)
    gate_ps = M2[:, 384:512]
    G_ps = GY[:, 0:256].rearrange("p (g n) -> p g n", g=2)
    y_ps = GY[:, 256:384]
    y5_ps = GY[:, 384:512]

    R = {}
    alias = {"csdt_ps": "M2", "ss": "M2", "gate": "M2", "xs_tok": "M1", "Btok_ps": "M1", "G": "GY", "y": "GY", "y5": "GY"}
    psum_names = ("M1", "M2", "GY", "BC", "SN")

    def r(name):
        name = alias.get(name, name)
        if name not in R:
            R[name] = Reg(name, name in psum_names)
        return R[name]

    RWin, RWout = regs("Win", KC), regs("Wout", KC)
    RX, Rh = regs("X", KC), regs("h", KC)
    Ry = regs("y", 8)
    Rpj = regs("pj", 2, True)
    R["pj0"], R["pj1"] = Rpj
    for nk_, tn_ in zip(("bre", "bim", "cre", "cim", "NBre", "NBim", "n1", "n2"), range(1, 9)):
        R[nk_] = r("t%d" % tn_)

    sp_loads = [("gain", gain), ("convw", convw), ("convb", convb), ("dtb", dtb), ("alog", alog), ("dskip", dskip),
                ("ngain", ngain), ("d5", d5), ("bglu", bglu), ("ident", ident_f), ("are", sm["are"]), ("aim", sm["aim"]),
                ("ldt", sm["ldt"]), ("bre", nat["bre"]), ("bim", nat["bim"]), ("cre", nat["cre"]), ("cim", nat["cim"])]
    for k, tgt in sp_loads:
        P.D("pool", writes=[r(k)], out=tgt[:], in_=dd[k])
    xs_v = xsrc.rearrange("(kc p) t -> p kc t", p=128)
    xd_v = xdst.rearrange("(kc p) t -> p kc t", p=128)
    P.D("pool", reads=aslist(Rxsrc[0]), writes=RX, out=X[:], in_=xs_v[:, :, 0:NTE])
    win_v = dd["win"].rearrange("(kc p) f -> p kc f", p=128)
    wout_v = dd["wout"].rearrange("(kc p) f -> p kc f", p=128)
    for kc in range(KC):
        P.D("pool", writes=[RWin[kc]], out=Win[:, kc, :], in_=win_v[:, kc, :])
    for kc in range(KC):
        P.D("pool", writes=[RWout[kc]], out=Wout[:, kc, :], in_=wout_v[:, kc, :])
    P.D("pool", writes=[r("Wglu")], out=Wglu[:], in_=dd["wglu"].rearrange("(c p) o -> p c o", p=128))

    I = P.I
    I("dve", "tensor_copy", reads=[r("ident")], writes=[r("identb")], out=ident[:], in_=ident_f[:])
    I("act", "activation", reads=[r("alog")], writes=[r("aneg")], out=aneg[:], in_=alog[:], func=AF.Exp)
    I("dve", "tensor_scalar", reads=[r("aneg")], writes=[r("aneg")], out=aneg[:], in0=aneg[:], scalar1=-1.0, scalar2=None,
      op0=ALU.mult)
    for c8 in range(8):
        for k in range(4):
            I("dve", "tensor_scalar", reads=[r("ident"), r("convw")], writes=[r("diagw")], out=diagw[:, c8, k, :],
              in0=ident_f[:], scalar1=convw[:, c8, k:k + 1], scalar2=None, op0=ALU.mult)
    for c in range(4):
        I("dve", "tensor_scalar", reads=[r("ident"), r("dskip")], writes=[r("diagD")], out=diagD[:, c, :], in0=ident_f[:],
          scalar1=dskip[:, c:c + 1], scalar2=None, op0=ALU.mult)
        I("dve", "tensor_scalar", reads=[r("ident"), r("d5")], writes=[r("diagD5")], out=diagD5[:, c, :], in0=ident_f[:],
          scalar1=d5[:, c:c + 1], scalar2=None, op0=ALU.mult)
    I("dve", "tensor_copy", reads=[r("ident")], writes=[r("selH")], out=selH[:], in_=bview(ident_f[0:8, 0:8], 2, [8, 8, 128]))
    I("dve", "tensor_tensor_scan", reads=[r("ident"), C.R_ones], writes=[r("tri")], out=tri[:], data0=C.ones_f[:],
      data1=ident_f[:], initial=0.0, op0=ALU.mult, op1=ALU.add)
    I("dve", "tensor_scalar", reads=[r("tri")], writes=[r("maskneg")], out=maskneg[:], in0=tri[:], scalar1=1.0, scalar2=1e30,
      op0=ALU.subtract, op1=ALU.mult)
    I("dve", "memset", writes=[r("ST")], ap=ST[:], constant=0.0)
    I("dve", "memset", writes=[r("STb")], ap=STb[:], constant=0.0)
    I("dve", "memset", writes=[r("xbc%d" % c8) for c8 in range(8)], ap=xbc[:, :, 0:3], constant=0.0)
    I("dve", "memset", writes=[r("ire%d" % j) for j in range(4)] + [r("iim%d" % j) for j in range(4)], ap=Hp[:], constant=0.0)
    s = sm

    def tt(out, a, b, op, eng="dve", rd=(), wr=()):
        I(eng, "tensor_tensor", reads=[r(x) for x in rd], writes=[r(x) for x in wr], out=out, in0=a, in1=b, op=op)

    I("act", "activation", reads=[r("ldt")], writes=[r("dt")], out=s["dt"][:], in_=s["ldt"][:], func=AF.Exp)
    tt(s["dar"][:], s["dt"][:], s["are"][:], ALU.mult, rd=["dt", "are"], wr=["dar"])
    I("act", "activation", reads=[r("dar")], writes=[r("mag")], out=s["mag"][:], in_=s["dar"][:], func=AF.Exp)
    tt(s["th"][:], s["dt"][:], s["aim"][:], ALU.mult, rd=["dt", "aim"], wr=["th"])
    I("act", "activation", reads=[r("th")], writes=[r("s")], out=s["s"][:], in_=s["th"][:], func=AF.Sin, scale=1.0 / 32)
    I("act", "activation", reads=[r("th")], writes=[r("c")], out=s["c"][:], in_=s["th"][:], func=AF.Sin, scale=1.0 / 32,
      bias=float(np.pi / 2))

    def double_cs():
        tt(s["ta"][:], s["c"][:], s["c"][:], ALU.mult, rd=["c"], wr=["ta"])
        tt(s["tb"][:], s["s"][:], s["s"][:], ALU.mult, rd=["s"], wr=["tb"])
        I("dve", "scalar_tensor_tensor", reads=[r("c"), r("s")], writes=[r("s2")], out=s["s2"][:], in0=s["c"][:], scalar=2.0,
          in1=s["s"][:], op0=ALU.mult, op1=ALU.mult)
        tt(s["c"][:], s["ta"][:], s["tb"][:], ALU.subtract, rd=["ta", "tb"], wr=["c"])
        I("dve", "tensor_copy", reads=[r("s2")], writes=[r("s")], out=s["s"][:], in_=s["s2"][:])

    for _ in range(5):
        double_cs()
    for k in range(7):
        I("dve", "tensor_copy", reads=[r("c")], writes=[r("cm%d" % k)], out=cm[k][:], in_=s["c"][:])
        I("dve", "tensor_copy", reads=[r("s")], writes=[r("sm%d" % k)], out=smm[k][:], in_=s["s"][:])
        if k < 6:
            double_cs()
    c1, s1 = cm[0], smm[0]
    tt(s["abr"][:], s["mag"][:], c1[:], ALU.mult, rd=["mag", "cm0"], wr=["abr"])
    tt(s["abi"][:], s["mag"][:], s1[:], ALU.mult, rd=["mag", "sm0"], wr=["abi"])
    I("dve", "tensor_scalar", reads=[r("abr")], writes=[r("abr1")], out=s["abr1"][:], in0=s["abr"][:], scalar1=-1.0,
      scalar2=None, op0=ALU.add)
    tt(s["ta"][:], s["are"][:], s["are"][:], ALU.mult, rd=["are"], wr=["ta"])
    tt(s["tb"][:], s["aim"][:], s["aim"][:], ALU.mult, rd=["aim"], wr=["tb"])
    tt(s["den"][:], s["ta"][:], s["tb"][:], ALU.add, rd=["ta", "tb"], wr=["den"])
    I("dve", "reciprocal", reads=[r("den")], writes=[r("rden")], out=s["rden"][:], in_=s["den"][:])
    tt(s["u1"][:], s["abr1"][:], s["are"][:], ALU.mult, rd=["abr1", "are"], wr=["u1"])
    tt(s["u2"][:], s["abi"][:], s["aim"][:], ALU.mult, rd=["abi", "aim"], wr=["u2"])
    tt(s["u1"][:], s["u1"][:], s["u2"][:], ALU.add, rd=["u1", "u2"], wr=["u1"])
    tt(s["fre"][:], s["u1"][:], s["rden"][:], ALU.mult, rd=["u1", "rden"], wr=["fre"])
    tt(s["u1"][:], s["abi"][:], s["are"][:], ALU.mult, rd=["abi", "are"], wr=["u1"])
    tt(s["u2"][:], s["abr1"][:], s["aim"][:], ALU.mult, rd=["abr1", "aim"], wr=["u2"])
    tt(s["u1"][:], s["u1"][:], s["u2"][:], ALU.subtract, rd=["u1", "u2"], wr=["u1"])
    tt(s["fim"][:], s["u1"][:], s["rden"][:], ALU.mult, rd=["u1", "rden"], wr=["fim"])
    fre_b = bview(s["fre"][:], 2, [128, 16, 16])
    fim_b = bview(s["fim"][:], 2, [128, 16, 16])
    tt(nat["n1"][:], nat["bre"][:], fre_b, ALU.mult, rd=["bre", "fre"], wr=["n1"])
    tt(nat["n2"][:], nat["bim"][:], fim_b, ALU.mult, rd=["bim", "fim"], wr=["n2"])
    tt(nat["NBre"][:], nat["n1"][:], nat["n2"][:], ALU.subtract, rd=["n1", "n2"], wr=["NBre"])
    tt(nat["n1"][:], nat["bim"][:], fre_b, ALU.mult, rd=["bim", "fre"], wr=["n1"])
    tt(nat["n2"][:], nat["bre"][:], fim_b, ALU.mult, rd=["bre", "fim"], wr=["n2"])
    tt(nat["NBim"][:], nat["n1"][:], nat["n2"][:], ALU.add, rd=["n1", "n2"], wr=["NBim"])
    I("dve", "memset", writes=[r("maskZ")], ap=maskZ[:], constant=0.0)
    for pr in range(16):
        rr = pr % 4
        I("dve", "memset", writes=[r("maskZ")], ap=maskZ[0:64, pr, 2 * rr:2 * rr + 1], constant=1.0)
        I("dve", "memset", writes=[r("maskZ")], ap=maskZ[64:128, pr, 2 * rr + 1:2 * rr + 2], constant=1.0)
    mz_b = bview(maskZ[:], 3, [128, 16, 8, 16])
    LOv = LO[:].rearrange("p a q (g i) -> p a q g i", g=8)
    LDv = LD[:]
    for q, key in enumerate(("NBre", "NBim")):
        tt(Zall, bview(nat[key][:], 2, [128, 16, 8, 16]), mz_b, ALU.mult, rd=[key, "maskZ"], wr=["Zall"])
        for j in range(4):
            for k4 in range(4):
                I("pe", "transpose", reads=[r("Zall"), r("ident")], writes=[r("BC")], out=BC[:, k4, :],
                  in_=Zall[:, 4 * j + k4, :, :].rearrange("p g i -> p (g i)"), identity=ident_f[:])
            I("act", "activation", reads=[r("BC")], writes=[r("LD")], out=LDv[:, 4 * j:4 * j + 4, q, :], in_=BC[:, 0:4, :],
              func=AF.Copy)
    tt(LOv[:, :, 0, :, :], bview(nat["cre"][:], 2, [128, 16, 8, 16]), mz_b, ALU.mult, rd=["cre", "maskZ"], wr=["LO"])
    I("dve", "tensor_scalar", reads=[r("cim")], writes=[r("n1")], out=nat["n1"][:], in0=nat["cim"][:], scalar1=-1.0, scalar2=None,
      op0=ALU.mult)
    tt(LOv[:, :, 1, :, :], bview(nat["n1"][:], 2, [128, 16, 8, 16]), mz_b, ALU.mult, rd=["n1", "maskZ"], wr=["LO"])
    I("dve", "tensor_copy", reads=[r("cm0")], writes=[r("Ec")], out=Ec[:, :, 0:1], in_=bview(cm[0][:], 2, [128, 16, 1]))
    I("dve", "tensor_copy", reads=[r("sm0")], writes=[r("Es")], out=Es[:, :, 0:1], in_=bview(smm[0][:], 2, [128, 16, 1]))
    TLN = ["tmpL%d" % j for j in range(8)]
    tA = tmpL[:].rearrange("p h l -> p (h l)")
    tB = H[:].rearrange("p a q l -> p (a q l)")
    for k in range(7):
        m = 1 << k
        cb = bview(cm[k][:], 2, [128, 16, m])
        sbv = bview(smm[k][:], 2, [128, 16, m])
        vA = tA[:, 0:16 * m].rearrange("p (a m) -> p a m", a=16)
        vB = tB[:, 0:16 * m].rearrange("p (a m) -> p a m", a=16)
        tt(vA, Ec[:, :, 0:m], cb, ALU.mult, rd=["Ec", "cm%d" % k], wr=TLN)
        tt(vB, Es[:, :, 0:m], sbv, ALU.mult, rd=["Es", "sm%d" % k], wr=["H"])
        tt(Ec[:, :, m:2 * m], vA, vB, ALU.subtract, rd=TLN + ["H"], wr=["Ec"])
        tt(vA, Es[:, :, 0:m], cb, ALU.mult, rd=["Es", "cm%d" % k], wr=TLN)
        tt(vB, Ec[:, :, 0:m], sbv, ALU.mult, rd=["Ec", "sm%d" % k], wr=["H"])
        tt(Es[:, :, m:2 * m], vA, vB, ALU.add, rd=TLN + ["H"], wr=["Es"])

    xs_v = xsrc.rearrange("(kc p) t -> p kc t", p=128)
    xd_v = xdst.rearrange("(kc p) t -> p kc t", p=128)
    pjn = [0]

    BCf = BC[:].rearrange("p h l -> p (h l)")
    pjA = [PJ[:, 0, 0:NTE], PJ[:, 1, 0:NTE], BCf[:, 0:NTE], SN[:, 0:NTE], GY[:, 0:NTE]]
    RpjA = [Rpj[0], Rpj[1], r("BC"), r("SN"), r("GY")]

    def nextpj():
        pjn[0] += 1
        return pjn[0] % len(pjA)

    def proj(col0, ncols, rhs_slice=None):
        sl = nextpj()
        for kc in range(KC):
            I("pe", "matmul", reads=[RWin[kc], Rh[kc]], writes=[RpjA[sl]], out=pjA[sl][0:ncols, :],
              lhsT=Win[:, kc, col0:col0 + ncols], rhs=hb[:, kc, :], start=(kc == 0), stop=(kc == KC - 1))
        return sl

    for t in range(ntiles):
        tsl = slice(t * NTE, (t + 1) * NTE)
        if t > 0:
            P.D("sp", reads=aslist(Rxsrc[t]), writes=RX, out=X[:], in_=xs_v[:, :, tsl])
        sl0 = nextpj()
        for kc in range(KC):
            s2 = kc % 2
            I("act", "activation", reads=[RX[kc]], writes=[r("sq%d" % s2)], out=sq[s2][:], in_=X[:, kc, :], func=AF.Square)
            I("pe", "matmul", reads=[r("sq%d" % s2), C.R_ones], writes=[RpjA[sl0]], out=pjA[sl0], lhsT=C.ones_f[:],
              rhs=sq[s2][:], start=(kc == 0), stop=(kc == KC - 1))
        I("act", "activation", reads=[RpjA[sl0]], writes=[r("rt")], out=rt[:], in_=pjA[sl0], func=AF.Ln,
          scale=1.0 / D, bias=EPS)
        I("act", "activation", reads=[r("rt")], writes=[r("rstd")], out=rstd[:], in_=rt[:], func=AF.Exp, scale=-0.5)
        for kc in range(KC):
            I("dve", "scalar_tensor_tensor", reads=[RX[kc], r("gain"), r("rstd")], writes=[Rh[kc]], out=hb[:, kc, :],
              in0=X[:, kc, :], scalar=gain[:, kc:kc + 1], in1=rstd[:], op0=ALU.mult, op1=ALU.mult)
        for c in range(4):
            sl = proj(c * 128, 128)
            I("act", "activation", reads=[RpjA[sl]], writes=[r("zs%d" % c)], out=zs[:, c, :], in_=pjA[sl], func=AF.Silu)
        for c8 in range(8):
            sl = proj(512 + c8 * 128, 128)
            I("act", "activation", reads=[RpjA[sl]], writes=[r("xbc%d" % c8)], out=xbc[:, c8, 3:3 + NTE], in_=pjA[sl],
              func=AF.Copy)
            sl = nextpj()
            for k in range(4):
                I("pe", "matmul", reads=[r("diagw"), r("xbc%d" % c8)], writes=[RpjA[sl]], out=pjA[sl],
                  lhsT=diagw[:, c8, k, :], rhs=xbc[:, c8, k:k + NTE], start=(k == 0), stop=(k == 3))
            if c8 < 4:
                dst, rn = xsT[:, c8, :], "xsT%d" % c8
            elif c8 < 6:
                dst, rn = BT[:, c8 - 4, :], "BT%d" % (c8 - 4)
            else:
                dst, rn = CT[:, c8 - 6, :], "CT%d" % (c8 - 6)
            I("act", "activation", reads=[RpjA[sl], r("convb")], writes=[r(rn)], out=dst, in_=pjA[sl], func=AF.Silu,
              bias=convb[:, c8:c8 + 1])
            I("pool", "tensor_copy", reads=[r("xbc%d" % c8)], writes=[r("xbc%d" % c8)], out=xbc[:, c8, 0:3],
              in_=xbc[:, c8, NTE:NTE + 3])
        sl = proj(1536, 8)
        I("act", "activation", reads=[RpjA[sl], r("dtb")], writes=[r("dte1")], out=dte1[:], in_=pjA[sl][0:8, :], func=AF.Exp,
          bias=dtb[:, 0:1])
        I("act", "activation", reads=[r("dte1")], writes=[r("dtsp")], out=dtsp[:], in_=dte1[:], func=AF.Ln, bias=1.0)
        I("dve", "tensor_scalar", reads=[r("dtsp"), r("aneg")], writes=[r("dA")], out=dA[:], in0=dtsp[:], scalar1=aneg[:, 0:1],
          scalar2=None, op0=ALU.mult)
        for c in range(4):
            sl = proj(1544 + c * 128, 128)
            I("act", "activation", reads=[RpjA[sl]], writes=[r("uT%d" % c)], out=uT[:, c, :], in_=pjA[sl], func=AF.Copy)

        tail_prev = None
        for qc in range(NTE // QC):
            lr = slice(qc * QC, (qc + 1) * QC)

            def seg0():
                I("dve", "tensor_tensor_scan", reads=[r("dA"), C.R_ones], writes=[r("cs")], out=cs[:, lr],
                  data0=C.ones_f[0:8, :], data1=dA[:, lr], initial=0.0, op0=ALU.mult, op1=ALU.add)
                I("pe", "transpose", reads=[r("cs"), r("ident")], writes=[r("csdt_ps")], out=csdt_ps[:, 0:8], in_=cs[:, lr],
                  identity=ident_f[0:8, 0:8])
                I("pe", "transpose", reads=[r("dtsp"), r("ident")], writes=[r("csdt_ps")], out=csdt_ps[:, 8:16], in_=dtsp[:, lr],
                  identity=ident_f[0:8, 0:8])
                I("act", "activation", reads=[r("csdt_ps")], writes=[r("csdt")], out=csdt[:], in_=csdt_ps, func=AF.Copy)
                for hh in range(8):
                    I("pe", "matmul", reads=[r("selH"), r("cs")], writes=[r("BC")], out=BC[:, hh, :], lhsT=selH[:, hh, :],
                      rhs=cs[:, lr], start=True, stop=True)
                for hh in range(8):
                    I("dve", "scalar_tensor_tensor", reads=[r("BC"), r("csdt"), r("maskneg")], writes=[r("tmpL%d" % hh)],
                      out=tmpL[:, hh, :], in0=BC[:, hh, :], scalar=csdt[:, hh:hh + 1], in1=maskneg[:], op0=ALU.subtract,
                      op1=ALU.add)
                I("dve", "tensor_tensor", reads=[r("BC"), r("csdt")], writes=[r("d1")], out=d1[:], in0=BC[:, :, QC - 1],
                  in1=csdt[:, 0:8], op=ALU.subtract)
                I("act", "activation", reads=[r("BC")], writes=[r("ecs")], out=ecs[:], in_=BC[:], func=AF.Exp)
                I("act", "activation", reads=[r("BC")], writes=[r("eend")], out=eend[:], in_=BC[:, :, QC - 1], func=AF.Exp)
                I("act", "activation", reads=[], writes=[r("tmpL%d" % j) for j in range(8)], out=tmpL[:], in_=tmpL[:], func=AF.Exp)
                I("act", "activation", reads=[r("d1")], writes=[r("dte")], out=dte[:], in_=d1[:], func=AF.Exp)
                for g in range(2):
                    I("pe", "matmul", reads=[r("BT%d" % g), r("CT%d" % g)], writes=[r("G")], out=G_ps[:, g, :],
                      lhsT=BT[:, g, lr], rhs=CT[:, g, lr], start=True, stop=True)

            def seg1():
                I("dve", "tensor_tensor", reads=[r("tmpL%d" % j) for j in range(8)] + [r("G")], writes=[r("Mt")],
                  out=Mt[:].rearrange("p (g j) l -> p g j l", g=2), in0=tmpL[:].rearrange("p (g j) l -> p g j l", g=2),
                  in1=bview(G_ps, 2, [128, 2, 4, 128]), op=ALU.mult)
                I("pool", "tensor_tensor", reads=[r("CT0"), r("CT1"), r("ecs")], writes=[r("Cs")],
                  out=Cs[:].rearrange("p (g j) l -> p g j l", g=2), in0=ecs[:].rearrange("p (g j) l -> p g j l", g=2),
                  in1=bview(CT[:, :, lr], 2, [128, 2, 4, 128]), op=ALU.mult)
                for c in range(4):
                    I("pe", "transpose", reads=[r("xsT%d" % c), r("identb")], writes=[r("xs_tok")], out=xs_tok[:, c, :],
                      in_=xsT[:, c, lr], identity=ident[:])
                for g in range(2):
                    I("pe", "transpose", reads=[r("BT%d" % g), r("identb")], writes=[r("Btok_ps")], out=Btok_ps[:, g, :],
                      in_=BT[:, g, lr], identity=ident[:])
                I("dve", "tensor_tensor", reads=[r("xs_tok"), r("csdt")], writes=[r("xdt")],
                  out=xdt[:].rearrange("p (h q) -> p h q", h=8), in0=M1[:, 0:512].rearrange("p (h q) -> p h q", h=8),
                  in1=bview(csdt[:, 8:16], 2, [128, 8, 64]), op=ALU.mult)
                I("act", "activation", reads=[r("Btok_ps")], writes=[r("Btok")], out=Btok[:], in_=Btok_ps, func=AF.Copy)
                I("pool", "tensor_tensor", reads=[r("xdt"), r("dte")], writes=[r("xw")],
                  out=xw[:].rearrange("p (h q) -> p h q", h=8), in0=xdt[:].rearrange("p (h q) -> p h q", h=8),
                  in1=bview(dte[:], 2, [128, 8, 64]), op=ALU.mult)

            def seg_y(g):
                for c in (2 * g, 2 * g + 1):
                    for e in range(2):
                        hh = 2 * c + e
                        ysl = y_ps[e * 64:(e + 1) * 64, :]
                        I("pe", "matmul", reads=[r("xdt"), r("Mt")], writes=[r("y")], out=ysl,
                          lhsT=xdt[:, hh * 64:(hh + 1) * 64], rhs=Mt[:, hh, :], start=True, stop=False)
                        I("pe", "matmul", reads=[r("STb"), r("Cs")], writes=[r("y")], out=ysl,
                          lhsT=STb[:, hh * 64:(hh + 1) * 64], rhs=Cs[:, hh, :], start=False, stop=False)
                        I("pe", "matmul", reads=[r("diagD"), r("xsT%d" % c)], writes=[r("y")], out=ysl,
                          lhsT=diagD[:, c, e * 64:(e + 1) * 64], rhs=xsT[:, c, lr], start=False, stop=True)
                    I("dve", "tensor_tensor", reads=[r("y"), r("zs%d" % c)], writes=[r("yg%d" % c)], out=yg[:, c, :], in0=y_ps,
                      in1=zs[:, c, lr], op=ALU.mult)
                    I("act", "activation", reads=[r("yg%d" % c)], writes=[r("sqy%d" % (c % 2))], out=sqy[c % 2][:],
                      in_=yg[:, c, :], func=AF.Square)
                    I("pe", "matmul", reads=[r("sqy%d" % (c % 2)), C.R_ones], writes=[r("ss")], out=ss_ps[:, g, :],
                      lhsT=C.ones_f[:], rhs=sqy[c % 2][:], start=(c % 2 == 0), stop=(c % 2 == 1))
                I("act", "activation", reads=[r("ss")], writes=[r("rtg")], out=rtg[:], in_=ss_ps[:, g, :], func=AF.Ln,
                  scale=1.0 / 256, bias=EPS)
                I("act", "activation", reads=[r("rtg")], writes=[r("rstdg")], out=rstdg[:], in_=rtg[:], func=AF.Exp, scale=-0.5)
                for c2 in (2 * g, 2 * g + 1):
                    I("dve", "scalar_tensor_tensor", reads=[r("yg%d" % c2), r("ngain"), r("rstdg")], writes=[Ry[c2]],
                      out=yT[:, c2, lr], in0=yg[:, c2, :], scalar=ngain[:, c2:c2 + 1], in1=rstdg[:], op0=ALU.mult,
                      op1=ALU.mult)

            def seg_state():
                for g in range(2):
                    I("pe", "matmul", reads=[r("Btok"), r("xw")], writes=[r("SN")], out=SN[:, g * 256:(g + 1) * 256],
                      lhsT=Btok[:, g, :], rhs=xw[:, g * 256:(g + 1) * 256], start=True, stop=True)
                I("pool", "tensor_tensor", reads=[r("ST"), r("eend")], writes=[r("ST")],
                  out=ST[:].rearrange("p (h q) -> p h q", h=8), in0=ST[:].rearrange("p (h q) -> p h q", h=8),
                  in1=bview(eend[:], 2, [128, 8, 64]), op=ALU.mult)
                I("dve", "tensor_tensor", reads=[r("SN"), r("ST")], writes=[r("ST")], out=ST[:], in0=SN[:], in1=ST[:], op=ALU.add)
                I("act", "activation", reads=[r("ST")], writes=[r("STb")], out=STb[:], in_=ST[:], func=AF.Copy)

            zc = qc % 2

            def s5A(T4, lr=lr):
                p4 = slice(4 * T4, 4 * T4 + 4)
                xz = xmb[T4 % 2]
                for rr in range(4):
                    for q in range(2):
                        I("pe", "matmul", reads=[r("LD"), r("uT%d" % T4)], writes=[Rpj[0], Rpj[1]], out=drv[:, rr, q, :],
                          lhsT=LD[:, 4 * T4 + rr, q, :], rhs=uT[:, T4, lr], start=True, stop=True)
                dre, dim_ = drv[:, :, 0, :], drv[:, :, 1, :]
                Ec4, Es4 = Ec[:, p4, :], Es[:, p4, :]
                tt(t1[:], dre, Ec4, ALU.mult, rd=["pj0", "pj1", "Ec"], wr=["t1"])
                tt(t2[:], dim_, Es4, ALU.mult, rd=["pj0", "pj1", "Es"], wr=["t2"])
                tt(t3[:], dim_, Ec4, ALU.mult, rd=["pj0", "pj1", "Ec"], wr=["t3"])
                tt(t4[:], dre, Es4, ALU.mult, rd=["pj0", "pj1", "Es"], wr=["t4"])
                tt(xz[:, :, 0, :], t1[:], t2[:], ALU.add, eng="pool", rd=["t1", "t2"], wr=["xm%d" % (T4 % 2), "Zall"])
                tt(xz[:, :, 1, :], t3[:], t4[:], ALU.subtract, eng="pool", rd=["t3", "t4"], wr=["xm%d" % (T4 % 2), "Zall"])

            def s5B(T4):
                p4 = slice(4 * T4, 4 * T4 + 4)
                xz = xmb[T4 % 2]
                hz = T4 % 2
                Ec4, Es4 = Ec[:, p4, :], Es[:, p4, :]
                for rr in range(4):
                    pr = 4 * T4 + rr
                    mg = s["mag"][:, pr:pr + 1].broadcast_to([128, QC])
                    I("dve", "tensor_tensor_scan", reads=[r("xm%d" % (T4 % 2)), r("mag"), r("ire%d" % T4)],
                      writes=[r("ht%d_0" % rr), r("Zall")], out=ht[:, rr, 0, :], data0=mg, data1=xz[:, rr, 0, :],
                      initial=Hp[:, pr, 0:1], op0=ALU.mult, op1=ALU.add)
                    I("dve", "tensor_tensor_scan", reads=[r("xm%d" % (T4 % 2)), r("mag"), r("iim%d" % T4)],
                      writes=[r("ht%d_1" % rr), r("Zall")], out=ht[:, rr, 1, :], data0=mg, data1=xz[:, rr, 1, :],
                      initial=Hp[:, pr, 1:2], op0=ALU.mult, op1=ALU.add)
                tt(t5[:], ht[:, :, 0, :], Ec4, ALU.mult, rd=["ht%d_0" % j for j in range(4)] + ["Ec"], wr=["t5"])
                tt(t6[:], ht[:, :, 1, :], Es4, ALU.mult, rd=["ht%d_1" % j for j in range(4)] + ["Es"], wr=["t6"])
                tt(H[:, :, 0, :], t5[:], t6[:], ALU.subtract, eng="pool", rd=["t5", "t6"], wr=["H"])
                tt(t7[:], ht[:, :, 1, :], Ec4, ALU.mult, eng="pool", rd=["ht%d_1" % j for j in range(4)] + ["Ec"], wr=["t7"])
                tt(t8[:], ht[:, :, 0, :], Es4, ALU.mult, eng="pool", rd=["ht%d_0" % j for j in range(4)] + ["Es"], wr=["t8"])
                tt(H[:, :, 1, :], t7[:], t8[:], ALU.add, eng="pool", rd=["t7", "t8"], wr=["H"])
                I("act", "activation", reads=[r("H")], writes=[r("Hb%d" % hz)], out=Hbb[hz][:], in_=H[:], func=AF.Copy)
                I("act", "activation", reads=[r("H")], writes=[r("ire%d" % T4), r("iim%d" % T4)], out=Hp[:, p4, :],
                  in_=H[:, :, :, QC - 1], func=AF.Copy)

            def s5C1(T4, lr=lr, zc=zc):
                hz = T4 % 2
                n = 0
                for rr in range(4):
                    for q in range(2):
                        I("pe", "matmul", reads=[r("LO"), r("Hb%d" % hz)], writes=[r("y5")], out=y5_ps,
                          lhsT=LO[:, 4 * T4 + rr, q, :], rhs=Hbb[hz][:, rr, q, :], start=(n == 0), stop=False)
                        n += 1
                I("pe", "matmul", reads=[r("diagD5"), r("uT%d" % T4)], writes=[r("y5")], out=y5_ps, lhsT=diagD5[:, T4, :],
                  rhs=uT[:, T4, lr], start=False, stop=True)
                I("act", "activation", reads=[r("y5")], writes=[r("y5s%d_%d" % (zc, T4))], out=y5sb[zc][:, T4, :], in_=y5_ps,
                  func=AF.Copy)

            def s5C2(T4, zc=zc):
                yv = y5sb[zc][:, T4, :]
                gz = T4 % 2
                I("act", "activation", reads=[r("y5s%d_%d" % (zc, T4))], writes=[r("ga")], out=ga[:], in_=yv, func=AF.Square,
                  scale=0.21145921496651535)
                I("dve", "scalar_tensor_tensor", reads=[r("ga"), r("y5s%d_%d" % (zc, T4))], writes=[r("gb")], out=gb[:], in0=ga[:],
                  scalar=1.0, in1=yv, op0=ALU.add, op1=ALU.mult)
                I("act", "activation", reads=[r("gb")], writes=[r("gth%d" % gz)], out=gthb[gz][:], in_=gb[:], func=AF.Tanh,
                  scale=0.7978845608028654)

            def s5C3(T4, zc=zc):
                yv = y5sb[zc][:, T4, :]
                gz = T4 % 2
                I("dve", "scalar_tensor_tensor", reads=[r("gth%d" % gz), r("y5s%d_%d" % (zc, T4))], writes=[r("ge%d_%d" % (zc, T4))],
                  out=geb[zc][:, T4, :], in0=gthb[gz][:], scalar=1.0, in1=yv, op0=ALU.add, op1=ALU.mult)

            def epilogue(lr=lr, zc=zc):
                for o in range(4):
                    for T4 in range(4):
                        I("pe", "matmul", reads=[r("Wglu"), r("ge%d_%d" % (zc, T4))], writes=[r("gate")], out=gate_ps,
                          lhsT=Wglu[:, T4, o * 128:(o + 1) * 128], rhs=geb[zc][:, T4, :], start=(T4 == 0), stop=(T4 == 3))
                    I("act", "activation", reads=[r("gate"), r("bglu")], writes=[r("sig")], out=sig[:], in_=gate_ps,
                      func=AF.Sigmoid, scale=0.5, bias=bglu[:, o:o + 1])
                    I("dve", "tensor_tensor", reads=[r("sig"), r("y5s%d_%d" % (zc, o))], writes=[Ry[4 + o]], out=yT[:, 4 + o, lr],
                      in0=y5sb[zc][:, o, :], in1=sig[:], op=ALU.mult)

            s5A(0)
            seg0()
            s5A(1)
            if tail_prev is not None:
                tail_prev()
            s5B(0)
            seg1()
            s5A(2)
            s5B(1)
            s5C1(0)
            seg_y(0)
            s5A(3)
            s5B(2)
            s5C1(1)
            s5C2(0)
            seg_y(1)
            s5B(3)
            s5C1(2)
            s5C2(1)
            s5C3(0)
            seg_state()
            s5C1(3)
            s5C2(2)
            s5C3(1)

            def tail(c2=s5C2, c3=s5C3, ep=epilogue):
                c2(3)
                c3(2)
                c3(3)
                ep()
            tail_prev = tail
        tail_prev()
        tail_prev = None
        for dc in range(KC):
            sl = nextpj()
            for c in range(8):
                I("pe", "matmul", reads=[RWout[c], Ry[c]], writes=[RpjA[sl]], out=pjA[sl],
                  lhsT=Wout[:, c, dc * 128:(dc + 1) * 128], rhs=yT[:, c, :], start=(c == 0), stop=(c == 7))
            I("dve", "tensor_tensor", reads=[RpjA[sl], RX[dc]], writes=[RX[dc]], out=X[:, dc, :], in0=pjA[sl],
              in1=X[:, dc, :], op=ALU.add)
        P.D("sp", reads=RX, writes=aslist(Rxdst[t]), out=xd_v[:, :, tsl], in_=X[:])


DEPTH = 4
SEQ = 4096
NREG = SEQ // 256


def _bias_tiles(rel_bias):
    l = np.arange(128)[:, None]
    k = np.arange(640)[None, :]
    idx = np.clip(512 + l - k, -128, 128) + 128
    return np.ascontiguousarray(np.transpose(rel_bias[:, idx], (1, 0, 2)))


def _col(v):
    n = v.shape[0] // 128
    return np.ascontiguousarray(v.reshape(n, 128).T)


def _even_inputs(inp, i, layer):
    nat = lambda a: np.ascontiguousarray(a.reshape(16, 2, 64).transpose(1, 2, 0).reshape(128, 16))
    return {
        "win": inp['even_w_in'][i], "wout": inp['even_w_out'][i], "wglu": inp['s5_w_glu'][i],
        "gain": _col(inp['mix_norm'][layer]),
        "convw": np.ascontiguousarray(inp['ssd_conv_w'][i].reshape(4, 8, 128).transpose(2, 1, 0)),
        "convb": _col(inp['ssd_conv_b'][i]),
        "dtb": inp['ssd_dt_bias'][i].reshape(8, 1).copy(), "alog": inp['ssd_a_log'][i].reshape(8, 1).copy(),
        "dskip": _col(np.repeat(inp['ssd_d'][i], 64)),
        "ngain": _col(inp['ssd_norm'][i]),
        "d5": _col(inp['s5_d'][i].reshape(-1)),
        "bglu": _col(inp['s5_b_glu'][i]),
        "are": nat(inp['s5_a_re'][i]), "aim": nat(inp['s5_a_im'][i]),
        "ldt": np.ascontiguousarray(
            np.repeat(inp['s5_log_dt'][i].reshape(16, 2, 1), 64, axis=2).transpose(1, 2, 0).reshape(128, 16)),
        "bre": np.ascontiguousarray(inp['s5_b_re'][i].reshape(16, 2, 64, 16).transpose(1, 2, 0, 3).reshape(128, 16, 16)),
        "bim": np.ascontiguousarray(inp['s5_b_im'][i].reshape(16, 2, 64, 16).transpose(1, 2, 0, 3).reshape(128, 16, 16)),
        "cre": np.ascontiguousarray(inp['s5_c_re'][i].reshape(16, 2, 16, 64).transpose(1, 3, 0, 2).reshape(128, 16, 16)),
        "cim": np.ascontiguousarray(inp['s5_c_im'][i].reshape(16, 2, 16, 64).transpose(1, 3, 0, 2).reshape(128, 16, 16)),
    }


def _odd_inputs(inp, i, layer):
    return {
        "win": inp['odd_w_in'][i], "wout": inp['odd_w_out'][i], "pw": inp['pool_w'][i],
        "bt": _bias_tiles(inp['attn_rel_bias'][i]),
        "qg": np.tile(inp['attn_q_norm'][i], 2).reshape(128, 1).copy(),
        "kg": np.tile(inp['attn_k_norm'][i], 2).reshape(128, 1).copy(),
        "psc": _col(inp['pool_scale'][i]),
        "gain": _col(inp['mix_norm'][layer]),
    }


def host_layout(inp):
    m = {"ident": np.eye(128, dtype=np.float32)}
    f1 = (inp['ffn1_w_gate'], inp['ffn1_w_up'], inp['ffn1_w_down'], inp['ffn1_norm'])
    f2 = (inp['ffn2_w_gate'], inp['ffn2_w_up'], inp['ffn2_w_down'], inp['ffn2_norm'])
    for l in range(DEPTH):
        for w, (wg, wu, wd, gn) in ((1, f1), (2, f2)):
            m["f%d_%d_wg" % (w, l)] = wg[l]
            m["f%d_%d_wu" % (w, l)] = wu[l]
            m["f%d_%d_wd" % (w, l)] = wd[l]
            m["f%d_%d_gn" % (w, l)] = _col(gn[l])
        d = _even_inputs(inp, l // 2, l) if l % 2 == 0 else _odd_inputs(inp, l // 2, l)
        for k, v in d.items():
            m["m%d_%s" % (l, k)] = v
    return {k: np.ascontiguousarray(v, dtype=np.float32) for k, v in m.items()}


def build_program(shapes, phases=None, seq=SEQ):
    nc = bass.Bass("TRN2", target_bir_lowering=False)
    di = lambda n, s: nc.dram_tensor(n, list(s), F32, kind="ExternalInput").ap()
    xin = di("xT", [D, seq])
    dd = {k: di(k, s) for k, s in shapes.items()}
    yout = nc.dram_tensor("yT", [D, seq], F32, kind="ExternalOutput").ap()
    xs = nc.dram_tensor("xscratch", [D, seq], F32, kind="Internal").ap()
    if phases is None:
        phases = [(l, k) for l in range(DEPTH) for k in ("f1", "mix", "f2")]
    nreg = seq // 256
    pair = lambda R_: [[R_[2 * t], R_[2 * t + 1]] for t in range(len(R_) // 2)]
    with ExitStack() as es:
        P = Prog(nc, es)
        C = Ctx(nc, P)
        C.es = es
        emit_consts(C)
        Rin, Rsc, Rout = regs("xin", nreg), regs("xsc", nreg), regs("xout", nreg)
        with nc.Block() as block:
            for pi, (l, kind) in enumerate(phases):
                src, Rs = (xin, Rin) if pi == 0 else (xs, Rsc)
                dst, Rd = (yout, Rout) if pi == len(phases) - 1 else (xs, Rsc)
                with ExitStack() as pes:
                    C.es = pes
                    if kind in ("f1", "f2"):
                        w = 1 if kind == "f1" else 2
                        pre = "f%d_%d_" % (w, l)
                        emit_ffn(C, src, dst, pair(Rs), pair(Rd), dd[pre + "wg"], dd[pre + "wu"], dd[pre + "wd"], dd[pre + "gn"], seq // NT)
                    elif l % 2 == 0:
                        sub = {k[len("m%d_" % l):]: v for k, v in dd.items() if k.startswith("m%d_" % l)}
                        sub["ident"] = dd["ident"]
                        emit_even(C, src, dst, Rs, Rd, sub, seq // NTE)
                    else:
                        pre = "m%d_" % l
                        emit_odd(C, src, dst, pair(Rs), pair(Rd), dd[pre + "win"], dd[pre + "wout"], dd[pre + "pw"], dd[pre + "bt"],
                                 dd[pre + "qg"], dd[pre + "kg"], dd[pre + "psc"], dd[pre + "gain"], dd["ident"], seq // NT)
                    P.all_barrier()
                    P.reorder = (kind == "mix" and l % 2 == 0)
                    P.flush(block)
    return nc, P


_CACHE = {}


def kernel(**inputs):
    inp = {k: np.asarray(v) for k, v in inputs.items()}
    x = inp["x"]
    B = x.shape[0]
    wts = host_layout(inp)
    key = "full"
    if key not in _CACHE:
        _CACHE[key] = build_program({k: v.shape for k, v in wts.items()})
    nc, _ = _CACHE[key]
    n_cores = 8
    work = [0, 1, 4, 5][:B]
    zeros = {k: np.zeros_like(v) for k, v in wts.items()}
    zeros["xT"] = np.zeros((x.shape[2], x.shape[1]), np.float32)
    in_maps = []
    for c in range(n_cores):
        if c in work:
            m = dict(wts)
            m["xT"] = np.ascontiguousarray(x[work.index(c)].T)
        else:
            m = zeros
        in_maps.append(m)
    res = run_bass_kernel_spmd(nc, in_maps, core_ids=list(range(n_cores)))
    out = np.stack([np.ascontiguousarray(res.results[c]["yT"].T) for c in work], axis=0)
    return out.astype(np.float32)
```

```python
from concourse.bass_utils import run_bass_kernel_spmd
import concourse.bass as bass
import concourse.mybir as mybir

ENGS = ("pe", "act", "dve", "pool", "sp")
SEM_CAP = 24000
N_DMA_SLOTS = 24
import os as _os0
SCHED_WINDOW = int(_os0.environ.get('K_WIN', '40'))


class Reg:
    __slots__ = ("name", "w", "rs", "excl")

    def __init__(self, name, excl=False):
        self.name = name
        self.excl = excl
        self.w = None
        self.rs = []


class Op:
    __slots__ = ("eng", "seq", "fn", "deps", "is_dma", "dur", "lat", "grp_open", "grp_cont",
                 "pos", "inc", "waits", "dma_slot", "dma_val", "flushed", "t_end", "done")

    def __init__(self, eng, seq, fn):
        self.eng = eng
        self.seq = seq
        self.fn = fn
        self.deps = []
        self.is_dma = False
        self.dur = 0.1
        self.lat = 0.0
        self.grp_open = False
        self.grp_cont = False
        self.pos = -1
        self.inc = False
        self.waits = []
        self.dma_slot = None
        self.dma_val = 0
        self.flushed = False
        self.t_end = 0.0
        self.done = False


def _free(ap):
    try:
        return int(ap.free_size())
    except Exception:
        return 128


class Prog:
    def __init__(self, nc, es):
        self.nc = nc
        self.es = es
        self.rec = {e: [] for e in ENGS}
        self.seq = 0
        self.csems = {e: [] for e in ENGS}
        self.dma_sems = [es.enter_context(nc.semaphore("dq%d" % i)) for i in range(N_DMA_SLOTS)]
        self.dma_uses = [0] * N_DMA_SLOTS
        self.dma_rr = 0
        self.counts = {e: 0 for e in ENGS}
        self.n_emitted = 0
        self.reorder = True
        self.pe_open = False
        self.bar_ap = None

    def _new(self, eng, fn):
        o = Op(eng, self.seq, fn)
        self.seq += 1
        return o

    def _dep(self, o, p):
        if p is None or p is o or p.flushed:
            return
        o.deps.append(p)

    def _track(self, o, reads, writes):
        locks = [x for x in list(reads) + list(writes) if x.excl]
        reads = [x for x in reads if not x.excl]
        writes = [x for x in writes if not x.excl]
        for r in reads:
            self._dep(o, r.w)
        for w in writes:
            self._dep(o, w.w)
            for t in w.rs:
                self._dep(o, t)
        for l in locks:
            self._dep(o, l.w)
        for r in reads:
            r.rs.append(o)
        for w in writes:
            w.w = o
            w.rs = []
        for l in locks:
            l.w = o

    def I(self, eng, name, reads=(), writes=(), **kw):
        o = self._new(eng, lambda h: getattr(h, name)(**kw))
        n = _free(kw.get("out", kw.get("ap")))
        if eng == "pe":
            if name == "matmul":
                n = _free(kw["rhs"])
                o.dur = 0.03 + 0.00046 * n
                if kw.get("lhsT") is not None and kw["lhsT"].dtype == mybir.dt.float32:
                    o.dur *= 4
                st, sp_ = kw.get("start"), kw.get("stop")
                o.grp_cont = (st is False)
                o.grp_open = (sp_ is False)
            else:
                o.dur = 0.12
        elif eng == "dve":
            o.dur = 0.08 + 0.00105 * n * (2 if name == "tensor_tensor_scan" else 1)
        elif eng == "act":
            o.dur = 0.22 + 0.00085 * n
        else:
            o.dur = 0.2 + 0.0023 * n
        self._track(o, reads, writes)
        self.rec[eng].append(o)
        return o

    def D(self, eng, reads=(), writes=(), **kw):
        o = self._new(eng, lambda h: h.dma_start(**kw))
        o.is_dma = True
        o.dur = 0.06
        o.lat = 2.0 + _free(kw["out"]) * 128 * 4 / 150e3
        self._track(o, reads, writes)
        self.rec[eng].append(o)
        return o

    def all_barrier(self):
        allops = [o for e in ENGS for o in self.rec[e] if o.fn is not None]
        w = self._new("sp", None)
        w.dur = 0.0
        w.deps = list(allops)
        self.rec["sp"].append(w)
        if self.bar_ap is None:
            for e in ENGS:
                if e == "sp":
                    continue
                o = self._new(e, None)
                o.dur = 0.0
                o.deps = list(allops)
                self.rec[e].append(o)
            return
        a, b = self.bar_ap
        rel = self.D("sp", out=b, in_=a)
        rel.deps = [w]
        rel.lat = 2.0
        for e in ENGS:
            if e == "sp":
                continue
            o = self._new(e, None)
            o.dur = 0.0
            o.deps = [rel]
            self.rec[e].append(o)

    def _schedule(self):
        rec = self.rec
        if not self.reorder:
            for e in ENGS:
                for o in rec[e]:
                    o.t_end = float(o.seq)
                    o.lat = 0.0
            return {e: list(rec[e]) for e in ENGS}
        order = {e: [] for e in ENGS}
        nxt = {e: 0 for e in ENGS}
        taken = {e: [False] * len(rec[e]) for e in ENGS}
        tfree = {e: 0.0 for e in ENGS}
        remaining = sum(len(rec[e]) for e in ENGS)
        for e in ENGS:
            for o in rec[e]:
                o.done = False
        import os as _os
        XLAT = float(_os.environ.get('K_XLAT', '1.3'))
        SLAT = float(_os.environ.get('K_SLAT', '0.6'))
        pe_open = 0
        pe_last = -1

        def ready_time(o, e):
            t = 0.0
            for p in o.deps:
                if not p.done:
                    return None
                c = p.t_end + (SLAT if (p.eng == e and not p.is_dma) else XLAT)
                if c > t:
                    t = c
            return t

        while remaining:
            best = None
            for e in ENGS:
                ol = rec[e]
                tk = taken[e]
                i = nxt[e]
                while i < len(ol) and tk[i]:
                    i += 1
                nxt[e] = i
                if i >= len(ol):
                    continue
                if e == "pe" and pe_open > 0:
                    q = pe_last + 1
                    while q < len(ol) and tk[q]:
                        q += 1
                    cand = [q] if q < len(ol) else []
                elif e == "sp":
                    cand = [i]
                else:
                    cand = []
                    j = i
                    while j < len(ol) and len(cand) < SCHED_WINDOW:
                        if not tk[j]:
                            o = ol[j]
                            if o.is_dma and j != i:
                                break
                            if not (e == "pe" and o.grp_cont and j != i):
                                cand.append(j)
                            if o.is_dma:
                                break
                        j += 1
                for j in cand:
                    o = ol[j]
                    rt = ready_time(o, e)
                    if rt is None:
                        continue
                    if e == "pe" and j != i and o.grp_open and not o.grp_cont and pe_open == 0:
                        depth, k, ok = 0, j, True
                        members = set()
                        while k < len(ol):
                            m = ol[k]
                            if not tk[k]:
                                members.add(id(m))
                                if m.grp_open and not m.grp_cont:
                                    depth += 1
                                elif m.grp_cont and not m.grp_open:
                                    depth -= 1
                                for p in m.deps:
                                    if not p.done and id(p) not in members:
                                        ok = False
                                        break
                                if not ok or depth == 0:
                                    break
                            k += 1
                        if not ok:
                            continue
                    st = rt if rt > tfree[e] else tfree[e]
                    key = (st, o.seq)
                    if best is None or key < best[0]:
                        best = (key, e, j, o, st)
                    if st <= tfree[e]:
                        break
            if best is None:
                for e in ENGS:
                    i = nxt[e]
                    if i < len(rec[e]):
                        o = rec[e][i]
                        print("STUCK", e, i, "of", len(rec[e]), "seq", o.seq, "dma", o.is_dma, "cont", o.grp_cont, "open", o.grp_open,
                              "pe_open", pe_open, "pe_last", pe_last,
                              "unmet", [(p.eng, p.seq, p.is_dma, rec[p.eng].index(p) if p in rec[p.eng] else -1) for p in o.deps if not p.done][:6])
            assert best is not None, "scheduler deadlock (dependency cycle?)"
            _, e, j, o, st = best
            o.t_end = st + o.dur + o.lat
            o.done = True
            order[e].append(o)
            tfree[e] = st + o.dur
            taken[e][j] = True
            remaining -= 1
            if e == "pe":
                pe_last = j
                if o.grp_open and not o.grp_cont:
                    pe_open += 1
                elif o.grp_cont and not o.grp_open:
                    pe_open -= 1
        self.sim_makespan = max(tfree.values())
        self.sim_busy = {e: sum(o.dur for o in rec[e]) for e in ENGS}
        return order

    def _csem(self, e, epoch):
        while len(self.csems[e]) <= epoch:
            self.csems[e].append(self.es.enter_context(self.nc.semaphore("c_%s_%d" % (e, len(self.csems[e])))))
        return self.csems[e][epoch]

    def flush(self, block):
        order = self._schedule()
        for e in ENGS:
            for i, o in enumerate(order[e]):
                o.pos = i
        dmas = sorted([o for e in ENGS for o in order[e] if o.is_dma], key=lambda o: (o.t_end - o.lat, o.seq))
        slot_prev = {}
        for o in dmas:
            slot = self.dma_rr
            self.dma_rr = (self.dma_rr + 1) % N_DMA_SLOTS
            if slot in slot_prev:
                o.deps.append(slot_prev[slot])
            slot_prev[slot] = o
            self.dma_uses[slot] += 1
            o.dma_slot = slot
            o.dma_val = 16 * self.dma_uses[slot]
        seen = {e: {e2: -1 for e2 in ENGS} for e in ENGS}
        seen_dma = {e: {} for e in ENGS}
        for e in ENGS:
            for o in order[e]:
                o.waits = []
                if len(o.deps) > 8:
                    o.deps.sort(key=lambda p: -(p.dma_val if p.is_dma else p.pos))
                for p in o.deps:
                    if p.flushed or (p.fn is None and not p.is_dma):
                        continue
                    if p.is_dma:
                        if seen_dma[e].get(p.dma_slot, 0) >= p.dma_val:
                            continue
                        seen_dma[e][p.dma_slot] = p.dma_val
                        o.waits.append(p)
                    else:
                        if p.eng == e:
                            assert p.pos < o.pos, "same-engine dependency order violated"
                            if e == "pe":
                                continue
                        if seen[e][p.eng] >= p.pos:
                            continue
                        seen[e][p.eng] = p.pos
                        p.inc = True
                        o.waits.append(p)
        cnt = {}
        for e in ENGS:
            c = self.counts[e]
            for o in order[e]:
                if o.inc and not o.is_dma and o.fn is not None:
                    c += 1
                    cnt[id(o)] = c
            if c > 0:
                self._csem(e, (c - 1) // SEM_CAP)
            self.counts[e] = c

        def emit(e, h):
            for o in order[e]:
                for p in o.waits:
                    if p.is_dma:
                        h.wait_ge(self.dma_sems[p.dma_slot], p.dma_val)
                    else:
                        ep, v = divmod(cnt[id(p)] - 1, SEM_CAP)
                        h.wait_ge(self._csem(p.eng, ep), v + 1)
                if o.fn is None:
                    continue
                ins = o.fn(h)
                if o.is_dma:
                    ins.then_inc(self.dma_sems[o.dma_slot], 16)
                elif o.inc:
                    ep, v = divmod(cnt[id(o)] - 1, SEM_CAP)
                    ins.then_inc(self._csem(e, ep), 1)
                self.n_emitted += 1

        block.tensor(lambda h: emit("pe", h))
        block.scalar(lambda h: emit("act", h))
        block.vector(lambda h: emit("dve", h))
        block.gpsimd(lambda h: emit("pool", h))
        block.sync(lambda h: emit("sp", h))
        for e in ENGS:
            for o in order[e]:
                o.flushed = True
                o.deps = []
                o.fn = None
            self.rec[e] = []


import numpy as np
from contextlib import ExitStack
import concourse.bass as bass
import concourse.mybir as mybir

F32 = mybir.dt.float32
BF16 = mybir.dt.bfloat16
AF = mybir.ActivationFunctionType
ALU = mybir.AluOpType
AX = mybir.AxisListType

D = 1024
KC = 8
FF = 2816
FC = 22
NT = 512
EPS = 1e-6


class Ctx:
    def __init__(self, nc, P):
        self.nc = nc
        self.P = P
        self.es = None
        self.n = 0

    def sb(self, shape, dt, name=None):
        self.n += 1
        return self.es.enter_context(self.nc.sbuf_tensor("%s_%d" % (name or "t", self.n), shape, dt))

    def ps(self, shape, dt, name=None):
        self.n += 1
        return self.es.enter_context(self.nc.psum_tensor("%s_%d" % (name or "p", self.n), shape, dt))


def aslist(x):
    return list(x) if isinstance(x, (list, tuple)) else [x]


def regs(prefix, n, excl=False):
    return [Reg("%s%d" % (prefix, i), excl) for i in range(n)]


def emit_consts(C):
    P, nc = C.P, C.nc
    C.ones_f = C.sb([128, 128], F32, "ones")
    C.R_ones = Reg("ones")
    P.I("dve", "memset", writes=[C.R_ones], ap=C.ones_f[:], constant=1.0)


def emit_norm(C, X, RX, gain, Rgain, hb, Rh, sq, Rsq, ssq_ps, Rssq, rt, Rrt, rstd, Rrstd):
    P = C.P
    for kc in range(KC):
        s = kc % 2
        P.I("act", "activation", reads=[RX[kc]], writes=[Rsq[s]], out=sq[s][:], in_=X[:, kc, :], func=AF.Square)
        P.I("pe", "matmul", reads=[Rsq[s], C.R_ones], writes=[Rssq], out=ssq_ps[:], lhsT=C.ones_f[:], rhs=sq[s][:],
            start=(kc == 0), stop=(kc == KC - 1))
    P.I("act", "activation", reads=[Rssq], writes=[Rrt], out=rt[:], in_=ssq_ps[:], func=AF.Ln, scale=1.0 / D, bias=EPS)
    P.I("act", "activation", reads=[Rrt], writes=[Rrstd], out=rstd[:], in_=rt[:], func=AF.Exp, scale=-0.5)
    for kc in range(KC):
        P.I("dve", "scalar_tensor_tensor", reads=[RX[kc], Rgain, Rrstd], writes=[Rh[kc]], out=hb[:, kc, :],
            in0=X[:, kc, :], scalar=gain[:, kc:kc + 1], in1=rstd[:], op0=ALU.mult, op1=ALU.mult)


def emit_ffn(C, xsrc, xdst, Rxsrc, Rxdst, wg_d, wu_d, wd_d, gain_d, ntiles):
    P, nc = C.P, C.nc
    Wg = C.sb([128, KC, FF], BF16, "Wg")
    Wu = C.sb([128, KC, FF], BF16, "Wu")
    Wd = C.sb([128, FC, D], BF16, "Wd")
    gain = C.sb([128, KC], F32, "gain")
    xt = [C.sb([128, KC, NT], F32, "xt") for _ in range(2)]
    sq = [C.sb([128, NT], F32, "sq") for _ in range(2)]
    hb = C.sb([128, KC, NT], BF16, "h")
    act = C.sb([128, FC, NT], BF16, "act")
    sg = [C.sb([128, NT], BF16, "sg") for _ in range(2)]
    rt = C.sb([128, NT], F32, "rt")
    rstd = C.sb([128, NT], F32, "rstd")
    ssq_ps = C.ps([128, NT], F32, "ssq")
    g_ps = [C.ps([128, NT], F32, "g") for _ in range(2)]
    u_ps = [C.ps([128, NT], F32, "u") for _ in range(2)]
    o_ps = [C.ps([128, NT], F32, "o") for _ in range(2)]

    RWg, RWu, RWd = regs("Wg", KC), regs("Wu", KC), regs("Wd", FC)
    Rgain = Reg("gain")
    Rxt = [regs("xt%d_" % b, KC) for b in range(2)]
    Rsq = regs("sq", 2)
    Rh = regs("h", KC)
    Ract = regs("act", FC)
    Rsg = regs("sg", 2)
    Rrt, Rrstd, Rssq = Reg("rt"), Reg("rstd"), Reg("ssq", True)
    Rg, Ru, Ro = regs("g", 2, True), regs("u", 2, True), regs("o", 2, True)

    xs_v = xsrc.rearrange("(kc p) t -> p kc t", p=128)
    xd_v = xdst.rearrange("(kc p) t -> p kc t", p=128)

    def load(t, q="sp"):
        b = t % 2
        P.D(q, reads=aslist(Rxsrc[t]), writes=Rxt[b], out=xt[b][:], in_=xs_v[:, :, t * NT:(t + 1) * NT])

    P.D("pool", writes=[Rgain], out=gain[:], in_=gain_d)
    load(0, "pool")
    if ntiles > 1:
        load(1, "pool")
    wg_v = wg_d.rearrange("(kc p) f -> p kc f", p=128)
    wu_v = wu_d.rearrange("(kc p) f -> p kc f", p=128)
    wd_v = wd_d.rearrange("(fc p) d -> p fc d", p=128)
    FG = [(0, 6), (6, 12), (12, 17), (17, 22)]
    RWg, RWu = regs("Wg", len(FG)), regs("Wu", len(FG))
    fgrp = {}
    for gi, (f0, f1) in enumerate(FG):
        for f in range(f0, f1):
            fgrp[f] = gi
        P.D("pool", writes=[RWg[gi]], out=Wg[:, :, f0 * 128:f1 * 128], in_=wg_v[:, :, f0 * 128:f1 * 128])
        P.D("pool", writes=[RWu[gi]], out=Wu[:, :, f0 * 128:f1 * 128], in_=wu_v[:, :, f0 * 128:f1 * 128])
    for fc in range(FC):
        P.D("pool", writes=[RWd[fc]], out=Wd[:, fc, :], in_=wd_v[:, fc, :])

    for t in range(ntiles):
        b = t % 2
        if t + 1 < ntiles and t >= 1:
            load(t + 1)
        X = xt[b]
        emit_norm(C, X, Rxt[b], gain, Rgain, hb, Rh, sq, Rsq, ssq_ps, Rssq, rt, Rrt, rstd, Rrstd)
        for f in range(FC):
            s = f % 2
            for kc in range(KC):
                P.I("pe", "matmul", reads=[RWg[fgrp[f]], Rh[kc]], writes=[Rg[s]], out=g_ps[s][:], lhsT=Wg[:, kc, f * 128:(f + 1) * 128], rhs=hb[:, kc, :],
                    start=(kc == 0), stop=(kc == KC - 1))
            for kc in range(KC):
                P.I("pe", "matmul", reads=[RWu[fgrp[f]], Rh[kc]], writes=[Ru[s]], out=u_ps[s][:], lhsT=Wu[:, kc, f * 128:(f + 1) * 128], rhs=hb[:, kc, :],
                    start=(kc == 0), stop=(kc == KC - 1))
            P.I("act", "activation", reads=[Rg[s]], writes=[Rsg[s]], out=sg[s][:], in_=g_ps[s][:], func=AF.Silu)
            P.I("dve", "tensor_tensor", reads=[Ru[s], Rsg[s]], writes=[Ract[f]], out=act[:, f, :], in0=u_ps[s][:], in1=sg[s][:], op=ALU.mult)
        for dc in range(KC):
            s = dc % 2
            for f in range(FC):
                P.I("pe", "matmul", reads=[RWd[f], Ract[f]], writes=[Ro[s]], out=o_ps[s][:], lhsT=Wd[:, f, dc * 128:(dc + 1) * 128], rhs=act[:, f, :],
                    start=(f == 0), stop=(f == FC - 1))
            P.I("dve", "scalar_tensor_tensor", reads=[Ro[s], Rxt[b][dc]], writes=[Rxt[b][dc]], out=X[:, dc, :], in0=o_ps[s][:], scalar=0.5, in1=X[:, dc, :], op0=ALU.mult, op1=ALU.add)
        P.D("sp", reads=Rxt[b], writes=aslist(Rxdst[t]), out=xd_v[:, :, t * NT:(t + 1) * NT], in_=X[:])


def emit_odd(C, xsrc, xdst, Rxsrc, Rxdst, win_d, wout_d, poolw_d, bt_d, qg_d, kg_d, pscale_d, gain_d, ident_d, ntiles):
    P, nc = C.P, C.nc
    T = ntiles * NT
    Win = C.sb([128, KC, 2048], BF16, "Win")
    Wout = C.sb([128, KC, D], BF16, "Wout")
    Wp = C.sb([128, 4, 128], BF16, "Wp")
    bt = C.sb([128, 8, 640], F32, "bt")
    qg = C.sb([128, 1], F32, "qg")
    kg = C.sb([128, 1], F32, "kg")
    pscale = C.sb([128, 4], F32, "pscale")
    gain = C.sb([128, KC], F32, "gain")
    ident_f = C.sb([128, 128], F32, "identf")
    ident = C.sb([128, 128], BF16, "ident")
    ones_bf = C.sb([128, 64], BF16, "onesbf")
    bones = C.sb([128, 128], F32, "bones")
    KnT = C.sb([128, 4, T], BF16, "KnT")
    V = C.sb([128, ntiles * 4, 512], BF16, "V")
    X = C.sb([128, KC, NT], F32, "X")
    hb = C.sb([128, KC, NT], BF16, "h")
    qnT = C.sb([128, 4, NT], BF16, "qnT")
    yT = C.sb([128, 8, NT], BF16, "yT")
    U = C.sb([128, 4, 16 + NT], F32, "U")
    sA = C.sb([128, 16 + NT], F32, "sA")
    sB = C.sb([128, 16 + NT], F32, "sB")
    t16 = C.sb([128, 16], F32, "t16")
    pooled = C.sb([128, 4, NT], BF16, "pooled")
    rc = C.sb([128, 4, 16], F32, "rc")
    sq = [C.sb([128, NT], F32, "sq") for _ in range(2)]
    rt = C.sb([128, NT], F32, "rt")
    rstd = C.sb([128, NT], F32, "rstd")
    sbs = [C.sb([128, 640], F32, "sbs") for _ in range(2)]
    nmx = [C.sb([128, 1], F32, "nmx") for _ in range(2)]
    Pb = [C.sb([128, 640], BF16, "Pb") for _ in range(2)]
    PT = [C.sb([128, 5, 128], BF16, "PT") for _ in range(2)]
    rinv = C.sb([128, 128], F32, "rinv")
    pj = [C.ps([128, NT], F32, "pj") for _ in range(2)]
    S = [C.ps([128, 1024], F32, "S") for _ in range(2)]
    PTp = C.ps([128, 8, 128], BF16, "PTp")
    OR = C.ps([128, 512], F32, "OR")

    RWin, RWout = regs("Win", KC), regs("Wout", KC)
    RWp, Rbt, Rqg, Rkg, Rpsc, Rgain = Reg("Wp"), Reg("bt"), Reg("qg"), Reg("kg"), Reg("psc"), Reg("gain")
    Rident, Ridf, Rones, Rbones, Rrc = Reg("ident"), Reg("identf"), Reg("onesbf"), Reg("bones"), Reg("rc")
    RKn = [regs("Kn%d_" % t, 4) for t in range(ntiles)]
    RV = regs("V", ntiles * 4)
    RX, Rh = regs("X", KC), regs("h", KC)
    Rqn, Ry = regs("qn", 4), regs("y", 8)
    RU, RsA, RsB, Rt16, Rpooled = regs("U", 4), Reg("sA"), Reg("sB"), Reg("t16"), regs("pooled", 4)
    Rsq = regs("sq", 2)
    Rrt, Rrstd = Reg("rt"), Reg("rstd")
    Rsb, Rmx, RPb, RPT, Rrinv = regs("sb", 2), regs("mx", 2), regs("Pb", 2), regs("PT", 2), Reg("rinv")
    Rpj, RS, RPTp, ROR = regs("pj", 2, True), regs("S", 2, True), Reg("PTp", True), Reg("OR", True)

    P.D("pool", writes=[Rgain], out=gain[:], in_=gain_d)
    P.D("pool", writes=[Rqg], out=qg[:], in_=qg_d)
    P.D("pool", writes=[Rkg], out=kg[:], in_=kg_d)
    P.D("pool", writes=[Rpsc], out=pscale[:], in_=pscale_d)
    P.D("pool", writes=[Ridf], out=ident_f[:], in_=ident_d)
    P.D("pool", writes=[Rbt], out=bt[:], in_=bt_d)
    xs_v = xsrc.rearrange("(kc p) t -> p kc t", p=128)
    xd_v = xdst.rearrange("(kc p) t -> p kc t", p=128)
    P.D("pool", reads=aslist(Rxsrc[0]), writes=RX, out=X[:], in_=xs_v[:, :, 0:NT])
    win_v = win_d.rearrange("(kc p) f -> p kc f", p=128)
    wout_v = wout_d.rearrange("(kc p) f -> p kc f", p=128)
    for kc in range(KC):
        P.D("pool", writes=[RWin[kc]], out=Win[:, kc, :], in_=win_v[:, kc, :])
    for kc in range(KC):
        P.D("pool", writes=[RWout[kc]], out=Wout[:, kc, :], in_=wout_v[:, kc, :])
    P.D("pool", writes=[RWp], out=Wp[:], in_=poolw_d.rearrange("g i o -> i g o"))
    P.I("dve", "tensor_copy", reads=[Ridf], writes=[Rident], out=ident[:], in_=ident_f[:])
    P.I("dve", "memset", writes=[Rones], ap=ones_bf[:], constant=1.0)
    P.I("dve", "memset", writes=[Rbones], ap=bones[:], constant=0.0)
    P.I("dve", "memset", reads=[], writes=[Rbones], ap=bones[0:64, 0:64], constant=1.0)
    P.I("dve", "memset", reads=[], writes=[Rbones], ap=bones[64:128, 64:128], constant=1.0)
    P.I("dve", "memset", writes=[Rbt], ap=bt[0:64, :, 576:640], constant=-1e30)
    P.I("dve", "memset", writes=[Rbt], ap=bt[64:128, :, 0:64], constant=-1e30)
    for g in range(4):
        P.I("pool", "memset", writes=[RU[g]], ap=U[:, g, 0:16], constant=0.0)
    WIN = (2, 4, 8, 16)
    for g, w in enumerate(WIN):
        for pos in range(w - 1):
            P.I("pool", "memset", writes=[Rrc], ap=rc[:, g, pos:pos + 1], constant=1.0 / (pos + 1))
        P.I("pool", "memset", writes=[Rrc], ap=rc[:, g, w - 1:16], constant=1.0 / w)

    xs_v = xsrc.rearrange("(kc p) t -> p kc t", p=128)
    xd_v = xdst.rearrange("(kc p) t -> p kc t", p=128)
    pjn = [0]

    def nextpj():
        pjn[0] += 1
        return pjn[0] % len(pj)

    for t in range(ntiles):
        tsl = slice(t * NT, (t + 1) * NT)
        if t > 0:
            P.D("sp", reads=aslist(Rxsrc[t]), writes=RX, out=X[:], in_=xs_v[:, :, tsl])
        s0 = nextpj()
        emit_norm(C, X, RX, gain, Rgain, hb, Rh, sq, Rsq, pj[s0], Rpj[s0], rt, Rrt, rstd, Rrstd)
        for col0, gv, Rgv, is_q in ((0, qg, Rqg, True), (512, kg, Rkg, False)):
            for c in range(4):
                s = nextpj()
                for kc in range(KC):
                    P.I("pe", "matmul", reads=[RWin[kc], Rh[kc]], writes=[Rpj[s]], out=pj[s][:],
                        lhsT=Win[:, kc, col0 + c * 128:col0 + (c + 1) * 128], rhs=hb[:, kc, :],
                        start=(kc == 0), stop=(kc == KC - 1))
                s3 = 1 - s
                P.I("act", "activation", reads=[Rpj[s]], writes=[Rsq[0]], out=sq[0][:], in_=pj[s][:], func=AF.Square)
                P.I("pe", "matmul", reads=[Rsq[0], Rbones], writes=[Rpj[s3]], out=pj[s3][:], lhsT=bones[:], rhs=sq[0][:],
                    start=True, stop=True)
                P.I("act", "activation", reads=[Rpj[s3]], writes=[Rrt], out=rt[:], in_=pj[s3][:], func=AF.Ln,
                    scale=1.0 / 64, bias=EPS)
                P.I("act", "activation", reads=[Rrt], writes=[Rrstd], out=rstd[:], in_=rt[:], func=AF.Exp, scale=-0.5)
                if is_q:
                    dst, Rdst = qnT[:, c, :], Rqn[c]
                else:
                    dst, Rdst = KnT[:, c, tsl], RKn[t][c]
                P.I("dve", "scalar_tensor_tensor", reads=[Rpj[s], Rgv, Rrstd], writes=[Rdst], out=dst,
                    in0=pj[s][:], scalar=gv[:, 0:1], in1=rstd[:], op0=ALU.mult, op1=ALU.mult)
        for tb in range(4):
            s = nextpj()
            for kc in range(KC):
                P.I("pe", "matmul", reads=[RWin[kc], Rh[kc]], writes=[Rpj[s]], out=pj[s][:],
                    lhsT=hb[:, kc, tb * 128:(tb + 1) * 128], rhs=Win[:, kc, 1024:1536],
                    start=(kc == 0), stop=(kc == KC - 1))
            P.I("act", "activation", reads=[Rpj[s]], writes=[RV[4 * t + tb]], out=V[:, 4 * t + tb, :], in_=pj[s][:],
                func=AF.Copy)
        for g, w in enumerate(WIN):
            s = nextpj()
            for kc in range(KC):
                P.I("pe", "matmul", reads=[RWin[kc], Rh[kc]], writes=[Rpj[s]], out=pj[s][:],
                    lhsT=Win[:, kc, 1536 + g * 128:1536 + (g + 1) * 128], rhs=hb[:, kc, :],
                    start=(kc == 0), stop=(kc == KC - 1))
            P.I("act", "activation", reads=[Rpj[s]], writes=[RU[g]], out=U[:, g, 16:16 + NT], in_=pj[s][:], func=AF.Copy)
            L = 16 + NT
            P.I("pool", "tensor_tensor", reads=[RU[g]], writes=[RsA], out=sA[:, 1:L], in0=U[:, g, 1:L], in1=U[:, g, 0:L - 1], op=ALU.add)
            fin, Rfin = sA, RsA
            if w >= 4:
                P.I("pool", "tensor_tensor", reads=[RsA], writes=[RsB], out=sB[:, 3:L], in0=sA[:, 3:L], in1=sA[:, 1:L - 2], op=ALU.add)
                fin, Rfin = sB, RsB
            if w >= 8:
                P.I("pool", "tensor_tensor", reads=[RsB], writes=[RsA], out=sA[:, 7:L], in0=sB[:, 7:L], in1=sB[:, 3:L - 4], op=ALU.add)
                fin, Rfin = sA, RsA
            if w >= 16:
                P.I("pool", "tensor_tensor", reads=[RsA], writes=[RsB], out=sB[:, 15:L], in0=sA[:, 15:L], in1=sA[:, 7:L - 8], op=ALU.add)
                fin, Rfin = sB, RsB
            P.I("dve", "scalar_tensor_tensor", reads=[Rfin, RU[g]], writes=[Rpooled[g]], out=pooled[:, g, :],
                in0=fin[:, 16:L], scalar=1.0 / w, in1=U[:, g, 16:L], op0=ALU.mult, op1=ALU.subtract)
            if t == 0:
                P.I("dve", "tensor_tensor", reads=[Rfin, Rrc], writes=[Rt16], out=t16[:], in0=fin[:, 16:32], in1=rc[:, g, :], op=ALU.mult)
                P.I("dve", "tensor_tensor", reads=[Rt16, RU[g]], writes=[Rpooled[g]], out=pooled[:, g, 0:16], in0=t16[:], in1=U[:, g, 16:32], op=ALU.subtract)
            P.I("pool", "tensor_copy", reads=[RU[g]], writes=[RU[g]], out=U[:, g, 0:16], in_=U[:, g, NT:NT + 16])
            s = nextpj()
            P.I("pe", "matmul", reads=[RWp, Rpooled[g]], writes=[Rpj[s]], out=pj[s][:], lhsT=Wp[:, g, :], rhs=pooled[:, g, :],
                start=True, stop=True)
            P.I("dve", "tensor_scalar", reads=[Rpj[s], Rpsc], writes=[Ry[4 + g]], out=yT[:, 4 + g, :], in0=pj[s][:],
                scalar1=pscale[:, g:g + 1], scalar2=None, op0=ALU.mult)
        items = [(mb, c, e) for mb in range(4) for c in range(4) for e in range(2)]

        def geom(mb):
            m = 4 * t + mb
            nkb = min(m, 4) + 1
            j0 = m - (nkb - 1)
            return m, nkb, j0, (5 - nkb) * 128, nkb * 128

        def stageA(i):
            mb, c, e = items[i]
            m, nkb, j0, koff, nk = geom(mb)
            z = i % 2
            hh = 2 * c + e
            hs = e * 64
            lsl = slice(mb * 128, (mb + 1) * 128)
            kregs = [RKn[tt][c] for tt in range(j0 // 4, t + 1)]
            for a_, b_ in ([(0, min(nk, 512))] + ([(512, nk)] if nk > 512 else [])):
                P.I("pe", "matmul", reads=[Rqn[c]] + kregs, writes=[RS[z]], out=S[z][:, a_:b_],
                    lhsT=qnT[hs:hs + 64, c, lsl], rhs=KnT[hs:hs + 64, c, j0 * 128 + a_:j0 * 128 + b_],
                    start=True, stop=True)
            P.I("dve", "scalar_tensor_tensor", reads=[RS[z], Rbt], writes=[Rsb[z]], out=sbs[z][:, 0:nk], in0=S[z][:, 0:nk],
                scalar=0.125, in1=bt[:, hh, koff:koff + nk], op0=ALU.mult, op1=ALU.add)
            P.I("dve", "tensor_reduce", reads=[Rsb[z]], writes=[Rmx[z]], out=nmx[z][:], in_=sbs[z][:, 0:nk], axis=AX.X,
                op=ALU.max, negate=True)
            P.I("act", "activation", reads=[Rsb[z], Rmx[z]], writes=[RPb[z]], out=Pb[z][:, 0:nk], in_=sbs[z][:, 0:nk],
                func=AF.Exp, bias=nmx[z][:], scale=1.0)

        def stageB1(i):
            mb, c, e = items[i]
            m, nkb, j0, koff, nk = geom(mb)
            z = i % 2
            for kb in range(nkb):
                P.I("pe", "transpose", reads=[RPb[z], Rident], writes=[RPTp], out=PTp[:, kb, :],
                    in_=Pb[z][:, kb * 128:(kb + 1) * 128], identity=ident[:])
            P.I("act", "activation", reads=[RPTp], writes=[RPT[e]], out=PT[e][:, 0:nkb, :], in_=PTp[:, 0:nkb, :],
                func=AF.Copy)

        def stageB2(i):
            mb, c, e = items[i]
            m, nkb, j0, koff, nk = geom(mb)
            hh = 2 * c + e
            hs = e * 64
            lsl = slice(mb * 128, (mb + 1) * 128)
            for kb in range(nkb):
                P.I("pe", "matmul", reads=[RV[j0 + kb], RPT[e]], writes=[ROR], out=OR[hs:hs + 64, 0:128],
                    lhsT=V[:, j0 + kb, hh * 64:(hh + 1) * 64], rhs=PT[e][:, kb, :],
                    start=(kb == 0), stop=(kb == nkb - 1))
            for kb in range(nkb):
                P.I("pe", "matmul", reads=[Rones, RPT[e]], writes=[ROR], out=OR[hs:hs + 64, 128:256],
                    lhsT=ones_bf[:, 0:64], rhs=PT[e][:, kb, :], start=(kb == 0), stop=(kb == nkb - 1))
            if e == 1:
                P.I("dve", "reciprocal", reads=[ROR], writes=[Rrinv], out=rinv[:], in_=OR[:, 128:256])
                P.I("dve", "tensor_tensor", reads=[ROR, Rrinv], writes=[Ry[c]], out=yT[:, c, lsl], in0=OR[:, 0:128],
                    in1=rinv[:], op=ALU.mult)

        n_it = len(items)
        stageA(0)
        stageA(1)
        stageB1(0)
        for i in range(n_it):
            if i + 2 < n_it:
                stageA(i + 2)
            if i + 1 < n_it:
                stageB1(i + 1)
            stageB2(i)
        for dc in range(KC):
            s = nextpj()
            for c in range(8):
                P.I("pe", "matmul", reads=[RWout[c], Ry[c]], writes=[Rpj[s]], out=pj[s][:],
                    lhsT=Wout[:, c, dc * 128:(dc + 1) * 128], rhs=yT[:, c, :], start=(c == 0), stop=(c == 7))
            P.I("dve", "tensor_tensor", reads=[Rpj[s], RX[dc]], writes=[RX[dc]], out=X[:, dc, :], in0=pj[s][:],
                in1=X[:, dc, :], op=ALU.add)
        P.D("sp", reads=RX, writes=aslist(Rxdst[t]), out=xd_v[:, :, tsl], in_=X[:])


NTE = 256
QC = 128


def bview(ap, axis, shape):
    return ap.unsqueeze(axis).broadcast_to(shape)


def emit_even(C, xsrc, xdst, Rxsrc, Rxdst, dd, ntiles):
    P, nc = C.P, C.nc
    sb, ps = C.sb, C.ps
    Win = sb([128, KC, 2056], BF16, "Win")
    Wout = sb([128, KC, D], BF16, "Wout")
    Wglu = sb([128, 4, 512], BF16, "Wglu")
    diagw = sb([128, 8, 4, 128], BF16, "diagw")
    diagD = sb([128, 4, 128], BF16, "diagD")
    diagD5 = sb([128, 4, 128], BF16, "diagD5")
    gain = sb([128, KC], F32, "gain")
    convw = sb([128, 8, 4], F32, "convw")
    convb = sb([128, 8], F32, "convb")
    dtb = sb([8, 1], F32, "dtb")
    alog = sb([8, 1], F32, "alog")
    aneg = sb([8, 1], F32, "aneg")
    dskip = sb([128, 4], F32, "dskip")
    ngain = sb([128, 4], F32, "ngain")
    d5 = sb([128, 4], F32, "d5")
    bglu = sb([128, 4], F32, "bglu")
    ident_f = sb([128, 128], F32, "identf")
    ident = sb([128, 128], BF16, "ident")
    selH = sb([8, 8, 128], F32, "selH")
    tri = sb([128, 128], F32, "tri")
    maskneg = sb([128, 128], F32, "maskneg")
    X = sb([128, KC, NTE], F32, "X")
    hb = sb([128, KC, NTE], BF16, "h")
    zs = sb([128, 4, NTE], BF16, "zs")
    xbc = sb([128, 8, 3 + NTE], BF16, "xbc")
    xsT = sb([128, 4, NTE], BF16, "xsT")
    BT = sb([128, 2, NTE], BF16, "BT")
    CT = sb([128, 2, NTE], BF16, "CT")
    uT = sb([128, 4, NTE], BF16, "uT")
    yT = sb([128, 8, NTE], BF16, "yT")
    dte1 = sb([8, NTE], F32, "dte1")
    dtsp = sb([8, NTE], F32, "dtsp")
    dA = sb([8, NTE], F32, "dA")
    cs = sb([8, NTE], F32, "cs")
    sq = [sb([128, NTE], F32, "sq") for _ in range(2)]
    rt = sb([128, NTE], F32, "rt")
    rstd = sb([128, NTE], F32, "rstd")
    csdt = sb([128, 16], F32, "csdt")
    tmpL = sb([128, 8, 128], F32, "tmpL")
    ecs = sb([128, 8, 128], BF16, "ecs")
    eend = sb([128, 8], F32, "eend")
    d1 = sb([128, 8], F32, "d1")
    dte = sb([128, 8], F32, "dte")
    Mt = sb([128, 8, 128], BF16, "Mt")
    Cs = sb([128, 8, 128], BF16, "Cs")
    xdt = sb([128, 512], BF16, "xdt")
    xw = sb([128, 512], BF16, "xw")
    Btok = sb([128, 2, 128], BF16, "Btok")
    yg = sb([128, 4, 128], F32, "yg")
    sqy = [sb([128, 128], F32, "sqy") for _ in range(2)]
    rtg = sb([128, 128], F32, "rtg")
    rstdg = sb([128, 128], F32, "rstdg")
    ST = sb([128, 512], F32, "ST")
    STb = sb([128, 512], BF16, "STb")
    LD = sb([128, 16, 2, 128], BF16, "LD")
    LO = sb([128, 16, 2, 128], BF16, "LO")
    Ec = sb([128, 16, 128], F32, "Ec")
    Es = sb([128, 16, 128], F32, "Es")
    big = sb([128, 2048], F32, "big")
    Zall = big[:].rearrange("p (a g i) -> p a g i", a=16, g=8)
    maskZ = sb([128, 16, 8], F32, "maskZ")
    sm = {k: sb([128, 16], F32, k) for k in ("are", "aim", "ldt", "dt", "dar", "mag", "th", "c", "s", "ta", "tb", "c2", "s2",
                                              "abr", "abi", "abr1", "den", "rden", "fre", "fim", "u1", "u2",
                                              "ire", "iim", "v1", "v2")}
    cm = [sb([128, 16], F32, "cm%d" % k) for k in range(7)]
    smm = [sb([128, 16], F32, "sm%d" % k) for k in range(7)]
    Hp = sb([128, 16, 2], F32, "Hp")
    t1 = sb([128, 4, 128], F32, "t1")
    t2 = sb([128, 4, 128], F32, "t2")
    t3 = sb([128, 4, 128], F32, "t3")
    t4 = sb([128, 4, 128], F32, "t4")
    t5 = sb([128, 4, 128], F32, "t5")
    t6 = sb([128, 4, 128], F32, "t6")
    t7 = sb([128, 4, 128], F32, "t7")
    t8 = sb([128, 4, 128], F32, "t8")
    xm2 = sb([128, 4, 2, 128], F32, "xm2")
    nat = {}
    for nk_, tb_ in zip(("bre", "bim", "cre", "cim", "NBre", "NBim", "n1", "n2"), (t1, t2, t3, t4, t5, t6, t7, t8)):
        nat[nk_] = tb_[:].rearrange("p a l -> p (a l)")[:, 0:256].rearrange("p (a j) -> p a j", a=16)
    xm = big[:, 0:1024].rearrange("p (a q l) -> p a q l", a=4, q=2)
    ht = big[:, 1024:2048].rearrange("p (a q l) -> p a q l", a=4, q=2)
    xmb = [xm, xm2[:]]
    H = sb([128, 4, 2, 128], F32, "H")
    Hbb = [sb([128, 4, 2, 128], BF16, "Hb") for _ in range(2)]
    y5sb = [sb([128, 4, 128], F32, "y5s") for _ in range(2)]
    ga = sb([128, 128], F32, "ga")
    gb = sb([128, 128], F32, "gb")
    gthb = [sb([128, 128], F32, "gth") for _ in range(2)]
    geb = [sb([128, 4, 128], BF16, "ge") for _ in range(2)]
    sig = sb([128, 128], F32, "sig")
    PJ = ps([128, 2, 512], F32, "PJ")
    M1 = ps([128, 1024], BF16, "M1")
    M2 = ps([128, 512], F32, "M2")
    BC = ps([128, 8, 128], F32, "BC")
    GY = ps([128, 512], F32, "GY")
    SN = ps([128, 512], F32, "SN")
    PJf = PJ[:].rearrange("p s n -> p (s n)")
    drv = PJf.rearrange("p (r q n) -> p r q n", r=4, q=2)
    xs_tok = M1[:, 0:512].rearrange("p (c n) -> p c n", c=4)
    Btok_ps = M1[:, 512:768].rearrange("p (g n) -> p g n", g=2)
    csdt_ps = M2[:, 0:16]
    ss_ps = M2[:, 128:384].rearrange("p (g n) -> p g n", g=2)
    gate_ps = M2[:, 384:512]
    G_ps = GY[:, 0:256].rearrange("p (g n) -> p g n", g=2)
    y_ps = GY[:, 256:384]
    y5_ps = GY[:, 384:512]

    R = {}
    alias = {"csdt_ps": "M2", "ss": "M2", "gate": "M2", "xs_tok": "M1", "Btok_ps": "M1", "G": "GY", "y": "GY", "y5": "GY"}
    psum_names = ("M1", "M2", "GY", "BC", "SN")

    def r(name):
        name = alias.get(name, name)
        if name not in R:
            R[name] = Reg(name, name in psum_names)
        return R[name]

    RWin, RWout = regs("Win", KC), regs("Wout", KC)
    RX, Rh = regs("X", KC), regs("h", KC)
    Ry = regs("y", 8)
    Rpj = regs("pj", 2, True)
    R["pj0"], R["pj1"] = Rpj
    for nk_, tn_ in zip(("bre", "bim", "cre", "cim", "NBre", "NBim", "n1", "n2"), range(1, 9)):
        R[nk_] = r("t%d" % tn_)

    sp_loads = [("gain", gain), ("convw", convw), ("convb", convb), ("dtb", dtb), ("alog", alog), ("dskip", dskip),
                ("ngain", ngain), ("d5", d5), ("bglu", bglu), ("ident", ident_f), ("are", sm["are"]), ("aim", sm["aim"]),
                ("ldt", sm["ldt"]), ("bre", nat["bre"]), ("bim", nat["bim"]), ("cre", nat["cre"]), ("cim", nat["cim"])]
    for k, tgt in sp_loads:
        P.D("pool", writes=[r(k)], out=tgt[:], in_=dd[k])
    xs_v = xsrc.rearrange("(kc p) t -> p kc t", p=128)
    xd_v = xdst.rearrange("(kc p) t -> p kc t", p=128)
    P.D("pool", reads=aslist(Rxsrc[0]), writes=RX, out=X[:], in_=xs_v[:, :, 0:NTE])
    win_v = dd["win"].rearrange("(kc p) f -> p kc f", p=128)
    wout_v = dd["wout"].rearrange("(kc p) f -> p kc f", p=128)
    for kc in range(KC):
        P.D("pool", writes=[RWin[kc]], out=Win[:, kc, :], in_=win_v[:, kc, :])
    for kc in range(KC):
        P.D("pool", writes=[RWout[kc]], out=Wout[:, kc, :], in_=wout_v[:, kc, :])
    P.D("pool", writes=[r("Wglu")], out=Wglu[:], in_=dd["wglu"].rearrange("(c p) o -> p c o", p=128))

    I = P.I
    I("dve", "tensor_copy", reads=[r("ident")], writes=[r("identb")], out=ident[:], in_=ident_f[:])
    I("act", "activation", reads=[r("alog")], writes=[r("aneg")], out=aneg[:], in_=alog[:], func=AF.Exp)
    I("dve", "tensor_scalar", reads=[r("aneg")], writes=[r("aneg")], out=aneg[:], in0=aneg[:], scalar1=-1.0, scalar2=None,
      op0=ALU.mult)
    for c8 in range(8):
        for k in range(4):
            I("dve", "tensor_scalar", reads=[r("ident"), r("convw")], writes=[r("diagw")], out=diagw[:, c8, k, :],
              in0=ident_f[:], scalar1=convw[:, c8, k:k + 1], scalar2=None, op0=ALU.mult)
    for c in range(4):
        I("dve", "tensor_scalar", reads=[r("ident"), r("dskip")], writes=[r("diagD")], out=diagD[:, c, :], in0=ident_f[:],
          scalar1=dskip[:, c:c + 1], scalar2=None, op0=ALU.mult)
        I("dve", "tensor_scalar", reads=[r("ident"), r("d5")], writes=[r("diagD5")], out=diagD5[:, c, :], in0=ident_f[:],
          scalar1=d5[:, c:c + 1], scalar2=None, op0=ALU.mult)
    I("dve", "tensor_copy", reads=[r("ident")], writes=[r("selH")], out=selH[:], in_=bview(ident_f[0:8, 0:8], 2, [8, 8, 128]))
    I("dve", "tensor_tensor_scan", reads=[r("ident"), C.R_ones], writes=[r("tri")], out=tri[:], data0=C.ones_f[:],
      data1=ident_f[:], initial=0.0, op0=ALU.mult, op1=ALU.add)
    I("dve", "tensor_scalar", reads=[r("tri")], writes=[r("maskneg")], out=maskneg[:], in0=tri[:], scalar1=1.0, scalar2=1e30,
      op0=ALU.subtract, op1=ALU.mult)
    I("dve", "memset", writes=[r("ST")], ap=ST[:], constant=0.0)
    I("dve", "memset", writes=[r("STb")], ap=STb[:], constant=0.0)
    I("dve", "memset", writes=[r("xbc%d" % c8) for c8 in range(8)], ap=xbc[:, :, 0:3], constant=0.0)
    I("dve", "memset", writes=[r("ire%d" % j) for j in range(4)] + [r("iim%d" % j) for j in range(4)], ap=Hp[:], constant=0.0)
    s = sm

    def tt(out, a, b, op, eng="dve", rd=(), wr=()):
        I(eng, "tensor_tensor", reads=[r(x) for x in rd], writes=[r(x) for x in wr], out=out, in0=a, in1=b, op=op)

    I("act", "activation", reads=[r("ldt")], writes=[r("dt")], out=s["dt"][:], in_=s["ldt"][:], func=AF.Exp)
    tt(s["dar"][:], s["dt"][:], s["are"][:], ALU.mult, rd=["dt", "are"], wr=["dar"])
    I("act", "activation", reads=[r("dar")], writes=[r("mag")], out=s["mag"][:], in_=s["dar"][:], func=AF.Exp)
    tt(s["th"][:], s["dt"][:], s["aim"][:], ALU.mult, rd=["dt", "aim"], wr=["th"])
    I("act", "activation", reads=[r("th")], writes=[r("s")], out=s["s"][:], in_=s["th"][:], func=AF.Sin, scale=1.0 / 32)
    I("act", "activation", reads=[r("th")], writes=[r("c")], out=s["c"][:], in_=s["th"][:], func=AF.Sin, scale=1.0 / 32,
      bias=float(np.pi / 2))

    def double_cs():
        tt(s["ta"][:], s["c"][:], s["c"][:], ALU.mult, rd=["c"], wr=["ta"])
        tt(s["tb"][:], s["s"][:], s["s"][:], ALU.mult, rd=["s"], wr=["tb"])
        I("dve", "scalar_tensor_tensor", reads=[r("c"), r("s")], writes=[r("s2")], out=s["s2"][:], in0=s["c"][:], scalar=2.0,
          in1=s["s"][:], op0=ALU.mult, op1=ALU.mult)
        tt(s["c"][:], s["ta"][:], s["tb"][:], ALU.subtract, rd=["ta", "tb"], wr=["c"])
        I("dve", "tensor_copy", reads=[r("s2")], writes=[r("s")], out=s["s"][:], in_=s["s2"][:])

    for _ in range(5):
        double_cs()
    for k in range(7):
        I("dve", "tensor_copy", reads=[r("c")], writes=[r("cm%d" % k)], out=cm[k][:], in_=s["c"][:])
        I("dve", "tensor_copy", reads=[r("s")], writes=[r("sm%d" % k)], out=smm[k][:], in_=s["s"][:])
        if k < 6:
            double_cs()
    c1, s1 = cm[0], smm[0]
    tt(s["abr"][:], s["mag"][:], c1[:], ALU.mult, rd=["mag", "cm0"], wr=["abr"])
    tt(s["abi"][:], s["mag"][:], s1[:], ALU.mult, rd=["mag", "sm0"], wr=["abi"])
    I("dve", "tensor_scalar", reads=[r("abr")], writes=[r("abr1")], out=s["abr1"][:], in0=s["abr"][:], scalar1=-1.0,
      scalar2=None, op0=ALU.add)
    tt(s["ta"][:], s["are"][:], s["are"][:], ALU.mult, rd=["are"], wr=["ta"])
    tt(s["tb"][:], s["aim"][:], s["aim"][:], ALU.mult, rd=["aim"], wr=["tb"])
    tt(s["den"][:], s["ta"][:], s["tb"][:], ALU.add, rd=["ta", "tb"], wr=["den"])
    I("dve", "reciprocal", reads=[r("den")], writes=[r("rden")], out=s["rden"][:], in_=s["den"][:])
    tt(s["u1"][:], s["abr1"][:], s["are"][:], ALU.mult, rd=["abr1", "are"], wr=["u1"])
    tt(s["u2"][:], s["abi"][:], s["aim"][:], ALU.mult, rd=["abi", "aim"], wr=["u2"])
    tt(s["u1"][:], s["u1"][:], s["u2"][:], ALU.add, rd=["u1", "u2"], wr=["u1"])
    tt(s["fre"][:], s["u1"][:], s["rden"][:], ALU.mult, rd=["u1", "rden"], wr=["fre"])
    tt(s["u1"][:], s["abi"][:], s["are"][:], ALU.mult, rd=["abi", "are"], wr=["u1"])
    tt(s["u2"][:], s["abr1"][:], s["aim"][:], ALU.mult, rd=["abr1", "aim"], wr=["u2"])
    tt(s["u1"][:], s["u1"][:], s["u2"][:], ALU.subtract, rd=["u1", "u2"], wr=["u1"])
    tt(s["fim"][:], s["u1"][:], s["rden"][:], ALU.mult, rd=["u1", "rden"], wr=["fim"])
    fre_b = bview(s["fre"][:], 2, [128, 16, 16])
    fim_b = bview(s["fim"][:], 2, [128, 16, 16])
    tt(nat["n1"][:], nat["bre"][:], fre_b, ALU.mult, rd=["bre", "fre"], wr=["n1"])
    tt(nat["n2"][:], nat["bim"][:], fim_b, ALU.mult, rd=["bim", "fim"], wr=["n2"])
    tt(nat["NBre"][:], nat["n1"][:], nat["n2"][:], ALU.subtract, rd=["n1", "n2"], wr=["NBre"])
    tt(nat["n1"][:], nat["bim"][:], fre_b, ALU.mult, rd=["bim", "fre"], wr=["n1"])
    tt(nat["n2"][:], nat["bre"][:], fim_b, ALU.mult, rd=["bre", "fim"], wr=["n2"])
    tt(nat["NBim"][:], nat["n1"][:], nat["n2"][:], ALU.add, rd=["n1", "n2"], wr=["NBim"])
    I("dve", "memset", writes=[r("maskZ")], ap=maskZ[:], constant=0.0)
    for pr in range(16):
        rr = pr % 4
        I("dve", "memset", writes=[r("maskZ")], ap=maskZ[0:64, pr, 2 * rr:2 * rr + 1], constant=1.0)
        I("dve", "memset", writes=[r("maskZ")], ap=maskZ[64:128, pr, 2 * rr + 1:2 * rr + 2], constant=1.0)
    mz_b = bview(maskZ[:], 3, [128, 16, 8, 16])
    LOv = LO[:].rearrange("p a q (g i) -> p a q g i", g=8)
    LDv = LD[:]
    for q, key in enumerate(("NBre", "NBim")):
        tt(Zall, bview(nat[key][:], 2, [128, 16, 8, 16]), mz_b, ALU.mult, rd=[key, "maskZ"], wr=["Zall"])
        for j in range(4):
            for k4 in range(4):
                I("pe", "transpose", reads=[r("Zall"), r("ident")], writes=[r("BC")], out=BC[:, k4, :],
                  in_=Zall[:, 4 * j + k4, :, :].rearrange("p g i -> p (g i)"), identity=ident_f[:])
            I("act", "activation", reads=[r("BC")], writes=[r("LD")], out=LDv[:, 4 * j:4 * j + 4, q, :], in_=BC[:, 0:4, :],
              func=AF.Copy)
    tt(LOv[:, :, 0, :, :], bview(nat["cre"][:], 2, [128, 16, 8, 16]), mz_b, ALU.mult, rd=["cre", "maskZ"], wr=["LO"])
    I("dve", "tensor_scalar", reads=[r("cim")], writes=[r("n1")], out=nat["n1"][:], in0=nat["cim"][:], scalar1=-1.0, scalar2=None,
      op0=ALU.mult)
    tt(LOv[:, :, 1, :, :], bview(nat["n1"][:], 2, [128, 16, 8, 16]), mz_b, ALU.mult, rd=["n1", "maskZ"], wr=["LO"])
    I("dve", "tensor_copy", reads=[r("cm0")], writes=[r("Ec")], out=Ec[:, :, 0:1], in_=bview(cm[0][:], 2, [128, 16, 1]))
    I("dve", "tensor_copy", reads=[r("sm0")], writes=[r("Es")], out=Es[:, :, 0:1], in_=bview(smm[0][:], 2, [128, 16, 1]))
    TLN = ["tmpL%d" % j for j in range(8)]
    tA = tmpL[:].rearrange("p h l -> p (h l)")
    tB = H[:].rearrange("p a q l -> p (a q l)")
    for k in range(7):
        m = 1 << k
        cb = bview(cm[k][:], 2, [128, 16, m])
        sbv = bview(smm[k][:], 2, [128, 16, m])
        vA = tA[:, 0:16 * m].rearrange("p (a m) -> p a m", a=16)
        vB = tB[:, 0:16 * m].rearrange("p (a m) -> p a m", a=16)
        tt(vA, Ec[:, :, 0:m], cb, ALU.mult, rd=["Ec", "cm%d" % k], wr=TLN)
        tt(vB, Es[:, :, 0:m], sbv, ALU.mult, rd=["Es", "sm%d" % k], wr=["H"])
        tt(Ec[:, :, m:2 * m], vA, vB, ALU.subtract, rd=TLN + ["H"], wr=["Ec"])
        tt(vA, Es[:, :, 0:m], cb, ALU.mult, rd=["Es", "cm%d" % k], wr=TLN)
        tt(vB, Ec[:, :, 0:m], sbv, ALU.mult, rd=["Ec", "sm%d" % k], wr=["H"])
        tt(Es[:, :, m:2 * m], vA, vB, ALU.add, rd=TLN + ["H"], wr=["Es"])

    xs_v = xsrc.rearrange("(kc p) t -> p kc t", p=128)
    xd_v = xdst.rearrange("(kc p) t -> p kc t", p=128)
    pjn = [0]

    BCf = BC[:].rearrange("p h l -> p (h l)")
    pjA = [PJ[:, 0, 0:NTE], PJ[:, 1, 0:NTE], BCf[:, 0:NTE], SN[:, 0:NTE], GY[:, 0:NTE]]
    RpjA = [Rpj[0], Rpj[1], r("BC"), r("SN"), r("GY")]

    def nextpj():
        pjn[0] += 1
        return pjn[0] % len(pjA)

    def proj(col0, ncols, rhs_slice=None):
        sl = nextpj()
        for kc in range(KC):
            I("pe", "matmul", reads=[RWin[kc], Rh[kc]], writes=[RpjA[sl]], out=pjA[sl][0:ncols, :],
              lhsT=Win[:, kc, col0:col0 + ncols], rhs=hb[:, kc, :], start=(kc == 0), stop=(kc == KC - 1))
        return sl

    for t in range(ntiles):
        tsl = slice(t * NTE, (t + 1) * NTE)
        if t > 0:
            P.D("sp", reads=aslist(Rxsrc[t]), writes=RX, out=X[:], in_=xs_v[:, :, tsl])
        sl0 = nextpj()
        for kc in range(KC):
            s2 = kc % 2
            I("act", "activation", reads=[RX[kc]], writes=[r("sq%d" % s2)], out=sq[s2][:], in_=X[:, kc, :], func=AF.Square)
            I("pe", "matmul", reads=[r("sq%d" % s2), C.R_ones], writes=[RpjA[sl0]], out=pjA[sl0], lhsT=C.ones_f[:],
              rhs=sq[s2][:], start=(kc == 0), stop=(kc == KC - 1))
        I("act", "activation", reads=[RpjA[sl0]], writes=[r("rt")], out=rt[:], in_=pjA[sl0], func=AF.Ln,
          scale=1.0 / D, bias=EPS)
        I("act", "activation", reads=[r("rt")], writes=[r("rstd")], out=rstd[:], in_=rt[:], func=AF.Exp, scale=-0.5)
        for kc in range(KC):
            I("dve", "scalar_tensor_tensor", reads=[RX[kc], r("gain"), r("rstd")], writes=[Rh[kc]], out=hb[:, kc, :],
              in0=X[:, kc, :], scalar=gain[:, kc:kc + 1], in1=rstd[:], op0=ALU.mult, op1=ALU.mult)
        for c in range(4):
            sl = proj(c * 128, 128)
            I("act", "activation", reads=[RpjA[sl]], writes=[r("zs%d" % c)], out=zs[:, c, :], in_=pjA[sl], func=AF.Silu)
        for c8 in range(8):
            sl = proj(512 + c8 * 128, 128)
            I("act", "activation", reads=[RpjA[sl]], writes=[r("xbc%d" % c8)], out=xbc[:, c8, 3:3 + NTE], in_=pjA[sl],
              func=AF.Copy)
            sl = nextpj()
            for k in range(4):
                I("pe", "matmul", reads=[r("diagw"), r("xbc%d" % c8)], writes=[RpjA[sl]], out=pjA[sl],
                  lhsT=diagw[:, c8, k, :], rhs=xbc[:, c8, k:k + NTE], start=(k == 0), stop=(k == 3))
            if c8 < 4:
                dst, rn = xsT[:, c8, :], "xsT%d" % c8
            elif c8 < 6:
                dst, rn = BT[:, c8 - 4, :], "BT%d" % (c8 - 4)
            else:
                dst, rn = CT[:, c8 - 6, :], "CT%d" % (c8 - 6)
            I("act", "activation", reads=[RpjA[sl], r("convb")], writes=[r(rn)], out=dst, in_=pjA[sl], func=AF.Silu,
              bias=convb[:, c8:c8 + 1])
            I("pool", "tensor_copy", reads=[r("xbc%d" % c8)], writes=[r("xbc%d" % c8)], out=xbc[:, c8, 0:3],
              in_=xbc[:, c8, NTE:NTE + 3])
        sl = proj(1536, 8)
        I("act", "activation", reads=[RpjA[sl], r("dtb")], writes=[r("dte1")], out=dte1[:], in_=pjA[sl][0:8, :], func=AF.Exp,
          bias=dtb[:, 0:1])
        I("act", "activation", reads=[r("dte1")], writes=[r("dtsp")], out=dtsp[:], in_=dte1[:], func=AF.Ln, bias=1.0)
        I("dve", "tensor_scalar", reads=[r("dtsp"), r("aneg")], writes=[r("dA")], out=dA[:], in0=dtsp[:], scalar1=aneg[:, 0:1],
          scalar2=None, op0=ALU.mult)
        for c in range(4):
            sl = proj(1544 + c * 128, 128)
            I("act", "activation", reads=[RpjA[sl]], writes=[r("uT%d" % c)], out=uT[:, c, :], in_=pjA[sl], func=AF.Copy)

        tail_prev = None
        for qc in range(NTE // QC):
            lr = slice(qc * QC, (qc + 1) * QC)

            def seg0():
                I("dve", "tensor_tensor_scan", reads=[r("dA"), C.R_ones], writes=[r("cs")], out=cs[:, lr],
                  data0=C.ones_f[0:8, :], data1=dA[:, lr], initial=0.0, op0=ALU.mult, op1=ALU.add)
                I("pe", "transpose", reads=[r("cs"), r("ident")], writes=[r("csdt_ps")], out=csdt_ps[:, 0:8], in_=cs[:, lr],
                  identity=ident_f[0:8, 0:8])
                I("pe", "transpose", reads=[r("dtsp"), r("ident")], writes=[r("csdt_ps")], out=csdt_ps[:, 8:16], in_=dtsp[:, lr],
                  identity=ident_f[0:8, 0:8])
                I("act", "activation", reads=[r("csdt_ps")], writes=[r("csdt")], out=csdt[:], in_=csdt_ps, func=AF.Copy)
                for hh in range(8):
                    I("pe", "matmul", reads=[r("selH"), r("cs")], writes=[r("BC")], out=BC[:, hh, :], lhsT=selH[:, hh, :],
                      rhs=cs[:, lr], start=True, stop=True)
                for hh in range(8):
                    I("dve", "scalar_tensor_tensor", reads=[r("BC"), r("csdt"), r("maskneg")], writes=[r("tmpL%d" % hh)],
                      out=tmpL[:, hh, :], in0=BC[:, hh, :], scalar=csdt[:, hh:hh + 1], in1=maskneg[:], op0=ALU.subtract,
                      op1=ALU.add)
                I("dve", "tensor_tensor", reads=[r("BC"), r("csdt")], writes=[r("d1")], out=d1[:], in0=BC[:, :, QC - 1],
                  in1=csdt[:, 0:8], op=ALU.subtract)
                I("act", "activation", reads=[r("BC")], writes=[r("ecs")], out=ecs[:], in_=BC[:], func=AF.Exp)
                I("act", "activation", reads=[r("BC")], writes=[r("eend")], out=eend[:], in_=BC[:, :, QC - 1], func=AF.Exp)
                I("act", "activation", reads=[], writes=[r("tmpL%d" % j) for j in range(8)], out=tmpL[:], in_=tmpL[:], func=AF.Exp)
                I("act", "activation", reads=[r("d1")], writes=[r("dte")], out=dte[:], in_=d1[:], func=AF.Exp)
                for g in range(2):
                    I("pe", "matmul", reads=[r("BT%d" % g), r("CT%d" % g)], writes=[r("G")], out=G_ps[:, g, :],
                      lhsT=BT[:, g, lr], rhs=CT[:, g, lr], start=True, stop=True)

            def seg1():
                I("dve", "tensor_tensor", reads=[r("tmpL%d" % j) for j in range(8)] + [r("G")], writes=[r("Mt")],
                  out=Mt[:].rearrange("p (g j) l -> p g j l", g=2), in0=tmpL[:].rearrange("p (g j) l -> p g j l", g=2),
                  in1=bview(G_ps, 2, [128, 2, 4, 128]), op=ALU.mult)
                I("pool", "tensor_tensor", reads=[r("CT0"), r("CT1"), r("ecs")], writes=[r("Cs")],
                  out=Cs[:].rearrange("p (g j) l -> p g j l", g=2), in0=ecs[:].rearrange("p (g j) l -> p g j l", g=2),
                  in1=bview(CT[:, :, lr], 2, [128, 2, 4, 128]), op=ALU.mult)
                for c in range(4):
                    I("pe", "transpose", reads=[r("xsT%d" % c), r("identb")], writes=[r("xs_tok")], out=xs_tok[:, c, :],
                      in_=xsT[:, c, lr], identity=ident[:])
                for g in range(2):
                    I("pe", "transpose", reads=[r("BT%d" % g), r("identb")], writes=[r("Btok_ps")], out=Btok_ps[:, g, :],
                      in_=BT[:, g, lr], identity=ident[:])
                I("dve", "tensor_tensor", reads=[r("xs_tok"), r("csdt")], writes=[r("xdt")],
                  out=xdt[:].rearrange("p (h q) -> p h q", h=8), in0=M1[:, 0:512].rearrange("p (h q) -> p h q", h=8),
                  in1=bview(csdt[:, 8:16], 2, [128, 8, 64]), op=ALU.mult)
                I("act", "activation", reads=[r("Btok_ps")], writes=[r("Btok")], out=Btok[:], in_=Btok_ps, func=AF.Copy)
                I("pool", "tensor_tensor", reads=[r("xdt"), r("dte")], writes=[r("xw")],
                  out=xw[:].rearrange("p (h q) -> p h q", h=8), in0=xdt[:].rearrange("p (h q) -> p h q", h=8),
                  in1=bview(dte[:], 2, [128, 8, 64]), op=ALU.mult)

            def seg_y(g):
                for c in (2 * g, 2 * g + 1):
                    for e in range(2):
                        hh = 2 * c + e
                        ysl = y_ps[e * 64:(e + 1) * 64, :]
                        I("pe", "matmul", reads=[r("xdt"), r("Mt")], writes=[r("y")], out=ysl,
                          lhsT=xdt[:, hh * 64:(hh + 1) * 64], rhs=Mt[:, hh, :], start=True, stop=False)
                        I("pe", "matmul", reads=[r("STb"), r("Cs")], writes=[r("y")], out=ysl,
                          lhsT=STb[:, hh * 64:(hh + 1) * 64], rhs=Cs[:, hh, :], start=False, stop=False)
                        I("pe", "matmul", reads=[r("diagD"), r("xsT%d" % c)], writes=[r("y")], out=ysl,
                          lhsT=diagD[:, c, e * 64:(e + 1) * 64], rhs=xsT[:, c, lr], start=False, stop=True)
                    I("dve", "tensor_tensor", reads=[r("y"), r("zs%d" % c)], writes=[r("yg%d" % c)], out=yg[:, c, :], in0=y_ps,
                      in1=zs[:, c, lr], op=ALU.mult)
                    I("act", "activation", reads=[r("yg%d" % c)], writes=[r("sqy%d" % (c % 2))], out=sqy[c % 2][:],
                      in_=yg[:, c, :], func=AF.Square)
                    I("pe", "matmul", reads=[r("sqy%d" % (c % 2)), C.R_ones], writes=[r("ss")], out=ss_ps[:, g, :],
                      lhsT=C.ones_f[:], rhs=sqy[c % 2][:], start=(c % 2 == 0), stop=(c % 2 == 1))
                I("act", "activation", reads=[r("ss")], writes=[r("rtg")], out=rtg[:], in_=ss_ps[:, g, :], func=AF.Ln,
                  scale=1.0 / 256, bias=EPS)
                I("act", "activation", reads=[r("rtg")], writes=[r("rstdg")], out=rstdg[:], in_=rtg[:], func=AF.Exp, scale=-0.5)
                for c2 in (2 * g, 2 * g + 1):
                    I("dve", "scalar_tensor_tensor", reads=[r("yg%d" % c2), r("ngain"), r("rstdg")], writes=[Ry[c2]],
                      out=yT[:, c2, lr], in0=yg[:, c2, :], scalar=ngain[:, c2:c2 + 1], in1=rstdg[:], op0=ALU.mult,
                      op1=ALU.mult)

            def seg_state():
                for g in range(2):
                    I("pe", "matmul", reads=[r("Btok"), r("xw")], writes=[r("SN")], out=SN[:, g * 256:(g + 1) * 256],
                      lhsT=Btok[:, g, :], rhs=xw[:, g * 256:(g + 1) * 256], start=True, stop=True)
                I("pool", "tensor_tensor", reads=[r("ST"), r("eend")], writes=[r("ST")],
                  out=ST[:].rearrange("p (h q) -> p h q", h=8), in0=ST[:].rearrange("p (h q) -> p h q", h=8),
                  in1=bview(eend[:], 2, [128, 8, 64]), op=ALU.mult)
                I("dve", "tensor_tensor", reads=[r("SN"), r("ST")], writes=[r("ST")], out=ST[:], in0=SN[:], in1=ST[:], op=ALU.add)
                I("act", "activation", reads=[r("ST")], writes=[r("STb")], out=STb[:], in_=ST[:], func=AF.Copy)

            zc = qc % 2

            def s5A(T4, lr=lr):
                p4 = slice(4 * T4, 4 * T4 + 4)
                xz = xmb[T4 % 2]
                for rr in range(4):
                    for q in range(2):
                        I("pe", "matmul", reads=[r("LD"), r("uT%d" % T4)], writes=[Rpj[0], Rpj[1]], out=drv[:, rr, q, :],
                          lhsT=LD[:, 4 * T4 + rr, q, :], rhs=uT[:, T4, lr], start=True, stop=True)
                dre, dim_ = drv[:, :, 0, :], drv[:, :, 1, :]
                Ec4, Es4 = Ec[:, p4, :], Es[:, p4, :]
                tt(t1[:], dre, Ec4, ALU.mult, rd=["pj0", "pj1", "Ec"], wr=["t1"])
                tt(t2[:], dim_, Es4, ALU.mult, rd=["pj0", "pj1", "Es"], wr=["t2"])
                tt(t3[:], dim_, Ec4, ALU.mult, rd=["pj0", "pj1", "Ec"], wr=["t3"])
                tt(t4[:], dre, Es4, ALU.mult, rd=["pj0", "pj1", "Es"], wr=["t4"])
                tt(xz[:, :, 0, :], t1[:], t2[:], ALU.add, eng="pool", rd=["t1", "t2"], wr=["xm%d" % (T4 % 2), "Zall"])
                tt(xz[:, :, 1, :], t3[:], t4[:], ALU.subtract, eng="pool", rd=["t3", "t4"], wr=["xm%d" % (T4 % 2), "Zall"])

            def s5B(T4):
                p4 = slice(4 * T4, 4 * T4 + 4)
                xz = xmb[T4 % 2]
                hz = T4 % 2
                Ec4, Es4 = Ec[:, p4, :], Es[:, p4, :]
                for rr in range(4):
                    pr = 4 * T4 + rr
                    mg = s["mag"][:, pr:pr + 1].broadcast_to([128, QC])
                    I("dve", "tensor_tensor_scan", reads=[r("xm%d" % (T4 % 2)), r("mag"), r("ire%d" % T4)],
                      writes=[r("ht%d_0" % rr), r("Zall")], out=ht[:, rr, 0, :], data0=mg, data1=xz[:, rr, 0, :],
                      initial=Hp[:, pr, 0:1], op0=ALU.mult, op1=ALU.add)
                    I("dve", "tensor_tensor_scan", reads=[r("xm%d" % (T4 % 2)), r("mag"), r("iim%d" % T4)],
                      writes=[r("ht%d_1" % rr), r("Zall")], out=ht[:, rr, 1, :], data0=mg, data1=xz[:, rr, 1, :],
                      initial=Hp[:, pr, 1:2], op0=ALU.mult, op1=ALU.add)
                tt(t5[:], ht[:, :, 0, :], Ec4, ALU.mult, rd=["ht%d_0" % j for j in range(4)] + ["Ec"], wr=["t5"])
                tt(t6[:], ht[:, :, 1, :], Es4, ALU.mult, rd=["ht%d_1" % j for j in range(4)] + ["Es"], wr=["t6"])
                tt(H[:, :, 0, :], t5[:], t6[:], ALU.subtract, eng="pool", rd=["t5", "t6"], wr=["H"])
                tt(t7[:], ht[:, :, 1, :], Ec4, ALU.mult, eng="pool", rd=["ht%d_1" % j for j in range(4)] + ["Ec"], wr=["t7"])
                tt(t8[:], ht[:, :, 0, :], Es4, ALU.mult, eng="pool", rd=["ht%d_0" % j for j in range(4)] + ["Es"], wr=["t8"])
                tt(H[:, :, 1, :], t7[:], t8[:], ALU.add, eng="pool", rd=["t7", "t8"], wr=["H"])
                I("act", "activation", reads=[r("H")], writes=[r("Hb%d" % hz)], out=Hbb[hz][:], in_=H[:], func=AF.Copy)
                I("act", "activation", reads=[r("H")], writes=[r("ire%d" % T4), r("iim%d" % T4)], out=Hp[:, p4, :],
                  in_=H[:, :, :, QC - 1], func=AF.Copy)

            def s5C1(T4, lr=lr, zc=zc):
                hz = T4 % 2
                n = 0
                for rr in range(4):
                    for q in range(2):
                        I("pe", "matmul", reads=[r("LO"), r("Hb%d" % hz)], writes=[r("y5")], out=y5_ps,
                          lhsT=LO[:, 4 * T4 + rr, q, :], rhs=Hbb[hz][:, rr, q, :], start=(n == 0), stop=False)
                        n += 1
                I("pe", "matmul", reads=[r("diagD5"), r("uT%d" % T4)], writes=[r("y5")], out=y5_ps, lhsT=diagD5[:, T4, :],
                  rhs=uT[:, T4, lr], start=False, stop=True)
                I("act", "activation", reads=[r("y5")], writes=[r("y5s%d_%d" % (zc, T4))], out=y5sb[zc][:, T4, :], in_=y5_ps,
                  func=AF.Copy)

            def s5C2(T4, zc=zc):
                yv = y5sb[zc][:, T4, :]
                gz = T4 % 2
                I("act", "activation", reads=[r("y5s%d_%d" % (zc, T4))], writes=[r("ga")], out=ga[:], in_=yv, func=AF.Square,
                  scale=0.21145921496651535)
                I("dve", "scalar_tensor_tensor", reads=[r("ga"), r("y5s%d_%d" % (zc, T4))], writes=[r("gb")], out=gb[:], in0=ga[:],
                  scalar=1.0, in1=yv, op0=ALU.add, op1=ALU.mult)
                I("act", "activation", reads=[r("gb")], writes=[r("gth%d" % gz)], out=gthb[gz][:], in_=gb[:], func=AF.Tanh,
                  scale=0.7978845608028654)

            def s5C3(T4, zc=zc):
                yv = y5sb[zc][:, T4, :]
                gz = T4 % 2
                I("dve", "scalar_tensor_tensor", reads=[r("gth%d" % gz), r("y5s%d_%d" % (zc, T4))], writes=[r("ge%d_%d" % (zc, T4))],
                  out=geb[zc][:, T4, :], in0=gthb[gz][:], scalar=1.0, in1=yv, op0=ALU.add, op1=ALU.mult)

            def epilogue(lr=lr, zc=zc):
                for o in range(4):
                    for T4 in range(4):
                        I("pe", "matmul", reads=[r("Wglu"), r("ge%d_%d" % (zc, T4))], writes=[r("gate")], out=gate_ps,
                          lhsT=Wglu[:, T4, o * 128:(o + 1) * 128], rhs=geb[zc][:, T4, :], start=(T4 == 0), stop=(T4 == 3))
                    I("act", "activation", reads=[r("gate"), r("bglu")], writes=[r("sig")], out=sig[:], in_=gate_ps,
                      func=AF.Sigmoid, scale=0.5, bias=bglu[:, o:o + 1])
                    I("dve", "tensor_tensor", reads=[r("sig"), r("y5s%d_%d" % (zc, o))], writes=[Ry[4 + o]], out=yT[:, 4 + o, lr],
                      in0=y5sb[zc][:, o, :], in1=sig[:], op=ALU.mult)

            s5A(0)
            seg0()
            s5A(1)
            if tail_prev is not None:
                tail_prev()
            s5B(0)
            seg1()
            s5A(2)
            s5B(1)
            s5C1(0)
            seg_y(0)
            s5A(3)
            s5B(2)
            s5C1(1)
            s5C2(0)
            seg_y(1)
            s5B(3)
            s5C1(2)
            s5C2(1)
            s5C3(0)
            seg_state()
            s5C1(3)
            s5C2(2)
            s5C3(1)

            def tail(c2=s5C2, c3=s5C3, ep=epilogue):
                c2(3)
                c3(2)
                c3(3)
                ep()
            tail_prev = tail
        tail_prev()
        tail_prev = None
        for dc in range(KC):
            sl = nextpj()
            for c in range(8):
                I("pe", "matmul", reads=[RWout[c], Ry[c]], writes=[RpjA[sl]], out=pjA[sl],
                  lhsT=Wout[:, c, dc * 128:(dc + 1) * 128], rhs=yT[:, c, :], start=(c == 0), stop=(c == 7))
            I("dve", "tensor_tensor", reads=[RpjA[sl], RX[dc]], writes=[RX[dc]], out=X[:, dc, :], in0=pjA[sl],
              in1=X[:, dc, :], op=ALU.add)
        P.D("sp", reads=RX, writes=aslist(Rxdst[t]), out=xd_v[:, :, tsl], in_=X[:])


DEPTH = 4
SEQ = 4096
NREG = SEQ // 256


def _bias_tiles(rel_bias):
    l = np.arange(128)[:, None]
    k = np.arange(640)[None, :]
    idx = np.clip(512 + l - k, -128, 128) + 128
    return np.ascontiguousarray(np.transpose(rel_bias[:, idx], (1, 0, 2)))


def _col(v):
    n = v.shape[0] // 128
    return np.ascontiguousarray(v.reshape(n, 128).T)


def _even_inputs(inp, i, layer):
    nat = lambda a: np.ascontiguousarray(a.reshape(16, 2, 64).transpose(1, 2, 0).reshape(128, 16))
    return {
        "win": inp['even_w_in'][i], "wout": inp['even_w_out'][i], "wglu": inp['s5_w_glu'][i],
        "gain": _col(inp['mix_norm'][layer]),
        "convw": np.ascontiguousarray(inp['ssd_conv_w'][i].reshape(4, 8, 128).transpose(2, 1, 0)),
        "convb": _col(inp['ssd_conv_b'][i]),
        "dtb": inp['ssd_dt_bias'][i].reshape(8, 1).copy(), "alog": inp['ssd_a_log'][i].reshape(8, 1).copy(),
        "dskip": _col(np.repeat(inp['ssd_d'][i], 64)),
        "ngain": _col(inp['ssd_norm'][i]),
        "d5": _col(inp['s5_d'][i].reshape(-1)),
        "bglu": _col(inp['s5_b_glu'][i]),
        "are": nat(inp['s5_a_re'][i]), "aim": nat(inp['s5_a_im'][i]),
        "ldt": np.ascontiguousarray(
            np.repeat(inp['s5_log_dt'][i].reshape(16, 2, 1), 64, axis=2).transpose(1, 2, 0).reshape(128, 16)),
        "bre": np.ascontiguousarray(inp['s5_b_re'][i].reshape(16, 2, 64, 16).transpose(1, 2, 0, 3).reshape(128, 16, 16)),
        "bim": np.ascontiguousarray(inp['s5_b_im'][i].reshape(16, 2, 64, 16).transpose(1, 2, 0, 3).reshape(128, 16, 16)),
        "cre": np.ascontiguousarray(inp['s5_c_re'][i].reshape(16, 2, 16, 64).transpose(1, 3, 0, 2).reshape(128, 16, 16)),
        "cim": np.ascontiguousarray(inp['s5_c_im'][i].reshape(16, 2, 16, 64).transpose(1, 3, 0, 2).reshape(128, 16, 16)),
    }


def _odd_inputs(inp, i, layer):
    return {
        "win": inp['odd_w_in'][i], "wout": inp['odd_w_out'][i], "pw": inp['pool_w'][i],
        "bt": _bias_tiles(inp['attn_rel_bias'][i]),
        "qg": np.tile(inp['attn_q_norm'][i], 2).reshape(128, 1).copy(),
        "kg": np.tile(inp['attn_k_norm'][i], 2).reshape(128, 1).copy(),
        "psc": _col(inp['pool_scale'][i]),
        "gain": _col(inp['mix_norm'][layer]),
    }


def host_layout(inp):
    m = {"ident": np.eye(128, dtype=np.float32)}
    f1 = (inp['ffn1_w_gate'], inp['ffn1_w_up'], inp['ffn1_w_down'], inp['ffn1_norm'])
    f2 = (inp['ffn2_w_gate'], inp['ffn2_w_up'], inp['ffn2_w_down'], inp['ffn2_norm'])
    for l in range(DEPTH):
        for w, (wg, wu, wd, gn) in ((1, f1), (2, f2)):
            m["f%d_%d_wg" % (w, l)] = wg[l]
            m["f%d_%d_wu" % (w, l)] = wu[l]
            m["f%d_%d_wd" % (w, l)] = wd[l]
            m["f%d_%d_gn" % (w, l)] = _col(gn[l])
        d = _even_inputs(inp, l // 2, l) if l % 2 == 0 else _odd_inputs(inp, l // 2, l)
        for k, v in d.items():
            m["m%d_%s" % (l, k)] = v
    return {k: np.ascontiguousarray(v, dtype=np.float32) for k, v in m.items()}


def build_program(shapes, phases=None, seq=SEQ):
    nc = bass.Bass("TRN2", target_bir_lowering=False)
    di = lambda n, s: nc.dram_tensor(n, list(s), F32, kind="ExternalInput").ap()
    xin = di("xT", [D, seq])
    dd = {k: di(k, s) for k, s in shapes.items()}
    yout = nc.dram_tensor("yT", [D, seq], F32, kind="ExternalOutput").ap()
    xs = nc.dram_tensor("xscratch", [D, seq], F32, kind="Internal").ap()
    if phases is None:
        phases = [(l, k) for l in range(DEPTH) for k in ("f1", "mix", "f2")]
    nreg = seq // 256
    pair = lambda R_: [[R_[2 * t], R_[2 * t + 1]] for t in range(len(R_) // 2)]
    with ExitStack() as es:
        P = Prog(nc, es)
        P.bar_ap = (nc.dram_tensor("bar_src", [1, 16], F32, kind="Internal").ap(),
                    nc.dram_tensor("bar_dst", [1, 16], F32, kind="Internal").ap())
        C = Ctx(nc, P)
        C.es = es
        emit_consts(C)
        Rin, Rsc, Rout = regs("xin", nreg), regs("xsc", nreg), regs("xout", nreg)
        with nc.Block() as block:
            for pi, (l, kind) in enumerate(phases):
                src, Rs = (xin, Rin) if pi == 0 else (xs, Rsc)
                dst, Rd = (yout, Rout) if pi == len(phases) - 1 else (xs, Rsc)
                with ExitStack() as pes:
                    C.es = pes
                    if kind in ("f1", "f2"):
                        w = 1 if kind == "f1" else 2
                        pre = "f%d_%d_" % (w, l)
                        emit_ffn(C, src, dst, pair(Rs), pair(Rd), dd[pre + "wg"], dd[pre + "wu"], dd[pre + "wd"], dd[pre + "gn"], seq // NT)
                    elif l % 2 == 0:
                        sub = {k[len("m%d_" % l):]: v for k, v in dd.items() if k.startswith("m%d_" % l)}
                        sub["ident"] = dd["ident"]
                        emit_even(C, src, dst, Rs, Rd, sub, seq // NTE)
                    else:
                        pre = "m%d_" % l
                        emit_odd(C, src, dst, pair(Rs), pair(Rd), dd[pre + "win"], dd[pre + "wout"], dd[pre + "pw"], dd[pre + "bt"],
                                 dd[pre + "qg"], dd[pre + "kg"], dd[pre + "psc"], dd[pre + "gain"], dd["ident"], seq // NT)
                    P.all_barrier()
                    P.reorder = (kind == "mix" and l % 2 == 0)
                    P.flush(block)
    return nc, P


_CACHE = {}


def kernel(**inputs):
    inp = {k: np.asarray(v) for k, v in inputs.items()}
    x = inp["x"]
    B = x.shape[0]
    wts = host_layout(inp)
    key = "full"
    if key not in _CACHE:
        _CACHE[key] = build_program({k: v.shape for k, v in wts.items()})
    nc, _ = _CACHE[key]
    n_cores = 8
    work = [0, 1, 4, 5][:B]
    zeros = {k: np.zeros_like(v) for k, v in wts.items()}
    zeros["xT"] = np.zeros((x.shape[2], x.shape[1]), np.float32)
    in_maps = []
    for c in range(n_cores):
        if c in work:
            m = dict(wts)
            m["xT"] = np.ascontiguousarray(x[work.index(c)].T)
        else:
            m = zeros
        in_maps.append(m)
    res = run_bass_kernel_spmd(nc, in_maps, core_ids=list(range(n_cores)))
    out = np.stack([np.ascontiguousarray(res.results[c]["yT"].T) for c in work], axis=0)
    return out.astype(np.float32)
```
